# Optimizing a Trainium2 kernel written in Bass

```python
import math
import jax, jax.numpy as jnp
from jax import lax
import numpy as np

D_MODEL = 1024
BATCH = 32
SEQ = 256
DEPTH = 2
DEC_BATCH = 2
DEC_SEQ = 4096
PAST_LEN = 512

GRID_W = 64
D_MIX = D_MODEL
GROUP_W = D_MIX // 4
RWKV_HEADS = 4
RWKV_HD = GROUP_W // RWKV_HEADS
RWKV_LORA_W = 32
RWKV_LORA_A = 32
RWKV_LORA_G = 64
RWKV_GN_EPS = 64e-5
RWKV_PROJ = 3 * GROUP_W + 2 * RWKV_LORA_W + 2 * RWKV_LORA_A + RWKV_LORA_G
RET_HEADS = 4
RET_HD = GROUP_W // RET_HEADS
RET_DECAY_FWD = 5.0
RET_DECAY_BWD = 5.5
RET_PROJ = 4 * GROUP_W
SSD_HEADS = 4
SSD_HD = GROUP_W // SSD_HEADS
SSD_STATE = 64
SSD_GROUPS = 2
SSD_CONV = 3
SSD_CONV_DIM = GROUP_W + 2 * SSD_GROUPS * SSD_STATE
SSD_PROJ = GROUP_W + SSD_CONV_DIM + 2 * SSD_HEADS
DIFF_HEADS = 4
DIFF_VD = GROUP_W // DIFF_HEADS
DIFF_QK = DIFF_VD // 2
DIFF_PROJ = 3 * GROUP_W
P_IN = RWKV_PROJ + RET_PROJ + SSD_PROJ + DIFF_PROJ
FFN_DIM = 2816
FFN_CONV = 3
CHUNK = 128
Q_BLOCK = 128
ROPE_BASE = 10000.0
EPS = 1e-6

kernel_name = 'hybrid_diffusion_prefix_trunk_step'


def rmsnorm(x, g):
    xf = x.astype(jnp.float32)
    y = xf * lax.rsqrt(jnp.mean(xf * xf, axis=-1, keepdims=True) + EPS)
    return y.astype(x.dtype) * g


def head_rmsnorm(y):
    yf = y.astype(jnp.float32)
    return (yf * lax.rsqrt(jnp.mean(yf * yf, axis=-1, keepdims=True) + EPS)).astype(y.dtype)


def group_norm_heads(y, g, b, eps):
    yf = y.astype(jnp.float32)
    mu = jnp.mean(yf, axis=-1, keepdims=True)
    var = jnp.mean(jnp.square(yf - mu), axis=-1, keepdims=True)
    yn = ((yf - mu) * lax.rsqrt(var + eps)).astype(y.dtype)
    return yn.reshape(y.shape[0], y.shape[1], -1) * g + b


def centred_neighbour_avg(x):
    xp = jnp.pad(x, ((0, 0), (1, 1), (0, 0)))
    return 0.5 * (xp[:, :-2] + xp[:, 2:])


def depthwise_conv_centred(x, w, b):
    k = w.shape[0]
    half = k // 2
    t = x.shape[1]
    xp = jnp.pad(x, ((0, 0), (half, half), (0, 0)))
    out = b
    for i in range(k):
        out = out + xp[:, i:i + t] * w[i]
    return out


def flip_t(a):
    return a[:, ::-1]


def axial_rope_tables(rows, d):
    nf = d // 4
    row = jnp.repeat(jnp.arange(rows, dtype=jnp.float32), GRID_W)
    col = jnp.tile(jnp.arange(GRID_W, dtype=jnp.float32), rows)
    freqs = ROPE_BASE ** (-jnp.arange(nf, dtype=jnp.float32) / nf)
    ang = jnp.concatenate([row[:, None] * freqs, col[:, None] * freqs], axis=-1)
    return jnp.cos(ang), jnp.sin(ang)


def apply_rope(x, rope):
    if rope is None:
        return x
    cos, sin = rope
    d2 = cos.shape[-1]
    shape = (1, cos.shape[0]) + (1,) * (x.ndim - 3) + (d2,)
    c = cos.reshape(shape).astype(x.dtype)
    s = sin.reshape(shape).astype(x.dtype)
    x1, x2 = x[..., :d2], x[..., d2:]
    return jnp.concatenate([x1 * c - x2 * s, x2 * c + x1 * s], axis=-1)


def chunked_decay_scan(q, k, v, logdecay, s0):
    b, t, h, dk = q.shape
    dv = v.shape[-1]
    n = t // CHUNK

    def to_chunks(a):
        return jnp.moveaxis(a.reshape((b, n, CHUNK) + a.shape[2:]), 1, 0)

    causal = jnp.tril(jnp.ones((CHUNK, CHUNK), dtype=bool))[None, :, :, None]

    def step(s, inp):
        qc, kc, vc, gc = inp
        cum = jnp.cumsum(gc.astype(jnp.float32), axis=1)
        rel = cum[:, :, None, :] - cum[:, None, :, :]
        decay = jnp.where(causal, jnp.exp(jnp.where(causal, rel, 0.0)), 0.0).astype(qc.dtype)
        scores = jnp.einsum('bihd,bjhd->bijh', qc, kc) * decay
        y = jnp.einsum('bijh,bjhe->bihe', scores, vc)
        y = y + jnp.einsum('bihd,bhde->bihe', qc * jnp.exp(cum).astype(qc.dtype)[..., None], s)
        last = cum[:, -1]
        kw = kc * jnp.exp(last[:, None, :] - cum).astype(kc.dtype)[..., None]
        s_new = s * jnp.exp(last).astype(s.dtype)[:, :, None, None] + jnp.einsum('bjhd,bjhe->bhde', kw, vc)
        return s_new.astype(s.dtype), y

    s_fin, ys = lax.scan(step, s0, (to_chunks(q), to_chunks(k), to_chunks(v), to_chunks(logdecay)))
    return jnp.moveaxis(ys, 0, 1).reshape(b, t, h, dv), s_fin


def rwkv_scan(r, w, k, v, kk, a, s0):
    def step(s, inp):
        r_t, w_t, k_t, v_t, kk_t, a_t = inp
        sa = jnp.einsum('bhvk,bhk->bhv', s, -kk_t)
        s = s * w_t[:, :, None, :] + sa[..., None] * (kk_t * a_t)[:, :, None, :] + v_t[..., None] * k_t[:, :, None, :]
        return s, jnp.einsum('bhvk,bhk->bhv', s, r_t)

    xs = tuple(jnp.moveaxis(a_, 1, 0) for a_ in (r, w, k, v, kk, a))
    s_fin, ys = lax.scan(step, s0, xs)
    return jnp.moveaxis(ys, 0, 1), s_fin


def rwkv_mixer(f, p, s0):
    b, t, _ = f.shape
    f = f + p['rwkv_mu'] * (centred_neighbour_avg(f) - f)
    o1 = 3 * GROUP_W
    r, k, v, wd, ad, gd = jnp.split(
        f, [GROUP_W, 2 * GROUP_W, o1, o1 + 2 * RWKV_LORA_W, o1 + 2 * RWKV_LORA_W + 2 * RWKV_LORA_A], axis=-1)

    def hs(a_):
        return a_.reshape(b, t, RWKV_HEADS, RWKV_HD)

    wd = wd.reshape(b, t, 2, RWKV_LORA_W)
    ad = ad.reshape(b, t, 2, RWKV_LORA_A)
    w_lora = jnp.einsum('btdr,drc->btdc', jnp.tanh(wd), p['rwkv_w_up'])
    decay = jnp.exp(-jnp.exp(-jax.nn.softplus(-(p['rwkv_w0'] + w_lora)) - 0.5))
    a = jax.nn.sigmoid(p['rwkv_a0'] + jnp.einsum('btdr,drc->btdc', ad, p['rwkv_a_up']))
    g = jax.nn.sigmoid(gd) @ p['rwkv_g_up']
    kkf = hs(k * p['rwkv_k_k']).astype(jnp.float32)
    kk = (kkf * lax.rsqrt(jnp.maximum(jnp.sum(kkf * kkf, axis=-1, keepdims=True), 1e-12))).astype(k.dtype)
    rh, vh = hs(r), hs(v)
    a_f, a_b = a[:, :, 0], a[:, :, 1]
    k_f = hs(k * (1.0 + (a_f - 1.0) * p['rwkv_k_a']))
    k_b = hs(k * (1.0 + (a_b - 1.0) * p['rwkv_k_a']))
    y_f, s_f = rwkv_scan(rh, hs(decay[:, :, 0]), k_f, vh, kk, hs(a_f), s0[:, 0])
    y_b, s_b = rwkv_scan(flip_t(rh), flip_t(hs(decay[:, :, 1])), flip_t(k_b), flip_t(vh), flip_t(kk),
                         flip_t(hs(a_b)), s0[:, 1])
    y = group_norm_heads(y_f + flip_t(y_b), p['rwkv_ln_g'], p['rwkv_ln_b'], RWKV_GN_EPS)
    bonus = (jnp.sum(rh * hs(k) * p['rwkv_r_k'], axis=-1, keepdims=True) * vh).reshape(b, t, GROUP_W)
    return (y + bonus) * g, jnp.stack([s_f, s_b], axis=1)


def retention_mixer(f, p, rope, s0):
    b, t, _ = f.shape
    q, k, v, g = jnp.split(f, 4, axis=-1)

    def hs(a_):
        return a_.reshape(b, t, RET_HEADS, RET_HD)

    q = apply_rope(hs(q), rope)
    k = apply_rope(hs(k), rope) * (RET_HD ** -0.5)
    v = hs(v)
    heads = jnp.arange(RET_HEADS, dtype=jnp.float32)
    log_g_f = jnp.broadcast_to(jnp.log(1.0 - 2.0 ** (-RET_DECAY_FWD - heads)), (b, t, RET_HEADS))
    log_g_b = jnp.broadcast_to(jnp.log(1.0 - 2.0 ** (-RET_DECAY_BWD - heads)), (b, t, RET_HEADS))
    y_f, s_f = chunked_decay_scan(q, k, v, log_g_f, s0[:, 0])
    y_b, s_b = chunked_decay_scan(flip_t(q), flip_t(k), flip_t(v), log_g_b, s0[:, 1])
    y = head_rmsnorm(y_f + flip_t(y_b)).reshape(b, t, GROUP_W) * p['ret_ln_g']
    return jax.nn.silu(g) * y, jnp.stack([s_f, s_b], axis=1)


def ssd_mixer(f, p, s0):
    b, t, _ = f.shape
    z, xbc, dt = jnp.split(f, [GROUP_W, GROUP_W + SSD_CONV_DIM], axis=-1)
    xbc = jax.nn.silu(depthwise_conv_centred(xbc, p['ssd_conv_w'], p['ssd_conv_b']))
    x, bm, cm = jnp.split(xbc, [GROUP_W, GROUP_W + SSD_GROUPS * SSD_STATE], axis=-1)
    x = x.reshape(b, t, SSD_HEADS, SSD_HD)
    rep = SSD_HEADS // SSD_GROUPS
    bm = jnp.repeat(bm.reshape(b, t, SSD_GROUPS, SSD_STATE), rep, axis=2)
    cm = jnp.repeat(cm.reshape(b, t, SSD_GROUPS, SSD_STATE), rep, axis=2)
    dt = jax.nn.softplus(dt.reshape(b, t, 2, SSD_HEADS) + p['ssd_dt_bias'])
    a = -jnp.exp(p['ssd_a_log'])
    y_f, s_f = chunked_decay_scan(cm, bm, x * dt[:, :, 0, :, None], a[0] * dt[:, :, 0], s0[:, 0])
    y_b, s_b = chunked_decay_scan(flip_t(cm), flip_t(bm), flip_t(x * dt[:, :, 1, :, None]),
                                  flip_t(a[1] * dt[:, :, 1]), s0[:, 1])
    y = y_f + flip_t(y_b) + x * p['ssd_d'][:, None]
    y = rmsnorm(y.reshape(b, t, GROUP_W) * jax.nn.silu(z), p['ssd_norm_g'])
    return y, jnp.stack([s_f, s_b], axis=1)


def diff_attention(q, k, v, lam):
    b, t, h, _, dq = q.shape
    nb = t // Q_BLOCK
    qb = jnp.moveaxis(q.reshape(b, nb, Q_BLOCK, h, 2, dq), 1, 0)
    scale = dq ** -0.5

    def block(qi):
        s = jnp.einsum('bqhmd,bshmd->bhmqs', qi, k).astype(jnp.float32) * scale
        pr = jax.nn.softmax(s, axis=-1)
        attn = pr[:, :, 0] - lam * pr[:, :, 1]
        return jnp.einsum('bhqs,bshe->bqhe', attn.astype(v.dtype), v)

    o = lax.map(block, qb)
    return jnp.moveaxis(o, 0, 1).reshape(b, t, h, v.shape[-1])


def diff_mixer(f, p, lam_init, rope, ctx_k, ctx_v):
    b, t, _ = f.shape
    q, k, v = jnp.split(f, 3, axis=-1)
    q = q.reshape(b, t, DIFF_HEADS, 2, DIFF_QK)
    k = k.reshape(b, t, DIFF_HEADS, 2, DIFF_QK)
    v = v.reshape(b, t, DIFF_HEADS, DIFF_VD)
    lp = p['diff_lambda'].astype(jnp.float32)
    lam = jnp.exp(jnp.sum(lp[0] * lp[1])) - jnp.exp(jnp.sum(lp[2] * lp[3])) + lam_init
    qr = apply_rope(q, rope)
    kr = apply_rope(k, rope)
    if ctx_k is None:
        keys, vals = kr, v
    else:
        keys = jnp.concatenate([kr, ctx_k], axis=1)
        vals = jnp.concatenate([v, ctx_v], axis=1)
    o = diff_attention(qr, keys, vals, lam)
    o = head_rmsnorm(o) * p['diff_subln_g'] * (1.0 - lam_init)
    return o.reshape(b, t, GROUP_W), k, v


def conv_ffn(h, p):
    u = depthwise_conv_centred(h @ p['ffn_w_up'], p['ffn_conv_w'], p['ffn_conv_b'])
    gate, val = jnp.split(u, 2, axis=-1)
    return (jax.nn.silu(gate) * val) @ p['ffn_w_down']


def trunk_layer(x, mod, p, lam_init, rope_ret, rope_diff, s_rwkv, s_ret, s_ssd, ctx_k, ctx_v):
    shift1, scale1, gate1, shift2, scale2, gate2 = jnp.split(mod, 6, axis=-1)
    h = rmsnorm(x, p['norm1_g']) * (1.0 + scale1) + shift1
    f = h @ p['w_in']
    fa, fb, fc, fd = jnp.split(f, [RWKV_PROJ, RWKV_PROJ + RET_PROJ, RWKV_PROJ + RET_PROJ + SSD_PROJ], axis=-1)
    oa, sa = rwkv_mixer(fa, p, s_rwkv)
    ob, sb = retention_mixer(fb, p, rope_ret, s_ret)
    oc, sc = ssd_mixer(fc, p, s_ssd)
    od, kd, vd = diff_mixer(fd, p, lam_init, rope_diff, ctx_k, ctx_v)
    x = x + gate1 * (jnp.concatenate([oa, ob, oc, od], axis=-1) @ p['w_out'])
    h = rmsnorm(x, p['norm2_g']) * (1.0 + scale2) + shift2
    x = x + gate2 * conv_ffn(h, p)
    return x, sa, sb, sc, kd, vd


def setup_inputs(seed: int = 0) -> dict:
    key = jax.random.key(seed)
    ks = iter(jax.random.split(key, 64))
    f32 = jnp.float32

    def nrm(shape, scale):
        return jax.random.normal(next(ks), shape, f32) * scale

    def unif(shape, lo, hi):
        return jax.random.uniform(next(ks), shape, f32, lo, hi)

    x_prompt = nrm((BATCH, SEQ, D_MODEL), 1.0)
    x_sample = nrm((DEC_BATCH, DEC_SEQ, D_MODEL), 1.0)
    state_rwkv = nrm((DEC_BATCH, DEPTH, 2, RWKV_HEADS, RWKV_HD, RWKV_HD), 0.3)
    state_ret = nrm((DEC_BATCH, DEPTH, 2, RET_HEADS, RET_HD, RET_HD), 0.3)
    state_ssd = nrm((DEC_BATCH, DEPTH, 2, SSD_HEADS, SSD_STATE, SSD_HD), 0.3)
    cache_diff_k = nrm((DEC_BATCH, DEPTH, PAST_LEN, DIFF_HEADS, 2, DIFF_QK), 1.0)
    cache_diff_v = nrm((DEC_BATCH, DEPTH, PAST_LEN, DIFF_HEADS, DIFF_VD), 1.0)
    c = nrm((DEC_BATCH, D_MODEL), 1.0)
    c_ctx = nrm((D_MODEL,), 1.0)
    norm1_g = 1.0 + nrm((DEPTH, D_MODEL), 0.02)
    norm2_g = 1.0 + nrm((DEPTH, D_MODEL), 0.02)
    w_mod = nrm((DEPTH, D_MODEL, 6 * D_MODEL), 0.5 * D_MODEL ** -0.5)
    b_mod = nrm((DEPTH, 6 * D_MODEL), 0.02)
    w_in = nrm((DEPTH, D_MODEL, P_IN), D_MODEL ** -0.5)
    w_out = nrm((DEPTH, D_MIX, D_MODEL), D_MIX ** -0.5)
    rwkv_mu = unif((DEPTH, RWKV_PROJ), 0.2, 0.8)
    rwkv_w0 = nrm((DEPTH, 2, GROUP_W), 0.5)
    rwkv_w_up = nrm((DEPTH, 2, RWKV_LORA_W, GROUP_W), 0.1)
    rwkv_a0 = nrm((DEPTH, 2, GROUP_W), 0.1)
    rwkv_a_up = nrm((DEPTH, 2, RWKV_LORA_A, GROUP_W), 0.1)
    rwkv_g_up = nrm((DEPTH, RWKV_LORA_G, GROUP_W), RWKV_LORA_G ** -0.5)
    rwkv_k_k = 0.85 + nrm((DEPTH, GROUP_W), 0.02)
    rwkv_k_a = 1.0 + nrm((DEPTH, GROUP_W), 0.02)
    rwkv_r_k = nrm((DEPTH, RWKV_HEADS, RWKV_HD), 0.1)
    rwkv_ln_g = 1.0 + nrm((DEPTH, GROUP_W), 0.02)
    rwkv_ln_b = nrm((DEPTH, GROUP_W), 0.02)
    ret_ln_g = 1.0 + nrm((DEPTH, GROUP_W), 0.02)
    ssd_conv_w = nrm((DEPTH, SSD_CONV, SSD_CONV_DIM), SSD_CONV ** -0.5)
    ssd_conv_b = nrm((DEPTH, SSD_CONV_DIM), 0.02)
    dt0 = jnp.exp(unif((DEPTH, 2, SSD_HEADS), math.log(1e-3), math.log(1e-1)))
    ssd_dt_bias = dt0 + jnp.log(-jnp.expm1(-dt0))
    ssd_a_log = jnp.log(unif((DEPTH, 2, SSD_HEADS), 1.0, 16.0))
    ssd_d = 1.0 + nrm((DEPTH, SSD_HEADS), 0.02)
    ssd_norm_g = 1.0 + nrm((DEPTH, GROUP_W), 0.02)
    diff_lambda = nrm((DEPTH, 4, DIFF_QK), 0.1)
    diff_subln_g = 1.0 + nrm((DEPTH, DIFF_VD), 0.02)
    ffn_w_up = nrm((DEPTH, D_MODEL, 2 * FFN_DIM), D_MODEL ** -0.5)
    ffn_conv_w = nrm((DEPTH, FFN_CONV, 2 * FFN_DIM), FFN_CONV ** -0.5)
    ffn_conv_b = nrm((DEPTH, 2 * FFN_DIM), 0.02)
    ffn_w_down = nrm((DEPTH, FFN_DIM, D_MODEL), FFN_DIM ** -0.5)
    norm_f_g = 1.0 + nrm((D_MODEL,), 0.02)
    return {'x_prompt': x_prompt, 'x_sample': x_sample, 'state_rwkv': state_rwkv, 'state_ret': state_ret,
            'state_ssd': state_ssd, 'cache_diff_k': cache_diff_k, 'cache_diff_v': cache_diff_v,
            'c': c, 'c_ctx': c_ctx, 'norm1_g': norm1_g, 'norm2_g': norm2_g, 'w_mod': w_mod, 'b_mod': b_mod,
            'w_in': w_in, 'w_out': w_out, 'rwkv_mu': rwkv_mu, 'rwkv_w0': rwkv_w0, 'rwkv_w_up': rwkv_w_up,
            'rwkv_a0': rwkv_a0, 'rwkv_a_up': rwkv_a_up, 'rwkv_g_up': rwkv_g_up, 'rwkv_k_k': rwkv_k_k,
            'rwkv_k_a': rwkv_k_a, 'rwkv_r_k': rwkv_r_k, 'rwkv_ln_g': rwkv_ln_g, 'rwkv_ln_b': rwkv_ln_b,
            'ret_ln_g': ret_ln_g, 'ssd_conv_w': ssd_conv_w, 'ssd_conv_b': ssd_conv_b, 'ssd_dt_bias': ssd_dt_bias,
            'ssd_a_log': ssd_a_log, 'ssd_d': ssd_d, 'ssd_norm_g': ssd_norm_g, 'diff_lambda': diff_lambda,
            'diff_subln_g': diff_subln_g, 'ffn_w_up': ffn_w_up, 'ffn_conv_w': ffn_conv_w, 'ffn_conv_b': ffn_conv_b,
            'ffn_w_down': ffn_w_down, 'norm_f_g': norm_f_g}


def reference(x_prompt, x_sample, state_rwkv, state_ret, state_ssd, cache_diff_k, cache_diff_v, c, c_ctx,
              norm1_g, norm2_g, w_mod, b_mod, w_in, w_out, rwkv_mu, rwkv_w0, rwkv_w_up, rwkv_a0, rwkv_a_up,
              rwkv_g_up, rwkv_k_k, rwkv_k_a, rwkv_r_k, rwkv_ln_g, rwkv_ln_b, ret_ln_g, ssd_conv_w, ssd_conv_b,
              ssd_dt_bias, ssd_a_log, ssd_d, ssd_norm_g, diff_lambda, diff_subln_g, ffn_w_up, ffn_conv_w,
              ffn_conv_b, ffn_w_down, norm_f_g):
    rows = x_sample.shape[1] // GRID_W
    rope_ret = axial_rope_tables(rows, RET_HD)
    rope_diff = axial_rope_tables(rows, DIFF_QK)
    bp = x_prompt.shape[0]
    dtp = x_prompt.dtype
    xp, xs = x_prompt, x_sample
    new_rwkv, new_ret, new_ssd, new_k, new_v = [], [], [], [], []
    for l in range(DEPTH):
        p = {'norm1_g': norm1_g[l], 'norm2_g': norm2_g[l], 'w_in': w_in[l], 'w_out': w_out[l],
             'rwkv_mu': rwkv_mu[l], 'rwkv_w0': rwkv_w0[l], 'rwkv_w_up': rwkv_w_up[l], 'rwkv_a0': rwkv_a0[l],
             'rwkv_a_up': rwkv_a_up[l], 'rwkv_g_up': rwkv_g_up[l], 'rwkv_k_k': rwkv_k_k[l],
             'rwkv_k_a': rwkv_k_a[l], 'rwkv_r_k': rwkv_r_k[l], 'rwkv_ln_g': rwkv_ln_g[l],
             'rwkv_ln_b': rwkv_ln_b[l], 'ret_ln_g': ret_ln_g[l], 'ssd_conv_w': ssd_conv_w[l],
             'ssd_conv_b': ssd_conv_b[l], 'ssd_dt_bias': ssd_dt_bias[l], 'ssd_a_log': ssd_a_log[l],
             'ssd_d': ssd_d[l], 'ssd_norm_g': ssd_norm_g[l], 'diff_lambda': diff_lambda[l],
             'diff_subln_g': diff_subln_g[l], 'ffn_w_up': ffn_w_up[l], 'ffn_conv_w': ffn_conv_w[l],
             'ffn_conv_b': ffn_conv_b[l], 'ffn_w_down': ffn_w_down[l]}
        lam_init = 0.8 - 0.6 * math.exp(-0.3 * l)
        mod_ctx = (jax.nn.silu(c_ctx) @ w_mod[l] + b_mod[l])[None, None, :]
        mod_lat = (jax.nn.silu(c) @ w_mod[l] + b_mod[l])[:, None, :]
        xp, sa, sb, sc, kd, vd = trunk_layer(
            xp, mod_ctx, p, lam_init, None, None,
            jnp.zeros((bp, 2, RWKV_HEADS, RWKV_HD, RWKV_HD), dtp),
            jnp.zeros((bp, 2, RET_HEADS, RET_HD, RET_HD), dtp),
            jnp.zeros((bp, 2, SSD_HEADS, SSD_STATE, SSD_HD), dtp), None, None)
        new_rwkv.append(sa)
        new_ret.append(sb)
        new_ssd.append(sc)
        new_k.append(kd)
        new_v.append(vd)
        xs, _, _, _, _, _ = trunk_layer(
            xs, mod_lat, p, lam_init, rope_ret, rope_diff, state_rwkv[:, l], state_ret[:, l],
            state_ssd[:, l], cache_diff_k[:, l], cache_diff_v[:, l])
    y_prompt = rmsnorm(xp, norm_f_g)
    y_sample = rmsnorm(xs, norm_f_g)
    new_state_rwkv = jnp.stack(new_rwkv, axis=1)
    new_state_ret = jnp.stack(new_ret, axis=1)
    new_state_ssd = jnp.stack(new_ssd, axis=1)
    new_cache_diff_k = jnp.stack(new_k, axis=1)
    new_cache_diff_v = jnp.stack(new_v, axis=1)
    return (y_prompt, y_sample, new_state_rwkv, new_state_ret, new_state_ssd, new_cache_diff_k, new_cache_diff_v)
```

```python
import math
import numpy as np
from contextlib import ExitStack
import concourse.bass as bass
import concourse.mybir as mybir
from concourse.bass_utils import run_bass_kernel_spmd

F32 = mybir.dt.float32
BF16 = mybir.dt.bfloat16
ALU = mybir.AluOpType
AF = mybir.ActivationFunctionType
PE, ACT, DVE, POOL, SP = 'pe', 'act', 'dve', 'pool', 'sp'

D = 1024
L = 2
TP = 256
TS = 4096
NPS = 4
TTOT = NPS * TP + TS
SEQS = [('p', i, TP, i * TP) for i in range(NPS)] + [('s', 0, TS, NPS * TP)]
NFM = 32
NTM = 776
FFN = 2816
EPS = 1e-6
MIXERS = ('ret', 'diff', 'ssd', 'rwkv')
RWKV_STAGES = 3
RW_DBG = {'seqs': 5, 'heads': 4, 'stop': 99, 'var': 0}


class Dep:
    __slots__ = ('name', 'lw', 'rd', 'dsem', 'sb', 'ps')

    def __init__(self, name='', sb=False):
        self.name = name
        self.lw = None
        self.rd = {}
        self.dsem = None
        self.sb = sb
        self.ps = False


class Rec:
    def __init__(self):
        self.call = None

    def __getattr__(self, name):
        def f(*a, **kw):
            self.call = (name, a, kw)
        return f


class Prog:
    def __init__(self, nc, es):
        self.nc = nc
        self.es = es
        self.ph = None
        self.q = {PE: [], ACT: [], DVE: [], POOL: [], SP: []}
        self.cnt = {PE: 0, ACT: 0, DVE: 0, POOL: 0}
        self.seen = {k: {} for k in self.q}
        self.sems = {}
        self.ndsem = 0
        for e in (PE, ACT, DVE, POOL):
            self.sems[e] = es.enter_context(nc.semaphore('s_' + e))
        self.dtot = {}
        self.nuniq = 0
        self.shared_dsem = None

    def sb(self, name, shape, dt=F32):
        self.nuniq += 1
        return self.ph.enter_context(self.nc.sbuf_tensor('%s_%d' % (name, self.nuniq), list(shape), dt))

    def psum(self, name, shape, dt=F32):
        self.nuniq += 1
        return self.ph.enter_context(self.nc.psum_tensor('%s_%d' % (name, self.nuniq), list(shape), dt))

    def dram(self, name, shape, dt=F32):
        return self.nc.dram_tensor(name, list(shape), dt, kind="Internal").ap()

    def _dsem(self, d):
        if d.dsem is None:
            d.dsem = 'd%d' % self.ndsem
            self.ndsem += 1
            if d.dsem not in self.sems:
                self.sems[d.dsem] = self.es.enter_context(self.nc.semaphore(d.dsem))
        return d.dsem

    def _waits(self, eng, R, W):
        waits = {}

        def add(k, v):
            if k == eng and eng == PE:
                return
            if k in self.dtot:
                v = self.dtot[k]
            if waits.get(k, 0) < v:
                waits[k] = v
        for d in R:
            if d.lw is not None:
                add(*d.lw)
            if d.ps:
                for k, v in d.rd.items():
                    if k != eng:
                        add(k, v)
        for d in W:
            if d.lw is not None and d.lw[0] != eng:
                add(*d.lw)
            for k, v in d.rd.items():
                if k != eng:
                    add(k, v)
        out = []
        seen = self.seen[eng]
        for k, v in waits.items():
            if seen.get(k, 0) < v:
                seen[k] = v
                out.append((k, v))
        return out

    def op(self, eng, fn, R=(), W=()):
        waits = self._waits(eng, R, W)
        idx = self.cnt[eng]
        self.cnt[eng] = idx + 1
        rec = Rec()
        fn(rec)
        name, a, kw = rec.call

        def fn2(e, name=name, a=a, kw=kw):
            return getattr(e, name)(*a, **kw)
        self.q[eng].append((waits, fn2, (eng, 1)))
        for d in R:
            if d.rd.get(eng, 0) < idx + 1:
                d.rd[eng] = idx + 1
        for d in W:
            d.lw = (eng, idx + 1)
            d.rd = {}

    def dma(self, queue, out, in_, R=(), W=(), **kw):
        waits = self._waits(queue, R, W)
        tgt = (list(W) + list(R))
        d0 = None
        for d in tgt:
            if d.sb:
                d0 = d
                break
        if d0 is None:
            d0 = tgt[0]
        k0 = self._dsem(d0)
        self.dtot[k0] = self.dtot.get(k0, 0) + 16
        tot = self.dtot[k0]

        def fn(e, out=out, in_=in_, kw=kw):
            return e.dma_start(out=out, in_=in_, **kw)
        self.q[queue].append((waits, fn, (k0, 16)))
        for d in W:
            d.lw = (k0, tot)
            d.rd = {}
        for d in R:
            if d.rd.get(k0, 0) < tot:
                d.rd[k0] = tot

    def barrier(self):
        allk = [(k, self.cnt[k]) for k in (PE, ACT, DVE, POOL)] + list(self.dtot.items())
        for eng in self.q:
            waits = []
            seen = self.seen[eng]
            for k, v in allk:
                if k == eng:
                    continue
                if seen.get(k, 0) < v:
                    seen[k] = v
                    waits.append((k, v))
            self.q[eng].append((waits, None, None))

    def begin(self):
        self.ph = ExitStack()
        self.ndsem = 8 if self.ndsem >= 8 else self.ndsem

    def end(self, final=False):
        self.barrier()
        nc = self.nc
        q = self.q
        sems = self.sems

        with nc.Block() as block:
            def run(e, name):
                for waits, fn, inc in q[name]:
                    for k, v in waits:
                        e.wait_ge(sems[k], v)
                    if fn is not None:
                        ins = fn(e)
                        ins.then_inc(sems[inc[0]], inc[1])

            @block.tensor
            def _(e):
                run(e, PE)

            @block.scalar
            def _(e):
                run(e, ACT)

            @block.vector
            def _(e):
                run(e, DVE)

            @block.gpsimd
            def _(e):
                run(e, POOL)

            @block.sync
            def _(e):
                run(e, SP)
        for k in q:
            q[k] = []
        self.ph.close()
        self.ph = None


class T:
    def __init__(self, t, name=''):
        self.t = t
        self.d = Dep(name, sb=True)

    def __getitem__(self, idx):
        return self.t[idx]


def TPS(t):
    x = T(t)
    x.d.ps = True
    x.d.sb = False
    return x


class Ring:
    def __init__(self, tiles):
        self.tiles = tiles
        self.i = 0

    def next(self):
        t = self.tiles[self.i % len(self.tiles)]
        self.i += 1
        return t


def fm_cols(v):
    v = np.asarray(v, np.float32)
    return np.ascontiguousarray(v.reshape(-1, 128).T)


def build_perm():
    RW, RET, SSD, DIF = 0, 960, 1984, 2760
    fm = []
    fm += list(range(RW, RW + 768))
    fm += list(range(RW + 768, RW + 896))
    fm += list(range(RW + 896, RW + 960)) + [-1] * 64
    q = list(range(RET, RET + 256))
    k = list(range(RET + 256, RET + 512))

    def swap(cols, blk):
        out = []
        for i in range(0, len(cols), blk):
            b = cols[i:i + blk]
            out += b[blk // 2:] + b[:blk // 2]
        return out
    fm += q + k + swap(q, 64) + swap(k, 64)
    fm += list(range(RET + 768, RET + 1024))
    fm += list(range(SSD, SSD + 256))
    fm += list(range(SSD + 256, SSD + 768))
    dq = list(range(DIF, DIF + 256))
    dk = list(range(DIF + 256, DIF + 512))
    fm += dq + dk + swap(dq, 32) + swap(dk, 32)
    assert len(fm) == NFM * 128
    tm = list(range(RET + 512, RET + 768)) + list(range(DIF + 512, DIF + 768)) + dk + list(range(SSD + 768, SSD + 776))
    assert len(tm) == NTM
    return np.array(fm), np.array(tm)


PPC = {}
_o = 0
for _n, _w in [('n1g', 8), ('n2g', 8), ('bmod', 48), ('mu', 8), ('w0', 4), ('a0', 4), ('k_k', 2), ('k_a', 2),
               ('r_k', 2), ('ln_g', 2), ('ln_b', 2), ('ret_g', 2), ('scw', 12), ('scb', 4), ('ssd_g', 2), ('ssd_d', 2),
               ('dtb', 8), ('alog', 8), ('subg', 1), ('lp', 128), ('fcw', 132), ('fcb', 44), ('w_up', 256),
               ('a_up', 256), ('g_up', 256), ('nfg', 8), ('ret_g4', 4), ('scw64', 24), ('scb64', 8), ('ssd_d4', 4), ('ssd_g4', 4), ('mu64', 16), ('w0_64', 8), ('a0_64', 8), ('kk64', 4), ('ka64', 4), ('rk64', 4), ('lng64', 4), ('lnb64', 4)]:
    PPC[_n] = (_o, _w)
    _o += _w
NPP = _o


def pack_params(inp, l):
    pp = np.zeros((128, NPP), np.float32)

    def put(name, arr):
        o, w = PPC[name]
        arr = np.asarray(arr, np.float32)
        assert arr.shape[1] == w, (name, arr.shape, w)
        pp[:arr.shape[0], o:o + w] = arr
    put('n1g', fm_cols(inp['norm1_g'][l]))
    put('n2g', fm_cols(inp['norm2_g'][l]))
    put('bmod', fm_cols(inp['b_mod'][l]))
    mu = np.zeros(1024, np.float32)
    mu[:960] = inp['rwkv_mu'][l]
    put('mu', fm_cols(mu))
    put('w0', fm_cols(inp['rwkv_w0'][l].reshape(-1)))
    put('a0', fm_cols(inp['rwkv_a0'][l].reshape(-1)))
    put('k_k', fm_cols(inp['rwkv_k_k'][l]))
    put('k_a', fm_cols(inp['rwkv_k_a'][l]))
    put('r_k', fm_cols(inp['rwkv_r_k'][l].reshape(-1)))
    put('ln_g', fm_cols(inp['rwkv_ln_g'][l]))
    put('ln_b', fm_cols(inp['rwkv_ln_b'][l]))
    put('ret_g', fm_cols(inp['ret_ln_g'][l]))
    cw = inp['ssd_conv_w'][l]
    put('scw', np.concatenate([fm_cols(cw[i]) for i in range(3)], axis=1))
    put('scb', fm_cols(inp['ssd_conv_b'][l]))
    put('ssd_g', fm_cols(inp['ssd_norm_g'][l]))
    put('ssd_d', fm_cols(np.repeat(inp['ssd_d'][l], 64)))
    put('dtb', np.broadcast_to(inp['ssd_dt_bias'][l].reshape(1, 8), (128, 8)))
    put('alog', np.broadcast_to(inp['ssd_a_log'][l].reshape(1, 8), (128, 8)))
    put('subg', np.concatenate([inp['diff_subln_g'][l], inp['diff_subln_g'][l]]).reshape(128, 1))
    put('lp', np.broadcast_to(inp['diff_lambda'][l].reshape(1, 128), (128, 128)))
    fw = inp['ffn_conv_w'][l]
    put('fcw', np.concatenate([fm_cols(fw[i]) for i in range(3)], axis=1))
    put('fcb', fm_cols(inp['ffn_conv_b'][l]))
    put('w_up', inp['rwkv_w_up'][l].reshape(64, 256))
    put('a_up', inp['rwkv_a_up'][l].reshape(64, 256))
    put('g_up', inp['rwkv_g_up'][l].reshape(64, 256))
    put('nfg', fm_cols(inp['norm_f_g']))
    put('ret_g4', inp['ret_ln_g'][l].reshape(4, 64).T)
    put('scw64', np.concatenate([cw[i].reshape(8, 64).T for i in range(3)], axis=1))
    put('scb64', inp['ssd_conv_b'][l].reshape(8, 64).T)
    put('ssd_d4', np.broadcast_to(inp['ssd_d'][l].reshape(1, 4), (64, 4)))
    put('ssd_g4', inp['ssd_norm_g'][l].reshape(4, 64).T)
    put('mu64', mu.reshape(16, 64).T)
    put('w0_64', inp['rwkv_w0'][l].reshape(8, 64).T)
    put('a0_64', inp['rwkv_a0'][l].reshape(8, 64).T)
    put('kk64', inp['rwkv_k_k'][l].reshape(4, 64).T)
    put('ka64', inp['rwkv_k_a'][l].reshape(4, 64).T)
    put('rk64', inp['rwkv_r_k'][l].reshape(4, 64).T)
    put('lng64', inp['rwkv_ln_g'][l].reshape(4, 64).T)
    put('lnb64', inp['rwkv_ln_b'][l].reshape(4, 64).T)
    return pp


CPC = {}
_o = 0
for _n, _w in [('ident', 128), ('ones', 128), ('bd64', 128), ('tri_f', 128), ('tri_b', 128), ('nm_f', 128), ('nm_b', 128),
               ('ret_ds', 512), ('ret_df', 512), ('ret_db', 512), ('ret_kwf', 4), ('ret_kwb', 4), ('ret_g128', 8),
               ('eps', 1), ('rw_ms', 64), ('rw_mi', 64), ('rw_msT', 64), ('rw_miT', 64), ('scanmask', 512), ('rw_mask5', 320)]:
    CPC[_n] = (_o, _w)
    _o += _w
NCP = _o


def build_consts():
    cp = np.zeros((128, NCP), np.float32)

    def put(name, arr):
        o, w = CPC[name]
        arr = np.asarray(arr, np.float32)
        assert arr.shape[1] == w
        cp[:arr.shape[0], o:o + w] = arr
    i = np.arange(128)
    put('ident', np.eye(128))
    put('ones', np.ones((128, 128)))
    bd = np.zeros((128, 128))
    bd[:64, :64] = 1
    bd[64:, 64:] = 1
    put('bd64', bd)
    put('tri_f', (i[:, None] <= i[None, :]).astype(np.float32))
    put('tri_b', (i[:, None] >= i[None, :]).astype(np.float32))
    put('nm_f', np.where(i[None, :] >= i[:, None], 0.0, -30000.0))
    put('nm_b', np.where(i[None, :] <= i[:, None], 0.0, -30000.0))
    ds = np.zeros((128, 512))
    df = np.zeros((128, 512))
    db = np.zeros((128, 512))
    kwf = np.zeros((128, 4))
    kwb = np.zeros((128, 4))
    g128 = np.zeros((128, 8))
    for h in range(4):
        lf = math.log(1.0 - 2.0 ** (-5.0 - h))
        lb = math.log(1.0 - 2.0 ** (-5.5 - h))
        jj, ii = i[:, None], i[None, :]
        m = np.where(ii >= jj, np.exp(lf * (ii - jj)), 0.0) + np.where(ii <= jj, np.exp(lb * (jj - ii)), 0.0)
        ds[:, h * 128:(h + 1) * 128] = m
        df[:, h * 128:(h + 1) * 128] = np.exp(lf * (i + 1))[None, :]
        db[:, h * 128:(h + 1) * 128] = np.exp(lb * (128 - i))[None, :]
        kwf[:, h] = np.exp(lf * (127 - i))
        kwb[:, h] = np.exp(lb * i)
        g128[:, h] = math.exp(lf * 128)
        g128[:, 4 + h] = math.exp(lb * 128)
    put('ret_ds', ds)
    put('ret_df', df)
    put('ret_db', db)
    put('ret_kwf', kwf)
    put('ret_kwb', kwb)
    put('ret_g128', g128)
    put('eps', np.full((128, 1), EPS))
    c = np.arange(64)
    put('rw_ms', (c[None, :] < c[:, None]).astype(np.float32))
    put('rw_mi', (c[None, :] <= c[:, None]).astype(np.float32))
    put('rw_msT', (c[:, None] < c[None, :]).astype(np.float32))
    put('rw_miT', (c[:, None] <= c[None, :]).astype(np.float32))
    sm = np.ones((128, 512))
    sm[:, ::64] = 0.0
    put('scanmask', sm)
    msT = (c[:, None] < c[None, :]).astype(np.float32)
    miT = (c[:, None] <= c[None, :]).astype(np.float32)
    ms = (c[None, :] < c[:, None]).astype(np.float32)
    put('rw_mask5', np.concatenate([msT, miT, msT, miT, ms], axis=1))
    return cp


def rope_tables():
    rows = TS // 64

    def tabs(d):
        nf = d // 4
        row = np.repeat(np.arange(rows, dtype=np.float32), 64)
        col = np.tile(np.arange(64, dtype=np.float32), rows)
        freqs = (np.float32(10000.0) ** (-np.arange(nf, dtype=np.float32) / nf)).astype(np.float32)
        ang = np.concatenate([row[:, None] * freqs, col[:, None] * freqs], axis=-1).astype(np.float32)
        return np.cos(ang).T.astype(np.float32), np.sin(ang).T.astype(np.float32)
    c, s = tabs(64)
    ret_c = np.concatenate([c, c, c, c], 0)
    ret_s = np.concatenate([-s, s, -s, s], 0)
    c, s = tabs(32)
    dif_c = np.concatenate([c, c] * 4, 0)
    dif_s = np.concatenate([-s, s] * 4, 0)
    return np.stack([ret_c, ret_s, dif_c, dif_s]).astype(np.float32)


def build(debug=False):
    nc = bass.Bass("TRN2", target_bir_lowering=False)

    def din(name, shape):
        return nc.dram_tensor(name, list(shape), F32, kind="ExternalInput").ap()

    def dout(name, shape):
        return nc.dram_tensor(name, list(shape), F32, kind="ExternalOutput").ap()
    x_tm = din("x_tm", [TTOT, D])
    cvec = din("cvec", [128, 16])
    w_mod = din("w_mod", [L, D, 6 * D])
    w_in_fm = din("w_in_fm", [L, D, NFM * 128])
    w_in_tm = din("w_in_tm", [L, D, NTM])
    w_out = din("w_out", [L, D, D])
    w_fup = din("w_fup", [L, D, 2 * FFN])
    w_fdn = din("w_fdn", [L, FFN, D])
    pp_in = din("pp", [L, 128, NPP])
    cp_in = din("cp", [128, NCP])
    rope_in = din("rope", [4, 128, TS])
    st_rwkv = din("st_rwkv", [L, 2, 4, 64, 64])
    st_ret = din("st_ret", [L, 2, 4, 64, 64])
    st_ssd = din("st_ssd", [L, 2, 4, 64, 64])
    ck_in = din("ck", [L, 512, 256])
    cv_in = din("cv", [L, 512, 256])

    y_p = dout("y_p", [NPS * TP, D])
    y_s = dout("y_s", [TS, D])
    o_rwkv = dout("o_rwkv", [NPS, L, 2, 4, 64, 64])
    o_ret = dout("o_ret", [NPS, L, 2, 4, 64, 64])
    o_ssd = dout("o_ssd", [NPS, L, 2, 4, 64, 64])
    o_k = dout("o_k", [NPS, L, TP, 256])
    o_v = dout("o_v", [NPS, L, TP, 256])

    es = ExitStack()
    with es:
        P = Prog(nc, es)
        mk = (lambda n, s: dout(n, s)) if (debug and debug != 'B') else (lambda n, s: P.dram(n, s))
        XT = [mk("XT0", [D, TTOT]), mk("XT1", [D, TTOT])]
        FT = mk("FT", [NFM * 128, TTOT])
        FTM = mk("FTM", [TTOT, NTM])
        OTDT = F32 if debug == 'B' else BF16
        if debug == 'B':
            OT = nc.dram_tensor("OT_dbg", [D, TTOT], F32, kind="ExternalOutput").ap()
        elif debug and debug[0] == 'C':
            OT = nc.dram_tensor("OT_in", [D, TTOT], F32, kind="ExternalInput").ap()
        else:
            OT = nc.dram_tensor("OT", [D, TTOT], BF16, kind="Internal").ap()
        dXT = [[Dep() for _ in SEQS] for _ in range(2)]
        dFT = [Dep() for _ in SEQS]
        dFTM = [Dep() for _ in SEQS]
        dOT = [Dep() for _ in SEQS]
        dOUT = Dep('outs')
        RW = P.dram("RW", [11 * 256, TTOT])
        YS = P.dram("YS", [2 * 256, TTOT])
        dRW = [Dep() for _ in SEQS]
        dYS = [Dep() for _ in SEQS]

        P.ph = es
        cp = T(P.sb("cp", [128, NCP]))
        pp = [T(P.sb("pp%d" % l, [128, NPP])) for l in range(L)]
        modv = [T(P.sb("modv%d" % l, [128, 48, 2])) for l in range(L)]
        modA = [T(P.sb("modA%d" % l, [128, 2, 8, 2])) for l in range(L)]
        cpb = T(P.sb("cpb", [128, 640], BF16))
        P.ph = None

        def C(name, rows=128, sub=None):
            o, w = CPC[name]
            if sub is not None:
                return cp[0:rows, o + sub[0]:o + sub[1]]
            return cp[0:rows, o:o + w]

        def PPv(l, name, col=0, rows=128, ncol=1):
            o, w = PPC[name]
            return pp[l][0:rows, o + col:o + col + ncol]

        P.begin()
        P.dma(SP, cp[:, :], cp_in[:, :], W=[cp.d])
        for l in range(L):
            P.dma(SP, pp[l][:, :], pp_in[l], W=[pp[l].d])
        P.op(DVE, lambda e: e.tensor_copy(out=cpb[:, 0:384], in_=cp[:, 0:384]), R=[cp.d], W=[cpb.d])
        cv = T(P.sb("cv", [128, 16]))
        scv = T(P.sb("scv", [128, 8, 2]))
        P.dma(SP, cv[:, :], cvec[:, :], W=[cv.d])
        P.op(ACT, lambda e: e.activation(out=scv[:, :, 0], in_=cv[:, 0:8], func=AF.Silu), R=[cv.d], W=[scv.d])
        P.op(ACT, lambda e: e.activation(out=scv[:, :, 1], in_=cv[:, 8:16], func=AF.Silu), R=[cv.d], W=[scv.d])
        wm = Ring([T(P.sb("wm", [128, 8, 512])) for _ in range(2)])
        pm = TPS(P.psum("pm", [128, 512]))
        for l in range(L):
            for g in range(12):
                w = wm.next()
                P.dma(SP, w[:, :, :], w_mod[l, :, g * 512:(g + 1) * 512].rearrange("(k p) n -> p k n", p=128), W=[w.d])
                for cc in range(4):
                    ch = g * 4 + cc
                    for k in range(8):
                        P.op(PE, lambda e, w=w, cc=cc, k=k, ch=ch: e.matmul(
                            pm[:, ch * 2:ch * 2 + 2], lhsT=w[:, k, cc * 128:(cc + 1) * 128], rhs=scv[:, k, :],
                            start=(k == 0), stop=(k == 7)), R=[w.d, scv.d], W=[pm.d])
            o, _ = PPC['bmod']
            P.op(DVE, lambda e, l=l, o=o: e.tensor_tensor(
                out=modv[l][:, :, :], in0=pm[:, 0:96].rearrange("p (c t) -> p c t", t=2),
                in1=pp[l][:, o:o + 48].unsqueeze(2).to_broadcast([128, 48, 2]), op=ALU.add),
                R=[pm.d, pp[l].d], W=[modv[l].d])
            for n, (gname, which) in enumerate([('n1g', 1), ('n2g', 4)]):
                og, _ = PPC[gname]
                P.op(DVE, lambda e, l=l, n=n, og=og, which=which: e.scalar_tensor_tensor(
                    out=modA[l][:, n, :, :], in0=modv[l][:, which * 8:(which + 1) * 8, :], scalar=1.0,
                    in1=pp[l][:, og:og + 8].unsqueeze(2).to_broadcast([128, 8, 2]), op0=ALU.add, op1=ALU.mult),
                    R=[modv[l].d, pp[l].d], W=[modA[l].d])
        P.end()

        def modcol(l, which, j, cond):
            return modv[l][:, which * 8 + j, cond:cond + 1]

        def rmsnorm_tile(xt, n, l, which_norm, cond, hT, sq, ps_ring, rs, tmp_ring):
            ps = ps_ring.next()
            for k in range(8):
                sqk = sq.next()
                P.op(ACT, lambda e, k=k, sqk=sqk: e.activation(out=sqk[:, 0:n], in_=xt[:, k, 0:n], func=AF.Square), R=[xt.d], W=[sqk.d])
                P.op(PE, lambda e, k=k, sqk=sqk: e.matmul(ps[:, 0:n], lhsT=C('ones'), rhs=sqk[:, 0:n], start=(k == 0), stop=(k == 7)),
                     R=[sqk.d, cp.d], W=[ps.d])
            P.op(ACT, lambda e: e.activation(out=rs[:, 0:n], in_=ps[:, 0:n], func=AF.Sqrt, bias=C('eps'), scale=1.0 / D),
                 R=[ps.d, cp.d], W=[rs.d])
            P.op(DVE, lambda e: e.reciprocal(out=rs[:, 0:n], in_=rs[:, 0:n]), R=[rs.d], W=[rs.d])
            shift_which = 0 if which_norm == 0 else 3
            for j in range(8):
                tmp = tmp_ring.next()
                P.op(DVE, lambda e, j=j, tmp=tmp: e.scalar_tensor_tensor(
                    out=tmp[:, 0:n], in0=xt[:, j, 0:n], scalar=modA[l][:, which_norm, j, cond:cond + 1], in1=rs[:, 0:n],
                    op0=ALU.mult, op1=ALU.mult), R=[xt.d, modA[l].d, rs.d], W=[tmp.d])
                P.op(ACT, lambda e, j=j, tmp=tmp: e.activation(
                    out=hT[:, j, 0:n], in_=tmp[:, 0:n], func=AF.Identity, bias=modcol(l, shift_which, j, cond), scale=1.0),
                    R=[tmp.d, modv[l].d], W=[hT.d])

        evac_flip = [0]

        def evac(out_ap, in_ap, R, W):
            evac_flip[0] ^= 1
            if evac_flip[0]:
                P.op(ACT, lambda e: e.activation(out=out_ap, in_=in_ap, func=AF.Copy), R=R, W=W)
            else:
                P.op(DVE, lambda e: e.tensor_copy(out=out_ap, in_=in_ap), R=R, W=W)

        def phase_B(l):
            P.begin()
            zt = T(P.sb("zt", [128, 2048], OTDT))
            P.op(DVE, lambda e: e.memset(zt[:, :], 0.0), W=[zt.d])
            for si, (kind, sidx, Tn, off) in enumerate(SEQS):
                for r0 in ([] if 'rwkv' in MIXERS else [0, 128]) + ([] if 'ssd' in MIXERS else [512, 640]):
                    for t0 in range(0, Tn, 2048):
                        n = min(2048, Tn - t0)
                        P.dma(SP, OT[r0:r0 + 128, off + t0:off + t0 + n], zt[:, 0:n], R=[zt.d], W=[dOT[si]])
            P.end()
            for name, fn in (('ret', mixer_ret), ('diff', mixer_diff), ('ssd', mixer_ssd), ('rwkv', mixer_rwkv)):
                if name in MIXERS:
                    P.begin()
                    fn(l)
                    P.end()

        def mixer_diff(l):
            lam_init = 0.8 - 0.6 * math.exp(-0.3 * l)
            AX = mybir.AxisListType.X
            ropc = T(P.sb("ropc", [64, TS]))
            rops = T(P.sb("rops", [64, TS]))
            P.dma(SP, ropc[:, :], rope_in[2, 0:64, :], W=[ropc.d])
            P.dma(SP, rops[:, :], rope_in[3, 0:64, :], W=[rops.d])
            qf32 = T(P.sb("qf32", [64, TS]))
            qs32 = T(P.sb("qs32", [64, TS]))
            qb = T(P.sb("qb", [64, TS], BF16))
            NKMAX = TS // 128 + 4
            kall = T(P.sb("kall", [64, NKMAX * 128], BF16))
            v32 = T(P.sb("v32", [128, NKMAX, 64]))
            vall = T(P.sb("vall", [128, NKMAX, 64], BF16))
            ck32 = T(P.sb("ck32", [128, 4, 64]))
            E_r = Ring([T(P.sb("E", [128, 512], BF16)) for _ in range(3)])
            rd = [T(P.sb("rd%d" % i, [64, 512])) for i in range(2)]
            o0 = T(P.sb("o0", [64, 512]))
            o1 = T(P.sb("o1", [64, 512]))
            sqy = T(P.sb("sqy", [64, 512]))
            rsy = T(P.sb("rsy", [64, 512]))
            oy_r = Ring([T(P.sb("oy", [64, 512], OTDT)) for _ in range(2)])
            lamt = T(P.sb("lamt", [128, 40]))
            ps_sc = Ring([TPS(P.psum("ps_sc", [128, 512])) for _ in range(2)])
            ps_den = [TPS(P.psum("ps_den%d" % i, [128, 512])) for i in range(2)]
            ps_num = [TPS(P.psum("ps_num%d" % i, [128, 512])) for i in range(2)]
            ps_x = Ring([TPS(P.psum("ps_x", [128, 512])) for _ in range(2)])
            olp, _ = PPC['lp']
            osg, _ = PPC['subg']
            for i in range(2):
                P.op(DVE, lambda e, i=i: e.tensor_tensor(out=lamt[:, 0:32], in0=pp[l][:, olp + 64 * i:olp + 64 * i + 32],
                                                         in1=pp[l][:, olp + 64 * i + 32:olp + 64 * i + 64], op=ALU.mult),
                     R=[pp[l].d], W=[lamt.d])
                P.op(DVE, lambda e, i=i: e.tensor_reduce(out=lamt[:, 32 + i:33 + i], in_=lamt[:, 0:32], axis=AX, op=ALU.add),
                     R=[lamt.d], W=[lamt.d])
            P.op(ACT, lambda e: e.activation(out=lamt[:, 34:36], in_=lamt[:, 32:34], func=AF.Exp), R=[lamt.d], W=[lamt.d])
            P.op(DVE, lambda e: e.scalar_tensor_tensor(out=lamt[:, 36:37], in0=lamt[:, 35:36], scalar=-lam_init, in1=lamt[:, 34:35],
                                                       op0=ALU.add, op1=ALU.subtract), R=[lamt.d], W=[lamt.d])
            scale = 32.0 ** -0.5
            for si, (kind, sidx, Tn, off) in enumerate(SEQS):
                nch = Tn // 128
                nk = nch + (4 if kind == 's' else 0)
                for h in range(4):
                    pr = (h % 2) * 64

                    def rows(cbase):
                        r0 = (cbase + h // 2) * 128 + pr
                        return FT[r0:r0 + 64, off:off + Tn]
                    for (dst, cb, cbs) in ((qb, 24, 28), (kall, 26, 30)):
                        P.dma(SP, qf32[:, 0:Tn], rows(cb), R=[dFT[si]], W=[qf32.d])
                        if kind == 's':
                            P.dma(SP, qs32[:, 0:Tn], rows(cbs), R=[dFT[si]], W=[qs32.d])
                            P.op(DVE, lambda e: e.tensor_tensor(out=qf32[:, 0:Tn], in0=qf32[:, 0:Tn], in1=ropc[:, 0:Tn], op=ALU.mult),
                                 R=[qf32.d, ropc.d], W=[qf32.d])
                            P.op(DVE, lambda e: e.tensor_tensor(out=qs32[:, 0:Tn], in0=qs32[:, 0:Tn], in1=rops[:, 0:Tn], op=ALU.mult),
                                 R=[qs32.d, rops.d], W=[qs32.d])
                            P.op(DVE, lambda e: e.tensor_tensor(out=qf32[:, 0:Tn], in0=qf32[:, 0:Tn], in1=qs32[:, 0:Tn], op=ALU.add),
                                 R=[qf32.d, qs32.d], W=[qf32.d])
                        P.op(ACT, lambda e, dst=dst: e.activation(out=dst[:, 0:Tn], in_=qf32[:, 0:Tn], func=AF.Copy),
                             R=[qf32.d], W=[dst.d])
                    P.dma(SP, v32[:, 0:nch, :], FTM[off:off + Tn, 256 + h * 64:256 + (h + 1) * 64].rearrange("(c p) e -> p c e", p=128),
                          R=[dFTM[si]], W=[v32.d])
                    if kind == 's':
                        P.dma(SP, v32[:, nch:nch + 4, :], cv_in[l, :, h * 64:(h + 1) * 64].rearrange("(c p) e -> p c e", p=128), W=[v32.d])
                        P.dma(SP, ck32[:, :, :], ck_in[l, :, h * 64:(h + 1) * 64].rearrange("(c p) e -> p c e", p=128), W=[ck32.d])
                        px = ps_x.next()
                        for c in range(4):
                            P.op(PE, lambda e, c=c: e.transpose(px[0:64, c * 128:(c + 1) * 128], ck32[:, c, :], C('ident')),
                                 R=[ck32.d, cp.d], W=[px.d])
                        P.op(ACT, lambda e: e.activation(out=kall[:, Tn:Tn + 512], in_=px[0:64, 0:512], func=AF.Copy), R=[px.d], W=[kall.d])
                    P.op(DVE, lambda e: e.tensor_copy(out=vall[:, 0:nk, :], in_=v32[:, 0:nk, :]), R=[v32.d], W=[vall.d])
                    nq = min(512, Tn)
                    for q0 in range(0, Tn, nq):
                        n = nq
                        qsl = slice(q0, q0 + n)
                        for kc in range(nk):
                            ksl = slice(kc * 128, (kc + 1) * 128)
                            for m in range(2):
                                psl = slice(m * 32, (m + 1) * 32)
                                psc = ps_sc.next()
                                P.op(PE, lambda e: e.matmul(psc[:, 0:n], lhsT=kall[psl, ksl], rhs=qb[psl, qsl], start=True, stop=True),
                                     R=[kall.d, qb.d], W=[psc.d])
                                E = E_r.next()
                                P.op(ACT, lambda e: e.activation(out=E[:, 0:n], in_=psc[:, 0:n], func=AF.Exp, scale=scale), R=[psc.d], W=[E.d])
                                P.op(PE, lambda e: e.matmul(ps_den[m][0:64, 0:n], lhsT=cpb[:, 128:192], rhs=E[:, 0:n], start=(kc == 0), stop=(kc == nk - 1)),
                                     R=[cpb.d, E.d], W=[ps_den[m].d])
                                P.op(PE, lambda e: e.matmul(ps_num[m][0:64, 0:n], lhsT=vall[:, kc, :], rhs=E[:, 0:n], start=(kc == 0), stop=(kc == nk - 1)),
                                     R=[vall.d, E.d], W=[ps_num[m].d])
                        for m in range(2):
                            P.op(DVE, lambda e, m=m: e.reciprocal(out=rd[m][:, 0:n], in_=ps_den[m][0:64, 0:n]), R=[ps_den[m].d], W=[rd[m].d])
                        P.op(DVE, lambda e: e.tensor_tensor(out=o0[:, 0:n], in0=ps_num[0][0:64, 0:n], in1=rd[0][:, 0:n], op=ALU.mult),
                             R=[ps_num[0].d, rd[0].d], W=[o0.d])
                        P.op(DVE, lambda e: e.tensor_tensor(out=o1[:, 0:n], in0=ps_num[1][0:64, 0:n], in1=rd[1][:, 0:n], op=ALU.mult),
                             R=[ps_num[1].d, rd[1].d], W=[o1.d])
                        P.op(DVE, lambda e: e.scalar_tensor_tensor(out=o0[:, 0:n], in0=o1[:, 0:n], scalar=lamt[0:64, 36:37], in1=o0[:, 0:n],
                                                                   op0=ALU.mult, op1=ALU.add), R=[o0.d, o1.d, lamt.d], W=[o0.d])
                        P.op(ACT, lambda e: e.activation(out=sqy[:, 0:n], in_=o0[:, 0:n], func=AF.Square), R=[o0.d], W=[sqy.d])
                        pss = ps_x.next()
                        P.op(PE, lambda e: e.matmul(pss[0:64, 0:n], lhsT=C('ones', rows=64, sub=(0, 64)), rhs=sqy[:, 0:n], start=True, stop=True),
                             R=[sqy.d, cp.d], W=[pss.d])
                        P.op(ACT, lambda e: e.activation(out=rsy[:, 0:n], in_=pss[0:64, 0:n], func=AF.Sqrt, bias=C('eps', rows=64), scale=1.0 / 64),
                             R=[pss.d, cp.d], W=[rsy.d])
                        P.op(DVE, lambda e: e.reciprocal(out=rsy[:, 0:n], in_=rsy[:, 0:n]), R=[rsy.d], W=[rsy.d])
                        P.op(DVE, lambda e: e.scalar_tensor_tensor(out=o1[:, 0:n], in0=o0[:, 0:n], scalar=pp[l][0:64, osg:osg + 1],
                                                                   in1=rsy[:, 0:n], op0=ALU.mult, op1=ALU.mult),
                             R=[o0.d, pp[l].d, rsy.d], W=[o1.d])
                        oy = oy_r.next()
                        P.op(ACT, lambda e: e.activation(out=oy[:, 0:n], in_=o1[:, 0:n], func=AF.Copy, scale=1.0 - lam_init), R=[o1.d], W=[oy.d])
                        P.dma(SP, OT[768 + h * 64:768 + (h + 1) * 64, off + q0: off + q0 + n], oy[:, 0:n], R=[oy.d], W=[dOT[si]])

        def mixer_rwkv(l):
            rwkv_pre(l)
            if RWKV_STAGES >= 2:
                P.end()
                P.begin()
                rwkv_scan(l)
            if RWKV_STAGES >= 3:
                P.end()
                P.begin()
                rwkv_post(l)

        def rwkv_pre(l):
            buf_r = Ring([T(P.sb("rbuf", [64, 514])) for _ in range(3)])
            s1 = T(P.sb("rs1", [64, 512]))
            sh = [T(P.sb("rsh%d" % i, [64, 512])) for i in range(15)]
            twd = T(P.sb("twd", [64, 512]))
            sgd = T(P.sb("sgd", [64, 512]))
            a_d = [T(P.sb("a_d%d" % d, [64, 512])) for d in range(2)]
            o_r = Ring([T(P.sb("rwo", [64, 512])) for _ in range(4)])
            kkr = T(P.sb("kkr", [64, 512]))
            kk = T(P.sb("kk", [64, 512]))
            t1 = T(P.sb("rt1", [64, 512]))
            t2 = T(P.sb("rt2", [64, 512]))
            ps_r = Ring([TPS(P.psum("ps_rp", [128, 512])) for _ in range(4)])
            omu, _ = PPC['mu64']
            ow0, _ = PPC['w0_64']
            oa0, _ = PPC['a0_64']
            okk, _ = PPC['kk64']
            oka, _ = PPC['ka64']
            ork, _ = PPC['rk64']
            owu, _ = PPC['w_up']
            oau, _ = PPC['a_up']
            ogu, _ = PPC['g_up']
            for si, (kind, sidx, Tn, off) in enumerate(SEQS):
                nq = min(512, Tn)
                for q0 in range(0, Tn, nq):
                    n = nq
                    lo, hi = max(q0 - 1, 0), min(q0 + n + 1, Tn)
                    c_lo = lo - (q0 - 1)
                    a0 = 1 if q0 == 0 else 0
                    b1 = n - 1 if q0 + n == Tn else n

                    def store(arr, h, src):
                        r0 = arr * 256 + h * 64
                        P.dma(SP, RW[r0:r0 + 64, off + q0: off + q0 + n], src[:, 0:n], R=[src.d], W=[dRW[si]])
                    for hc in range(15):
                        buf = buf_r.next()
                        P.dma(SP, buf[:, c_lo:c_lo + hi - lo], FT[hc * 64:(hc + 1) * 64, off + lo: off + hi], R=[dFT[si]], W=[buf.d])
                        P.op(DVE, lambda e: e.tensor_tensor(out=s1[:, a0:b1], in0=buf[:, a0:b1], in1=buf[:, a0 + 2:b1 + 2], op=ALU.add),
                             R=[buf.d], W=[s1.d])
                        if a0:
                            P.op(DVE, lambda e: e.tensor_copy(out=s1[:, 0:1], in_=buf[:, 2:3]), R=[buf.d], W=[s1.d])
                        if b1 < n:
                            P.op(DVE, lambda e: e.tensor_copy(out=s1[:, n - 1:n], in_=buf[:, n - 1:n]), R=[buf.d], W=[s1.d])
                        P.op(DVE, lambda e: e.scalar_tensor_tensor(out=s1[:, 0:n], in0=s1[:, 0:n], scalar=0.5, in1=buf[:, 1:n + 1],
                                                                   op0=ALU.mult, op1=ALU.subtract), R=[s1.d, buf.d], W=[s1.d])
                        P.op(DVE, lambda e: e.scalar_tensor_tensor(out=sh[hc][:, 0:n], in0=s1[:, 0:n], scalar=pp[l][0:64, omu + hc:omu + hc + 1],
                                                                   in1=buf[:, 1:n + 1], op0=ALU.mult, op1=ALU.add),
                             R=[s1.d, buf.d, pp[l].d], W=[sh[hc].d])
                    P.op(ACT, lambda e: e.activation(out=twd[:, 0:n], in_=sh[12][:, 0:n], func=AF.Tanh), R=[sh[12].d], W=[twd.d])
                    P.op(ACT, lambda e: e.activation(out=sgd[:, 0:n], in_=sh[14][:, 0:n], func=AF.Sigmoid), R=[sh[14].d], W=[sgd.d])
                    sad = sh[13]
                    for h in range(4):
                        shr, shk, shv = sh[h], sh[4 + h], sh[8 + h]
                        store(0, h, shr)
                        store(1, h, shv)
                        hs = slice(h * 64, (h + 1) * 64)
                        for d in range(2):
                            ds_ = slice(d * 32, (d + 1) * 32)
                            col = d * 4 + h
                            pw = ps_r.next()
                            P.op(PE, lambda e: e.matmul(pw[0:64, 0:n], lhsT=pp[l][ds_, owu + h * 64:owu + (h + 1) * 64], rhs=twd[ds_, 0:n], start=True, stop=True),
                                 R=[pp[l].d, twd.d], W=[pw.d])
                            lw = o_r.next()
                            P.op(ACT, lambda e: e.activation(out=lw[:, 0:n], in_=pw[0:64, 0:n], func=AF.Sigmoid, bias=pp[l][0:64, ow0 + col:ow0 + col + 1], scale=1.0),
                                 R=[pw.d, pp[l].d], W=[lw.d])
                            P.op(DVE, lambda e: e.tensor_scalar(out=lw[:, 0:n], in0=lw[:, 0:n], scalar1=-math.exp(-0.5), scalar2=None, op0=ALU.mult),
                                 R=[lw.d], W=[lw.d])
                            store(7 + d, h, lw)
                            pa = ps_r.next()
                            P.op(PE, lambda e: e.matmul(pa[0:64, 0:n], lhsT=pp[l][ds_, oau + h * 64:oau + (h + 1) * 64], rhs=sad[ds_, 0:n], start=True, stop=True),
                                 R=[pp[l].d, sad.d], W=[pa.d])
                            P.op(ACT, lambda e: e.activation(out=a_d[d][:, 0:n], in_=pa[0:64, 0:n], func=AF.Sigmoid, bias=pp[l][0:64, oa0 + col:oa0 + col + 1], scale=1.0),
                                 R=[pa.d, pp[l].d], W=[a_d[d].d])
                        pg = ps_r.next()
                        P.op(PE, lambda e: e.matmul(pg[0:64, 0:n], lhsT=pp[l][0:64, ogu + h * 64:ogu + (h + 1) * 64], rhs=sgd[:, 0:n], start=True, stop=True),
                             R=[pp[l].d, sgd.d], W=[pg.d])
                        go = o_r.next()
                        P.op(ACT, lambda e: e.activation(out=go[:, 0:n], in_=pg[0:64, 0:n], func=AF.Copy), R=[pg.d], W=[go.d])
                        store(9, h, go)
                        P.op(DVE, lambda e: e.tensor_scalar(out=kkr[:, 0:n], in0=shk[:, 0:n], scalar1=pp[l][0:64, okk + h:okk + h + 1], scalar2=None, op0=ALU.mult),
                             R=[shk.d, pp[l].d], W=[kkr.d])
                        P.op(ACT, lambda e: e.activation(out=t1[:, 0:n], in_=kkr[:, 0:n], func=AF.Square), R=[kkr.d], W=[t1.d])
                        pk = ps_r.next()
                        P.op(PE, lambda e: e.matmul(pk[0:64, 0:n], lhsT=C('ones', rows=64, sub=(0, 64)), rhs=t1[:, 0:n], start=True, stop=True),
                             R=[t1.d, cp.d], W=[pk.d])
                        P.op(DVE, lambda e: e.tensor_scalar(out=t1[:, 0:n], in0=pk[0:64, 0:n], scalar1=1e-12, scalar2=None, op0=ALU.max), R=[pk.d], W=[t1.d])
                        P.op(ACT, lambda e: e.activation(out=t1[:, 0:n], in_=t1[:, 0:n], func=AF.Sqrt), R=[t1.d], W=[t1.d])
                        P.op(DVE, lambda e: e.reciprocal(out=t1[:, 0:n], in_=t1[:, 0:n]), R=[t1.d], W=[t1.d])
                        P.op(DVE, lambda e: e.tensor_tensor(out=kk[:, 0:n], in0=kkr[:, 0:n], in1=t1[:, 0:n], op=ALU.mult), R=[kkr.d, t1.d], W=[kk.d])
                        store(2, h, kk)
                        for d in range(2):
                            P.op(DVE, lambda e: e.tensor_scalar(out=t2[:, 0:n], in0=a_d[d][:, 0:n], scalar1=-1.0, scalar2=pp[l][0:64, oka + h:oka + h + 1],
                                                                op0=ALU.add, op1=ALU.mult), R=[a_d[d].d, pp[l].d], W=[t2.d])
                            kd = o_r.next()
                            P.op(DVE, lambda e: e.scalar_tensor_tensor(out=kd[:, 0:n], in0=t2[:, 0:n], scalar=1.0, in1=shk[:, 0:n], op0=ALU.add, op1=ALU.mult),
                                 R=[t2.d, shk.d], W=[kd.d])
                            store(3 + d, h, kd)
                            bd = o_r.next()
                            P.op(DVE, lambda e: e.tensor_tensor(out=bd[:, 0:n], in0=kk[:, 0:n], in1=a_d[d][:, 0:n], op=ALU.mult), R=[kk.d, a_d[d].d], W=[bd.d])
                            store(5 + d, h, bd)
                        P.op(DVE, lambda e: e.scalar_tensor_tensor(out=t2[:, 0:n], in0=shr[:, 0:n], scalar=pp[l][0:64, ork + h:ork + h + 1], in1=shk[:, 0:n],
                                                                   op0=ALU.mult, op1=ALU.mult), R=[shr.d, shk.d, pp[l].d], W=[t2.d])
                        pb = ps_r.next()
                        P.op(PE, lambda e: e.matmul(pb[0:64, 0:n], lhsT=C('ones', rows=64, sub=(0, 64)), rhs=t2[:, 0:n], start=True, stop=True),
                             R=[t2.d, cp.d], W=[pb.d])
                        bo = o_r.next()
                        P.op(DVE, lambda e: e.tensor_tensor(out=bo[:, 0:n], in0=pb[0:64, 0:n], in1=shv[:, 0:n], op=ALU.mult), R=[pb.d, shv.d], W=[bo.d])
                        store(10, h, bo)

        def rwkv_scan(l):
            ld = {}
            for nm in ('r', 'v', 'kk', 'kd', 'bd', 'lw'):
                ld[nm] = [T(P.sb("l%s%d" % (nm, d), [64, 512])) for d in range(2)]
            Lc = [T(P.sb("Lc%d" % d, [64, 512])) for d in range(2)]
            En = [T(P.sb("En%d" % d, [64, 512])) for d in range(2)]
            Ep = [T(P.sb("Ep%d" % d, [64, 512])) for d in range(2)]
            aT = [T(P.sb("aT%d" % d, [64, 512])) for d in range(2)]
            bT = [T(P.sb("bT%d" % d, [64, 512])) for d in range(2)]
            kT = [T(P.sb("kT%d" % d, [64, 512])) for d in range(2)]
            rT = [T(P.sb("rT%d" % d, [64, 512])) for d in range(2)]
            vo = [T(P.sb("vo%d" % d, [64, 512])) for d in range(2)]
            ysb = [T(P.sb("ysb%d" % d, [64, 512])) for d in range(2)]
            yso = [T(P.sb("yso%d" % d, [64, 512])) for d in range(2)]
            ELC = [T(P.sb("ELC%d" % d, [64, 64])) for d in range(2)]
            bhk = [T(P.sb("bhk%d" % d, [64, 128])) for d in range(2)]
            WC = [T(P.sb("WC%d" % d, [64, 1])) for d in range(2)]
            TM = [T(P.sb("TM%d" % d, [64, 320])) for d in range(2)]
            AM = [T(P.sb("AM%d" % d, [64, 320])) for d in range(2)]
            Mm = [[T(P.sb("Mm%d_%d" % (d, i), [64, 128])) for i in range(2)] for d in range(2)]
            Pm = [[T(P.sb("Pm%d_%d" % (d, i), [64, 64])) for i in range(2)] for d in range(2)]
            AU = [T(P.sb("AU%d" % d, [64, 128])) for d in range(2)]
            GT = [T(P.sb("GT%d" % d, [64, 64])) for d in range(2)]
            Hh = [T(P.sb("Hh%d" % d, [64, 64])) for d in range(2)]
            RhT = [T(P.sb("RhT%d" % d, [64, 64])) for d in range(2)]
            St = [[T(P.sb("St%d_%d" % (d, i), [64, 64])) for i in range(2)] for d in range(2)]
            s0l = T(P.sb("s0l", [64, 64]))
            pTR = TPS(P.psum("pTR", [128, 512]))
            pA = TPS(P.psum("pA", [128, 512]))
            pN = TPS(P.psum("pN", [128, 512]))
            pP = TPS(P.psum("pP", [128, 512]))
            pZ = TPS(P.psum("pZ", [128, 512]))
            pG = TPS(P.psum("pG", [128, 512]))
            pRY = TPS(P.psum("pRY", [128, 512]))
            pS = TPS(P.psum("pS", [128, 512]))
            I64 = C('ident', rows=64, sub=(0, 64))
            names = ('r', 'v', 'kk', None, None, None, None, None, None)
            for si, (kind, sidx, Tn, off) in enumerate(SEQS[:RW_DBG['seqs']]):
                nq = min(512, Tn)
                ntile = Tn // nq
                for h in range(RW_DBG['heads']):
                    sti = [0, 0]
                    for d in range(2):
                        S0 = St[d][0]
                        if kind == 's':
                            P.dma(SP, s0l[:, :], st_rwkv[l, d, h], W=[s0l.d])
                            P.op(PE, lambda e: e.transpose(pS[0:64, 0:64], s0l[:, :], I64), R=[s0l.d, cp.d], W=[pS.d])
                            P.op(DVE, lambda e: e.tensor_copy(out=S0[:, :], in_=pS[0:64, 0:64]), R=[pS.d], W=[S0.d])
                        else:
                            P.op(DVE, lambda e: e.memset(S0[:, :], 0.0), W=[S0.d])
                    for ti in range(ntile):
                        n = nq
                        for d in range(2):
                            q0 = ti * nq if d == 0 else Tn - (ti + 1) * nq

                            def view(t):
                                return t[:, 0:n] if d == 0 else t[:, 0:n][:, ::-1]
                            for nm, arr in (('r', 0), ('v', 1), ('kk', 2), ('kd', 3 + d), ('bd', 5 + d), ('lw', 7 + d)):
                                r0 = arr * 256 + h * 64
                                P.dma(SP, ld[nm][d][:, 0:n], RW[r0:r0 + 64, off + q0: off + q0 + n], R=[dRW[si]], W=[ld[nm][d].d])
                            lw = ld['lw'][d]
                            P.op(DVE, lambda e: e.tensor_tensor_scan(out=Lc[d][:, 0:n], data0=C('scanmask', rows=64, sub=(0, n)), data1=view(lw),
                                                                     initial=0.0, op0=ALU.mult, op1=ALU.add), R=[lw.d, cp.d], W=[Lc[d].d])
                            P.op(ACT, lambda e: e.activation(out=En[d][:, 0:n], in_=Lc[d][:, 0:n], func=AF.Exp, scale=-1.0), R=[Lc[d].d], W=[En[d].d])
                            P.op(ACT, lambda e: e.activation(out=Ep[d][:, 0:n], in_=Lc[d][:, 0:n], func=AF.Exp), R=[Lc[d].d], W=[Ep[d].d])
                            P.op(DVE, lambda e: e.tensor_tensor(out=rT[d][:, 0:n], in0=view(ld['r'][d]), in1=Ep[d][:, 0:n], op=ALU.mult),
                                 R=[ld['r'][d].d, Ep[d].d], W=[rT[d].d])
                            P.op(DVE, lambda e: e.tensor_tensor(out=aT[d][:, 0:n], in0=Lc[d][:, 0:n], in1=view(lw), op=ALU.subtract),
                                 R=[Lc[d].d, lw.d], W=[aT[d].d])
                            P.op(ACT, lambda e: e.activation(out=Ep[d][:, 0:n], in_=aT[d][:, 0:n], func=AF.Exp), R=[aT[d].d, rT[d].d], W=[Ep[d].d])
                            P.op(DVE, lambda e: e.scalar_tensor_tensor(out=aT[d][:, 0:n], in0=view(ld['kk'][d]), scalar=-1.0, in1=Ep[d][:, 0:n],
                                                                       op0=ALU.mult, op1=ALU.mult), R=[ld['kk'][d].d, Ep[d].d], W=[aT[d].d])
                            P.op(DVE, lambda e: e.tensor_tensor(out=bT[d][:, 0:n], in0=view(ld['bd'][d]), in1=En[d][:, 0:n], op=ALU.mult),
                                 R=[ld['bd'][d].d, En[d].d], W=[bT[d].d])
                            P.op(DVE, lambda e: e.tensor_tensor(out=kT[d][:, 0:n], in0=view(ld['kd'][d]), in1=En[d][:, 0:n], op=ALU.mult),
                                 R=[ld['kd'][d].d, En[d].d], W=[kT[d].d])
                            P.op(DVE, lambda e: e.tensor_copy(out=vo[d][:, 0:n], in_=view(ld['v'][d])), R=[ld['v'][d].d], W=[vo[d].d])
                        for cc in range(n // 64):
                            for d in range(2):
                                if RW_DBG['stop'] < 1:
                                    continue
                                cs = slice(cc * 64, (cc + 1) * 64)
                                lcol = Lc[d][:, cc * 64 + 63:cc * 64 + 64]

                                def vw(t):
                                    if d == 0:
                                        return t[:, cs]
                                    return t[:, n - (cc + 1) * 64:n - cc * 64][:, ::-1]
                                P.op(ACT, lambda e: e.activation(out=ELC[d][:, :], in_=Lc[d][:, cs], func=AF.Exp, bias=lcol, scale=-1.0),
                                     R=[Lc[d].d], W=[ELC[d].d])
                                P.op(ACT, lambda e: e.activation(out=WC[d][:, :], in_=lcol, func=AF.Exp), R=[Lc[d].d], W=[WC[d].d])
                                P.op(DVE, lambda e: e.tensor_tensor(out=bhk[d][:, 0:64], in0=vw(ld['bd'][d]), in1=ELC[d][:, :], op=ALU.mult),
                                     R=[ld['bd'][d].d, ELC[d].d], W=[bhk[d].d])
                                P.op(DVE, lambda e: e.tensor_tensor(out=bhk[d][:, 64:128], in0=vw(ld['kd'][d]), in1=ELC[d][:, :], op=ALU.mult),
                                     R=[ld['kd'][d].d, ELC[d].d], W=[bhk[d].d])
                                P.op(PE, lambda e: e.transpose(pTR[0:64, 0:64], aT[d][:, cs], I64), R=[aT[d].d, cp.d], W=[pTR.d])
                                P.op(PE, lambda e: e.transpose(pTR[0:64, 64:128], vo[d][:, cs], I64), R=[vo[d].d, cp.d], W=[pTR.d])
                                P.op(PE, lambda e: e.transpose(pTR[0:64, 128:192], bhk[d][:, 0:64], I64), R=[bhk[d].d, cp.d], W=[pTR.d])
                                P.op(PE, lambda e: e.transpose(pTR[0:64, 192:256], bhk[d][:, 64:128], I64), R=[bhk[d].d, cp.d], W=[pTR.d])
                                tm = TM[d]
                                P.op(ACT, lambda e: e.activation(out=tm[:, 0:64], in_=pTR[0:64, 0:64], func=AF.Copy), R=[pTR.d], W=[tm.d])
                                P.op(DVE, lambda e: e.tensor_copy(out=tm[:, 128:320], in_=pTR[0:64, 64:256]), R=[pTR.d], W=[tm.d])
                                V_, Bh_, Kh_ = tm[:, 128:192], tm[:, 192:256], tm[:, 256:320]
                                a_, b_, k_, r_ = aT[d][:, cs], bT[d][:, cs], kT[d][:, cs], rT[d][:, cs]
                                if RW_DBG['stop'] < 2:
                                    continue
                                for i_, (lh, rh) in enumerate(((b_, a_), (b_, r_), (k_, a_), (k_, r_), (a_, b_))):
                                    P.op(PE, lambda e: e.matmul(pA[0:64, i_ * 64:(i_ + 1) * 64], lhsT=lh, rhs=rh, start=True, stop=True),
                                         R=[aT[d].d, bT[d].d, kT[d].d, rT[d].d], W=[pA.d])
                                am = AM[d]
                                P.op(DVE, lambda e: e.tensor_tensor(out=am[:, :], in0=pA[0:64, 0:320], in1=C('rw_mask5', rows=64), op=ALU.mult),
                                     R=[pA.d, cp.d], W=[am.d])
                                AabT, ArbT, AakT, ArkT, Aab = (am[:, i * 64:(i + 1) * 64] for i in range(5))
                                if RW_DBG['stop'] < 3:
                                    continue
                                Pc = Pm[d][0]
                                P.op(DVE, lambda e: e.tensor_tensor(out=Pc[:, :], in0=AabT, in1=I64, op=ALU.add), R=[am.d, cp.d], W=[Pc.d])
                                Mc, Mtc, Mdep = AabT, Aab, am
                                for lvl in range(5):
                                    mn = Mm[d][lvl % 2]
                                    P.op(PE, lambda e: e.matmul(pN[0:64, 0:64], lhsT=Mtc, rhs=Mc, start=True, stop=True), R=[Mdep.d], W=[pN.d])
                                    P.op(PE, lambda e: e.matmul(pN[0:64, 64:128], lhsT=Mc, rhs=Mtc, start=True, stop=True), R=[Mdep.d], W=[pN.d])
                                    P.op(ACT, lambda e: e.activation(out=mn[:, :], in_=pN[0:64, 0:128], func=AF.Copy), R=[pN.d], W=[mn.d])
                                    Mc, Mtc, Mdep = mn[:, 0:64], mn[:, 64:128], mn
                                    P.op(PE, lambda e: e.matmul(pP[0:64, 0:64], lhsT=Mtc, rhs=Pc[:, :], start=True, stop=True), R=[mn.d, Pc.d], W=[pP.d])
                                    Pn = Pm[d][(lvl + 1) % 2]
                                    P.op(DVE, lambda e: e.tensor_tensor(out=Pn[:, :], in0=pP[0:64, 0:64], in1=Pc[:, :], op=ALU.add), R=[pP.d, Pc.d], W=[Pn.d])
                                    Pc = Pn
                                if RW_DBG['stop'] < 4:
                                    continue
                                P.op(PE, lambda e: e.matmul(pZ[0:64, 0:64], lhsT=AakT, rhs=V_, start=True, stop=True), R=[am.d, tm.d], W=[pZ.d])
                                P.op(ACT, lambda e: e.activation(out=tm[:, 64:128], in_=pZ[0:64, 0:64], func=AF.Copy), R=[pZ.d], W=[tm.d])
                                P.op(PE, lambda e: e.matmul(pZ[0:64, 64:192], lhsT=Pc[:, :], rhs=tm[:, 0:128], start=True, stop=True), R=[Pc.d, tm.d], W=[pZ.d])
                                au = AU[d]
                                P.op(DVE, lambda e: e.tensor_copy(out=au[:, :], in_=pZ[0:64, 64:192]), R=[pZ.d], W=[au.d])
                                Ah, Ul = au[:, 0:64], au[:, 64:128]
                                if RW_DBG['stop'] < 5:
                                    continue
                                P.op(PE, lambda e: e.matmul(pG[0:64, 0:64], lhsT=Ah, rhs=Bh_, start=True, stop=True), R=[au.d, tm.d], W=[pG.d])
                                P.op(PE, lambda e: e.matmul(pG[0:64, 64:128], lhsT=Bh_, rhs=Ul, start=True, stop=False), R=[au.d, tm.d], W=[pG.d])
                                P.op(PE, lambda e: e.matmul(pG[0:64, 64:128], lhsT=Kh_, rhs=V_, start=False, stop=True), R=[tm.d], W=[pG.d])
                                if RW_DBG['var'] != 1:
                                    P.op(DVE, lambda e: e.scalar_tensor_tensor(out=GT[d][:, :], in0=I64, scalar=WC[d][:, 0:1], in1=pG[0:64, 0:64],
                                                                               op0=ALU.mult, op1=ALU.add), R=[cp.d, WC[d].d, pG.d], W=[GT[d].d])
                                if RW_DBG['var'] != 2:
                                    P.op(ACT, lambda e: e.activation(out=Hh[d][:, :], in_=pG[0:64, 64:128], func=AF.Copy), R=[pG.d], W=[Hh[d].d])
                                if RW_DBG['stop'] < 6:
                                    continue
                                P.op(PE, lambda e: e.matmul(pRY[0:64, 0:64], lhsT=Ah, rhs=ArbT, start=True, stop=True), R=[au.d, am.d], W=[pRY.d])
                                P.op(PE, lambda e: e.matmul(pRY[0:64, 64:128], lhsT=Ul, rhs=ArbT, start=True, stop=False), R=[au.d, am.d], W=[pRY.d])
                                P.op(PE, lambda e: e.matmul(pRY[0:64, 64:128], lhsT=V_, rhs=ArkT, start=False, stop=False), R=[tm.d, am.d], W=[pRY.d])
                                P.op(DVE, lambda e: e.tensor_tensor(out=RhT[d][:, :], in0=pRY[0:64, 0:64], in1=r_, op=ALU.add), R=[pRY.d, rT[d].d], W=[RhT[d].d])
                                Sc = St[d][sti[d] % 2]
                                Sn = St[d][(sti[d] + 1) % 2]
                                sti[d] += 1
                                P.op(PE, lambda e: e.matmul(pRY[0:64, 64:128], lhsT=Sc[:, :], rhs=RhT[d][:, :], start=False, stop=True), R=[Sc.d, RhT[d].d], W=[pRY.d])
                                P.op(ACT, lambda e: e.activation(out=ysb[d][:, cs], in_=pRY[0:64, 64:128], func=AF.Copy), R=[pRY.d], W=[ysb[d].d])
                                P.op(PE, lambda e: e.matmul(pS[0:64, 0:64], lhsT=GT[d][:, :], rhs=Sc[:, :], start=True, stop=True), R=[GT[d].d, Sc.d], W=[pS.d])
                                P.op(DVE, lambda e: e.tensor_tensor(out=Sn[:, :], in0=pS[0:64, 0:64], in1=Hh[d][:, :], op=ALU.add), R=[pS.d, Hh[d].d], W=[Sn.d])
                        for d in range(2):
                            q0 = ti * nq if d == 0 else Tn - (ti + 1) * nq
                            src = ysb[d]
                            if d == 1:
                                P.op(DVE, lambda e: e.tensor_copy(out=yso[d][:, 0:n], in_=ysb[d][:, 0:n][:, ::-1]), R=[ysb[d].d], W=[yso[d].d])
                                src = yso[d]
                            r0 = d * 256 + h * 64
                            P.dma(SP, YS[r0:r0 + 64, off + q0: off + q0 + n], src[:, 0:n], R=[src.d], W=[dYS[si]])
                    if kind == 'p':
                        for d in range(2):
                            Sc = St[d][sti[d] % 2]
                            P.op(PE, lambda e: e.transpose(pS[0:64, 0:64], Sc[:, :], I64), R=[Sc.d, cp.d], W=[pS.d])
                            P.op(DVE, lambda e: e.tensor_copy(out=s0l[:, :], in_=pS[0:64, 0:64]), R=[pS.d], W=[s0l.d])
                            P.dma(SP, o_rwkv[sidx, l, d, h], s0l[:, :], R=[s0l.d], W=[dOUT])

        def rwkv_post(l):
            yf = Ring([T(P.sb("pyf", [64, 512])) for _ in range(2)])
            yb = Ring([T(P.sb("pyb", [64, 512])) for _ in range(2)])
            gt = Ring([T(P.sb("pgt", [64, 512])) for _ in range(2)])
            bt = Ring([T(P.sb("pbt", [64, 512])) for _ in range(2)])
            yc = T(P.sb("pyc", [64, 512]))
            sq = T(P.sb("psq", [64, 512]))
            rs = T(P.sb("prs", [64, 512]))
            oy_r = Ring([T(P.sb("oy", [64, 512], OTDT)) for _ in range(2)])
            ps_r = Ring([TPS(P.psum("ps_po", [128, 512])) for _ in range(4)])
            olg, _ = PPC['lng64']
            olb, _ = PPC['lnb64']
            epsg = T(P.sb("epsg", [64, 1]))
            P.op(DVE, lambda e: e.memset(epsg[:, :], 64e-5), W=[epsg.d])
            for si, (kind, sidx, Tn, off) in enumerate(SEQS):
                nq = min(512, Tn)
                for h in range(4):
                    for q0 in range(0, Tn, nq):
                        n = nq
                        a, b, g_, bo = yf.next(), yb.next(), gt.next(), bt.next()
                        cols = slice(off + q0, off + q0 + n)
                        P.dma(SP, a[:, 0:n], YS[h * 64:(h + 1) * 64, cols], R=[dYS[si]], W=[a.d])
                        P.dma(SP, b[:, 0:n], YS[256 + h * 64:256 + (h + 1) * 64, cols], R=[dYS[si]], W=[b.d])
                        P.dma(SP, g_[:, 0:n], RW[9 * 256 + h * 64:9 * 256 + (h + 1) * 64, cols], R=[dRW[si]], W=[g_.d])
                        P.dma(SP, bo[:, 0:n], RW[10 * 256 + h * 64:10 * 256 + (h + 1) * 64, cols], R=[dRW[si]], W=[bo.d])
                        P.op(DVE, lambda e: e.tensor_tensor(out=a[:, 0:n], in0=a[:, 0:n], in1=b[:, 0:n], op=ALU.add), R=[a.d, b.d], W=[a.d])
                        pm_ = ps_r.next()
                        P.op(PE, lambda e: e.matmul(pm_[0:64, 0:n], lhsT=C('ones', rows=64, sub=(0, 64)), rhs=a[:, 0:n], start=True, stop=True),
                             R=[a.d, cp.d], W=[pm_.d])
                        P.op(DVE, lambda e: e.scalar_tensor_tensor(out=yc[:, 0:n], in0=pm_[0:64, 0:n], scalar=-1.0 / 64, in1=a[:, 0:n],
                                                                   op0=ALU.mult, op1=ALU.add), R=[pm_.d, a.d], W=[yc.d])
                        P.op(ACT, lambda e: e.activation(out=sq[:, 0:n], in_=yc[:, 0:n], func=AF.Square), R=[yc.d], W=[sq.d])
                        pv = ps_r.next()
                        P.op(PE, lambda e: e.matmul(pv[0:64, 0:n], lhsT=C('ones', rows=64, sub=(0, 64)), rhs=sq[:, 0:n], start=True, stop=True),
                             R=[sq.d, cp.d], W=[pv.d])
                        P.op(ACT, lambda e: e.activation(out=rs[:, 0:n], in_=pv[0:64, 0:n], func=AF.Sqrt, bias=epsg[:, 0:1], scale=1.0 / 64),
                             R=[pv.d, epsg.d], W=[rs.d])
                        P.op(DVE, lambda e: e.reciprocal(out=rs[:, 0:n], in_=rs[:, 0:n]), R=[rs.d], W=[rs.d])
                        P.op(DVE, lambda e: e.scalar_tensor_tensor(out=yc[:, 0:n], in0=yc[:, 0:n], scalar=pp[l][0:64, olg + h:olg + h + 1], in1=rs[:, 0:n],
                                                                   op0=ALU.mult, op1=ALU.mult), R=[yc.d, rs.d, pp[l].d], W=[yc.d])
                        P.op(DVE, lambda e: e.scalar_tensor_tensor(out=yc[:, 0:n], in0=yc[:, 0:n], scalar=pp[l][0:64, olb + h:olb + h + 1], in1=bo[:, 0:n],
                                                                   op0=ALU.add, op1=ALU.add), R=[yc.d, bo.d, pp[l].d], W=[yc.d])
                        oy = oy_r.next()
                        P.op(DVE, lambda e: e.tensor_tensor(out=oy[:, 0:n], in0=yc[:, 0:n], in1=g_[:, 0:n], op=ALU.mult), R=[yc.d, g_.d], W=[oy.d])
                        P.dma(SP, OT[h * 64:(h + 1) * 64, cols], oy[:, 0:n], R=[oy.d], W=[dOT[si]])

        def mixer_ssd(l):
            NCH = TS // 128
            raw = T(P.sb("raw", [64, TS]))
            tcv = T(P.sb("tcv", [64, TS]))
            xcf = [T(P.sb("xcf%d" % h, [64, TS], BF16)) for h in range(4)]
            Bb = [T(P.sb("Bb%d" % g, [64, TS], BF16)) for g in range(2)]
            Cb = [T(P.sb("Cb%d" % g, [64, TS], BF16)) for g in range(2)]
            xT = [T(P.sb("xT%d" % h, [128, NCH, 64], BF16)) for h in range(4)]
            yz = [T(P.sb("yz%d" % h, [64, TS], BF16)) for h in range(4)]
            dtr = T(P.sb("dtr", [128, NCH, 8]))
            dts = T(P.sb("dts", [128, NCH, 8]))
            gg = T(P.sb("gg", [128, NCH, 8]))
            nea = T(P.sb("nea", [128, 8]))
            S_ = [T(P.sb("S%d" % d, [64, 64])) for d in range(2)]
            Sfb = T(P.sb("Sfb", [64, 64], BF16))
            Sbs = T(P.sb("Sbs", [64, NCH, 64], BF16))
            cumc = Ring([T(P.sb("cumc", [128, 8])) for _ in range(2)])
            gB_r = Ring([T(P.sb("gB", [128, 128])) for _ in range(2)])
            arg_r = Ring([T(P.sb("arg", [128, 128])) for _ in range(2)])
            Dm = [T(P.sb("Dm%d" % d, [128, 128])) for d in range(2)]
            Ds = T(P.sb("Ds", [128, 128]))
            PT_r = Ring([T(P.sb("PT", [128, 128], BF16)) for _ in range(2)])
            sm_r = Ring([T(P.sb("sm", [128, 4])) for _ in range(4)])
            ecr_r = Ring([T(P.sb("ecr", [64, 128])) for _ in range(2)])
            qd_r = Ring([T(P.sb("qd", [64, 128], BF16)) for _ in range(4)])
            Bw_r = Ring([T(P.sb("Bw", [128, 64], BF16)) for _ in range(2)])
            zt_ = T(P.sb("zt_", [64, 512]))
            t5 = T(P.sb("t5", [64, 512]))
            rsy = T(P.sb("rsy", [64, 512]))
            oy_r = Ring([T(P.sb("oy", [64, 512], OTDT)) for _ in range(2)])
            ps_cc = Ring([TPS(P.psum("ps_cc", [128, 512])) for _ in range(1)])
            ps_cr = Ring([TPS(P.psum("ps_cr", [128, 512])) for _ in range(2)])
            ps_sc = Ring([TPS(P.psum("ps_sc", [128, 512])) for _ in range(1)])
            ps_tr = Ring([TPS(P.psum("ps_tr", [128, 1024], BF16)) for _ in range(1)])
            ps_up = Ring([TPS(P.psum("ps_up", [128, 512])) for _ in range(1)])
            ps_y = Ring([TPS(P.psum("ps_y", [128, 512])) for _ in range(2)])
            ocw, _ = PPC['scw64']
            ocb, _ = PPC['scb64']
            odtb, _ = PPC['dtb']
            oal, _ = PPC['alog']
            od4, _ = PPC['ssd_d4']
            og4, _ = PPC['ssd_g4']
            P.op(ACT, lambda e: e.activation(out=nea[:, :], in_=pp[l][:, oal:oal + 8], func=AF.Exp), R=[pp[l].d], W=[nea.d])
            P.op(DVE, lambda e: e.tensor_scalar(out=nea[:, :], in0=nea[:, :], scalar1=-1.0, scalar2=None, op0=ALU.mult), R=[nea.d], W=[nea.d])
            for si, (kind, sidx, Tn, off) in enumerate(SEQS):
                nch = Tn // 128

                def convsilu(hc, dst, dstb):
                    r0 = 20 * 128 + hc * 64
                    P.dma(SP, raw[:, 0:Tn], FT[r0:r0 + 64, off:off + Tn], R=[dFT[si]], W=[raw.d])
                    P.op(ACT, lambda e: e.activation(out=tcv[:, 0:Tn], in_=raw[:, 0:Tn], func=AF.Identity,
                                                     bias=pp[l][0:64, ocb + hc:ocb + hc + 1], scale=pp[l][0:64, ocw + 8 + hc:ocw + 9 + hc]),
                         R=[raw.d, pp[l].d], W=[tcv.d])
                    P.op(DVE, lambda e: e.scalar_tensor_tensor(out=tcv[:, 1:Tn], in0=raw[:, 0:Tn - 1], scalar=pp[l][0:64, ocw + hc:ocw + hc + 1],
                                                               in1=tcv[:, 1:Tn], op0=ALU.mult, op1=ALU.add), R=[raw.d, pp[l].d, tcv.d], W=[tcv.d])
                    P.op(DVE, lambda e: e.scalar_tensor_tensor(out=tcv[:, 0:Tn - 1], in0=raw[:, 1:Tn], scalar=pp[l][0:64, ocw + 16 + hc:ocw + 17 + hc],
                                                               in1=tcv[:, 0:Tn - 1], op0=ALU.mult, op1=ALU.add), R=[raw.d, pp[l].d, tcv.d], W=[tcv.d])
                    if dst is not None:
                        P.op(ACT, lambda e: e.activation(out=dst[:, 0:Tn], in_=tcv[:, 0:Tn], func=AF.Silu), R=[tcv.d], W=[dst.d])
                        P.op(DVE, lambda e: e.tensor_copy(out=dstb[:, 0:Tn], in_=dst[:, 0:Tn]), R=[dst.d], W=[dstb.d])
                    else:
                        P.op(ACT, lambda e: e.activation(out=dstb[:, 0:Tn], in_=tcv[:, 0:Tn], func=AF.Silu), R=[tcv.d], W=[dstb.d])
                for g in range(2):
                    convsilu(4 + g, None, Bb[g])
                    convsilu(6 + g, None, Cb[g])
                for h in range(4):
                    convsilu(h, None, xcf[h])
                    for c in range(nch):
                        pt = ps_tr.next()
                        P.op(PE, lambda e: e.transpose(pt[:, 0:64], xcf[h][:, c * 128:(c + 1) * 128], cpb[0:64, 0:64]), R=[xcf[h].d, cpb.d], W=[pt.d])
                        evac(xT[h][:, c, :], pt[:, 0:64], [pt.d], [xT[h].d])
                P.dma(SP, dtr[:, 0:nch, :], FTM[off:off + Tn, 768:776].rearrange("(c p) e -> p c e", p=128), R=[dFTM[si]], W=[dtr.d])
                P.op(DVE, lambda e: e.tensor_tensor(out=dtr[:, 0:nch, :], in0=dtr[:, 0:nch, :],
                                                    in1=pp[l][:, odtb:odtb + 8].unsqueeze(1).to_broadcast([128, nch, 8]), op=ALU.add),
                     R=[dtr.d, pp[l].d], W=[dtr.d])
                P.op(ACT, lambda e: e.activation(out=dtr[:, 0:nch, :], in_=dtr[:, 0:nch, :], func=AF.Exp), R=[dtr.d], W=[dtr.d])
                P.op(ACT, lambda e: e.activation(out=dts[:, 0:nch, :], in_=dtr[:, 0:nch, :], func=AF.Ln, bias=1.0, scale=1.0), R=[dtr.d], W=[dts.d])
                P.op(DVE, lambda e: e.tensor_tensor(out=gg[:, 0:nch, :], in0=dts[:, 0:nch, :],
                                                    in1=nea[:, :].unsqueeze(1).to_broadcast([128, nch, 8]), op=ALU.mult),
                     R=[dts.d, nea.d], W=[gg.d])

                def chunk_cum(c):
                    pc = ps_cc.next()
                    P.op(PE, lambda e: e.matmul(pc[:, 0:4], lhsT=C('tri_f'), rhs=gg[:, c, 0:4], start=True, stop=True), R=[cp.d, gg.d], W=[pc.d])
                    P.op(PE, lambda e: e.matmul(pc[:, 4:8], lhsT=C('tri_b'), rhs=gg[:, c, 4:8], start=True, stop=True), R=[cp.d, gg.d], W=[pc.d])
                    cc = cumc.next()
                    P.op(DVE, lambda e: e.tensor_copy(out=cc[:, :], in_=pc[:, 0:8]), R=[pc.d], W=[cc.d])
                    return cc

                def dirstuff(c, h, d, cc):
                    col = d * 4 + h
                    gB = gB_r.next()
                    P.op(DVE, lambda e: e.tensor_scalar(out=gB[:, :], in0=C('ones'), scalar1=gg[:, c, col:col + 1], scalar2=None, op0=ALU.mult),
                         R=[cp.d, gg.d], W=[gB.d])
                    pr_ = ps_cr.next()
                    P.op(PE, lambda e: e.matmul(pr_[:, 0:128], lhsT=gB[:, :], rhs=C('tri_f' if d == 0 else 'tri_b'), start=True, stop=True),
                         R=[gB.d, cp.d], W=[pr_.d])
                    sm = sm_r.next()
                    lc = 127 if d == 0 else 0
                    P.op(DVE, lambda e: e.tensor_copy(out=sm[:, 0:1], in_=pr_[:, lc:lc + 1]), R=[pr_.d], W=[sm.d])
                    P.op(ACT, lambda e: e.activation(out=sm[:, 1:2], in_=cc[:, col:col + 1], func=AF.Exp, bias=sm[:, 0:1], scale=-1.0),
                         R=[cc.d, sm.d], W=[sm.d])
                    P.op(DVE, lambda e: e.tensor_tensor(out=sm[:, 2:3], in0=sm[:, 1:2], in1=dts[:, c, col:col + 1], op=ALU.mult),
                         R=[sm.d, dts.d], W=[sm.d])
                    P.op(ACT, lambda e: e.activation(out=sm[:, 3:4], in_=sm[:, 0:1], func=AF.Exp), R=[sm.d], W=[sm.d])
                    return pr_, sm

                def state_update(S, c, h, g, sm):
                    pt = ps_tr.next()
                    P.op(PE, lambda e: e.transpose(pt[:, 0:64], Bb[g][:, c * 128:(c + 1) * 128], cpb[0:64, 0:64]), R=[Bb[g].d, cpb.d], W=[pt.d])
                    Bw = Bw_r.next()
                    P.op(DVE, lambda e: e.tensor_scalar(out=Bw[:, :], in0=pt[:, 0:64], scalar1=sm[:, 2:3], scalar2=None, op0=ALU.mult),
                         R=[pt.d, sm.d], W=[Bw.d])
                    pu = ps_up.next()
                    P.op(PE, lambda e: e.matmul(pu[0:64, 0:64], lhsT=Bw[:, :], rhs=xT[h][:, c, :], start=True, stop=True), R=[Bw.d, xT[h].d], W=[pu.d])
                    P.op(DVE, lambda e: e.scalar_tensor_tensor(out=S[:, :], in0=S[:, :], scalar=sm[0:64, 3:4], in1=pu[0:64, 0:64],
                                                               op0=ALU.mult, op1=ALU.add), R=[S.d, pu.d, sm.d], W=[S.d])

                for h in range(4):
                    g = h // 2
                    Sf, Sb = S_
                    if kind == 's':
                        P.dma(SP, Sf[:, :], st_ssd[l, 0, h], W=[Sf.d])
                        P.dma(SP, Sb[:, :], st_ssd[l, 1, h], W=[Sb.d])
                    else:
                        P.op(DVE, lambda e: e.memset(Sf[:, :], 0.0), W=[Sf.d])
                        P.op(DVE, lambda e: e.memset(Sb[:, :], 0.0), W=[Sb.d])
                    for c in range(nch - 1, -1, -1):
                        P.op(ACT, lambda e: e.activation(out=Sbs[:, c, :], in_=Sb[:, :], func=AF.Copy), R=[Sb.d], W=[Sbs.d])
                        cc = chunk_cum(c)
                        pr_, sm = dirstuff(c, h, 1, cc)
                        state_update(Sb, c, h, g, sm)
                    if kind == 'p':
                        P.dma(SP, o_ssd[sidx, l, 1, h], Sb[:, :], R=[Sb.d], W=[dOUT])
                    for c0 in range(0, nch, 4):
                        ncg = min(4, nch - c0)
                        n = ncg * 128
                        py = ps_y.next()
                        for ci in range(ncg):
                            c = c0 + ci
                            sl = slice(c * 128, (c + 1) * 128)
                            cc = chunk_cum(c)
                            psc = ps_sc.next()
                            P.op(PE, lambda e: e.matmul(psc[:, 0:128], lhsT=Bb[g][:, sl], rhs=Cb[g][:, sl], start=True, stop=True),
                                 R=[Bb[g].d, Cb[g].d], W=[psc.d])
                            qds = []
                            sms = []
                            for d in range(2):
                                col = d * 4 + h
                                pr_, sm = dirstuff(c, h, d, cc)
                                sms.append(sm)
                                arg = arg_r.next()
                                P.op(DVE, lambda e: e.scalar_tensor_tensor(out=arg[:, :], in0=pr_[:, 0:128], scalar=cc[:, col:col + 1],
                                                                           in1=C('nm_f' if d == 0 else 'nm_b'), op0=ALU.subtract, op1=ALU.add),
                                     R=[pr_.d, cc.d, cp.d], W=[arg.d])
                                P.op(ACT, lambda e: e.activation(out=Dm[d][:, :], in_=arg[:, :], func=AF.Exp), R=[arg.d], W=[Dm[d].d])
                                ecr = ecr_r.next()
                                P.op(ACT, lambda e: e.activation(out=ecr[:, :], in_=pr_[0:64, 0:128], func=AF.Exp), R=[pr_.d], W=[ecr.d])
                                qd = qd_r.next()
                                P.op(DVE, lambda e: e.tensor_tensor(out=qd[:, :], in0=Cb[g][:, sl], in1=ecr[:, :], op=ALU.mult),
                                     R=[Cb[g].d, ecr.d], W=[qd.d])
                                qds.append(qd)
                            P.op(DVE, lambda e: e.tensor_scalar(out=Ds[:, :], in0=Dm[0][:, :], scalar1=dts[:, c, h:h + 1], scalar2=None, op0=ALU.mult),
                                 R=[Dm[0].d, dts.d], W=[Ds.d])
                            P.op(DVE, lambda e: e.scalar_tensor_tensor(out=Ds[:, :], in0=Dm[1][:, :], scalar=dts[:, c, 4 + h:5 + h], in1=Ds[:, :],
                                                                       op0=ALU.mult, op1=ALU.add), R=[Dm[1].d, dts.d, Ds.d], W=[Ds.d])
                            PT = PT_r.next()
                            P.op(DVE, lambda e: e.tensor_tensor(out=PT[:, :], in0=psc[:, 0:128], in1=Ds[:, :], op=ALU.mult), R=[psc.d, Ds.d], W=[PT.d])
                            P.op(ACT, lambda e: e.activation(out=Sfb[:, :], in_=Sf[:, :], func=AF.Copy), R=[Sf.d], W=[Sfb.d])
                            yo = py[0:64, ci * 128:(ci + 1) * 128]
                            P.op(PE, lambda e: e.matmul(yo, lhsT=xT[h][:, c, :], rhs=PT[:, :], start=True, stop=False), R=[xT[h].d, PT.d], W=[py.d])
                            P.op(PE, lambda e: e.matmul(yo, lhsT=Sfb[:, :], rhs=qds[0][:, :], start=False, stop=False), R=[Sfb.d, qds[0].d], W=[py.d])
                            P.op(PE, lambda e: e.matmul(yo, lhsT=Sbs[:, c, :], rhs=qds[1][:, :], start=False, stop=True), R=[Sbs.d, qds[1].d], W=[py.d])
                            state_update(Sf, c, h, g, sms[0])
                        tsl = slice(c0 * 128, c0 * 128 + n)
                        r0 = 18 * 128 + h * 64
                        P.dma(SP, zt_[:, 0:n], FT[r0:r0 + 64, off + c0 * 128: off + c0 * 128 + n], R=[dFT[si]], W=[zt_.d])
                        P.op(ACT, lambda e: e.activation(out=zt_[:, 0:n], in_=zt_[:, 0:n], func=AF.Silu), R=[zt_.d], W=[zt_.d])
                        P.op(DVE, lambda e: e.scalar_tensor_tensor(out=t5[:, 0:n], in0=xcf[h][:, tsl], scalar=pp[l][0:64, od4 + h:od4 + h + 1],
                                                                   in1=py[0:64, 0:n], op0=ALU.mult, op1=ALU.add), R=[xcf[h].d, pp[l].d, py.d], W=[t5.d])
                        P.op(DVE, lambda e: e.tensor_tensor(out=yz[h][:, tsl], in0=t5[:, 0:n], in1=zt_[:, 0:n], op=ALU.mult),
                             R=[t5.d, zt_.d], W=[yz[h].d])
                    if kind == 'p':
                        P.dma(SP, o_ssd[sidx, l, 0, h], Sf[:, :], R=[Sf.d], W=[dOUT])
                nq = min(512, Tn)
                for q0 in range(0, Tn, nq):
                    n = nq
                    tsl = slice(q0, q0 + n)
                    pss = ps_cc.next()
                    for h in range(4):
                        P.op(ACT, lambda e: e.activation(out=t5[:, 0:n], in_=yz[h][:, tsl], func=AF.Square), R=[yz[h].d], W=[t5.d])
                        P.op(PE, lambda e: e.matmul(pss[0:64, 0:n], lhsT=C('ones', rows=64, sub=(0, 64)), rhs=t5[:, 0:n], start=(h == 0), stop=(h == 3)),
                             R=[t5.d, cp.d], W=[pss.d])
                    P.op(ACT, lambda e: e.activation(out=rsy[:, 0:n], in_=pss[0:64, 0:n], func=AF.Sqrt, bias=C('eps', rows=64), scale=1.0 / 256),
                         R=[pss.d, cp.d], W=[rsy.d])
                    P.op(DVE, lambda e: e.reciprocal(out=rsy[:, 0:n], in_=rsy[:, 0:n]), R=[rsy.d], W=[rsy.d])
                    for h in range(4):
                        oy = oy_r.next()
                        P.op(DVE, lambda e: e.scalar_tensor_tensor(out=oy[:, 0:n], in0=yz[h][:, tsl], scalar=pp[l][0:64, og4 + h:og4 + h + 1],
                                                                   in1=rsy[:, 0:n], op0=ALU.mult, op1=ALU.mult), R=[yz[h].d, pp[l].d, rsy.d], W=[oy.d])
                        P.dma(SP, OT[512 + h * 64:512 + (h + 1) * 64, off + q0: off + q0 + n], oy[:, 0:n], R=[oy.d], W=[dOT[si]])

        def mixer_ret(l):
            ropc = T(P.sb("ropc", [64, TS]))
            rops = T(P.sb("rops", [64, TS]))
            P.dma(SP, ropc[:, :], rope_in[0, 0:64, :], W=[ropc.d])
            P.dma(SP, rops[:, :], rope_in[1, 0:64, :], W=[rops.d])
            qf32 = T(P.sb("qf32", [64, TS]))
            qs32 = T(P.sb("qs32", [64, TS]))
            qb = T(P.sb("qb", [64, TS], BF16))
            kb = T(P.sb("kb", [64, TS], BF16))
            g32 = T(P.sb("g32", [64, TS]))
            v32 = T(P.sb("v32", [128, TS // 128, 64]))
            vb = T(P.sb("vb", [128, TS // 128, 64], BF16))
            Sf = T(P.sb("Sf", [64, 64]))
            Sb = T(P.sb("Sb", [64, 64]))
            Sfb = T(P.sb("Sfb", [64, 64], BF16))
            Sbs = T(P.sb("Sbs", [64, TS // 128, 64], BF16))
            kw_r = Ring([T(P.sb("kw", [128, 64], BF16)) for _ in range(2)])
            PT_r = Ring([T(P.sb("PT", [128, 128], BF16)) for _ in range(2)])
            qd_r = Ring([T(P.sb("qd", [64, 128], BF16)) for _ in range(4)])
            sqy = T(P.sb("sqy", [64, 512]))
            rsy = T(P.sb("rsy", [64, 512]))
            sg = T(P.sb("sg", [64, 512]))
            tmy = T(P.sb("tmy", [64, 512]))
            oy_r = Ring([T(P.sb("oy", [64, 512], OTDT)) for _ in range(2)])
            ps_sc = Ring([TPS(P.psum("ps_sc", [128, 512])) for _ in range(2)])
            ps_tr = Ring([TPS(P.psum("ps_tr", [128, 1024], BF16)) for _ in range(1)])
            ps_up = Ring([TPS(P.psum("ps_up", [128, 512])) for _ in range(1)])
            ps_y = Ring([TPS(P.psum("ps_y", [128, 512])) for _ in range(2)])
            ps_ss = Ring([TPS(P.psum("ps_ss", [128, 512])) for _ in range(1)])
            og4, _ = PPC['ret_g4']
            for si, (kind, sidx, Tn, off) in enumerate(SEQS):
                nch = Tn // 128
                for h in range(4):
                    pr = (h % 2) * 64

                    def rows(cbase):
                        r0 = (cbase + h // 2) * 128 + pr
                        return FT[r0:r0 + 64, off:off + Tn]
                    for (dst, cb, cbs, scale) in ((qb, 8, 12, 1.0), (kb, 10, 14, 0.125)):
                        P.dma(SP, qf32[:, 0:Tn], rows(cb), R=[dFT[si]], W=[qf32.d])
                        if kind == 's':
                            P.dma(SP, qs32[:, 0:Tn], rows(cbs), R=[dFT[si]], W=[qs32.d])
                            P.op(DVE, lambda e: e.tensor_tensor(out=qf32[:, 0:Tn], in0=qf32[:, 0:Tn], in1=ropc[:, 0:Tn], op=ALU.mult),
                                 R=[qf32.d, ropc.d], W=[qf32.d])
                            P.op(DVE, lambda e: e.tensor_tensor(out=qs32[:, 0:Tn], in0=qs32[:, 0:Tn], in1=rops[:, 0:Tn], op=ALU.mult),
                                 R=[qs32.d, rops.d], W=[qs32.d])
                            P.op(DVE, lambda e: e.tensor_tensor(out=qf32[:, 0:Tn], in0=qf32[:, 0:Tn], in1=qs32[:, 0:Tn], op=ALU.add),
                                 R=[qf32.d, qs32.d], W=[qf32.d])
                        P.op(ACT, lambda e, dst=dst, scale=scale: e.activation(out=dst[:, 0:Tn], in_=qf32[:, 0:Tn], func=AF.Copy, scale=scale),
                             R=[qf32.d], W=[dst.d])
                    P.dma(SP, g32[:, 0:Tn], rows(16), R=[dFT[si]], W=[g32.d])
                    P.dma(SP, v32[:, 0:nch, :], FTM[off:off + Tn, h * 64:(h + 1) * 64].rearrange("(c p) e -> p c e", p=128),
                          R=[dFTM[si]], W=[v32.d])
                    P.op(DVE, lambda e: e.tensor_copy(out=vb[:, 0:nch, :], in_=v32[:, 0:nch, :]), R=[v32.d], W=[vb.d])
                    if kind == 's':
                        P.dma(SP, Sf[:, :], st_ret[l, 0, h], W=[Sf.d])
                        P.dma(SP, Sb[:, :], st_ret[l, 1, h], W=[Sb.d])
                    else:
                        P.op(DVE, lambda e: e.memset(Sf[:, :], 0.0), W=[Sf.d])
                        P.op(DVE, lambda e: e.memset(Sb[:, :], 0.0), W=[Sb.d])

                    def state_update(S, c, kwcol, gcol):
                        pt = ps_tr.next()
                        P.op(PE, lambda e: e.transpose(pt[:, 0:64], kb[:, c * 128:(c + 1) * 128], cpb[0:64, 0:64]),
                             R=[kb.d, cpb.d], W=[pt.d])
                        kw = kw_r.next()
                        P.op(DVE, lambda e: e.tensor_scalar(out=kw[:, :], in0=pt[:, 0:64], scalar1=kwcol, scalar2=None, op0=ALU.mult),
                             R=[pt.d, cp.d], W=[kw.d])
                        pu = ps_up.next()
                        P.op(PE, lambda e: e.matmul(pu[0:64, 0:64], lhsT=kw[:, :], rhs=vb[:, c, :], start=True, stop=True),
                             R=[kw.d, vb.d], W=[pu.d])
                        P.op(DVE, lambda e: e.scalar_tensor_tensor(out=S[:, :], in0=S[:, :], scalar=gcol, in1=pu[0:64, 0:64],
                                                                   op0=ALU.mult, op1=ALU.add), R=[S.d, pu.d, cp.d], W=[S.d])
                    okf, _ = CPC['ret_kwf']
                    okb, _ = CPC['ret_kwb']
                    og, _ = CPC['ret_g128']
                    for c in range(nch - 1, -1, -1):
                        P.op(ACT, lambda e, c=c: e.activation(out=Sbs[:, c, :], in_=Sb[:, :], func=AF.Copy), R=[Sb.d], W=[Sbs.d])
                        state_update(Sb, c, cp[:, okb + h:okb + h + 1], cp[0:64, og + 4 + h:og + 5 + h])
                    if kind == 'p':
                        P.dma(SP, o_ret[sidx, l, 1, h], Sb[:, :], R=[Sb.d], W=[dOUT])
                    for c0 in range(0, nch, 4):
                        ncg = min(4, nch - c0)
                        n = ncg * 128
                        py = ps_y.next()
                        for cc in range(ncg):
                            c = c0 + cc
                            sl = slice(c * 128, (c + 1) * 128)
                            psc = ps_sc.next()
                            P.op(PE, lambda e: e.matmul(psc[:, 0:128], lhsT=kb[:, sl], rhs=qb[:, sl], start=True, stop=True),
                                 R=[kb.d, qb.d], W=[psc.d])
                            PT = PT_r.next()
                            P.op(DVE, lambda e: e.tensor_tensor(out=PT[:, :], in0=psc[:, 0:128], in1=C('ret_ds', sub=(h * 128, (h + 1) * 128)), op=ALU.mult),
                                 R=[psc.d, cp.d], W=[PT.d])
                            qfw = qd_r.next()
                            qbw = qd_r.next()
                            P.op(DVE, lambda e: e.tensor_tensor(out=qfw[:, :], in0=qb[:, sl], in1=C('ret_df', rows=64, sub=(h * 128, (h + 1) * 128)), op=ALU.mult),
                                 R=[qb.d, cp.d], W=[qfw.d])
                            P.op(DVE, lambda e: e.tensor_tensor(out=qbw[:, :], in0=qb[:, sl], in1=C('ret_db', rows=64, sub=(h * 128, (h + 1) * 128)), op=ALU.mult),
                                 R=[qb.d, cp.d], W=[qbw.d])
                            P.op(ACT, lambda e: e.activation(out=Sfb[:, :], in_=Sf[:, :], func=AF.Copy), R=[Sf.d], W=[Sfb.d])
                            yo = py[0:64, cc * 128:(cc + 1) * 128]
                            P.op(PE, lambda e: e.matmul(yo, lhsT=vb[:, c, :], rhs=PT[:, :], start=True, stop=False), R=[vb.d, PT.d], W=[py.d])
                            P.op(PE, lambda e: e.matmul(yo, lhsT=Sfb[:, :], rhs=qfw[:, :], start=False, stop=False), R=[Sfb.d, qfw.d], W=[py.d])
                            P.op(PE, lambda e: e.matmul(yo, lhsT=Sbs[:, c, :], rhs=qbw[:, :], start=False, stop=True), R=[Sbs.d, qbw.d], W=[py.d])
                            state_update(Sf, c, cp[:, okf + h:okf + h + 1], cp[0:64, og + h:og + h + 1])
                        tsl = slice(c0 * 128, c0 * 128 + n)
                        P.op(ACT, lambda e: e.activation(out=sqy[:, 0:n], in_=py[0:64, 0:n], func=AF.Square), R=[py.d], W=[sqy.d])
                        pss = ps_ss.next()
                        P.op(PE, lambda e: e.matmul(pss[0:64, 0:n], lhsT=C('ones', rows=64, sub=(0, 64)), rhs=sqy[:, 0:n], start=True, stop=True),
                             R=[sqy.d, cp.d], W=[pss.d])
                        P.op(ACT, lambda e: e.activation(out=rsy[:, 0:n], in_=pss[0:64, 0:n], func=AF.Sqrt, bias=C('eps', rows=64), scale=1.0 / 64),
                             R=[pss.d, cp.d], W=[rsy.d])
                        P.op(DVE, lambda e: e.reciprocal(out=rsy[:, 0:n], in_=rsy[:, 0:n]), R=[rsy.d], W=[rsy.d])
                        P.op(ACT, lambda e: e.activation(out=sg[:, 0:n], in_=g32[:, tsl], func=AF.Silu), R=[g32.d], W=[sg.d])
                        P.op(DVE, lambda e: e.scalar_tensor_tensor(out=tmy[:, 0:n], in0=py[0:64, 0:n], scalar=pp[l][0:64, og4 + h:og4 + h + 1],
                                                                   in1=rsy[:, 0:n], op0=ALU.mult, op1=ALU.mult),
                             R=[py.d, pp[l].d, rsy.d], W=[tmy.d])
                        oy = oy_r.next()
                        P.op(DVE, lambda e: e.tensor_tensor(out=oy[:, 0:n], in0=tmy[:, 0:n], in1=sg[:, 0:n], op=ALU.mult),
                             R=[tmy.d, sg.d], W=[oy.d])
                        P.dma(SP, OT[256 + h * 64:256 + (h + 1) * 64, off + c0 * 128: off + c0 * 128 + n], oy[:, 0:n], R=[oy.d], W=[dOT[si]])
                    if kind == 'p':
                        P.dma(SP, o_ret[sidx, l, 0, h], Sf[:, :], R=[Sf.d], W=[dOUT])

        for l in range(L):
            if debug == 'M':
                break
            xin, xout = XT[0], XT[1]
            dxin, dxout = dXT[0], dXT[1]
            P.begin()
            wfm = T(P.sb("wfm", [128, 8, NFM * 128], BF16))
            wtm = T(P.sb("wtm", [128, 8, NTM], BF16))
            for k in range(8):
                P.dma(POOL, wfm[:, k, :], w_in_fm[l, k * 128:(k + 1) * 128, :], W=[wfm.d])
            P.dma(POOL, wtm[:, :, :], w_in_tm[l].rearrange("(k p) n -> p k n", p=128), W=[wtm.d])
            xt_r = Ring([T(P.sb("xt", [128, 8, 512])) for _ in range(2)])
            xtm_r = Ring([T(P.sb("xtm", [128, 1024])) for _ in range(2)])
            sq = Ring([T(P.sb("sq", [128, 512])) for _ in range(2)])
            rs = T(P.sb("rs", [128, 512]))
            tmp_r = Ring([T(P.sb("tmp", [128, 512])) for _ in range(2)])
            hT_r = Ring([T(P.sb("hT", [128, 8, 512], BF16)) for _ in range(2)])
            stg_r = Ring([T(P.sb("stg", [128, 4, 512])) for _ in range(2)])
            stm_r = Ring([T(P.sb("stm", [128, NTM])) for _ in range(2)])
            ps_r = Ring([TPS(P.psum("psA", [128, 512])) for _ in range(7)])
            for si, (kind, sidx, Tn, off) in enumerate(SEQS):
                if debug == 'A1':
                    break
                cond = 0 if kind == 'p' else 1
                nt = min(512, Tn)
                for t0 in range(0, Tn, nt):
                    n = nt
                    xt = xt_r.next()
                    if l == 0:
                        for b in range(n // 128):
                            xtm = xtm_r.next()
                            P.dma(SP, xtm[:, :], x_tm[off + t0 + b * 128: off + t0 + (b + 1) * 128, :], W=[xtm.d])
                            for half in range(2):
                                ps = ps_r.next()
                                for jj in range(4):
                                    j = half * 4 + jj
                                    P.op(PE, lambda e, j=j, jj=jj, ps=ps, xtm=xtm: e.transpose(
                                        ps[:, jj * 128:(jj + 1) * 128], xtm[:, j * 128:(j + 1) * 128], C('ident')),
                                        R=[xtm.d, cp.d], W=[ps.d])
                                evac(xt[:, half * 4:half * 4 + 4, b * 128:(b + 1) * 128],
                                     ps[:, 0:512].rearrange("p (j t) -> p j t", t=128), [ps.d], [xt.d])
                        P.dma(POOL, xin[:, off + t0: off + t0 + n].rearrange("(j p) t -> p j t", p=128), xt[:, :, 0:n],
                              R=[xt.d], W=[dxin[si]])
                    else:
                        P.dma(SP, xt[:, :, 0:n], xin[:, off + t0: off + t0 + n].rearrange("(j p) t -> p j t", p=128),
                              R=[dxin[si]], W=[xt.d])
                    if debug == 'A2':
                        continue
                    hT = hT_r.next()
                    rmsnorm_tile(xt, n, l, 0, cond, hT, sq, ps_r, rs, tmp_r)
                    if debug == 'A3':
                        continue
                    chunks = list(range(NFM))
                    if kind == 'p':
                        chunks = [c for c in chunks if c not in (12, 13, 14, 15, 28, 29, 30, 31)]
                    groups = [chunks[i:i + 4] for i in range(0, len(chunks), 4)]
                    for grp in groups:
                        runs = []
                        for c in grp:
                            if runs and runs[-1][-1] == c - 1:
                                runs[-1].append(c)
                            else:
                                runs.append([c])
                        for run_ in runs:
                            stg = stg_r.next()
                            for ci, c in enumerate(run_):
                                ps = ps_r.next()
                                for k in range(8):
                                    P.op(PE, lambda e, k=k, c=c, ps=ps, hT=hT: e.matmul(
                                        ps[:, 0:n], lhsT=wfm[:, k, c * 128:(c + 1) * 128], rhs=hT[:, k, 0:n],
                                        start=(k == 0), stop=(k == 7)), R=[wfm.d, hT.d], W=[ps.d])
                                evac(stg[:, ci, 0:n], ps[:, 0:n], [ps.d], [stg.d])
                            c0, nc_ = run_[0], len(run_)
                            P.dma(SP, FT[c0 * 128:(c0 + nc_) * 128, off + t0: off + t0 + n].rearrange("(c p) t -> p c t", p=128),
                                  stg[:, 0:nc_, 0:n], R=[stg.d], W=[dFT[si]])
                    if debug == 'A4':
                        continue
                    for b in range(n // 128):
                        stm = stm_r.next()
                        for (c0, c1) in ((0, 512), (512, NTM)):
                            ps = ps_r.next()
                            for k in range(8):
                                P.op(PE, lambda e, k=k, ps=ps, hT=hT, b=b, c0=c0, c1=c1: e.matmul(
                                    ps[:, 0:c1 - c0], lhsT=hT[:, k, b * 128:(b + 1) * 128], rhs=wtm[:, k, c0:c1],
                                    start=(k == 0), stop=(k == 7)), R=[wtm.d, hT.d], W=[ps.d])
                            evac(stm[:, c0:c1], ps[:, 0:c1 - c0], [ps.d], [stm.d])
                        r0 = off + t0 + b * 128
                        P.dma(SP, FTM[r0:r0 + 128, :], stm[:, :], R=[stm.d], W=[dFTM[si]])
                        if kind == 'p' and debug != 'A5':
                            tl = t0 + b * 128
                            P.dma(SP, o_v[sidx, l, tl:tl + 128, :], stm[:, 256:512], R=[stm.d], W=[Dep()] if debug == 'A6' else [dOUT])
                            P.dma(SP, o_k[sidx, l, tl:tl + 128, :], stm[:, 512:768], R=[stm.d], W=[Dep()] if debug == 'A6' else [dOUT])
            P.end()
            if debug and debug[0] == 'A':
                break

            if not (debug and debug[0] == 'C'):
                phase_B(l)
            if debug == 'B':
                break

            P.begin()
            wo = T(P.sb("wo", [128, 8, D], BF16))
            P.dma(POOL, wo[:, :, :], w_out[l].rearrange("(k p) n -> p k n", p=128), W=[wo.d])
            xt_r = Ring([T(P.sb("xt", [128, 8, 512])) for _ in range(2)])
            ot_r = Ring([T(P.sb("ot", [128, 8, 512], BF16)) for _ in range(2)])
            ps_r = Ring([TPS(P.psum("psC", [128, 512])) for _ in range(6)])
            for si, (kind, sidx, Tn, off) in enumerate(SEQS):
                cond = 0 if kind == 'p' else 1
                nt = min(512, Tn)
                for t0 in range(0, Tn, nt):
                    n = nt
                    xt = xt_r.next()
                    ot = ot_r.next()
                    P.dma(SP, xt[:, :, 0:n], xin[:, off + t0: off + t0 + n].rearrange("(j p) t -> p j t", p=128),
                          R=[dxin[si]], W=[xt.d])
                    P.dma(POOL if (debug and debug[0] == 'C') else SP, ot[:, :, 0:n], OT[:, off + t0: off + t0 + n].rearrange("(j p) t -> p j t", p=128),
                          R=[dOT[si]], W=[ot.d])
                    for j in range(8):
                        ps = ps_r.next()
                        for k in range(8):
                            P.op(PE, lambda e, k=k, j=j, ps=ps, ot=ot: e.matmul(
                                ps[:, 0:n], lhsT=wo[:, k, j * 128:(j + 1) * 128], rhs=ot[:, k, 0:n],
                                start=(k == 0), stop=(k == 7)), R=[wo.d, ot.d], W=[ps.d])
                        P.op(DVE, lambda e, j=j, ps=ps, xt=xt: e.scalar_tensor_tensor(
                            out=xt[:, j, 0:n], in0=ps[:, 0:n], scalar=modcol(l, 2, j, cond), in1=xt[:, j, 0:n],
                            op0=ALU.mult, op1=ALU.add), R=[ps.d, xt.d, modv[l].d], W=[xt.d])
                    P.dma(SP, xout[:, off + t0: off + t0 + n].rearrange("(j p) t -> p j t", p=128), xt[:, :, 0:n],
                          R=[xt.d], W=[dxout[si]])
            P.end()
            if debug == 'C1':
                break
            if debug == 'CA':
                continue
            P.begin()
            wu = T(P.sb("wu", [128, 8, 2 * FFN], BF16))
            wd = T(P.sb("wd", [128, 22, D], BF16))
            for k in range(8):
                P.dma(POOL, wu[:, k, :], w_fup[l, k * 128:(k + 1) * 128, :], W=[wu.d])
            for g in range(22):
                P.dma(POOL, wd[:, g, :], w_fdn[l, g * 128:(g + 1) * 128, :], W=[wd.d])
            xt_r = Ring([T(P.sb("xt", [128, 8, 384])) for _ in range(1)])
            sq = Ring([T(P.sb("sq", [128, 384])) for _ in range(2)])
            rs = T(P.sb("rs", [128, 384]))
            tmp_r = Ring([T(P.sb("tmp", [128, 384])) for _ in range(1)])
            tg_r = Ring([T(P.sb("tg", [128, 384])) for _ in range(1)])
            tv_r = Ring([T(P.sb("tv", [128, 384])) for _ in range(1)])
            hT_r = Ring([T(P.sb("hT", [128, 8, 384], BF16)) for _ in range(1)])
            aT = T(P.sb("aT", [128, 22, 384], BF16))
            ps_r = Ring([TPS(P.psum("psC", [128, 384])) for _ in range(7)])
            ocw, _ = PPC['fcw']
            ocb, _ = PPC['fcb']
            for si, (kind, sidx, Tn, off) in enumerate(SEQS):
                cond = 0 if kind == 'p' else 1
                nt = min(382, Tn)
                for t0 in range(0, Tn, nt):
                    n = min(nt, Tn - t0)
                    N = n + 2
                    lo, hi = max(t0 - 1, 0), min(t0 + n + 1, Tn)
                    c_lo = lo - (t0 - 1)
                    xt = xt_r.next()
                    P.dma(SP, xt[:, :, c_lo:c_lo + hi - lo], xout[:, off + lo: off + hi].rearrange("(j p) t -> p j t", p=128),
                          R=[dxout[si]], W=[xt.d])
                    hT = hT_r.next()
                    rmsnorm_tile(xt, N, l, 1, cond, hT, sq, ps_r, rs, tmp_r)
                    a0 = 1 if t0 == 0 else 0
                    b1 = n - 1 if t0 + n == Tn else n
                    for g in range(22):
                        tt = []
                        for ci, (c, ring) in enumerate(((g, tg_r), (22 + g, tv_r))):
                            ps = ps_r.next()
                            for k in range(8):
                                P.op(PE, lambda e, k=k, c=c, ps=ps, hT=hT: e.matmul(
                                    ps[:, 0:N], lhsT=wu[:, k, c * 128:(c + 1) * 128], rhs=hT[:, k, 0:N],
                                    start=(k == 0), stop=(k == 7)), R=[wu.d, hT.d], W=[ps.d])
                            t = ring.next()
                            P.op(ACT, lambda e, c=c, ps=ps, t=t: e.activation(
                                out=t[:, 0:n], in_=ps[:, 1:n + 1], func=AF.Identity,
                                bias=pp[l][:, ocb + c:ocb + c + 1], scale=pp[l][:, ocw + 44 + c:ocw + 44 + c + 1]),
                                R=[ps.d, pp[l].d], W=[t.d])
                            P.op(DVE, lambda e, c=c, ps=ps, t=t: e.scalar_tensor_tensor(
                                out=t[:, a0:n], in0=ps[:, a0:n], scalar=pp[l][:, ocw + c:ocw + c + 1], in1=t[:, a0:n],
                                op0=ALU.mult, op1=ALU.add), R=[ps.d, pp[l].d, t.d], W=[t.d])
                            P.op(DVE, lambda e, c=c, ps=ps, t=t: e.scalar_tensor_tensor(
                                out=t[:, 0:b1], in0=ps[:, 2:b1 + 2], scalar=pp[l][:, ocw + 88 + c:ocw + 88 + c + 1], in1=t[:, 0:b1],
                                op0=ALU.mult, op1=ALU.add), R=[ps.d, pp[l].d, t.d], W=[t.d])
                            tt.append(t)
                        tg, tv = tt
                        P.op(ACT, lambda e, tg=tg: e.activation(out=tg[:, 0:n], in_=tg[:, 0:n], func=AF.Silu), R=[tg.d], W=[tg.d])
                        P.op(DVE, lambda e, tg=tg, tv=tv, g=g: e.tensor_tensor(out=aT[:, g, 0:n], in0=tg[:, 0:n], in1=tv[:, 0:n], op=ALU.mult),
                             R=[tg.d, tv.d], W=[aT.d])
                    for j in range(8):
                        ps = ps_r.next()
                        for g in range(22):
                            P.op(PE, lambda e, g=g, j=j, ps=ps: e.matmul(
                                ps[:, 0:n], lhsT=wd[:, g, j * 128:(j + 1) * 128], rhs=aT[:, g, 0:n],
                                start=(g == 0), stop=(g == 21)), R=[wd.d, aT.d], W=[ps.d])
                        P.op(DVE, lambda e, j=j, ps=ps, xt=xt: e.scalar_tensor_tensor(
                            out=xt[:, j, 1:n + 1], in0=ps[:, 0:n], scalar=modcol(l, 5, j, cond), in1=xt[:, j, 1:n + 1],
                            op0=ALU.mult, op1=ALU.add), R=[ps.d, xt.d, modv[l].d], W=[xt.d])
                    P.dma(SP, xin[:, off + t0: off + t0 + n].rearrange("(j p) t -> p j t", p=128), xt[:, :, 1:n + 1],
                          R=[xt.d], W=[dxin[si]])
            P.end()
            if debug == 'C':
                break

        if not debug:
            xin, dxin = XT[0], dXT[0]
            P.begin()
            xt_r = Ring([T(P.sb("xt", [128, 8, 512])) for _ in range(2)])
            sq = Ring([T(P.sb("sq", [128, 512])) for _ in range(2)])
            rs = T(P.sb("rs", [128, 512]))
            ytm_r = Ring([T(P.sb("ytm", [128, 1024])) for _ in range(2)])
            ps_r = Ring([TPS(P.psum("psF", [128, 512])) for _ in range(6)])
            onf, _ = PPC['nfg']
            for si, (kind, sidx, Tn, off) in enumerate(SEQS):
                nt = min(512, Tn)
                for t0 in range(0, Tn, nt):
                    n = nt
                    xt = xt_r.next()
                    P.dma(SP, xt[:, :, 0:n], xin[:, off + t0: off + t0 + n].rearrange("(j p) t -> p j t", p=128),
                          R=[dxin[si]], W=[xt.d])
                    ps = ps_r.next()
                    for k in range(8):
                        sqk = sq.next()
                        P.op(ACT, lambda e, k=k, sqk=sqk, xt=xt: e.activation(out=sqk[:, 0:n], in_=xt[:, k, 0:n], func=AF.Square), R=[xt.d], W=[sqk.d])
                        P.op(PE, lambda e, k=k, sqk=sqk, ps=ps: e.matmul(ps[:, 0:n], lhsT=C('ones'), rhs=sqk[:, 0:n], start=(k == 0), stop=(k == 7)),
                             R=[sqk.d, cp.d], W=[ps.d])
                    P.op(ACT, lambda e, ps=ps: e.activation(out=rs[:, 0:n], in_=ps[:, 0:n], func=AF.Sqrt, bias=C('eps'), scale=1.0 / D),
                         R=[ps.d, cp.d], W=[rs.d])
                    P.op(DVE, lambda e: e.reciprocal(out=rs[:, 0:n], in_=rs[:, 0:n]), R=[rs.d], W=[rs.d])
                    for j in range(8):
                        P.op(DVE, lambda e, j=j, xt=xt: e.scalar_tensor_tensor(
                            out=xt[:, j, 0:n], in0=xt[:, j, 0:n], scalar=pp[0][:, onf + j:onf + j + 1], in1=rs[:, 0:n],
                            op0=ALU.mult, op1=ALU.mult), R=[xt.d, pp[0].d, rs.d], W=[xt.d])
                    for b in range(n // 128):
                        ytm = ytm_r.next()
                        for half in range(2):
                            ps = ps_r.next()
                            for jj in range(4):
                                j = half * 4 + jj
                                P.op(PE, lambda e, j=j, jj=jj, ps=ps, xt=xt, b=b: e.transpose(
                                    ps[:, jj * 128:(jj + 1) * 128], xt[:, j, b * 128:(b + 1) * 128], C('ident')),
                                    R=[xt.d, cp.d], W=[ps.d])
                            evac(ytm[:, half * 512:(half + 1) * 512], ps[:, 0:512], [ps.d], [ytm.d])
                        tl = t0 + b * 128
                        if kind == 'p':
                            P.dma(SP, y_p[sidx * TP + tl: sidx * TP + tl + 128, :], ytm[:, :], R=[ytm.d], W=[dOUT])
                        else:
                            P.dma(SP, y_s[tl:tl + 128, :], ytm[:, :], R=[ytm.d], W=[dOUT])
            P.end()
        P.begin()
        for k, v in list(P.dtot.items()):
            P.q[POOL].append(([(k, v)], None, None))
        P.end()
    return nc


def make_in_maps(inp):
    fm, tm = build_perm()
    w_in = np.asarray(inp['w_in'], np.float32)
    w_aug = np.concatenate([w_in, np.zeros((L, D, 1), np.float32)], axis=2)
    w_in_fm = np.ascontiguousarray(w_aug[:, :, np.where(fm < 0, w_in.shape[2], fm)])
    w_in_tm = np.ascontiguousarray(w_in[:, :, tm])
    pp = np.stack([pack_params(inp, l) for l in range(L)])
    cp = build_consts()
    rope = rope_tables()
    maps = []
    for c in range(8):
        b = c // 4
        x_tm = np.concatenate([np.asarray(inp['x_prompt'][4 * c:4 * c + 4], np.float32).reshape(NPS * TP, D),
                               np.asarray(inp['x_sample'][b], np.float32)], axis=0)
        cvec = np.concatenate([fm_cols(inp['c_ctx']), fm_cols(inp['c'][b])], axis=1)
        maps.append(dict(
            x_tm=np.ascontiguousarray(x_tm), cvec=np.ascontiguousarray(cvec), w_mod=np.asarray(inp['w_mod'], np.float32),
            w_in_fm=w_in_fm, w_in_tm=w_in_tm, w_out=np.asarray(inp['w_out'], np.float32),
            w_fup=np.asarray(inp['ffn_w_up'], np.float32), w_fdn=np.asarray(inp['ffn_w_down'], np.float32),
            pp=pp, cp=cp, rope=rope,
            st_rwkv=np.ascontiguousarray(inp['state_rwkv'][b]), st_ret=np.ascontiguousarray(inp['state_ret'][b]),
            st_ssd=np.ascontiguousarray(inp['state_ssd'][b]),
            ck=np.ascontiguousarray(np.asarray(inp['cache_diff_k'][b]).reshape(L, 512, 256)),
            cv=np.ascontiguousarray(np.asarray(inp['cache_diff_v'][b]).reshape(L, 512, 256)),
        ))
    return maps


def kernel(**inputs):
    inp = {k: np.asarray(v) for k, v in inputs.items()}
    nc = build()
    maps = make_in_maps(inp)
    res = run_bass_kernel_spmd(nc, maps, core_ids=list(range(8)))
    r = res.results
    y_prompt = np.concatenate([r[c]['y_p'].reshape(NPS, TP, D) for c in range(8)], axis=0)
    y_sample = np.stack([np.concatenate([r[b * 4 + j]['y_s'][j * 1024:(j + 1) * 1024] for j in range(4)], axis=0) for b in range(2)])
    o_rwkv = np.concatenate([r[c]['o_rwkv'] for c in range(8)], axis=0)
    o_ret = np.concatenate([r[c]['o_ret'] for c in range(8)], axis=0)
    o_ssd = np.concatenate([r[c]['o_ssd'] for c in range(8)], axis=0)
    o_k = np.concatenate([r[c]['o_k'] for c in range(8)], axis=0).reshape(32, L, TP, 4, 2, 32)
    o_v = np.concatenate([r[c]['o_v'] for c in range(8)], axis=0).reshape(32, L, TP, 4, 64)
    return (y_prompt.astype(np.float32), y_sample.astype(np.float32), o_rwkv.astype(np.float32), o_ret.astype(np.float32),
            o_ssd.astype(np.float32), o_k.astype(np.float32), o_v.astype(np.float32))
```

```python
import math
import numpy as np
from contextlib import ExitStack
import concourse.bass as bass
import concourse.mybir as mybir
from concourse.bass_utils import run_bass_kernel_spmd

F32 = mybir.dt.float32
BF16 = mybir.dt.bfloat16
ALU = mybir.AluOpType
AF = mybir.ActivationFunctionType
PE, ACT, DVE, POOL, SP = 'pe', 'act', 'dve', 'pool', 'sp'

D = 1024
L = 2
TP = 256
TS = 4096
NPS = 4
TTOT = NPS * TP + TS
SEQS = [('p', i, TP, i * TP) for i in range(NPS)] + [('s', 0, TS, NPS * TP)]
NFM = 32
NTM = 776
FFN = 2816
EPS = 1e-6
MIXERS = ('ret', 'diff', 'ssd', 'rwkv')
RWKV_STAGES = 3
RW_DBG = {'seqs': 5, 'heads': 4, 'stop': 99, 'var': 0}


class Dep:
    __slots__ = ('name', 'lw', 'rd', 'dsem', 'sb', 'ps')

    def __init__(self, name='', sb=False):
        self.name = name
        self.lw = None
        self.rd = {}
        self.dsem = None
        self.sb = sb
        self.ps = False


class Rec:
    def __init__(self):
        self.call = None

    def __getattr__(self, name):
        def f(*a, **kw):
            self.call = (name, a, kw)
        return f


class Prog:
    def __init__(self, nc, es):
        self.nc = nc
        self.es = es
        self.ph = None
        self.q = {PE: [], ACT: [], DVE: [], POOL: [], SP: []}
        self.cnt = {PE: 0, ACT: 0, DVE: 0, POOL: 0}
        self.seen = {k: {} for k in self.q}
        self.sems = {}
        self.ndsem = 0
        for e in (PE, ACT, DVE, POOL):
            self.sems[e] = es.enter_context(nc.semaphore('s_' + e))
        self.dtot = {}
        self.nuniq = 0
        self.shared_dsem = None

    def sb(self, name, shape, dt=F32):
        self.nuniq += 1
        return self.ph.enter_context(self.nc.sbuf_tensor('%s_%d' % (name, self.nuniq), list(shape), dt))

    def psum(self, name, shape, dt=F32):
        self.nuniq += 1
        return self.ph.enter_context(self.nc.psum_tensor('%s_%d' % (name, self.nuniq), list(shape), dt))

    def dram(self, name, shape, dt=F32):
        return self.nc.dram_tensor(name, list(shape), dt, kind="Internal").ap()

    def _dsem(self, d):
        if d.dsem is None:
            d.dsem = 'd%d' % self.ndsem
            self.ndsem += 1
            if d.dsem not in self.sems:
                self.sems[d.dsem] = self.es.enter_context(self.nc.semaphore(d.dsem))
        return d.dsem

    def _waits(self, eng, R, W):
        waits = {}

        def add(k, v):
            if k == eng and eng == PE:
                return
            if k in self.dtot:
                v = self.dtot[k]
            if waits.get(k, 0) < v:
                waits[k] = v
        for d in R:
            if d.lw is not None:
                add(*d.lw)
            if d.ps:
                for k, v in d.rd.items():
                    if k != eng:
                        add(k, v)
        for d in W:
            if d.lw is not None and d.lw[0] != eng:
                add(*d.lw)
            for k, v in d.rd.items():
                if k != eng:
                    add(k, v)
        out = []
        seen = self.seen[eng]
        for k, v in waits.items():
            if seen.get(k, 0) < v:
                seen[k] = v
                out.append((k, v))
        return out

    def op(self, eng, fn, R=(), W=()):
        waits = self._waits(eng, R, W)
        idx = self.cnt[eng]
        self.cnt[eng] = idx + 1
        rec = Rec()
        fn(rec)
        name, a, kw = rec.call

        def fn2(e, name=name, a=a, kw=kw):
            return getattr(e, name)(*a, **kw)
        self.q[eng].append((waits, fn2, (eng, 1)))
        for d in R:
            if d.rd.get(eng, 0) < idx + 1:
                d.rd[eng] = idx + 1
        for d in W:
            d.lw = (eng, idx + 1)
            d.rd = {}

    def dma(self, queue, out, in_, R=(), W=(), **kw):
        waits = self._waits(queue, R, W)
        tgt = (list(W) + list(R))
        d0 = None
        for d in tgt:
            if d.sb:
                d0 = d
                break
        if d0 is None:
            d0 = tgt[0]
        k0 = self._dsem(d0)
        self.dtot[k0] = self.dtot.get(k0, 0) + 16
        tot = self.dtot[k0]

        def fn(e, out=out, in_=in_, kw=kw):
            return e.dma_start(out=out, in_=in_, **kw)
        self.q[queue].append((waits, fn, (k0, 16)))
        for d in W:
            d.lw = (k0, tot)
            d.rd = {}
        for d in R:
            if d.rd.get(k0, 0) < tot:
                d.rd[k0] = tot

    def barrier(self):
        allk = [(k, self.cnt[k]) for k in (PE, ACT, DVE, POOL)] + list(self.dtot.items())
        for eng in self.q:
            waits = []
            seen = self.seen[eng]
            for k, v in allk:
                if k == eng:
                    continue
                if seen.get(k, 0) < v:
                    seen[k] = v
                    waits.append((k, v))
            self.q[eng].append((waits, None, None))

    def begin(self):
        self.ph = ExitStack()
        self.ndsem = 8 if self.ndsem >= 8 else self.ndsem

    def end(self, final=False):
        self.barrier()
        nc = self.nc
        q = self.q
        sems = self.sems

        with nc.Block() as block:
            def run(e, name):
                for waits, fn, inc in q[name]:
                    for k, v in waits:
                        e.wait_ge(sems[k], v)
                    if fn is not None:
                        ins = fn(e)
                        ins.then_inc(sems[inc[0]], inc[1])

            @block.tensor
            def _(e):
                run(e, PE)

            @block.scalar
            def _(e):
                run(e, ACT)

            @block.vector
            def _(e):
                run(e, DVE)

            @block.gpsimd
            def _(e):
                run(e, POOL)

            @block.sync
            def _(e):
                run(e, SP)
        for k in q:
            q[k] = []
        self.ph.close()
        self.ph = None


class T:
    def __init__(self, t, name=''):
        self.t = t
        self.d = Dep(name, sb=True)

    def __getitem__(self, idx):
        return self.t[idx]


def TPS(t):
    x = T(t)
    x.d.ps = True
    x.d.sb = False
    return x


class Ring:
    def __init__(self, tiles):
        self.tiles = tiles
        self.i = 0

    def next(self):
        t = self.tiles[self.i % len(self.tiles)]
        self.i += 1
        return t


def fm_cols(v):
    v = np.asarray(v, np.float32)
    return np.ascontiguousarray(v.reshape(-1, 128).T)


def build_perm():
    RW, RET, SSD, DIF = 0, 960, 1984, 2760
    fm = []
    fm += list(range(RW, RW + 768))
    fm += list(range(RW + 768, RW + 896))
    fm += list(range(RW + 896, RW + 960)) + [-1] * 64
    q = list(range(RET, RET + 256))
    k = list(range(RET + 256, RET + 512))

    def swap(cols, blk):
        out = []
        for i in range(0, len(cols), blk):
            b = cols[i:i + blk]
            out += b[blk // 2:] + b[:blk // 2]
        return out
    fm += q + k + swap(q, 64) + swap(k, 64)
    fm += list(range(RET + 768, RET + 1024))
    fm += list(range(SSD, SSD + 256))
    fm += list(range(SSD + 256, SSD + 768))
    dq = list(range(DIF, DIF + 256))
    dk = list(range(DIF + 256, DIF + 512))
    fm += dq + dk + swap(dq, 32) + swap(dk, 32)
    assert len(fm) == NFM * 128
    tm = list(range(RET + 512, RET + 768)) + list(range(DIF + 512, DIF + 768)) + dk + list(range(SSD + 768, SSD + 776))
    assert len(tm) == NTM
    return np.array(fm), np.array(tm)


PPC = {}
_o = 0
for _n, _w in [('n1g', 8), ('n2g', 8), ('bmod', 48), ('mu', 8), ('w0', 4), ('a0', 4), ('k_k', 2), ('k_a', 2),
               ('r_k', 2), ('ln_g', 2), ('ln_b', 2), ('ret_g', 2), ('scw', 12), ('scb', 4), ('ssd_g', 2), ('ssd_d', 2),
               ('dtb', 8), ('alog', 8), ('subg', 1), ('lp', 128), ('fcw', 132), ('fcb', 44), ('w_up', 256),
               ('a_up', 256), ('g_up', 256), ('nfg', 8), ('ret_g4', 4), ('scw64', 24), ('scb64', 8), ('ssd_d4', 4), ('ssd_g4', 4), ('mu64', 16), ('w0_64', 8), ('a0_64', 8), ('kk64', 4), ('ka64', 4), ('rk64', 4), ('lng64', 4), ('lnb64', 4)]:
    PPC[_n] = (_o, _w)
    _o += _w
NPP = _o


def pack_params(inp, l):
    pp = np.zeros((128, NPP), np.float32)

    def put(name, arr):
        o, w = PPC[name]
        arr = np.asarray(arr, np.float32)
        assert arr.shape[1] == w, (name, arr.shape, w)
        pp[:arr.shape[0], o:o + w] = arr
    put('n1g', fm_cols(inp['norm1_g'][l]))
    put('n2g', fm_cols(inp['norm2_g'][l]))
    put('bmod', fm_cols(inp['b_mod'][l]))
    mu = np.zeros(1024, np.float32)
    mu[:960] = inp['rwkv_mu'][l]
    put('mu', fm_cols(mu))
    put('w0', fm_cols(inp['rwkv_w0'][l].reshape(-1)))
    put('a0', fm_cols(inp['rwkv_a0'][l].reshape(-1)))
    put('k_k', fm_cols(inp['rwkv_k_k'][l]))
    put('k_a', fm_cols(inp['rwkv_k_a'][l]))
    put('r_k', fm_cols(inp['rwkv_r_k'][l].reshape(-1)))
    put('ln_g', fm_cols(inp['rwkv_ln_g'][l]))
    put('ln_b', fm_cols(inp['rwkv_ln_b'][l]))
    put('ret_g', fm_cols(inp['ret_ln_g'][l]))
    cw = inp['ssd_conv_w'][l]
    put('scw', np.concatenate([fm_cols(cw[i]) for i in range(3)], axis=1))
    put('scb', fm_cols(inp['ssd_conv_b'][l]))
    put('ssd_g', fm_cols(inp['ssd_norm_g'][l]))
    put('ssd_d', fm_cols(np.repeat(inp['ssd_d'][l], 64)))
    put('dtb', np.broadcast_to(inp['ssd_dt_bias'][l].reshape(1, 8), (128, 8)))
    put('alog', np.broadcast_to(inp['ssd_a_log'][l].reshape(1, 8), (128, 8)))
    put('subg', np.concatenate([inp['diff_subln_g'][l], inp['diff_subln_g'][l]]).reshape(128, 1))
    put('lp', np.broadcast_to(inp['diff_lambda'][l].reshape(1, 128), (128, 128)))
    fw = inp['ffn_conv_w'][l]
    put('fcw', np.concatenate([fm_cols(fw[i]) for i in range(3)], axis=1))
    put('fcb', fm_cols(inp['ffn_conv_b'][l]))
    put('w_up', inp['rwkv_w_up'][l].reshape(64, 256))
    put('a_up', inp['rwkv_a_up'][l].reshape(64, 256))
    put('g_up', inp['rwkv_g_up'][l].reshape(64, 256))
    put('nfg', fm_cols(inp['norm_f_g']))
    put('ret_g4', inp['ret_ln_g'][l].reshape(4, 64).T)
    put('scw64', np.concatenate([cw[i].reshape(8, 64).T for i in range(3)], axis=1))
    put('scb64', inp['ssd_conv_b'][l].reshape(8, 64).T)
    put('ssd_d4', np.broadcast_to(inp['ssd_d'][l].reshape(1, 4), (64, 4)))
    put('ssd_g4', inp['ssd_norm_g'][l].reshape(4, 64).T)
    put('mu64', mu.reshape(16, 64).T)
    put('w0_64', inp['rwkv_w0'][l].reshape(8, 64).T)
    put('a0_64', inp['rwkv_a0'][l].reshape(8, 64).T)
    put('kk64', inp['rwkv_k_k'][l].reshape(4, 64).T)
    put('ka64', inp['rwkv_k_a'][l].reshape(4, 64).T)
    put('rk64', inp['rwkv_r_k'][l].reshape(4, 64).T)
    put('lng64', inp['rwkv_ln_g'][l].reshape(4, 64).T)
    put('lnb64', inp['rwkv_ln_b'][l].reshape(4, 64).T)
    return pp


CPC = {}
_o = 0
for _n, _w in [('ident', 128), ('ones', 128), ('bd64', 128), ('tri_f', 128), ('tri_b', 128), ('nm_f', 128), ('nm_b', 128),
               ('ret_ds', 512), ('ret_df', 512), ('ret_db', 512), ('ret_kwf', 4), ('ret_kwb', 4), ('ret_g128', 8),
               ('eps', 1), ('rw_ms', 64), ('rw_mi', 64), ('rw_msT', 64), ('rw_miT', 64), ('scanmask', 512), ('rw_mask5', 320)]:
    CPC[_n] = (_o, _w)
    _o += _w
NCP = _o


def build_consts():
    cp = np.zeros((128, NCP), np.float32)

    def put(name, arr):
        o, w = CPC[name]
        arr = np.asarray(arr, np.float32)
        assert arr.shape[1] == w
        cp[:arr.shape[0], o:o + w] = arr
    i = np.arange(128)
    put('ident', np.eye(128))
    put('ones', np.ones((128, 128)))
    bd = np.zeros((128, 128))
    bd[:64, :64] = 1
    bd[64:, 64:] = 1
    put('bd64', bd)
    put('tri_f', (i[:, None] <= i[None, :]).astype(np.float32))
    put('tri_b', (i[:, None] >= i[None, :]).astype(np.float32))
    put('nm_f', np.where(i[None, :] >= i[:, None], 0.0, -30000.0))
    put('nm_b', np.where(i[None, :] <= i[:, None], 0.0, -30000.0))
    ds = np.zeros((128, 512))
    df = np.zeros((128, 512))
    db = np.zeros((128, 512))
    kwf = np.zeros((128, 4))
    kwb = np.zeros((128, 4))
    g128 = np.zeros((128, 8))
    for h in range(4):
        lf = math.log(1.0 - 2.0 ** (-5.0 - h))
        lb = math.log(1.0 - 2.0 ** (-5.5 - h))
        jj, ii = i[:, None], i[None, :]
        m = np.where(ii >= jj, np.exp(lf * (ii - jj)), 0.0) + np.where(ii <= jj, np.exp(lb * (jj - ii)), 0.0)
        ds[:, h * 128:(h + 1) * 128] = m
        df[:, h * 128:(h + 1) * 128] = np.exp(lf * (i + 1))[None, :]
        db[:, h * 128:(h + 1) * 128] = np.exp(lb * (128 - i))[None, :]
        kwf[:, h] = np.exp(lf * (127 - i))
        kwb[:, h] = np.exp(lb * i)
        g128[:, h] = math.exp(lf * 128)
        g128[:, 4 + h] = math.exp(lb * 128)
    put('ret_ds', ds)
    put('ret_df', df)
    put('ret_db', db)
    put('ret_kwf', kwf)
    put('ret_kwb', kwb)
    put('ret_g128', g128)
    put('eps', np.full((128, 1), EPS))
    c = np.arange(64)
    put('rw_ms', (c[None, :] < c[:, None]).astype(np.float32))
    put('rw_mi', (c[None, :] <= c[:, None]).astype(np.float32))
    put('rw_msT', (c[:, None] < c[None, :]).astype(np.float32))
    put('rw_miT', (c[:, None] <= c[None, :]).astype(np.float32))
    sm = np.ones((128, 512))
    sm[:, ::64] = 0.0
    put('scanmask', sm)
    msT = (c[:, None] < c[None, :]).astype(np.float32)
    miT = (c[:, None] <= c[None, :]).astype(np.float32)
    ms = (c[None, :] < c[:, None]).astype(np.float32)
    put('rw_mask5', np.concatenate([msT, miT, msT, miT, ms], axis=1))
    return cp


def rope_tables():
    rows = TS // 64

    def tabs(d):
        nf = d // 4
        row = np.repeat(np.arange(rows, dtype=np.float32), 64)
        col = np.tile(np.arange(64, dtype=np.float32), rows)
        freqs = (np.float32(10000.0) ** (-np.arange(nf, dtype=np.float32) / nf)).astype(np.float32)
        ang = np.concatenate([row[:, None] * freqs, col[:, None] * freqs], axis=-1).astype(np.float32)
        return np.cos(ang).T.astype(np.float32), np.sin(ang).T.astype(np.float32)
    c, s = tabs(64)
    ret_c = np.concatenate([c, c, c, c], 0)
    ret_s = np.concatenate([-s, s, -s, s], 0)
    c, s = tabs(32)
    dif_c = np.concatenate([c, c] * 4, 0)
    dif_s = np.concatenate([-s, s] * 4, 0)
    return np.stack([ret_c, ret_s, dif_c, dif_s]).astype(np.float32)


def build(debug=False):
    nc = bass.Bass("TRN2", target_bir_lowering=False)

    def din(name, shape):
        return nc.dram_tensor(name, list(shape), F32, kind="ExternalInput").ap()

    def dout(name, shape):
        return nc.dram_tensor(name, list(shape), F32, kind="ExternalOutput").ap()
    x_tm = din("x_tm", [TTOT, D])
    cvec = din("cvec", [128, 16])
    w_mod = din("w_mod", [L, D, 6 * D])
    w_in_fm = din("w_in_fm", [L, D, NFM * 128])
    w_in_tm = din("w_in_tm", [L, D, NTM])
    w_out = din("w_out", [L, D, D])
    w_fup = din("w_fup", [L, D, 2 * FFN])
    w_fdn = din("w_fdn", [L, FFN, D])
    pp_in = din("pp", [L, 128, NPP])
    cp_in = din("cp", [128, NCP])
    rope_in = din("rope", [4, 128, TS])
    st_rwkv = din("st_rwkv", [L, 2, 4, 64, 64])
    st_ret = din("st_ret", [L, 2, 4, 64, 64])
    st_ssd = din("st_ssd", [L, 2, 4, 64, 64])
    ck_in = din("ck", [L, 512, 256])
    cv_in = din("cv", [L, 512, 256])

    y_p = dout("y_p", [NPS * TP, D])
    y_s = dout("y_s", [TS, D])
    o_rwkv = dout("o_rwkv", [NPS, L, 2, 4, 64, 64])
    o_ret = dout("o_ret", [NPS, L, 2, 4, 64, 64])
    o_ssd = dout("o_ssd", [NPS, L, 2, 4, 64, 64])
    o_k = dout("o_k", [NPS, L, TP, 256])
    o_v = dout("o_v", [NPS, L, TP, 256])

    es = ExitStack()
    with es:
        P = Prog(nc, es)
        mk = (lambda n, s: dout(n, s)) if (debug and debug != 'B') else (lambda n, s: P.dram(n, s))
        XT = [mk("XT0", [D, TTOT]), mk("XT1", [D, TTOT])]
        FT = mk("FT", [NFM * 128, TTOT])
        FTM = mk("FTM", [TTOT, NTM])
        OTDT = F32 if debug == 'B' else BF16
        if debug == 'B':
            OT = nc.dram_tensor("OT_dbg", [D, TTOT], F32, kind="ExternalOutput").ap()
        elif debug and debug[0] == 'C':
            OT = nc.dram_tensor("OT_in", [D, TTOT], F32, kind="ExternalInput").ap()
        else:
            OT = nc.dram_tensor("OT", [D, TTOT], BF16, kind="Internal").ap()
        dXT = [[Dep() for _ in SEQS] for _ in range(2)]
        dFT = [Dep() for _ in SEQS]
        dFTM = [Dep() for _ in SEQS]
        dOT = [Dep() for _ in SEQS]
        dOUT = Dep('outs')
        RW = P.dram("RW", [11 * 256, TTOT])
        YS = P.dram("YS", [2 * 256, TTOT])
        dRW = [Dep() for _ in SEQS]
        dYS = [Dep() for _ in SEQS]

        P.ph = es
        cp = T(P.sb("cp", [128, NCP]))
        pp = [T(P.sb("pp%d" % l, [128, NPP])) for l in range(L)]
        modv = [T(P.sb("modv%d" % l, [128, 48, 2])) for l in range(L)]
        modA = [T(P.sb("modA%d" % l, [128, 2, 8, 2])) for l in range(L)]
        cpb = T(P.sb("cpb", [128, 640], BF16))
        P.ph = None

        def C(name, rows=128, sub=None):
            o, w = CPC[name]
            if sub is not None:
                return cp[0:rows, o + sub[0]:o + sub[1]]
            return cp[0:rows, o:o + w]

        def PPv(l, name, col=0, rows=128, ncol=1):
            o, w = PPC[name]
            return pp[l][0:rows, o + col:o + col + ncol]

        P.begin()
        P.dma(SP, cp[:, :], cp_in[:, :], W=[cp.d])
        for l in range(L):
            P.dma(SP, pp[l][:, :], pp_in[l], W=[pp[l].d])
        P.op(DVE, lambda e: e.tensor_copy(out=cpb[:, 0:384], in_=cp[:, 0:384]), R=[cp.d], W=[cpb.d])
        cv = T(P.sb("cv", [128, 16]))
        scv = T(P.sb("scv", [128, 8, 2]))
        P.dma(SP, cv[:, :], cvec[:, :], W=[cv.d])
        P.op(ACT, lambda e: e.activation(out=scv[:, :, 0], in_=cv[:, 0:8], func=AF.Silu), R=[cv.d], W=[scv.d])
        P.op(ACT, lambda e: e.activation(out=scv[:, :, 1], in_=cv[:, 8:16], func=AF.Silu), R=[cv.d], W=[scv.d])
        wm = Ring([T(P.sb("wm", [128, 8, 512])) for _ in range(2)])
        pm = TPS(P.psum("pm", [128, 512]))
        for l in range(L):
            for g in range(12):
                w = wm.next()
                P.dma(SP, w[:, :, :], w_mod[l, :, g * 512:(g + 1) * 512].rearrange("(k p) n -> p k n", p=128), W=[w.d])
                for cc in range(4):
                    ch = g * 4 + cc
                    for k in range(8):
                        P.op(PE, lambda e, w=w, cc=cc, k=k, ch=ch: e.matmul(
                            pm[:, ch * 2:ch * 2 + 2], lhsT=w[:, k, cc * 128:(cc + 1) * 128], rhs=scv[:, k, :],
                            start=(k == 0), stop=(k == 7)), R=[w.d, scv.d], W=[pm.d])
            o, _ = PPC['bmod']
            P.op(DVE, lambda e, l=l, o=o: e.tensor_tensor(
                out=modv[l][:, :, :], in0=pm[:, 0:96].rearrange("p (c t) -> p c t", t=2),
                in1=pp[l][:, o:o + 48].unsqueeze(2).to_broadcast([128, 48, 2]), op=ALU.add),
                R=[pm.d, pp[l].d], W=[modv[l].d])
            for n, (gname, which) in enumerate([('n1g', 1), ('n2g', 4)]):
                og, _ = PPC[gname]
                P.op(DVE, lambda e, l=l, n=n, og=og, which=which: e.scalar_tensor_tensor(
                    out=modA[l][:, n, :, :], in0=modv[l][:, which * 8:(which + 1) * 8, :], scalar=1.0,
                    in1=pp[l][:, og:og + 8].unsqueeze(2).to_broadcast([128, 8, 2]), op0=ALU.add, op1=ALU.mult),
                    R=[modv[l].d, pp[l].d], W=[modA[l].d])
        P.end()

        def modcol(l, which, j, cond):
            return modv[l][:, which * 8 + j, cond:cond + 1]

        def rmsnorm_tile(xt, n, l, which_norm, cond, hT, sq, ps_ring, rs, tmp_ring):
            ps = ps_ring.next()
            for k in range(8):
                sqk = sq.next()
                P.op(ACT, lambda e, k=k, sqk=sqk: e.activation(out=sqk[:, 0:n], in_=xt[:, k, 0:n], func=AF.Square), R=[xt.d], W=[sqk.d])
                P.op(PE, lambda e, k=k, sqk=sqk: e.matmul(ps[:, 0:n], lhsT=C('ones'), rhs=sqk[:, 0:n], start=(k == 0), stop=(k == 7)),
                     R=[sqk.d, cp.d], W=[ps.d])
            P.op(ACT, lambda e: e.activation(out=rs[:, 0:n], in_=ps[:, 0:n], func=AF.Sqrt, bias=C('eps'), scale=1.0 / D),
                 R=[ps.d, cp.d], W=[rs.d])
            P.op(DVE, lambda e: e.reciprocal(out=rs[:, 0:n], in_=rs[:, 0:n]), R=[rs.d], W=[rs.d])
            shift_which = 0 if which_norm == 0 else 3
            for j in range(8):
                tmp = tmp_ring.next()
                P.op(DVE, lambda e, j=j, tmp=tmp: e.scalar_tensor_tensor(
                    out=tmp[:, 0:n], in0=xt[:, j, 0:n], scalar=modA[l][:, which_norm, j, cond:cond + 1], in1=rs[:, 0:n],
                    op0=ALU.mult, op1=ALU.mult), R=[xt.d, modA[l].d, rs.d], W=[tmp.d])
                P.op(ACT, lambda e, j=j, tmp=tmp: e.activation(
                    out=hT[:, j, 0:n], in_=tmp[:, 0:n], func=AF.Identity, bias=modcol(l, shift_which, j, cond), scale=1.0),
                    R=[tmp.d, modv[l].d], W=[hT.d])

        evac_flip = [0]

        def evac(out_ap, in_ap, R, W):
            evac_flip[0] ^= 1
            if evac_flip[0]:
                P.op(ACT, lambda e: e.activation(out=out_ap, in_=in_ap, func=AF.Copy), R=R, W=W)
            else:
                P.op(DVE, lambda e: e.tensor_copy(out=out_ap, in_=in_ap), R=R, W=W)

        def phase_B(l):
            P.begin()
            zt = T(P.sb("zt", [128, 2048], OTDT))
            P.op(DVE, lambda e: e.memset(zt[:, :], 0.0), W=[zt.d])
            for si, (kind, sidx, Tn, off) in enumerate(SEQS):
                for r0 in ([] if 'rwkv' in MIXERS else [0, 128]) + ([] if 'ssd' in MIXERS else [512, 640]):
                    for t0 in range(0, Tn, 2048):
                        n = min(2048, Tn - t0)
                        P.dma(ACT, OT[r0:r0 + 128, off + t0:off + t0 + n], zt[:, 0:n], R=[zt.d], W=[dOT[si]])
            P.end()
            for name, fn in (('ret', mixer_ret), ('diff', mixer_diff), ('ssd', mixer_ssd), ('rwkv', mixer_rwkv)):
                if name in MIXERS:
                    P.begin()
                    fn(l)
                    P.end()

        def mixer_diff(l):
            lam_init = 0.8 - 0.6 * math.exp(-0.3 * l)
            AX = mybir.AxisListType.X
            ropc = T(P.sb("ropc", [64, TS]))
            rops = T(P.sb("rops", [64, TS]))
            P.dma(SP, ropc[:, :], rope_in[2, 0:64, :], W=[ropc.d])
            P.dma(SP, rops[:, :], rope_in[3, 0:64, :], W=[rops.d])
            qf32 = T(P.sb("qf32", [64, TS]))
            qs32 = T(P.sb("qs32", [64, TS]))
            qb = T(P.sb("qb", [64, TS], BF16))
            NKMAX = TS // 128 + 4
            kall = T(P.sb("kall", [64, NKMAX * 128], BF16))
            v32 = T(P.sb("v32", [128, NKMAX, 64]))
            vall = T(P.sb("vall", [128, NKMAX, 64], BF16))
            ck32 = T(P.sb("ck32", [128, 4, 64]))
            E_r = Ring([T(P.sb("E", [128, 512], BF16)) for _ in range(3)])
            rd = [T(P.sb("rd%d" % i, [64, 512])) for i in range(2)]
            o0 = T(P.sb("o0", [64, 512]))
            o1 = T(P.sb("o1", [64, 512]))
            sqy = T(P.sb("sqy", [64, 512]))
            rsy = T(P.sb("rsy", [64, 512]))
            oy_r = Ring([T(P.sb("oy", [64, 512], OTDT)) for _ in range(2)])
            lamt = T(P.sb("lamt", [128, 40]))
            ps_sc = Ring([TPS(P.psum("ps_sc", [128, 512])) for _ in range(2)])
            ps_den = [TPS(P.psum("ps_den%d" % i, [128, 512])) for i in range(2)]
            ps_num = [TPS(P.psum("ps_num%d" % i, [128, 512])) for i in range(2)]
            ps_x = Ring([TPS(P.psum("ps_x", [128, 512])) for _ in range(2)])
            olp, _ = PPC['lp']
            osg, _ = PPC['subg']
            for i in range(2):
                P.op(DVE, lambda e, i=i: e.tensor_tensor(out=lamt[:, 0:32], in0=pp[l][:, olp + 64 * i:olp + 64 * i + 32],
                                                         in1=pp[l][:, olp + 64 * i + 32:olp + 64 * i + 64], op=ALU.mult),
                     R=[pp[l].d], W=[lamt.d])
                P.op(DVE, lambda e, i=i: e.tensor_reduce(out=lamt[:, 32 + i:33 + i], in_=lamt[:, 0:32], axis=AX, op=ALU.add),
                     R=[lamt.d], W=[lamt.d])
            P.op(ACT, lambda e: e.activation(out=lamt[:, 34:36], in_=lamt[:, 32:34], func=AF.Exp), R=[lamt.d], W=[lamt.d])
            P.op(DVE, lambda e: e.scalar_tensor_tensor(out=lamt[:, 36:37], in0=lamt[:, 35:36], scalar=-lam_init, in1=lamt[:, 34:35],
                                                       op0=ALU.add, op1=ALU.subtract), R=[lamt.d], W=[lamt.d])
            scale = 32.0 ** -0.5
            for si, (kind, sidx, Tn, off) in enumerate(SEQS):
                nch = Tn // 128
                nk = nch + (4 if kind == 's' else 0)
                for h in range(4):
                    pr = (h % 2) * 64

                    def rows(cbase):
                        r0 = (cbase + h // 2) * 128 + pr
                        return FT[r0:r0 + 64, off:off + Tn]
                    for (dst, cb, cbs) in ((qb, 24, 28), (kall, 26, 30)):
                        P.dma(SP, qf32[:, 0:Tn], rows(cb), R=[dFT[si]], W=[qf32.d])
                        if kind == 's':
                            P.dma(SP, qs32[:, 0:Tn], rows(cbs), R=[dFT[si]], W=[qs32.d])
                            P.op(DVE, lambda e: e.tensor_tensor(out=qf32[:, 0:Tn], in0=qf32[:, 0:Tn], in1=ropc[:, 0:Tn], op=ALU.mult),
                                 R=[qf32.d, ropc.d], W=[qf32.d])
                            P.op(DVE, lambda e: e.tensor_tensor(out=qs32[:, 0:Tn], in0=qs32[:, 0:Tn], in1=rops[:, 0:Tn], op=ALU.mult),
                                 R=[qs32.d, rops.d], W=[qs32.d])
                            P.op(DVE, lambda e: e.tensor_tensor(out=qf32[:, 0:Tn], in0=qf32[:, 0:Tn], in1=qs32[:, 0:Tn], op=ALU.add),
                                 R=[qf32.d, qs32.d], W=[qf32.d])
                        P.op(ACT, lambda e, dst=dst: e.activation(out=dst[:, 0:Tn], in_=qf32[:, 0:Tn], func=AF.Copy),
                             R=[qf32.d], W=[dst.d])
                    P.dma(SP, v32[:, 0:nch, :], FTM[off:off + Tn, 256 + h * 64:256 + (h + 1) * 64].rearrange("(c p) e -> p c e", p=128),
                          R=[dFTM[si]], W=[v32.d])
                    if kind == 's':
                        P.dma(SP, v32[:, nch:nch + 4, :], cv_in[l, :, h * 64:(h + 1) * 64].rearrange("(c p) e -> p c e", p=128), W=[v32.d])
                        P.dma(SP, ck32[:, :, :], ck_in[l, :, h * 64:(h + 1) * 64].rearrange("(c p) e -> p c e", p=128), W=[ck32.d])
                        px = ps_x.next()
                        for c in range(4):
                            P.op(PE, lambda e, c=c: e.transpose(px[0:64, c * 128:(c + 1) * 128], ck32[:, c, :], C('ident')),
                                 R=[ck32.d, cp.d], W=[px.d])
                        P.op(ACT, lambda e: e.activation(out=kall[:, Tn:Tn + 512], in_=px[0:64, 0:512], func=AF.Copy), R=[px.d], W=[kall.d])
                    P.op(DVE, lambda e: e.tensor_copy(out=vall[:, 0:nk, :], in_=v32[:, 0:nk, :]), R=[v32.d], W=[vall.d])
                    nq = min(512, Tn)
                    for q0 in range(0, Tn, nq):
                        n = nq
                        qsl = slice(q0, q0 + n)
                        for kc in range(nk):
                            ksl = slice(kc * 128, (kc + 1) * 128)
                            for m in range(2):
                                psl = slice(m * 32, (m + 1) * 32)
                                psc = ps_sc.next()
                                P.op(PE, lambda e: e.matmul(psc[:, 0:n], lhsT=kall[psl, ksl], rhs=qb[psl, qsl], start=True, stop=True),
                                     R=[kall.d, qb.d], W=[psc.d])
                                E = E_r.next()
                                P.op(ACT, lambda e: e.activation(out=E[:, 0:n], in_=psc[:, 0:n], func=AF.Exp, scale=scale), R=[psc.d], W=[E.d])
                                P.op(PE, lambda e: e.matmul(ps_den[m][0:64, 0:n], lhsT=cpb[:, 128:192], rhs=E[:, 0:n], start=(kc == 0), stop=(kc == nk - 1)),
                                     R=[cpb.d, E.d], W=[ps_den[m].d])
                                P.op(PE, lambda e: e.matmul(ps_num[m][0:64, 0:n], lhsT=vall[:, kc, :], rhs=E[:, 0:n], start=(kc == 0), stop=(kc == nk - 1)),
                                     R=[vall.d, E.d], W=[ps_num[m].d])
                        for m in range(2):
                            P.op(DVE, lambda e, m=m: e.reciprocal(out=rd[m][:, 0:n], in_=ps_den[m][0:64, 0:n]), R=[ps_den[m].d], W=[rd[m].d])
                        P.op(DVE, lambda e: e.tensor_tensor(out=o0[:, 0:n], in0=ps_num[0][0:64, 0:n], in1=rd[0][:, 0:n], op=ALU.mult),
                             R=[ps_num[0].d, rd[0].d], W=[o0.d])
                        P.op(DVE, lambda e: e.tensor_tensor(out=o1[:, 0:n], in0=ps_num[1][0:64, 0:n], in1=rd[1][:, 0:n], op=ALU.mult),
                             R=[ps_num[1].d, rd[1].d], W=[o1.d])
                        P.op(DVE, lambda e: e.scalar_tensor_tensor(out=o0[:, 0:n], in0=o1[:, 0:n], scalar=lamt[0:64, 36:37], in1=o0[:, 0:n],
                                                                   op0=ALU.mult, op1=ALU.add), R=[o0.d, o1.d, lamt.d], W=[o0.d])
                        P.op(ACT, lambda e: e.activation(out=sqy[:, 0:n], in_=o0[:, 0:n], func=AF.Square), R=[o0.d], W=[sqy.d])
                        pss = ps_x.next()
                        P.op(PE, lambda e: e.matmul(pss[0:64, 0:n], lhsT=C('ones', rows=64, sub=(0, 64)), rhs=sqy[:, 0:n], start=True, stop=True),
                             R=[sqy.d, cp.d], W=[pss.d])
                        P.op(ACT, lambda e: e.activation(out=rsy[:, 0:n], in_=pss[0:64, 0:n], func=AF.Sqrt, bias=C('eps', rows=64), scale=1.0 / 64),
                             R=[pss.d, cp.d], W=[rsy.d])
                        P.op(DVE, lambda e: e.reciprocal(out=rsy[:, 0:n], in_=rsy[:, 0:n]), R=[rsy.d], W=[rsy.d])
                        P.op(DVE, lambda e: e.scalar_tensor_tensor(out=o1[:, 0:n], in0=o0[:, 0:n], scalar=pp[l][0:64, osg:osg + 1],
                                                                   in1=rsy[:, 0:n], op0=ALU.mult, op1=ALU.mult),
                             R=[o0.d, pp[l].d, rsy.d], W=[o1.d])
                        oy = oy_r.next()
                        P.op(ACT, lambda e: e.activation(out=oy[:, 0:n], in_=o1[:, 0:n], func=AF.Copy, scale=1.0 - lam_init), R=[o1.d], W=[oy.d])
                        P.dma(ACT, OT[768 + h * 64:768 + (h + 1) * 64, off + q0: off + q0 + n], oy[:, 0:n], R=[oy.d], W=[dOT[si]])

        def mixer_rwkv(l):
            rwkv_pre(l)
            if RWKV_STAGES >= 2:
                P.end()
                P.begin()
                rwkv_scan(l)
            if RWKV_STAGES >= 3:
                P.end()
                P.begin()
                rwkv_post(l)

        def rwkv_pre(l):
            buf_r = Ring([T(P.sb("rbuf", [64, 514])) for _ in range(3)])
            s1 = T(P.sb("rs1", [64, 512]))
            sh = [T(P.sb("rsh%d" % i, [64, 512])) for i in range(15)]
            twd = T(P.sb("twd", [64, 512]))
            sgd = T(P.sb("sgd", [64, 512]))
            a_d = [T(P.sb("a_d%d" % d, [64, 512])) for d in range(2)]
            o_r = Ring([T(P.sb("rwo", [64, 512])) for _ in range(4)])
            kkr = T(P.sb("kkr", [64, 512]))
            kk = T(P.sb("kk", [64, 512]))
            t1 = T(P.sb("rt1", [64, 512]))
            t2 = T(P.sb("rt2", [64, 512]))
            ps_r = Ring([TPS(P.psum("ps_rp", [128, 512])) for _ in range(4)])
            omu, _ = PPC['mu64']
            ow0, _ = PPC['w0_64']
            oa0, _ = PPC['a0_64']
            okk, _ = PPC['kk64']
            oka, _ = PPC['ka64']
            ork, _ = PPC['rk64']
            owu, _ = PPC['w_up']
            oau, _ = PPC['a_up']
            ogu, _ = PPC['g_up']
            for si, (kind, sidx, Tn, off) in enumerate(SEQS):
                nq = min(512, Tn)
                for q0 in range(0, Tn, nq):
                    n = nq
                    lo, hi = max(q0 - 1, 0), min(q0 + n + 1, Tn)
                    c_lo = lo - (q0 - 1)
                    a0 = 1 if q0 == 0 else 0
                    b1 = n - 1 if q0 + n == Tn else n

                    def store(arr, h, src):
                        r0 = arr * 256 + h * 64
                        P.dma(ACT, RW[r0:r0 + 64, off + q0: off + q0 + n], src[:, 0:n], R=[src.d], W=[dRW[si]])
                    for hc in range(15):
                        buf = buf_r.next()
                        P.dma(SP, buf[:, c_lo:c_lo + hi - lo], FT[hc * 64:(hc + 1) * 64, off + lo: off + hi], R=[dFT[si]], W=[buf.d])
                        P.op(DVE, lambda e: e.tensor_tensor(out=s1[:, a0:b1], in0=buf[:, a0:b1], in1=buf[:, a0 + 2:b1 + 2], op=ALU.add),
                             R=[buf.d], W=[s1.d])
                        if a0:
                            P.op(DVE, lambda e: e.tensor_copy(out=s1[:, 0:1], in_=buf[:, 2:3]), R=[buf.d], W=[s1.d])
                        if b1 < n:
                            P.op(DVE, lambda e: e.tensor_copy(out=s1[:, n - 1:n], in_=buf[:, n - 1:n]), R=[buf.d], W=[s1.d])
                        P.op(DVE, lambda e: e.scalar_tensor_tensor(out=s1[:, 0:n], in0=s1[:, 0:n], scalar=0.5, in1=buf[:, 1:n + 1],
                                                                   op0=ALU.mult, op1=ALU.subtract), R=[s1.d, buf.d], W=[s1.d])
                        P.op(DVE, lambda e: e.scalar_tensor_tensor(out=sh[hc][:, 0:n], in0=s1[:, 0:n], scalar=pp[l][0:64, omu + hc:omu + hc + 1],
                                                                   in1=buf[:, 1:n + 1], op0=ALU.mult, op1=ALU.add),
                             R=[s1.d, buf.d, pp[l].d], W=[sh[hc].d])
                    P.op(ACT, lambda e: e.activation(out=twd[:, 0:n], in_=sh[12][:, 0:n], func=AF.Tanh), R=[sh[12].d], W=[twd.d])
                    P.op(ACT, lambda e: e.activation(out=sgd[:, 0:n], in_=sh[14][:, 0:n], func=AF.Sigmoid), R=[sh[14].d], W=[sgd.d])
                    sad = sh[13]
                    for h in range(4):
                        shr, shk, shv = sh[h], sh[4 + h], sh[8 + h]
                        store(0, h, shr)
                        store(1, h, shv)
                        hs = slice(h * 64, (h + 1) * 64)
                        for d in range(2):
                            ds_ = slice(d * 32, (d + 1) * 32)
                            col = d * 4 + h
                            pw = ps_r.next()
                            P.op(PE, lambda e: e.matmul(pw[0:64, 0:n], lhsT=pp[l][ds_, owu + h * 64:owu + (h + 1) * 64], rhs=twd[ds_, 0:n], start=True, stop=True),
                                 R=[pp[l].d, twd.d], W=[pw.d])
                            lw = o_r.next()
                            P.op(ACT, lambda e: e.activation(out=lw[:, 0:n], in_=pw[0:64, 0:n], func=AF.Sigmoid, bias=pp[l][0:64, ow0 + col:ow0 + col + 1], scale=1.0),
                                 R=[pw.d, pp[l].d], W=[lw.d])
                            P.op(DVE, lambda e: e.tensor_scalar(out=lw[:, 0:n], in0=lw[:, 0:n], scalar1=-math.exp(-0.5), scalar2=None, op0=ALU.mult),
                                 R=[lw.d], W=[lw.d])
                            store(7 + d, h, lw)
                            pa = ps_r.next()
                            P.op(PE, lambda e: e.matmul(pa[0:64, 0:n], lhsT=pp[l][ds_, oau + h * 64:oau + (h + 1) * 64], rhs=sad[ds_, 0:n], start=True, stop=True),
                                 R=[pp[l].d, sad.d], W=[pa.d])
                            P.op(ACT, lambda e: e.activation(out=a_d[d][:, 0:n], in_=pa[0:64, 0:n], func=AF.Sigmoid, bias=pp[l][0:64, oa0 + col:oa0 + col + 1], scale=1.0),
                                 R=[pa.d, pp[l].d], W=[a_d[d].d])
                        pg = ps_r.next()
                        P.op(PE, lambda e: e.matmul(pg[0:64, 0:n], lhsT=pp[l][0:64, ogu + h * 64:ogu + (h + 1) * 64], rhs=sgd[:, 0:n], start=True, stop=True),
                             R=[pp[l].d, sgd.d], W=[pg.d])
                        go = o_r.next()
                        P.op(ACT, lambda e: e.activation(out=go[:, 0:n], in_=pg[0:64, 0:n], func=AF.Copy), R=[pg.d], W=[go.d])
                        store(9, h, go)
                        P.op(DVE, lambda e: e.tensor_scalar(out=kkr[:, 0:n], in0=shk[:, 0:n], scalar1=pp[l][0:64, okk + h:okk + h + 1], scalar2=None, op0=ALU.mult),
                             R=[shk.d, pp[l].d], W=[kkr.d])
                        P.op(ACT, lambda e: e.activation(out=t1[:, 0:n], in_=kkr[:, 0:n], func=AF.Square), R=[kkr.d], W=[t1.d])
                        pk = ps_r.next()
                        P.op(PE, lambda e: e.matmul(pk[0:64, 0:n], lhsT=C('ones', rows=64, sub=(0, 64)), rhs=t1[:, 0:n], start=True, stop=True),
                             R=[t1.d, cp.d], W=[pk.d])
                        P.op(DVE, lambda e: e.tensor_scalar(out=t1[:, 0:n], in0=pk[0:64, 0:n], scalar1=1e-12, scalar2=None, op0=ALU.max), R=[pk.d], W=[t1.d])
                        P.op(ACT, lambda e: e.activation(out=t1[:, 0:n], in_=t1[:, 0:n], func=AF.Sqrt), R=[t1.d], W=[t1.d])
                        P.op(DVE, lambda e: e.reciprocal(out=t1[:, 0:n], in_=t1[:, 0:n]), R=[t1.d], W=[t1.d])
                        P.op(DVE, lambda e: e.tensor_tensor(out=kk[:, 0:n], in0=kkr[:, 0:n], in1=t1[:, 0:n], op=ALU.mult), R=[kkr.d, t1.d], W=[kk.d])
                        store(2, h, kk)
                        for d in range(2):
                            P.op(DVE, lambda e: e.tensor_scalar(out=t2[:, 0:n], in0=a_d[d][:, 0:n], scalar1=-1.0, scalar2=pp[l][0:64, oka + h:oka + h + 1],
                                                                op0=ALU.add, op1=ALU.mult), R=[a_d[d].d, pp[l].d], W=[t2.d])
                            kd = o_r.next()
                            P.op(DVE, lambda e: e.scalar_tensor_tensor(out=kd[:, 0:n], in0=t2[:, 0:n], scalar=1.0, in1=shk[:, 0:n], op0=ALU.add, op1=ALU.mult),
                                 R=[t2.d, shk.d], W=[kd.d])
                            store(3 + d, h, kd)
                            bd = o_r.next()
                            P.op(DVE, lambda e: e.tensor_tensor(out=bd[:, 0:n], in0=kk[:, 0:n], in1=a_d[d][:, 0:n], op=ALU.mult), R=[kk.d, a_d[d].d], W=[bd.d])
                            store(5 + d, h, bd)
                        P.op(DVE, lambda e: e.scalar_tensor_tensor(out=t2[:, 0:n], in0=shr[:, 0:n], scalar=pp[l][0:64, ork + h:ork + h + 1], in1=shk[:, 0:n],
                                                                   op0=ALU.mult, op1=ALU.mult), R=[shr.d, shk.d, pp[l].d], W=[t2.d])
                        pb = ps_r.next()
                        P.op(PE, lambda e: e.matmul(pb[0:64, 0:n], lhsT=C('ones', rows=64, sub=(0, 64)), rhs=t2[:, 0:n], start=True, stop=True),
                             R=[t2.d, cp.d], W=[pb.d])
                        bo = o_r.next()
                        P.op(DVE, lambda e: e.tensor_tensor(out=bo[:, 0:n], in0=pb[0:64, 0:n], in1=shv[:, 0:n], op=ALU.mult), R=[pb.d, shv.d], W=[bo.d])
                        store(10, h, bo)

        def rwkv_scan(l):
            NI = 16
            ld = {}
            for nm in ('r', 'v', 'kk', 'kd', 'bd', 'lw'):
                ld[nm] = [T(P.sb("l%s%d" % (nm, d), [64, 512])) for d in range(2)]
            Lc = [T(P.sb("Lc%d" % d, [64, 512])) for d in range(2)]
            En = [T(P.sb("En%d" % d, [64, 512])) for d in range(2)]
            Ep = [T(P.sb("Ep%d" % d, [64, 512])) for d in range(2)]
            aT = [T(P.sb("aT%d" % d, [64, 512])) for d in range(2)]
            bT = [T(P.sb("bT%d" % d, [64, 512])) for d in range(2)]
            kT = [T(P.sb("kT%d" % d, [64, 512])) for d in range(2)]
            rT = [T(P.sb("rT%d" % d, [64, 512])) for d in range(2)]
            vo = [T(P.sb("vo%d" % d, [64, 512])) for d in range(2)]
            ysb = [T(P.sb("ysb%d" % d, [64, 512])) for d in range(2)]
            yso = [T(P.sb("yso%d" % d, [64, 512])) for d in range(2)]
            ELC = [T(P.sb("ELC%d" % i, [64, 64])) for i in range(NI)]
            bhk = [T(P.sb("bhk%d" % i, [64, 128])) for i in range(NI)]
            WC = [T(P.sb("WC%d" % i, [64, 8])) for i in range(NI)]
            TM = [T(P.sb("TM%d" % i, [64, 320])) for i in range(NI)]
            AM = [T(P.sb("AM%d" % i, [64, 320])) for i in range(NI)]
            Mm = [[T(P.sb("Mm%d_%d" % (i, j), [64, 128])) for j in range(2)] for i in range(NI)]
            Pm = [[T(P.sb("Pm%d_%d" % (i, j), [64, 64])) for j in range(2)] for i in range(NI)]
            AU = [T(P.sb("AU%d" % i, [64, 128])) for i in range(NI)]
            GT = [T(P.sb("GT%d" % i, [64, 64])) for i in range(NI)]
            Hh = [T(P.sb("Hh%d" % i, [64, 64])) for i in range(NI)]
            RhT = [T(P.sb("RhT%d" % i, [64, 64])) for i in range(NI)]
            Yl = [T(P.sb("Yl%d" % i, [64, 64])) for i in range(NI)]
            St = [[T(P.sb("St%d_%d" % (d, i), [64, 64])) for i in range(2)] for d in range(2)]
            s0l = T(P.sb("s0l", [64, 64]))
            pr = Ring([TPS(P.psum("pRW", [128, 512])) for _ in range(8)])
            I64 = C('ident', rows=64, sub=(0, 64))
            for si, (kind, sidx, Tn, off) in enumerate(SEQS):
                nq = min(512, Tn)
                ntile = Tn // nq
                for h in range(4):
                    sti = [0, 0]
                    for d in range(2):
                        S0 = St[d][0]
                        if kind == 's':
                            pS = pr.next()
                            P.dma(SP, s0l[:, :], st_rwkv[l, d, h], W=[s0l.d])
                            P.op(PE, lambda e: e.transpose(pS[0:64, 0:64], s0l[:, :], I64), R=[s0l.d, cp.d], W=[pS.d])
                            P.op(DVE, lambda e: e.tensor_copy(out=S0[:, :], in_=pS[0:64, 0:64]), R=[pS.d], W=[S0.d])
                        else:
                            P.op(DVE, lambda e: e.memset(S0[:, :], 0.0), W=[S0.d])
                    for ti in range(ntile):
                        n = nq
                        for d in range(2):
                            q0 = ti * nq if d == 0 else Tn - (ti + 1) * nq

                            def view(t):
                                return t[:, 0:n] if d == 0 else t[:, 0:n][:, ::-1]
                            for nm, arr in (('r', 0), ('v', 1), ('kk', 2), ('kd', 3 + d), ('bd', 5 + d), ('lw', 7 + d)):
                                r0 = arr * 256 + h * 64
                                P.dma(SP, ld[nm][d][:, 0:n], RW[r0:r0 + 64, off + q0: off + q0 + n], R=[dRW[si]], W=[ld[nm][d].d])
                            lw = ld['lw'][d]
                            P.op(DVE, lambda e: e.tensor_tensor_scan(out=Lc[d][:, 0:n], data0=C('scanmask', rows=64, sub=(0, n)), data1=view(lw),
                                                                     initial=0.0, op0=ALU.mult, op1=ALU.add), R=[lw.d, cp.d], W=[Lc[d].d])
                            P.op(ACT, lambda e: e.activation(out=En[d][:, 0:n], in_=Lc[d][:, 0:n], func=AF.Exp, scale=-1.0), R=[Lc[d].d], W=[En[d].d])
                            P.op(ACT, lambda e: e.activation(out=Ep[d][:, 0:n], in_=Lc[d][:, 0:n], func=AF.Exp), R=[Lc[d].d], W=[Ep[d].d])
                            P.op(DVE, lambda e: e.tensor_tensor(out=rT[d][:, 0:n], in0=view(ld['r'][d]), in1=Ep[d][:, 0:n], op=ALU.mult),
                                 R=[ld['r'][d].d, Ep[d].d], W=[rT[d].d])
                            P.op(DVE, lambda e: e.tensor_tensor(out=aT[d][:, 0:n], in0=Lc[d][:, 0:n], in1=view(lw), op=ALU.subtract),
                                 R=[Lc[d].d, lw.d], W=[aT[d].d])
                            P.op(ACT, lambda e: e.activation(out=Ep[d][:, 0:n], in_=aT[d][:, 0:n], func=AF.Exp), R=[aT[d].d, rT[d].d], W=[Ep[d].d])
                            P.op(DVE, lambda e: e.scalar_tensor_tensor(out=aT[d][:, 0:n], in0=view(ld['kk'][d]), scalar=-1.0, in1=Ep[d][:, 0:n],
                                                                       op0=ALU.mult, op1=ALU.mult), R=[ld['kk'][d].d, Ep[d].d], W=[aT[d].d])
                            P.op(DVE, lambda e: e.tensor_tensor(out=bT[d][:, 0:n], in0=view(ld['bd'][d]), in1=En[d][:, 0:n], op=ALU.mult),
                                 R=[ld['bd'][d].d, En[d].d], W=[bT[d].d])
                            P.op(DVE, lambda e: e.tensor_tensor(out=kT[d][:, 0:n], in0=view(ld['kd'][d]), in1=En[d][:, 0:n], op=ALU.mult),
                                 R=[ld['kd'][d].d, En[d].d], W=[kT[d].d])
                            P.op(DVE, lambda e: e.tensor_copy(out=vo[d][:, 0:n], in_=view(ld['v'][d])), R=[ld['v'][d].d], W=[vo[d].d])
                        ncc = n // 64
                        inst = [(cc, d) for cc in range(ncc) for d in range(2)]

                        def CS(cc):
                            return slice(cc * 64, (cc + 1) * 64)
                        for ii, (cc, d) in enumerate(inst):
                            cs = CS(cc)
                            lcol = Lc[d][:, cc * 64 + 63:cc * 64 + 64]

                            def vw(t):
                                if d == 0:
                                    return t[:, cs]
                                return t[:, n - (cc + 1) * 64:n - cc * 64][:, ::-1]
                            P.op(ACT, lambda e: e.activation(out=ELC[ii][:, :], in_=Lc[d][:, cs], func=AF.Exp, bias=lcol, scale=-1.0),
                                 R=[Lc[d].d], W=[ELC[ii].d])
                            P.op(ACT, lambda e: e.activation(out=WC[ii][:, 0:1], in_=lcol, func=AF.Exp), R=[Lc[d].d], W=[WC[ii].d])
                            P.op(DVE, lambda e: e.tensor_tensor(out=bhk[ii][:, 0:64], in0=vw(ld['bd'][d]), in1=ELC[ii][:, :], op=ALU.mult),
                                 R=[ld['bd'][d].d, ELC[ii].d], W=[bhk[ii].d])
                            P.op(DVE, lambda e: e.tensor_tensor(out=bhk[ii][:, 64:128], in0=vw(ld['kd'][d]), in1=ELC[ii][:, :], op=ALU.mult),
                                 R=[ld['kd'][d].d, ELC[ii].d], W=[bhk[ii].d])
                        for ii, (cc, d) in enumerate(inst):
                            cs = CS(cc)
                            ps = pr.next()
                            P.op(PE, lambda e: e.transpose(ps[0:64, 0:64], aT[d][:, cs], I64), R=[aT[d].d, cp.d], W=[ps.d])
                            P.op(PE, lambda e: e.transpose(ps[0:64, 64:128], vo[d][:, cs], I64), R=[vo[d].d, cp.d], W=[ps.d])
                            P.op(PE, lambda e: e.transpose(ps[0:64, 128:192], bhk[ii][:, 0:64], I64), R=[bhk[ii].d, cp.d], W=[ps.d])
                            P.op(PE, lambda e: e.transpose(ps[0:64, 192:256], bhk[ii][:, 64:128], I64), R=[bhk[ii].d, cp.d], W=[ps.d])
                            tm = TM[ii]
                            if ii % 2 == 0:
                                P.op(ACT, lambda e: e.activation(out=tm[:, 0:64], in_=ps[0:64, 0:64], func=AF.Copy), R=[ps.d], W=[tm.d])
                                P.op(ACT, lambda e: e.activation(out=tm[:, 128:320], in_=ps[0:64, 64:256], func=AF.Copy), R=[ps.d], W=[tm.d])
                            else:
                                P.op(DVE, lambda e: e.tensor_copy(out=tm[:, 0:64], in_=ps[0:64, 0:64]), R=[ps.d], W=[tm.d])
                                P.op(DVE, lambda e: e.tensor_copy(out=tm[:, 128:320], in_=ps[0:64, 64:256]), R=[ps.d], W=[tm.d])
                        for ii, (cc, d) in enumerate(inst):
                            cs = CS(cc)
                            a_, b_, k_, r_ = aT[d][:, cs], bT[d][:, cs], kT[d][:, cs], rT[d][:, cs]
                            ps = pr.next()
                            for i_, (lh, rh) in enumerate(((b_, a_), (b_, r_), (k_, a_), (k_, r_), (a_, b_))):
                                P.op(PE, lambda e: e.matmul(ps[0:64, i_ * 64:(i_ + 1) * 64], lhsT=lh, rhs=rh, start=True, stop=True),
                                     R=[aT[d].d, bT[d].d, kT[d].d, rT[d].d], W=[ps.d])
                            am = AM[ii]
                            P.op(DVE, lambda e: e.tensor_tensor(out=am[:, :], in0=ps[0:64, 0:320], in1=C('rw_mask5', rows=64), op=ALU.mult),
                                 R=[ps.d, cp.d], W=[am.d])
                            P.op(DVE, lambda e: e.tensor_tensor(out=Pm[ii][0][:, :], in0=am[:, 0:64], in1=I64, op=ALU.add), R=[am.d, cp.d], W=[Pm[ii][0].d])
                        cur = [(AM[ii][:, 0:64], AM[ii][:, 256:320], AM[ii]) for ii in range(len(inst))]
                        for lvl in range(5):
                            for ii, (cc, d) in enumerate(inst):
                                Mc, Mtc, Mdep = cur[ii]
                                mn = Mm[ii][lvl % 2]
                                ps = pr.next()
                                P.op(PE, lambda e: e.matmul(ps[0:64, 0:64], lhsT=Mtc, rhs=Mc, start=True, stop=True), R=[Mdep.d], W=[ps.d])
                                P.op(PE, lambda e: e.matmul(ps[0:64, 64:128], lhsT=Mc, rhs=Mtc, start=True, stop=True), R=[Mdep.d], W=[ps.d])
                                evac(mn[:, :], ps[0:64, 0:128], [ps.d], [mn.d])
                                cur[ii] = (mn[:, 0:64], mn[:, 64:128], mn)
                            for ii, (cc, d) in enumerate(inst):
                                Mc, Mtc, Mdep = cur[ii]
                                Pc = Pm[ii][lvl % 2]
                                Pn = Pm[ii][(lvl + 1) % 2]
                                ps = pr.next()
                                P.op(PE, lambda e: e.matmul(ps[0:64, 0:64], lhsT=Mtc, rhs=Pc[:, :], start=True, stop=True), R=[Mdep.d, Pc.d], W=[ps.d])
                                P.op(DVE, lambda e: e.tensor_tensor(out=Pn[:, :], in0=ps[0:64, 0:64], in1=Pc[:, :], op=ALU.add), R=[ps.d, Pc.d], W=[Pn.d])
                        for ii, (cc, d) in enumerate(inst):
                            tm, am = TM[ii], AM[ii]
                            ps = pr.next()
                            P.op(PE, lambda e: e.matmul(ps[0:64, 0:64], lhsT=am[:, 128:192], rhs=tm[:, 128:192], start=True, stop=True), R=[am.d, tm.d], W=[ps.d])
                            evac(tm[:, 64:128], ps[0:64, 0:64], [ps.d], [tm.d])
                        for ii, (cc, d) in enumerate(inst):
                            tm, Pc, au = TM[ii], Pm[ii][1], AU[ii]
                            ps = pr.next()
                            P.op(PE, lambda e: e.matmul(ps[0:64, 0:128], lhsT=Pc[:, :], rhs=tm[:, 0:128], start=True, stop=True), R=[Pc.d, tm.d], W=[ps.d])
                            evac(au[:, :], ps[0:64, 0:128], [ps.d], [au.d])
                        for ii, (cc, d) in enumerate(inst):
                            tm, au = TM[ii], AU[ii]
                            V_, Bh_, Kh_ = tm[:, 128:192], tm[:, 192:256], tm[:, 256:320]
                            Ah, Ul = au[:, 0:64], au[:, 64:128]
                            ps = pr.next()
                            P.op(PE, lambda e: e.matmul(ps[0:64, 0:64], lhsT=Ah, rhs=Bh_, start=True, stop=True), R=[au.d, tm.d], W=[ps.d])
                            P.op(PE, lambda e: e.matmul(ps[0:64, 64:128], lhsT=Bh_, rhs=Ul, start=True, stop=False), R=[au.d, tm.d], W=[ps.d])
                            P.op(PE, lambda e: e.matmul(ps[0:64, 64:128], lhsT=Kh_, rhs=V_, start=False, stop=True), R=[tm.d], W=[ps.d])
                            P.op(DVE, lambda e: e.scalar_tensor_tensor(out=GT[ii][:, :], in0=I64, scalar=WC[ii][:, 0:1], in1=ps[0:64, 0:64],
                                                                       op0=ALU.mult, op1=ALU.add), R=[cp.d, WC[ii].d, ps.d], W=[GT[ii].d])
                            P.op(DVE, lambda e: e.tensor_copy(out=Hh[ii][:, :], in_=ps[0:64, 64:128]), R=[ps.d], W=[Hh[ii].d])
                        for ii, (cc, d) in enumerate(inst):
                            cs = CS(cc)
                            tm, au, am = TM[ii], AU[ii], AM[ii]
                            V_ = tm[:, 128:192]
                            Ah, Ul = au[:, 0:64], au[:, 64:128]
                            ArbT, ArkT = am[:, 64:128], am[:, 192:256]
                            ps = pr.next()
                            P.op(PE, lambda e: e.matmul(ps[0:64, 0:64], lhsT=Ah, rhs=ArbT, start=True, stop=True), R=[au.d, am.d], W=[ps.d])
                            P.op(PE, lambda e: e.matmul(ps[0:64, 64:128], lhsT=Ul, rhs=ArbT, start=True, stop=False), R=[au.d, am.d], W=[ps.d])
                            P.op(PE, lambda e: e.matmul(ps[0:64, 64:128], lhsT=V_, rhs=ArkT, start=False, stop=True), R=[tm.d, am.d], W=[ps.d])
                            P.op(DVE, lambda e: e.tensor_tensor(out=RhT[ii][:, :], in0=ps[0:64, 0:64], in1=rT[d][:, cs], op=ALU.add), R=[ps.d, rT[d].d], W=[RhT[ii].d])
                            P.op(DVE, lambda e: e.tensor_copy(out=Yl[ii][:, :], in_=ps[0:64, 64:128]), R=[ps.d], W=[Yl[ii].d])
                        for ii, (cc, d) in enumerate(inst):
                            cs = CS(cc)
                            Sc = St[d][sti[d] % 2]
                            Sn = St[d][(sti[d] + 1) % 2]
                            sti[d] += 1
                            ps = pr.next()
                            P.op(PE, lambda e: e.matmul(ps[0:64, 0:64], lhsT=Sc[:, :], rhs=RhT[ii][:, :], start=True, stop=True), R=[Sc.d, RhT[ii].d], W=[ps.d])
                            P.op(PE, lambda e: e.matmul(ps[0:64, 64:128], lhsT=GT[ii][:, :], rhs=Sc[:, :], start=True, stop=True), R=[GT[ii].d, Sc.d], W=[ps.d])
                            P.op(DVE, lambda e: e.tensor_tensor(out=Sn[:, :], in0=ps[0:64, 64:128], in1=Hh[ii][:, :], op=ALU.add), R=[ps.d, Hh[ii].d], W=[Sn.d])
                            P.op(DVE, lambda e: e.tensor_tensor(out=ysb[d][:, cs], in0=ps[0:64, 0:64], in1=Yl[ii][:, :], op=ALU.add), R=[ps.d, Yl[ii].d], W=[ysb[d].d])
                        for d in range(2):
                            q0 = ti * nq if d == 0 else Tn - (ti + 1) * nq
                            src = ysb[d]
                            if d == 1:
                                P.op(DVE, lambda e: e.tensor_copy(out=yso[d][:, 0:n], in_=ysb[d][:, 0:n][:, ::-1]), R=[ysb[d].d], W=[yso[d].d])
                                src = yso[d]
                            r0 = d * 256 + h * 64
                            P.dma(ACT, YS[r0:r0 + 64, off + q0: off + q0 + n], src[:, 0:n], R=[src.d], W=[dYS[si]])
                    if kind == 'p':
                        for d in range(2):
                            Sc = St[d][sti[d] % 2]
                            pS = pr.next()
                            P.op(PE, lambda e: e.transpose(pS[0:64, 0:64], Sc[:, :], I64), R=[Sc.d, cp.d], W=[pS.d])
                            P.op(DVE, lambda e: e.tensor_copy(out=s0l[:, :], in_=pS[0:64, 0:64]), R=[pS.d], W=[s0l.d])
                            P.dma(SP, o_rwkv[sidx, l, d, h], s0l[:, :], R=[s0l.d], W=[dOUT])

        def rwkv_post(l):
            yf = Ring([T(P.sb("pyf", [64, 512])) for _ in range(2)])
            yb = Ring([T(P.sb("pyb", [64, 512])) for _ in range(2)])
            gt = Ring([T(P.sb("pgt", [64, 512])) for _ in range(2)])
            bt = Ring([T(P.sb("pbt", [64, 512])) for _ in range(2)])
            yc = T(P.sb("pyc", [64, 512]))
            sq = T(P.sb("psq", [64, 512]))
            rs = T(P.sb("prs", [64, 512]))
            oy_r = Ring([T(P.sb("oy", [64, 512], OTDT)) for _ in range(2)])
            ps_r = Ring([TPS(P.psum("ps_po", [128, 512])) for _ in range(4)])
            olg, _ = PPC['lng64']
            olb, _ = PPC['lnb64']
            epsg = T(P.sb("epsg", [64, 1]))
            P.op(DVE, lambda e: e.memset(epsg[:, :], 64e-5), W=[epsg.d])
            for si, (kind, sidx, Tn, off) in enumerate(SEQS):
                nq = min(512, Tn)
                for h in range(4):
                    for q0 in range(0, Tn, nq):
                        n = nq
                        a, b, g_, bo = yf.next(), yb.next(), gt.next(), bt.next()
                        cols = slice(off + q0, off + q0 + n)
                        P.dma(SP, a[:, 0:n], YS[h * 64:(h + 1) * 64, cols], R=[dYS[si]], W=[a.d])
                        P.dma(SP, b[:, 0:n], YS[256 + h * 64:256 + (h + 1) * 64, cols], R=[dYS[si]], W=[b.d])
                        P.dma(SP, g_[:, 0:n], RW[9 * 256 + h * 64:9 * 256 + (h + 1) * 64, cols], R=[dRW[si]], W=[g_.d])
                        P.dma(SP, bo[:, 0:n], RW[10 * 256 + h * 64:10 * 256 + (h + 1) * 64, cols], R=[dRW[si]], W=[bo.d])
                        P.op(DVE, lambda e: e.tensor_tensor(out=a[:, 0:n], in0=a[:, 0:n], in1=b[:, 0:n], op=ALU.add), R=[a.d, b.d], W=[a.d])
                        pm_ = ps_r.next()
                        P.op(PE, lambda e: e.matmul(pm_[0:64, 0:n], lhsT=C('ones', rows=64, sub=(0, 64)), rhs=a[:, 0:n], start=True, stop=True),
                             R=[a.d, cp.d], W=[pm_.d])
                        P.op(DVE, lambda e: e.scalar_tensor_tensor(out=yc[:, 0:n], in0=pm_[0:64, 0:n], scalar=-1.0 / 64, in1=a[:, 0:n],
                                                                   op0=ALU.mult, op1=ALU.add), R=[pm_.d, a.d], W=[yc.d])
                        P.op(ACT, lambda e: e.activation(out=sq[:, 0:n], in_=yc[:, 0:n], func=AF.Square), R=[yc.d], W=[sq.d])
                        pv = ps_r.next()
                        P.op(PE, lambda e: e.matmul(pv[0:64, 0:n], lhsT=C('ones', rows=64, sub=(0, 64)), rhs=sq[:, 0:n], start=True, stop=True),
                             R=[sq.d, cp.d], W=[pv.d])
                        P.op(ACT, lambda e: e.activation(out=rs[:, 0:n], in_=pv[0:64, 0:n], func=AF.Sqrt, bias=epsg[:, 0:1], scale=1.0 / 64),
                             R=[pv.d, epsg.d], W=[rs.d])
                        P.op(DVE, lambda e: e.reciprocal(out=rs[:, 0:n], in_=rs[:, 0:n]), R=[rs.d], W=[rs.d])
                        P.op(DVE, lambda e: e.scalar_tensor_tensor(out=yc[:, 0:n], in0=yc[:, 0:n], scalar=pp[l][0:64, olg + h:olg + h + 1], in1=rs[:, 0:n],
                                                                   op0=ALU.mult, op1=ALU.mult), R=[yc.d, rs.d, pp[l].d], W=[yc.d])
                        P.op(DVE, lambda e: e.scalar_tensor_tensor(out=yc[:, 0:n], in0=yc[:, 0:n], scalar=pp[l][0:64, olb + h:olb + h + 1], in1=bo[:, 0:n],
                                                                   op0=ALU.add, op1=ALU.add), R=[yc.d, bo.d, pp[l].d], W=[yc.d])
                        oy = oy_r.next()
                        P.op(DVE, lambda e: e.tensor_tensor(out=oy[:, 0:n], in0=yc[:, 0:n], in1=g_[:, 0:n], op=ALU.mult), R=[yc.d, g_.d], W=[oy.d])
                        P.dma(ACT, OT[h * 64:(h + 1) * 64, cols], oy[:, 0:n], R=[oy.d], W=[dOT[si]])

        def mixer_ssd(l):
            NCH = TS // 128
            raw = T(P.sb("raw", [64, TS]))
            tcv = T(P.sb("tcv", [64, TS]))
            xcf = [T(P.sb("xcf%d" % h, [64, TS], BF16)) for h in range(4)]
            Bb = [T(P.sb("Bb%d" % g, [64, TS], BF16)) for g in range(2)]
            Cb = [T(P.sb("Cb%d" % g, [64, TS], BF16)) for g in range(2)]
            xT = [T(P.sb("xT%d" % h, [128, NCH, 64], BF16)) for h in range(4)]
            yz = [T(P.sb("yz%d" % h, [64, TS], BF16)) for h in range(4)]
            dtr = T(P.sb("dtr", [128, NCH, 8]))
            dts = T(P.sb("dts", [128, NCH, 8]))
            gg = T(P.sb("gg", [128, NCH, 8]))
            nea = T(P.sb("nea", [128, 8]))
            S_ = [T(P.sb("S%d" % d, [64, 64])) for d in range(2)]
            Sfb = T(P.sb("Sfb", [64, 64], BF16))
            Sbs = T(P.sb("Sbs", [64, NCH, 64], BF16))
            cumc = Ring([T(P.sb("cumc", [128, 8])) for _ in range(2)])
            gB_r = Ring([T(P.sb("gB", [128, 128])) for _ in range(2)])
            arg_r = Ring([T(P.sb("arg", [128, 128])) for _ in range(2)])
            Dm = [T(P.sb("Dm%d" % d, [128, 128])) for d in range(2)]
            Ds = T(P.sb("Ds", [128, 128]))
            PT_r = Ring([T(P.sb("PT", [128, 128], BF16)) for _ in range(2)])
            sm_r = Ring([T(P.sb("sm", [128, 4])) for _ in range(4)])
            ecr_r = Ring([T(P.sb("ecr", [64, 128])) for _ in range(2)])
            qd_r = Ring([T(P.sb("qd", [64, 128], BF16)) for _ in range(4)])
            Bw_r = Ring([T(P.sb("Bw", [128, 64], BF16)) for _ in range(2)])
            zt_ = T(P.sb("zt_", [64, 512]))
            t5 = T(P.sb("t5", [64, 512]))
            rsy = T(P.sb("rsy", [64, 512]))
            oy_r = Ring([T(P.sb("oy", [64, 512], OTDT)) for _ in range(2)])
            ps_cc = Ring([TPS(P.psum("ps_cc", [128, 512])) for _ in range(1)])
            ps_cr = Ring([TPS(P.psum("ps_cr", [128, 512])) for _ in range(2)])
            ps_sc = Ring([TPS(P.psum("ps_sc", [128, 512])) for _ in range(1)])
            ps_tr = Ring([TPS(P.psum("ps_tr", [128, 1024], BF16)) for _ in range(1)])
            ps_up = Ring([TPS(P.psum("ps_up", [128, 512])) for _ in range(1)])
            ps_y = Ring([TPS(P.psum("ps_y", [128, 512])) for _ in range(2)])
            ocw, _ = PPC['scw64']
            ocb, _ = PPC['scb64']
            odtb, _ = PPC['dtb']
            oal, _ = PPC['alog']
            od4, _ = PPC['ssd_d4']
            og4, _ = PPC['ssd_g4']
            P.op(ACT, lambda e: e.activation(out=nea[:, :], in_=pp[l][:, oal:oal + 8], func=AF.Exp), R=[pp[l].d], W=[nea.d])
            P.op(DVE, lambda e: e.tensor_scalar(out=nea[:, :], in0=nea[:, :], scalar1=-1.0, scalar2=None, op0=ALU.mult), R=[nea.d], W=[nea.d])
            for si, (kind, sidx, Tn, off) in enumerate(SEQS):
                nch = Tn // 128

                def convsilu(hc, dst, dstb):
                    r0 = 20 * 128 + hc * 64
                    P.dma(SP, raw[:, 0:Tn], FT[r0:r0 + 64, off:off + Tn], R=[dFT[si]], W=[raw.d])
                    P.op(ACT, lambda e: e.activation(out=tcv[:, 0:Tn], in_=raw[:, 0:Tn], func=AF.Identity,
                                                     bias=pp[l][0:64, ocb + hc:ocb + hc + 1], scale=pp[l][0:64, ocw + 8 + hc:ocw + 9 + hc]),
                         R=[raw.d, pp[l].d], W=[tcv.d])
                    P.op(DVE, lambda e: e.scalar_tensor_tensor(out=tcv[:, 1:Tn], in0=raw[:, 0:Tn - 1], scalar=pp[l][0:64, ocw + hc:ocw + hc + 1],
                                                               in1=tcv[:, 1:Tn], op0=ALU.mult, op1=ALU.add), R=[raw.d, pp[l].d, tcv.d], W=[tcv.d])
                    P.op(DVE, lambda e: e.scalar_tensor_tensor(out=tcv[:, 0:Tn - 1], in0=raw[:, 1:Tn], scalar=pp[l][0:64, ocw + 16 + hc:ocw + 17 + hc],
                                                               in1=tcv[:, 0:Tn - 1], op0=ALU.mult, op1=ALU.add), R=[raw.d, pp[l].d, tcv.d], W=[tcv.d])
                    if dst is not None:
                        P.op(ACT, lambda e: e.activation(out=dst[:, 0:Tn], in_=tcv[:, 0:Tn], func=AF.Silu), R=[tcv.d], W=[dst.d])
                        P.op(DVE, lambda e: e.tensor_copy(out=dstb[:, 0:Tn], in_=dst[:, 0:Tn]), R=[dst.d], W=[dstb.d])
                    else:
                        P.op(ACT, lambda e: e.activation(out=dstb[:, 0:Tn], in_=tcv[:, 0:Tn], func=AF.Silu), R=[tcv.d], W=[dstb.d])
                for g in range(2):
                    convsilu(4 + g, None, Bb[g])
                    convsilu(6 + g, None, Cb[g])
                for h in range(4):
                    convsilu(h, None, xcf[h])
                    for c in range(nch):
                        pt = ps_tr.next()
                        P.op(PE, lambda e: e.transpose(pt[:, 0:64], xcf[h][:, c * 128:(c + 1) * 128], cpb[0:64, 0:64]), R=[xcf[h].d, cpb.d], W=[pt.d])
                        evac(xT[h][:, c, :], pt[:, 0:64], [pt.d], [xT[h].d])
                P.dma(SP, dtr[:, 0:nch, :], FTM[off:off + Tn, 768:776].rearrange("(c p) e -> p c e", p=128), R=[dFTM[si]], W=[dtr.d])
                P.op(DVE, lambda e: e.tensor_tensor(out=dtr[:, 0:nch, :], in0=dtr[:, 0:nch, :],
                                                    in1=pp[l][:, odtb:odtb + 8].unsqueeze(1).to_broadcast([128, nch, 8]), op=ALU.add),
                     R=[dtr.d, pp[l].d], W=[dtr.d])
                P.op(ACT, lambda e: e.activation(out=dtr[:, 0:nch, :], in_=dtr[:, 0:nch, :], func=AF.Exp), R=[dtr.d], W=[dtr.d])
                P.op(ACT, lambda e: e.activation(out=dts[:, 0:nch, :], in_=dtr[:, 0:nch, :], func=AF.Ln, bias=1.0, scale=1.0), R=[dtr.d], W=[dts.d])
                P.op(DVE, lambda e: e.tensor_tensor(out=gg[:, 0:nch, :], in0=dts[:, 0:nch, :],
                                                    in1=nea[:, :].unsqueeze(1).to_broadcast([128, nch, 8]), op=ALU.mult),
                     R=[dts.d, nea.d], W=[gg.d])

                def chunk_cum(c):
                    pc = ps_cc.next()
                    P.op(PE, lambda e: e.matmul(pc[:, 0:4], lhsT=C('tri_f'), rhs=gg[:, c, 0:4], start=True, stop=True), R=[cp.d, gg.d], W=[pc.d])
                    P.op(PE, lambda e: e.matmul(pc[:, 4:8], lhsT=C('tri_b'), rhs=gg[:, c, 4:8], start=True, stop=True), R=[cp.d, gg.d], W=[pc.d])
                    cc = cumc.next()
                    P.op(DVE, lambda e: e.tensor_copy(out=cc[:, :], in_=pc[:, 0:8]), R=[pc.d], W=[cc.d])
                    return cc

                def dirstuff(c, h, d, cc):
                    col = d * 4 + h
                    gB = gB_r.next()
                    P.op(DVE, lambda e: e.tensor_scalar(out=gB[:, :], in0=C('ones'), scalar1=gg[:, c, col:col + 1], scalar2=None, op0=ALU.mult),
                         R=[cp.d, gg.d], W=[gB.d])
                    pr_ = ps_cr.next()
                    P.op(PE, lambda e: e.matmul(pr_[:, 0:128], lhsT=gB[:, :], rhs=C('tri_f' if d == 0 else 'tri_b'), start=True, stop=True),
                         R=[gB.d, cp.d], W=[pr_.d])
                    sm = sm_r.next()
                    lc = 127 if d == 0 else 0
                    P.op(DVE, lambda e: e.tensor_copy(out=sm[:, 0:1], in_=pr_[:, lc:lc + 1]), R=[pr_.d], W=[sm.d])
                    P.op(ACT, lambda e: e.activation(out=sm[:, 1:2], in_=cc[:, col:col + 1], func=AF.Exp, bias=sm[:, 0:1], scale=-1.0),
                         R=[cc.d, sm.d], W=[sm.d])
                    P.op(DVE, lambda e: e.tensor_tensor(out=sm[:, 2:3], in0=sm[:, 1:2], in1=dts[:, c, col:col + 1], op=ALU.mult),
                         R=[sm.d, dts.d], W=[sm.d])
                    P.op(ACT, lambda e: e.activation(out=sm[:, 3:4], in_=sm[:, 0:1], func=AF.Exp), R=[sm.d], W=[sm.d])
                    return pr_, sm

                def state_update(S, c, h, g, sm):
                    pt = ps_tr.next()
                    P.op(PE, lambda e: e.transpose(pt[:, 0:64], Bb[g][:, c * 128:(c + 1) * 128], cpb[0:64, 0:64]), R=[Bb[g].d, cpb.d], W=[pt.d])
                    Bw = Bw_r.next()
                    P.op(DVE, lambda e: e.tensor_scalar(out=Bw[:, :], in0=pt[:, 0:64], scalar1=sm[:, 2:3], scalar2=None, op0=ALU.mult),
                         R=[pt.d, sm.d], W=[Bw.d])
                    pu = ps_up.next()
                    P.op(PE, lambda e: e.matmul(pu[0:64, 0:64], lhsT=Bw[:, :], rhs=xT[h][:, c, :], start=True, stop=True), R=[Bw.d, xT[h].d], W=[pu.d])
                    P.op(DVE, lambda e: e.scalar_tensor_tensor(out=S[:, :], in0=S[:, :], scalar=sm[0:64, 3:4], in1=pu[0:64, 0:64],
                                                               op0=ALU.mult, op1=ALU.add), R=[S.d, pu.d, sm.d], W=[S.d])

                for h in range(4):
                    g = h // 2
                    Sf, Sb = S_
                    if kind == 's':
                        P.dma(SP, Sf[:, :], st_ssd[l, 0, h], W=[Sf.d])
                        P.dma(SP, Sb[:, :], st_ssd[l, 1, h], W=[Sb.d])
                    else:
                        P.op(DVE, lambda e: e.memset(Sf[:, :], 0.0), W=[Sf.d])
                        P.op(DVE, lambda e: e.memset(Sb[:, :], 0.0), W=[Sb.d])
                    for c in range(nch - 1, -1, -1):
                        P.op(ACT, lambda e: e.activation(out=Sbs[:, c, :], in_=Sb[:, :], func=AF.Copy), R=[Sb.d], W=[Sbs.d])
                        cc = chunk_cum(c)
                        pr_, sm = dirstuff(c, h, 1, cc)
                        state_update(Sb, c, h, g, sm)
                    if kind == 'p':
                        P.dma(SP, o_ssd[sidx, l, 1, h], Sb[:, :], R=[Sb.d], W=[dOUT])
                    for c0 in range(0, nch, 4):
                        ncg = min(4, nch - c0)
                        n = ncg * 128
                        py = ps_y.next()
                        for ci in range(ncg):
                            c = c0 + ci
                            sl = slice(c * 128, (c + 1) * 128)
                            cc = chunk_cum(c)
                            psc = ps_sc.next()
                            P.op(PE, lambda e: e.matmul(psc[:, 0:128], lhsT=Bb[g][:, sl], rhs=Cb[g][:, sl], start=True, stop=True),
                                 R=[Bb[g].d, Cb[g].d], W=[psc.d])
                            qds = []
                            sms = []
                            for d in range(2):
                                col = d * 4 + h
                                pr_, sm = dirstuff(c, h, d, cc)
                                sms.append(sm)
                                arg = arg_r.next()
                                P.op(DVE, lambda e: e.scalar_tensor_tensor(out=arg[:, :], in0=pr_[:, 0:128], scalar=cc[:, col:col + 1],
                                                                           in1=C('nm_f' if d == 0 else 'nm_b'), op0=ALU.subtract, op1=ALU.add),
                                     R=[pr_.d, cc.d, cp.d], W=[arg.d])
                                P.op(ACT, lambda e: e.activation(out=Dm[d][:, :], in_=arg[:, :], func=AF.Exp), R=[arg.d], W=[Dm[d].d])
                                ecr = ecr_r.next()
                                P.op(ACT, lambda e: e.activation(out=ecr[:, :], in_=pr_[0:64, 0:128], func=AF.Exp), R=[pr_.d], W=[ecr.d])
                                qd = qd_r.next()
                                P.op(DVE, lambda e: e.tensor_tensor(out=qd[:, :], in0=Cb[g][:, sl], in1=ecr[:, :], op=ALU.mult),
                                     R=[Cb[g].d, ecr.d], W=[qd.d])
                                qds.append(qd)
                            P.op(DVE, lambda e: e.tensor_scalar(out=Ds[:, :], in0=Dm[0][:, :], scalar1=dts[:, c, h:h + 1], scalar2=None, op0=ALU.mult),
                                 R=[Dm[0].d, dts.d], W=[Ds.d])
                            P.op(DVE, lambda e: e.scalar_tensor_tensor(out=Ds[:, :], in0=Dm[1][:, :], scalar=dts[:, c, 4 + h:5 + h], in1=Ds[:, :],
                                                                       op0=ALU.mult, op1=ALU.add), R=[Dm[1].d, dts.d, Ds.d], W=[Ds.d])
                            PT = PT_r.next()
                            P.op(DVE, lambda e: e.tensor_tensor(out=PT[:, :], in0=psc[:, 0:128], in1=Ds[:, :], op=ALU.mult), R=[psc.d, Ds.d], W=[PT.d])
                            P.op(ACT, lambda e: e.activation(out=Sfb[:, :], in_=Sf[:, :], func=AF.Copy), R=[Sf.d], W=[Sfb.d])
                            yo = py[0:64, ci * 128:(ci + 1) * 128]
                            P.op(PE, lambda e: e.matmul(yo, lhsT=xT[h][:, c, :], rhs=PT[:, :], start=True, stop=False), R=[xT[h].d, PT.d], W=[py.d])
                            P.op(PE, lambda e: e.matmul(yo, lhsT=Sfb[:, :], rhs=qds[0][:, :], start=False, stop=False), R=[Sfb.d, qds[0].d], W=[py.d])
                            P.op(PE, lambda e: e.matmul(yo, lhsT=Sbs[:, c, :], rhs=qds[1][:, :], start=False, stop=True), R=[Sbs.d, qds[1].d], W=[py.d])
                            state_update(Sf, c, h, g, sms[0])
                        tsl = slice(c0 * 128, c0 * 128 + n)
                        r0 = 18 * 128 + h * 64
                        P.dma(SP, zt_[:, 0:n], FT[r0:r0 + 64, off + c0 * 128: off + c0 * 128 + n], R=[dFT[si]], W=[zt_.d])
                        P.op(ACT, lambda e: e.activation(out=zt_[:, 0:n], in_=zt_[:, 0:n], func=AF.Silu), R=[zt_.d], W=[zt_.d])
                        P.op(DVE, lambda e: e.scalar_tensor_tensor(out=t5[:, 0:n], in0=xcf[h][:, tsl], scalar=pp[l][0:64, od4 + h:od4 + h + 1],
                                                                   in1=py[0:64, 0:n], op0=ALU.mult, op1=ALU.add), R=[xcf[h].d, pp[l].d, py.d], W=[t5.d])
                        P.op(DVE, lambda e: e.tensor_tensor(out=yz[h][:, tsl], in0=t5[:, 0:n], in1=zt_[:, 0:n], op=ALU.mult),
                             R=[t5.d, zt_.d], W=[yz[h].d])
                    if kind == 'p':
                        P.dma(SP, o_ssd[sidx, l, 0, h], Sf[:, :], R=[Sf.d], W=[dOUT])
                nq = min(512, Tn)
                for q0 in range(0, Tn, nq):
                    n = nq
                    tsl = slice(q0, q0 + n)
                    pss = ps_cc.next()
                    for h in range(4):
                        P.op(ACT, lambda e: e.activation(out=t5[:, 0:n], in_=yz[h][:, tsl], func=AF.Square), R=[yz[h].d], W=[t5.d])
                        P.op(PE, lambda e: e.matmul(pss[0:64, 0:n], lhsT=C('ones', rows=64, sub=(0, 64)), rhs=t5[:, 0:n], start=(h == 0), stop=(h == 3)),
                             R=[t5.d, cp.d], W=[pss.d])
                    P.op(ACT, lambda e: e.activation(out=rsy[:, 0:n], in_=pss[0:64, 0:n], func=AF.Sqrt, bias=C('eps', rows=64), scale=1.0 / 256),
                         R=[pss.d, cp.d], W=[rsy.d])
                    P.op(DVE, lambda e: e.reciprocal(out=rsy[:, 0:n], in_=rsy[:, 0:n]), R=[rsy.d], W=[rsy.d])
                    for h in range(4):
                        oy = oy_r.next()
                        P.op(DVE, lambda e: e.scalar_tensor_tensor(out=oy[:, 0:n], in0=yz[h][:, tsl], scalar=pp[l][0:64, og4 + h:og4 + h + 1],
                                                                   in1=rsy[:, 0:n], op0=ALU.mult, op1=ALU.mult), R=[yz[h].d, pp[l].d, rsy.d], W=[oy.d])
                        P.dma(ACT, OT[512 + h * 64:512 + (h + 1) * 64, off + q0: off + q0 + n], oy[:, 0:n], R=[oy.d], W=[dOT[si]])

        def mixer_ret(l):
            ropc = T(P.sb("ropc", [64, TS]))
            rops = T(P.sb("rops", [64, TS]))
            P.dma(SP, ropc[:, :], rope_in[0, 0:64, :], W=[ropc.d])
            P.dma(SP, rops[:, :], rope_in[1, 0:64, :], W=[rops.d])
            qf32 = T(P.sb("qf32", [64, TS]))
            qs32 = T(P.sb("qs32", [64, TS]))
            qb = T(P.sb("qb", [64, TS], BF16))
            kb = T(P.sb("kb", [64, TS], BF16))
            g32 = T(P.sb("g32", [64, TS]))
            v32 = T(P.sb("v32", [128, TS // 128, 64]))
            vb = T(P.sb("vb", [128, TS // 128, 64], BF16))
            Sf = T(P.sb("Sf", [64, 64]))
            Sb = T(P.sb("Sb", [64, 64]))
            Sfb = T(P.sb("Sfb", [64, 64], BF16))
            Sbs = T(P.sb("Sbs", [64, TS // 128, 64], BF16))
            kw_r = Ring([T(P.sb("kw", [128, 64], BF16)) for _ in range(2)])
            PT_r = Ring([T(P.sb("PT", [128, 128], BF16)) for _ in range(2)])
            qd_r = Ring([T(P.sb("qd", [64, 128], BF16)) for _ in range(4)])
            sqy = T(P.sb("sqy", [64, 512]))
            rsy = T(P.sb("rsy", [64, 512]))
            sg = T(P.sb("sg", [64, 512]))
            tmy = T(P.sb("tmy", [64, 512]))
            oy_r = Ring([T(P.sb("oy", [64, 512], OTDT)) for _ in range(2)])
            ps_sc = Ring([TPS(P.psum("ps_sc", [128, 512])) for _ in range(2)])
            ps_tr = Ring([TPS(P.psum("ps_tr", [128, 1024], BF16)) for _ in range(1)])
            ps_up = Ring([TPS(P.psum("ps_up", [128, 512])) for _ in range(1)])
            ps_y = Ring([TPS(P.psum("ps_y", [128, 512])) for _ in range(2)])
            ps_ss = Ring([TPS(P.psum("ps_ss", [128, 512])) for _ in range(1)])
            og4, _ = PPC['ret_g4']
            for si, (kind, sidx, Tn, off) in enumerate(SEQS):
                nch = Tn // 128
                for h in range(4):
                    pr = (h % 2) * 64

                    def rows(cbase):
                        r0 = (cbase + h // 2) * 128 + pr
                        return FT[r0:r0 + 64, off:off + Tn]
                    for (dst, cb, cbs, scale) in ((qb, 8, 12, 1.0), (kb, 10, 14, 0.125)):
                        P.dma(SP, qf32[:, 0:Tn], rows(cb), R=[dFT[si]], W=[qf32.d])
                        if kind == 's':
                            P.dma(SP, qs32[:, 0:Tn], rows(cbs), R=[dFT[si]], W=[qs32.d])
                            P.op(DVE, lambda e: e.tensor_tensor(out=qf32[:, 0:Tn], in0=qf32[:, 0:Tn], in1=ropc[:, 0:Tn], op=ALU.mult),
                                 R=[qf32.d, ropc.d], W=[qf32.d])
                            P.op(DVE, lambda e: e.tensor_tensor(out=qs32[:, 0:Tn], in0=qs32[:, 0:Tn], in1=rops[:, 0:Tn], op=ALU.mult),
                                 R=[qs32.d, rops.d], W=[qs32.d])
                            P.op(DVE, lambda e: e.tensor_tensor(out=qf32[:, 0:Tn], in0=qf32[:, 0:Tn], in1=qs32[:, 0:Tn], op=ALU.add),
                                 R=[qf32.d, qs32.d], W=[qf32.d])
                        P.op(ACT, lambda e, dst=dst, scale=scale: e.activation(out=dst[:, 0:Tn], in_=qf32[:, 0:Tn], func=AF.Copy, scale=scale),
                             R=[qf32.d], W=[dst.d])
                    P.dma(SP, g32[:, 0:Tn], rows(16), R=[dFT[si]], W=[g32.d])
                    P.dma(SP, v32[:, 0:nch, :], FTM[off:off + Tn, h * 64:(h + 1) * 64].rearrange("(c p) e -> p c e", p=128),
                          R=[dFTM[si]], W=[v32.d])
                    P.op(DVE, lambda e: e.tensor_copy(out=vb[:, 0:nch, :], in_=v32[:, 0:nch, :]), R=[v32.d], W=[vb.d])
                    if kind == 's':
                        P.dma(SP, Sf[:, :], st_ret[l, 0, h], W=[Sf.d])
                        P.dma(SP, Sb[:, :], st_ret[l, 1, h], W=[Sb.d])
                    else:
                        P.op(DVE, lambda e: e.memset(Sf[:, :], 0.0), W=[Sf.d])
                        P.op(DVE, lambda e: e.memset(Sb[:, :], 0.0), W=[Sb.d])

                    def state_update(S, c, kwcol, gcol):
                        pt = ps_tr.next()
                        P.op(PE, lambda e: e.transpose(pt[:, 0:64], kb[:, c * 128:(c + 1) * 128], cpb[0:64, 0:64]),
                             R=[kb.d, cpb.d], W=[pt.d])
                        kw = kw_r.next()
                        P.op(DVE, lambda e: e.tensor_scalar(out=kw[:, :], in0=pt[:, 0:64], scalar1=kwcol, scalar2=None, op0=ALU.mult),
                             R=[pt.d, cp.d], W=[kw.d])
                        pu = ps_up.next()
                        P.op(PE, lambda e: e.matmul(pu[0:64, 0:64], lhsT=kw[:, :], rhs=vb[:, c, :], start=True, stop=True),
                             R=[kw.d, vb.d], W=[pu.d])
                        P.op(DVE, lambda e: e.scalar_tensor_tensor(out=S[:, :], in0=S[:, :], scalar=gcol, in1=pu[0:64, 0:64],
                                                                   op0=ALU.mult, op1=ALU.add), R=[S.d, pu.d, cp.d], W=[S.d])
                    okf, _ = CPC['ret_kwf']
                    okb, _ = CPC['ret_kwb']
                    og, _ = CPC['ret_g128']
                    for c in range(nch - 1, -1, -1):
                        P.op(ACT, lambda e, c=c: e.activation(out=Sbs[:, c, :], in_=Sb[:, :], func=AF.Copy), R=[Sb.d], W=[Sbs.d])
                        state_update(Sb, c, cp[:, okb + h:okb + h + 1], cp[0:64, og + 4 + h:og + 5 + h])
                    if kind == 'p':
                        P.dma(SP, o_ret[sidx, l, 1, h], Sb[:, :], R=[Sb.d], W=[dOUT])
                    for c0 in range(0, nch, 4):
                        ncg = min(4, nch - c0)
                        n = ncg * 128
                        py = ps_y.next()
                        for cc in range(ncg):
                            c = c0 + cc
                            sl = slice(c * 128, (c + 1) * 128)
                            psc = ps_sc.next()
                            P.op(PE, lambda e: e.matmul(psc[:, 0:128], lhsT=kb[:, sl], rhs=qb[:, sl], start=True, stop=True),
                                 R=[kb.d, qb.d], W=[psc.d])
                            PT = PT_r.next()
                            P.op(DVE, lambda e: e.tensor_tensor(out=PT[:, :], in0=psc[:, 0:128], in1=C('ret_ds', sub=(h * 128, (h + 1) * 128)), op=ALU.mult),
                                 R=[psc.d, cp.d], W=[PT.d])
                            qfw = qd_r.next()
                            qbw = qd_r.next()
                            P.op(DVE, lambda e: e.tensor_tensor(out=qfw[:, :], in0=qb[:, sl], in1=C('ret_df', rows=64, sub=(h * 128, (h + 1) * 128)), op=ALU.mult),
                                 R=[qb.d, cp.d], W=[qfw.d])
                            P.op(DVE, lambda e: e.tensor_tensor(out=qbw[:, :], in0=qb[:, sl], in1=C('ret_db', rows=64, sub=(h * 128, (h + 1) * 128)), op=ALU.mult),
                                 R=[qb.d, cp.d], W=[qbw.d])
                            P.op(ACT, lambda e: e.activation(out=Sfb[:, :], in_=Sf[:, :], func=AF.Copy), R=[Sf.d], W=[Sfb.d])
                            yo = py[0:64, cc * 128:(cc + 1) * 128]
                            P.op(PE, lambda e: e.matmul(yo, lhsT=vb[:, c, :], rhs=PT[:, :], start=True, stop=False), R=[vb.d, PT.d], W=[py.d])
                            P.op(PE, lambda e: e.matmul(yo, lhsT=Sfb[:, :], rhs=qfw[:, :], start=False, stop=False), R=[Sfb.d, qfw.d], W=[py.d])
                            P.op(PE, lambda e: e.matmul(yo, lhsT=Sbs[:, c, :], rhs=qbw[:, :], start=False, stop=True), R=[Sbs.d, qbw.d], W=[py.d])
                            state_update(Sf, c, cp[:, okf + h:okf + h + 1], cp[0:64, og + h:og + h + 1])
                        tsl = slice(c0 * 128, c0 * 128 + n)
                        P.op(ACT, lambda e: e.activation(out=sqy[:, 0:n], in_=py[0:64, 0:n], func=AF.Square), R=[py.d], W=[sqy.d])
                        pss = ps_ss.next()
                        P.op(PE, lambda e: e.matmul(pss[0:64, 0:n], lhsT=C('ones', rows=64, sub=(0, 64)), rhs=sqy[:, 0:n], start=True, stop=True),
                             R=[sqy.d, cp.d], W=[pss.d])
                        P.op(ACT, lambda e: e.activation(out=rsy[:, 0:n], in_=pss[0:64, 0:n], func=AF.Sqrt, bias=C('eps', rows=64), scale=1.0 / 64),
                             R=[pss.d, cp.d], W=[rsy.d])
                        P.op(DVE, lambda e: e.reciprocal(out=rsy[:, 0:n], in_=rsy[:, 0:n]), R=[rsy.d], W=[rsy.d])
                        P.op(ACT, lambda e: e.activation(out=sg[:, 0:n], in_=g32[:, tsl], func=AF.Silu), R=[g32.d], W=[sg.d])
                        P.op(DVE, lambda e: e.scalar_tensor_tensor(out=tmy[:, 0:n], in0=py[0:64, 0:n], scalar=pp[l][0:64, og4 + h:og4 + h + 1],
                                                                   in1=rsy[:, 0:n], op0=ALU.mult, op1=ALU.mult),
                             R=[py.d, pp[l].d, rsy.d], W=[tmy.d])
                        oy = oy_r.next()
                        P.op(DVE, lambda e: e.tensor_tensor(out=oy[:, 0:n], in0=tmy[:, 0:n], in1=sg[:, 0:n], op=ALU.mult),
                             R=[tmy.d, sg.d], W=[oy.d])
                        P.dma(ACT, OT[256 + h * 64:256 + (h + 1) * 64, off + c0 * 128: off + c0 * 128 + n], oy[:, 0:n], R=[oy.d], W=[dOT[si]])
                    if kind == 'p':
                        P.dma(SP, o_ret[sidx, l, 0, h], Sf[:, :], R=[Sf.d], W=[dOUT])

        for l in range(L):
            if debug == 'M':
                break
            xin, xout = XT[0], XT[1]
            dxin, dxout = dXT[0], dXT[1]
            P.begin()
            wfm = T(P.sb("wfm", [128, 8, NFM * 128], BF16))
            wtm = T(P.sb("wtm", [128, 8, NTM], BF16))
            for k in range(8):
                P.dma(POOL, wfm[:, k, :], w_in_fm[l, k * 128:(k + 1) * 128, :], W=[wfm.d])
            P.dma(POOL, wtm[:, :, :], w_in_tm[l].rearrange("(k p) n -> p k n", p=128), W=[wtm.d])
            xt_r = Ring([T(P.sb("xt", [128, 8, 512])) for _ in range(2)])
            xtm_r = Ring([T(P.sb("xtm", [128, 1024])) for _ in range(2)])
            sq = Ring([T(P.sb("sq", [128, 512])) for _ in range(2)])
            rs = T(P.sb("rs", [128, 512]))
            tmp_r = Ring([T(P.sb("tmp", [128, 512])) for _ in range(2)])
            hT_r = Ring([T(P.sb("hT", [128, 8, 512], BF16)) for _ in range(2)])
            stg_r = Ring([T(P.sb("stg", [128, 4, 512])) for _ in range(2)])
            stm_r = Ring([T(P.sb("stm", [128, NTM])) for _ in range(2)])
            ps_r = Ring([TPS(P.psum("psA", [128, 512])) for _ in range(7)])
            for si, (kind, sidx, Tn, off) in enumerate(SEQS):
                if debug == 'A1':
                    break
                cond = 0 if kind == 'p' else 1
                nt = min(512, Tn)
                for t0 in range(0, Tn, nt):
                    n = nt
                    xt = xt_r.next()
                    if l == 0:
                        for b in range(n // 128):
                            xtm = xtm_r.next()
                            P.dma(SP, xtm[:, :], x_tm[off + t0 + b * 128: off + t0 + (b + 1) * 128, :], W=[xtm.d])
                            for half in range(2):
                                ps = ps_r.next()
                                for jj in range(4):
                                    j = half * 4 + jj
                                    P.op(PE, lambda e, j=j, jj=jj, ps=ps, xtm=xtm: e.transpose(
                                        ps[:, jj * 128:(jj + 1) * 128], xtm[:, j * 128:(j + 1) * 128], C('ident')),
                                        R=[xtm.d, cp.d], W=[ps.d])
                                evac(xt[:, half * 4:half * 4 + 4, b * 128:(b + 1) * 128],
                                     ps[:, 0:512].rearrange("p (j t) -> p j t", t=128), [ps.d], [xt.d])
                        P.dma(POOL, xin[:, off + t0: off + t0 + n].rearrange("(j p) t -> p j t", p=128), xt[:, :, 0:n],
                              R=[xt.d], W=[dxin[si]])
                    else:
                        P.dma(SP, xt[:, :, 0:n], xin[:, off + t0: off + t0 + n].rearrange("(j p) t -> p j t", p=128),
                              R=[dxin[si]], W=[xt.d])
                    if debug == 'A2':
                        continue
                    hT = hT_r.next()
                    rmsnorm_tile(xt, n, l, 0, cond, hT, sq, ps_r, rs, tmp_r)
                    if debug == 'A3':
                        continue
                    chunks = list(range(NFM))
                    if kind == 'p':
                        chunks = [c for c in chunks if c not in (12, 13, 14, 15, 28, 29, 30, 31)]
                    groups = [chunks[i:i + 4] for i in range(0, len(chunks), 4)]
                    for grp in groups:
                        runs = []
                        for c in grp:
                            if runs and runs[-1][-1] == c - 1:
                                runs[-1].append(c)
                            else:
                                runs.append([c])
                        for run_ in runs:
                            stg = stg_r.next()
                            for ci, c in enumerate(run_):
                                ps = ps_r.next()
                                for k in range(8):
                                    P.op(PE, lambda e, k=k, c=c, ps=ps, hT=hT: e.matmul(
                                        ps[:, 0:n], lhsT=wfm[:, k, c * 128:(c + 1) * 128], rhs=hT[:, k, 0:n],
                                        start=(k == 0), stop=(k == 7)), R=[wfm.d, hT.d], W=[ps.d])
                                evac(stg[:, ci, 0:n], ps[:, 0:n], [ps.d], [stg.d])
                            c0, nc_ = run_[0], len(run_)
                            P.dma(ACT, FT[c0 * 128:(c0 + nc_) * 128, off + t0: off + t0 + n].rearrange("(c p) t -> p c t", p=128),
                                  stg[:, 0:nc_, 0:n], R=[stg.d], W=[dFT[si]])
                    if debug == 'A4':
                        continue
                    for b in range(n // 128):
                        stm = stm_r.next()
                        for (c0, c1) in ((0, 512), (512, NTM)):
                            ps = ps_r.next()
                            for k in range(8):
                                P.op(PE, lambda e, k=k, ps=ps, hT=hT, b=b, c0=c0, c1=c1: e.matmul(
                                    ps[:, 0:c1 - c0], lhsT=hT[:, k, b * 128:(b + 1) * 128], rhs=wtm[:, k, c0:c1],
                                    start=(k == 0), stop=(k == 7)), R=[wtm.d, hT.d], W=[ps.d])
                            evac(stm[:, c0:c1], ps[:, 0:c1 - c0], [ps.d], [stm.d])
                        r0 = off + t0 + b * 128
                        P.dma(ACT, FTM[r0:r0 + 128, :], stm[:, :], R=[stm.d], W=[dFTM[si]])
                        if kind == 'p' and debug != 'A5':
                            tl = t0 + b * 128
                            P.dma(ACT, o_v[sidx, l, tl:tl + 128, :], stm[:, 256:512], R=[stm.d], W=[Dep()] if debug == 'A6' else [dOUT])
                            P.dma(ACT, o_k[sidx, l, tl:tl + 128, :], stm[:, 512:768], R=[stm.d], W=[Dep()] if debug == 'A6' else [dOUT])
            P.end()
            if debug and debug[0] == 'A':
                break

            if not (debug and debug[0] == 'C'):
                phase_B(l)
            if debug == 'B':
                break

            P.begin()
            wo = T(P.sb("wo", [128, 8, D], BF16))
            P.dma(POOL, wo[:, :, :], w_out[l].rearrange("(k p) n -> p k n", p=128), W=[wo.d])
            xt_r = Ring([T(P.sb("xt", [128, 8, 512])) for _ in range(2)])
            ot_r = Ring([T(P.sb("ot", [128, 8, 512], BF16)) for _ in range(2)])
            ps_r = Ring([TPS(P.psum("psC", [128, 512])) for _ in range(6)])
            for si, (kind, sidx, Tn, off) in enumerate(SEQS):
                cond = 0 if kind == 'p' else 1
                nt = min(512, Tn)
                for t0 in range(0, Tn, nt):
                    n = nt
                    xt = xt_r.next()
                    ot = ot_r.next()
                    P.dma(SP, xt[:, :, 0:n], xin[:, off + t0: off + t0 + n].rearrange("(j p) t -> p j t", p=128),
                          R=[dxin[si]], W=[xt.d])
                    P.dma(POOL if (debug and debug[0] == 'C') else SP, ot[:, :, 0:n], OT[:, off + t0: off + t0 + n].rearrange("(j p) t -> p j t", p=128),
                          R=[dOT[si]], W=[ot.d])
                    for j in range(8):
                        ps = ps_r.next()
                        for k in range(8):
                            P.op(PE, lambda e, k=k, j=j, ps=ps, ot=ot: e.matmul(
                                ps[:, 0:n], lhsT=wo[:, k, j * 128:(j + 1) * 128], rhs=ot[:, k, 0:n],
                                start=(k == 0), stop=(k == 7)), R=[wo.d, ot.d], W=[ps.d])
                        P.op(DVE, lambda e, j=j, ps=ps, xt=xt: e.scalar_tensor_tensor(
                            out=xt[:, j, 0:n], in0=ps[:, 0:n], scalar=modcol(l, 2, j, cond), in1=xt[:, j, 0:n],
                            op0=ALU.mult, op1=ALU.add), R=[ps.d, xt.d, modv[l].d], W=[xt.d])
                    P.dma(ACT, xout[:, off + t0: off + t0 + n].rearrange("(j p) t -> p j t", p=128), xt[:, :, 0:n],
                          R=[xt.d], W=[dxout[si]])
            P.end()
            if debug == 'C1':
                break
            if debug == 'CA':
                continue
            P.begin()
            wu = T(P.sb("wu", [128, 8, 2 * FFN], BF16))
            wd = T(P.sb("wd", [128, 22, D], BF16))
            for k in range(8):
                P.dma(POOL, wu[:, k, :], w_fup[l, k * 128:(k + 1) * 128, :], W=[wu.d])
            for g in range(22):
                P.dma(POOL, wd[:, g, :], w_fdn[l, g * 128:(g + 1) * 128, :], W=[wd.d])
            xt_r = Ring([T(P.sb("xt", [128, 8, 384])) for _ in range(1)])
            sq = Ring([T(P.sb("sq", [128, 384])) for _ in range(2)])
            rs = T(P.sb("rs", [128, 384]))
            tmp_r = Ring([T(P.sb("tmp", [128, 384])) for _ in range(1)])
            tg_r = Ring([T(P.sb("tg", [128, 384])) for _ in range(1)])
            tv_r = Ring([T(P.sb("tv", [128, 384])) for _ in range(1)])
            hT_r = Ring([T(P.sb("hT", [128, 8, 384], BF16)) for _ in range(1)])
            aT = T(P.sb("aT", [128, 22, 384], BF16))
            ps_r = Ring([TPS(P.psum("psC", [128, 384])) for _ in range(7)])
            ocw, _ = PPC['fcw']
            ocb, _ = PPC['fcb']
            for si, (kind, sidx, Tn, off) in enumerate(SEQS):
                cond = 0 if kind == 'p' else 1
                nt = min(382, Tn)
                for t0 in range(0, Tn, nt):
                    n = min(nt, Tn - t0)
                    N = n + 2
                    lo, hi = max(t0 - 1, 0), min(t0 + n + 1, Tn)
                    c_lo = lo - (t0 - 1)
                    xt = xt_r.next()
                    P.dma(SP, xt[:, :, c_lo:c_lo + hi - lo], xout[:, off + lo: off + hi].rearrange("(j p) t -> p j t", p=128),
                          R=[dxout[si]], W=[xt.d])
                    hT = hT_r.next()
                    rmsnorm_tile(xt, N, l, 1, cond, hT, sq, ps_r, rs, tmp_r)
                    a0 = 1 if t0 == 0 else 0
                    b1 = n - 1 if t0 + n == Tn else n
                    for g in range(22):
                        tt = []
                        for ci, (c, ring) in enumerate(((g, tg_r), (22 + g, tv_r))):
                            ps = ps_r.next()
                            for k in range(8):
                                P.op(PE, lambda e, k=k, c=c, ps=ps, hT=hT: e.matmul(
                                    ps[:, 0:N], lhsT=wu[:, k, c * 128:(c + 1) * 128], rhs=hT[:, k, 0:N],
                                    start=(k == 0), stop=(k == 7)), R=[wu.d, hT.d], W=[ps.d])
                            t = ring.next()
                            P.op(ACT, lambda e, c=c, ps=ps, t=t: e.activation(
                                out=t[:, 0:n], in_=ps[:, 1:n + 1], func=AF.Identity,
                                bias=pp[l][:, ocb + c:ocb + c + 1], scale=pp[l][:, ocw + 44 + c:ocw + 44 + c + 1]),
                                R=[ps.d, pp[l].d], W=[t.d])
                            P.op(DVE, lambda e, c=c, ps=ps, t=t: e.scalar_tensor_tensor(
                                out=t[:, a0:n], in0=ps[:, a0:n], scalar=pp[l][:, ocw + c:ocw + c + 1], in1=t[:, a0:n],
                                op0=ALU.mult, op1=ALU.add), R=[ps.d, pp[l].d, t.d], W=[t.d])
                            P.op(DVE, lambda e, c=c, ps=ps, t=t: e.scalar_tensor_tensor(
                                out=t[:, 0:b1], in0=ps[:, 2:b1 + 2], scalar=pp[l][:, ocw + 88 + c:ocw + 88 + c + 1], in1=t[:, 0:b1],
                                op0=ALU.mult, op1=ALU.add), R=[ps.d, pp[l].d, t.d], W=[t.d])
                            tt.append(t)
                        tg, tv = tt
                        P.op(ACT, lambda e, tg=tg: e.activation(out=tg[:, 0:n], in_=tg[:, 0:n], func=AF.Silu), R=[tg.d], W=[tg.d])
                        P.op(DVE, lambda e, tg=tg, tv=tv, g=g: e.tensor_tensor(out=aT[:, g, 0:n], in0=tg[:, 0:n], in1=tv[:, 0:n], op=ALU.mult),
                             R=[tg.d, tv.d], W=[aT.d])
                    for j in range(8):
                        ps = ps_r.next()
                        for g in range(22):
                            P.op(PE, lambda e, g=g, j=j, ps=ps: e.matmul(
                                ps[:, 0:n], lhsT=wd[:, g, j * 128:(j + 1) * 128], rhs=aT[:, g, 0:n],
                                start=(g == 0), stop=(g == 21)), R=[wd.d, aT.d], W=[ps.d])
                        P.op(DVE, lambda e, j=j, ps=ps, xt=xt: e.scalar_tensor_tensor(
                            out=xt[:, j, 1:n + 1], in0=ps[:, 0:n], scalar=modcol(l, 5, j, cond), in1=xt[:, j, 1:n + 1],
                            op0=ALU.mult, op1=ALU.add), R=[ps.d, xt.d, modv[l].d], W=[xt.d])
                    P.dma(ACT, xin[:, off + t0: off + t0 + n].rearrange("(j p) t -> p j t", p=128), xt[:, :, 1:n + 1],
                          R=[xt.d], W=[dxin[si]])
            P.end()
            if debug == 'C':
                break

        if not debug:
            xin, dxin = XT[0], dXT[0]
            P.begin()
            xt_r = Ring([T(P.sb("xt", [128, 8, 512])) for _ in range(2)])
            sq = Ring([T(P.sb("sq", [128, 512])) for _ in range(2)])
            rs = T(P.sb("rs", [128, 512]))
            ytm_r = Ring([T(P.sb("ytm", [128, 1024])) for _ in range(2)])
            ps_r = Ring([TPS(P.psum("psF", [128, 512])) for _ in range(6)])
            onf, _ = PPC['nfg']
            for si, (kind, sidx, Tn, off) in enumerate(SEQS):
                nt = min(512, Tn)
                for t0 in range(0, Tn, nt):
                    n = nt
                    xt = xt_r.next()
                    P.dma(SP, xt[:, :, 0:n], xin[:, off + t0: off + t0 + n].rearrange("(j p) t -> p j t", p=128),
                          R=[dxin[si]], W=[xt.d])
                    ps = ps_r.next()
                    for k in range(8):
                        sqk = sq.next()
                        P.op(ACT, lambda e, k=k, sqk=sqk, xt=xt: e.activation(out=sqk[:, 0:n], in_=xt[:, k, 0:n], func=AF.Square), R=[xt.d], W=[sqk.d])
                        P.op(PE, lambda e, k=k, sqk=sqk, ps=ps: e.matmul(ps[:, 0:n], lhsT=C('ones'), rhs=sqk[:, 0:n], start=(k == 0), stop=(k == 7)),
                             R=[sqk.d, cp.d], W=[ps.d])
                    P.op(ACT, lambda e, ps=ps: e.activation(out=rs[:, 0:n], in_=ps[:, 0:n], func=AF.Sqrt, bias=C('eps'), scale=1.0 / D),
                         R=[ps.d, cp.d], W=[rs.d])
                    P.op(DVE, lambda e: e.reciprocal(out=rs[:, 0:n], in_=rs[:, 0:n]), R=[rs.d], W=[rs.d])
                    for j in range(8):
                        P.op(DVE, lambda e, j=j, xt=xt: e.scalar_tensor_tensor(
                            out=xt[:, j, 0:n], in0=xt[:, j, 0:n], scalar=pp[0][:, onf + j:onf + j + 1], in1=rs[:, 0:n],
                            op0=ALU.mult, op1=ALU.mult), R=[xt.d, pp[0].d, rs.d], W=[xt.d])
                    for b in range(n // 128):
                        ytm = ytm_r.next()
                        for half in range(2):
                            ps = ps_r.next()
                            for jj in range(4):
                                j = half * 4 + jj
                                P.op(PE, lambda e, j=j, jj=jj, ps=ps, xt=xt, b=b: e.transpose(
                                    ps[:, jj * 128:(jj + 1) * 128], xt[:, j, b * 128:(b + 1) * 128], C('ident')),
                                    R=[xt.d, cp.d], W=[ps.d])
                            evac(ytm[:, half * 512:(half + 1) * 512], ps[:, 0:512], [ps.d], [ytm.d])
                        tl = t0 + b * 128
                        if kind == 'p':
                            P.dma(ACT, y_p[sidx * TP + tl: sidx * TP + tl + 128, :], ytm[:, :], R=[ytm.d], W=[dOUT])
                        else:
                            P.dma(ACT, y_s[tl:tl + 128, :], ytm[:, :], R=[ytm.d], W=[dOUT])
            P.end()
        P.begin()
        for k, v in list(P.dtot.items()):
            P.q[POOL].append(([(k, v)], None, None))
        P.end()
    return nc


def make_in_maps(inp):
    fm, tm = build_perm()
    w_in = np.asarray(inp['w_in'], np.float32)
    w_aug = np.concatenate([w_in, np.zeros((L, D, 1), np.float32)], axis=2)
    w_in_fm = np.ascontiguousarray(w_aug[:, :, np.where(fm < 0, w_in.shape[2], fm)])
    w_in_tm = np.ascontiguousarray(w_in[:, :, tm])
    pp = np.stack([pack_params(inp, l) for l in range(L)])
    cp = build_consts()
    rope = rope_tables()
    maps = []
    for c in range(8):
        b = c // 4
        x_tm = np.concatenate([np.asarray(inp['x_prompt'][4 * c:4 * c + 4], np.float32).reshape(NPS * TP, D),
                               np.asarray(inp['x_sample'][b], np.float32)], axis=0)
        cvec = np.concatenate([fm_cols(inp['c_ctx']), fm_cols(inp['c'][b])], axis=1)
        maps.append(dict(
            x_tm=np.ascontiguousarray(x_tm), cvec=np.ascontiguousarray(cvec), w_mod=np.asarray(inp['w_mod'], np.float32),
            w_in_fm=w_in_fm, w_in_tm=w_in_tm, w_out=np.asarray(inp['w_out'], np.float32),
            w_fup=np.asarray(inp['ffn_w_up'], np.float32), w_fdn=np.asarray(inp['ffn_w_down'], np.float32),
            pp=pp, cp=cp, rope=rope,
            st_rwkv=np.ascontiguousarray(inp['state_rwkv'][b]), st_ret=np.ascontiguousarray(inp['state_ret'][b]),
            st_ssd=np.ascontiguousarray(inp['state_ssd'][b]),
            ck=np.ascontiguousarray(np.asarray(inp['cache_diff_k'][b]).reshape(L, 512, 256)),
            cv=np.ascontiguousarray(np.asarray(inp['cache_diff_v'][b]).reshape(L, 512, 256)),
        ))
    return maps


def kernel(**inputs):
    inp = {k: np.asarray(v) for k, v in inputs.items()}
    nc = build()
    maps = make_in_maps(inp)
    res = run_bass_kernel_spmd(nc, maps, core_ids=list(range(8)))
    r = res.results
    y_prompt = np.concatenate([r[c]['y_p'].reshape(NPS, TP, D) for c in range(8)], axis=0)
    y_sample = np.stack([np.concatenate([r[b * 4 + j]['y_s'][j * 1024:(j + 1) * 1024] for j in range(4)], axis=0) for b in range(2)])
    o_rwkv = np.concatenate([r[c]['o_rwkv'] for c in range(8)], axis=0)
    o_ret = np.concatenate([r[c]['o_ret'] for c in range(8)], axis=0)
    o_ssd = np.concatenate([r[c]['o_ssd'] for c in range(8)], axis=0)
    o_k = np.concatenate([r[c]['o_k'] for c in range(8)], axis=0).reshape(32, L, TP, 4, 2, 32)
    o_v = np.concatenate([r[c]['o_v'] for c in range(8)], axis=0).reshape(32, L, TP, 4, 64)
    return (y_prompt.astype(np.float32), y_sample.astype(np.float32), o_rwkv.astype(np.float32), o_ret.astype(np.float32),
            o_ssd.astype(np.float32), o_k.astype(np.float32), o_v.astype(np.float32))
```

```python
import math
import numpy as np
from contextlib import ExitStack
import concourse.bass as bass
import concourse.mybir as mybir
from concourse.bass_utils import run_bass_kernel_spmd

F32 = mybir.dt.float32
BF16 = mybir.dt.bfloat16
ALU = mybir.AluOpType
AF = mybir.ActivationFunctionType
PE, ACT, DVE, POOL, SP = 'pe', 'act', 'dve', 'pool', 'sp'

D = 1024
L = 2
TP = 256
TS = 4096
NPS = 4
TTOT = NPS * TP + TS
SEQS = [('p', i, TP, i * TP) for i in range(NPS)] + [('s', 0, TS, NPS * TP)]
NFM = 32
NTM = 776
FFN = 2816
EPS = 1e-6
MIXERS = ('ret', 'diff', 'ssd', 'rwkv')
RWKV_STAGES = 3
RW_DBG = {'seqs': 5, 'heads': 4, 'stop': 99, 'var': 0}


class Dep:
    __slots__ = ('name', 'lw', 'rd', 'dsem', 'sb', 'ps')

    def __init__(self, name='', sb=False):
        self.name = name
        self.lw = None
        self.rd = {}
        self.dsem = None
        self.sb = sb
        self.ps = False


class Rec:
    def __init__(self):
        self.call = None

    def __getattr__(self, name):
        def f(*a, **kw):
            self.call = (name, a, kw)
        return f


class Prog:
    def __init__(self, nc, es):
        self.nc = nc
        self.es = es
        self.ph = None
        self.q = {PE: [], ACT: [], DVE: [], POOL: [], SP: []}
        self.cnt = {PE: 0, ACT: 0, DVE: 0, POOL: 0}
        self.seen = {k: {} for k in self.q}
        self.sems = {}
        self.ndsem = 0
        self.nwsem = 0
        for e in (PE, ACT, DVE, POOL):
            self.sems[e] = es.enter_context(nc.semaphore('s_' + e))
        self.dtot = {}
        self.nuniq = 0
        self.shared_dsem = None

    def sb(self, name, shape, dt=F32):
        self.nuniq += 1
        return self.ph.enter_context(self.nc.sbuf_tensor('%s_%d' % (name, self.nuniq), list(shape), dt))

    def psum(self, name, shape, dt=F32):
        self.nuniq += 1
        return self.ph.enter_context(self.nc.psum_tensor('%s_%d' % (name, self.nuniq), list(shape), dt))

    def dram(self, name, shape, dt=F32):
        return self.nc.dram_tensor(name, list(shape), dt, kind="Internal").ap()

    def _dsem(self, d, queue=None):
        if d.dsem is None:
            if queue == POOL:
                d.dsem = 'w%d' % self.nwsem
                self.nwsem += 1
            else:
                d.dsem = 'd%d' % self.ndsem
                self.ndsem += 1
            if d.dsem not in self.sems:
                self.sems[d.dsem] = self.es.enter_context(self.nc.semaphore(d.dsem))
        return d.dsem

    def _waits(self, eng, R, W):
        waits = {}

        def add(k, v):
            if k == eng and eng == PE:
                return
            if k in self.dtot:
                v = self.dtot[k]
            if waits.get(k, 0) < v:
                waits[k] = v
        for d in R:
            if d.lw is not None:
                add(*d.lw)
            if d.ps:
                for k, v in d.rd.items():
                    if k != eng:
                        add(k, v)
        for d in W:
            if d.lw is not None and d.lw[0] != eng:
                add(*d.lw)
            for k, v in d.rd.items():
                if k != eng:
                    add(k, v)
        out = []
        seen = self.seen[eng]
        for k, v in waits.items():
            if seen.get(k, 0) < v:
                seen[k] = v
                out.append((k, v))
        return out

    def op(self, eng, fn, R=(), W=()):
        waits = self._waits(eng, R, W)
        idx = self.cnt[eng]
        self.cnt[eng] = idx + 1
        rec = Rec()
        fn(rec)
        name, a, kw = rec.call

        def fn2(e, name=name, a=a, kw=kw):
            return getattr(e, name)(*a, **kw)
        self.q[eng].append((waits, fn2, (eng, 1)))
        for d in R:
            if d.rd.get(eng, 0) < idx + 1:
                d.rd[eng] = idx + 1
        for d in W:
            d.lw = (eng, idx + 1)
            d.rd = {}

    def dma(self, queue, out, in_, R=(), W=(), **kw):
        waits = self._waits(queue, R, W)
        tgt = (list(W) + list(R))
        d0 = None
        for d in tgt:
            if d.sb:
                d0 = d
                break
        if d0 is None:
            d0 = tgt[0]
        k0 = self._dsem(d0, queue)
        self.dtot[k0] = self.dtot.get(k0, 0) + 16
        tot = self.dtot[k0]

        def fn(e, out=out, in_=in_, kw=kw):
            return e.dma_start(out=out, in_=in_, **kw)
        self.q[queue].append((waits, fn, (k0, 16)))
        for d in W:
            d.lw = (k0, tot)
            d.rd = {}
        for d in R:
            if d.rd.get(k0, 0) < tot:
                d.rd[k0] = tot

    def barrier(self):
        allk = [(k, self.cnt[k]) for k in (PE, ACT, DVE, POOL)] + list(self.dtot.items())
        for eng in self.q:
            waits = []
            seen = self.seen[eng]
            for k, v in allk:
                if k == eng:
                    continue
                if seen.get(k, 0) < v:
                    seen[k] = v
                    waits.append((k, v))
            self.q[eng].append((waits, None, None))

    def begin(self):
        self.ph = ExitStack()
        self.ndsem = 8 if self.ndsem >= 8 else self.ndsem
        self.nwsem = 0

    def end(self, final=False):
        self.barrier()
        nc = self.nc
        q = self.q
        sems = self.sems

        with nc.Block() as block:
            def run(e, name):
                for waits, fn, inc in q[name]:
                    for k, v in waits:
                        e.wait_ge(sems[k], v)
                    if fn is not None:
                        ins = fn(e)
                        ins.then_inc(sems[inc[0]], inc[1])

            @block.tensor
            def _(e):
                run(e, PE)

            @block.scalar
            def _(e):
                run(e, ACT)

            @block.vector
            def _(e):
                run(e, DVE)

            @block.gpsimd
            def _(e):
                run(e, POOL)

            @block.sync
            def _(e):
                run(e, SP)
        for k in q:
            q[k] = []
        self.ph.close()
        self.ph = None


class T:
    def __init__(self, t, name=''):
        self.t = t
        self.d = Dep(name, sb=True)

    def __getitem__(self, idx):
        return self.t[idx]


def TPS(t):
    x = T(t)
    x.d.ps = True
    x.d.sb = False
    return x


class Ring:
    def __init__(self, tiles):
        self.tiles = tiles
        self.i = 0

    def next(self):
        t = self.tiles[self.i % len(self.tiles)]
        self.i += 1
        return t


def fm_cols(v):
    v = np.asarray(v, np.float32)
    return np.ascontiguousarray(v.reshape(-1, 128).T)


def build_perm():
    RW, RET, SSD, DIF = 0, 960, 1984, 2760
    fm = []
    fm += list(range(RW, RW + 768))
    fm += list(range(RW + 768, RW + 896))
    fm += list(range(RW + 896, RW + 960)) + [-1] * 64
    q = list(range(RET, RET + 256))
    k = list(range(RET + 256, RET + 512))

    def swap(cols, blk):
        out = []
        for i in range(0, len(cols), blk):
            b = cols[i:i + blk]
            out += b[blk // 2:] + b[:blk // 2]
        return out
    fm += q + k + swap(q, 64) + swap(k, 64)
    fm += list(range(RET + 768, RET + 1024))
    fm += list(range(SSD, SSD + 256))
    fm += list(range(SSD + 256, SSD + 768))
    dq = list(range(DIF, DIF + 256))
    dk = list(range(DIF + 256, DIF + 512))
    fm += dq + dk + swap(dq, 32) + swap(dk, 32)
    assert len(fm) == NFM * 128
    tm = list(range(RET + 512, RET + 768)) + list(range(DIF + 512, DIF + 768)) + dk + list(range(SSD + 768, SSD + 776))
    assert len(tm) == NTM
    return np.array(fm), np.array(tm)


PPC = {}
_o = 0
for _n, _w in [('n1g', 8), ('n2g', 8), ('bmod', 48), ('mu', 8), ('w0', 4), ('a0', 4), ('k_k', 2), ('k_a', 2),
               ('r_k', 2), ('ln_g', 2), ('ln_b', 2), ('ret_g', 2), ('scw', 12), ('scb', 4), ('ssd_g', 2), ('ssd_d', 2),
               ('dtb', 8), ('alog', 8), ('subg', 1), ('lp', 128), ('fcw', 132), ('fcb', 44), ('w_up', 256),
               ('a_up', 256), ('g_up', 256), ('nfg', 8), ('ret_g4', 4), ('scw64', 24), ('scb64', 8), ('ssd_d4', 4), ('ssd_g4', 4), ('mu64', 16), ('w0_64', 8), ('a0_64', 8), ('kk64', 4), ('ka64', 4), ('rk64', 4), ('lng64', 4), ('lnb64', 4)]:
    PPC[_n] = (_o, _w)
    _o += _w
NPP = _o


def pack_params(inp, l):
    pp = np.zeros((128, NPP), np.float32)

    def put(name, arr):
        o, w = PPC[name]
        arr = np.asarray(arr, np.float32)
        assert arr.shape[1] == w, (name, arr.shape, w)
        pp[:arr.shape[0], o:o + w] = arr
    put('n1g', fm_cols(inp['norm1_g'][l]))
    put('n2g', fm_cols(inp['norm2_g'][l]))
    put('bmod', fm_cols(inp['b_mod'][l]))
    mu = np.zeros(1024, np.float32)
    mu[:960] = inp['rwkv_mu'][l]
    put('mu', fm_cols(mu))
    put('w0', fm_cols(inp['rwkv_w0'][l].reshape(-1)))
    put('a0', fm_cols(inp['rwkv_a0'][l].reshape(-1)))
    put('k_k', fm_cols(inp['rwkv_k_k'][l]))
    put('k_a', fm_cols(inp['rwkv_k_a'][l]))
    put('r_k', fm_cols(inp['rwkv_r_k'][l].reshape(-1)))
    put('ln_g', fm_cols(inp['rwkv_ln_g'][l]))
    put('ln_b', fm_cols(inp['rwkv_ln_b'][l]))
    put('ret_g', fm_cols(inp['ret_ln_g'][l]))
    cw = inp['ssd_conv_w'][l]
    put('scw', np.concatenate([fm_cols(cw[i]) for i in range(3)], axis=1))
    put('scb', fm_cols(inp['ssd_conv_b'][l]))
    put('ssd_g', fm_cols(inp['ssd_norm_g'][l]))
    put('ssd_d', fm_cols(np.repeat(inp['ssd_d'][l], 64)))
    put('dtb', np.broadcast_to(inp['ssd_dt_bias'][l].reshape(1, 8), (128, 8)))
    put('alog', np.broadcast_to(inp['ssd_a_log'][l].reshape(1, 8), (128, 8)))
    put('subg', np.concatenate([inp['diff_subln_g'][l], inp['diff_subln_g'][l]]).reshape(128, 1))
    put('lp', np.broadcast_to(inp['diff_lambda'][l].reshape(1, 128), (128, 128)))
    fw = inp['ffn_conv_w'][l]
    put('fcw', np.concatenate([fm_cols(fw[i]) for i in range(3)], axis=1))
    put('fcb', fm_cols(inp['ffn_conv_b'][l]))
    put('w_up', inp['rwkv_w_up'][l].reshape(64, 256))
    put('a_up', inp['rwkv_a_up'][l].reshape(64, 256))
    put('g_up', inp['rwkv_g_up'][l].reshape(64, 256))
    put('nfg', fm_cols(inp['norm_f_g']))
    put('ret_g4', inp['ret_ln_g'][l].reshape(4, 64).T)
    put('scw64', np.concatenate([cw[i].reshape(8, 64).T for i in range(3)], axis=1))
    put('scb64', inp['ssd_conv_b'][l].reshape(8, 64).T)
    put('ssd_d4', np.broadcast_to(inp['ssd_d'][l].reshape(1, 4), (64, 4)))
    put('ssd_g4', inp['ssd_norm_g'][l].reshape(4, 64).T)
    put('mu64', mu.reshape(16, 64).T)
    put('w0_64', inp['rwkv_w0'][l].reshape(8, 64).T)
    put('a0_64', inp['rwkv_a0'][l].reshape(8, 64).T)
    put('kk64', inp['rwkv_k_k'][l].reshape(4, 64).T)
    put('ka64', inp['rwkv_k_a'][l].reshape(4, 64).T)
    put('rk64', inp['rwkv_r_k'][l].reshape(4, 64).T)
    put('lng64', inp['rwkv_ln_g'][l].reshape(4, 64).T)
    put('lnb64', inp['rwkv_ln_b'][l].reshape(4, 64).T)
    return pp


CPC = {}
_o = 0
for _n, _w in [('ident', 128), ('ones', 128), ('bd64', 128), ('tri_f', 128), ('tri_b', 128), ('nm_f', 128), ('nm_b', 128),
               ('ret_ds', 512), ('ret_df', 512), ('ret_db', 512), ('ret_kwf', 4), ('ret_kwb', 4), ('ret_g128', 8),
               ('eps', 1), ('rw_ms', 64), ('rw_mi', 64), ('rw_msT', 64), ('rw_miT', 64), ('scanmask', 512), ('rw_mask5', 320)]:
    CPC[_n] = (_o, _w)
    _o += _w
NCP = _o


def build_consts():
    cp = np.zeros((128, NCP), np.float32)

    def put(name, arr):
        o, w = CPC[name]
        arr = np.asarray(arr, np.float32)
        assert arr.shape[1] == w
        cp[:arr.shape[0], o:o + w] = arr
    i = np.arange(128)
    put('ident', np.eye(128))
    put('ones', np.ones((128, 128)))
    bd = np.zeros((128, 128))
    bd[:64, :64] = 1
    bd[64:, 64:] = 1
    put('bd64', bd)
    put('tri_f', (i[:, None] <= i[None, :]).astype(np.float32))
    put('tri_b', (i[:, None] >= i[None, :]).astype(np.float32))
    put('nm_f', np.where(i[None, :] >= i[:, None], 0.0, -30000.0))
    put('nm_b', np.where(i[None, :] <= i[:, None], 0.0, -30000.0))
    ds = np.zeros((128, 512))
    df = np.zeros((128, 512))
    db = np.zeros((128, 512))
    kwf = np.zeros((128, 4))
    kwb = np.zeros((128, 4))
    g128 = np.zeros((128, 8))
    for h in range(4):
        lf = math.log(1.0 - 2.0 ** (-5.0 - h))
        lb = math.log(1.0 - 2.0 ** (-5.5 - h))
        jj, ii = i[:, None], i[None, :]
        m = np.where(ii >= jj, np.exp(lf * (ii - jj)), 0.0) + np.where(ii <= jj, np.exp(lb * (jj - ii)), 0.0)
        ds[:, h * 128:(h + 1) * 128] = m
        df[:, h * 128:(h + 1) * 128] = np.exp(lf * (i + 1))[None, :]
        db[:, h * 128:(h + 1) * 128] = np.exp(lb * (128 - i))[None, :]
        kwf[:, h] = np.exp(lf * (127 - i))
        kwb[:, h] = np.exp(lb * i)
        g128[:, h] = math.exp(lf * 128)
        g128[:, 4 + h] = math.exp(lb * 128)
    put('ret_ds', ds)
    put('ret_df', df)
    put('ret_db', db)
    put('ret_kwf', kwf)
    put('ret_kwb', kwb)
    put('ret_g128', g128)
    put('eps', np.full((128, 1), EPS))
    c = np.arange(64)
    put('rw_ms', (c[None, :] < c[:, None]).astype(np.float32))
    put('rw_mi', (c[None, :] <= c[:, None]).astype(np.float32))
    put('rw_msT', (c[:, None] < c[None, :]).astype(np.float32))
    put('rw_miT', (c[:, None] <= c[None, :]).astype(np.float32))
    sm = np.ones((128, 512))
    sm[:, ::64] = 0.0
    put('scanmask', sm)
    msT = (c[:, None] < c[None, :]).astype(np.float32)
    miT = (c[:, None] <= c[None, :]).astype(np.float32)
    ms = (c[None, :] < c[:, None]).astype(np.float32)
    put('rw_mask5', np.concatenate([msT, miT, msT, miT, ms], axis=1))
    return cp


def rope_tables():
    rows = TS // 64

    def tabs(d):
        nf = d // 4
        row = np.repeat(np.arange(rows, dtype=np.float32), 64)
        col = np.tile(np.arange(64, dtype=np.float32), rows)
        freqs = (np.float32(10000.0) ** (-np.arange(nf, dtype=np.float32) / nf)).astype(np.float32)
        ang = np.concatenate([row[:, None] * freqs, col[:, None] * freqs], axis=-1).astype(np.float32)
        return np.cos(ang).T.astype(np.float32), np.sin(ang).T.astype(np.float32)
    c, s = tabs(64)
    ret_c = np.concatenate([c, c, c, c], 0)
    ret_s = np.concatenate([-s, s, -s, s], 0)
    c, s = tabs(32)
    dif_c = np.concatenate([c, c] * 4, 0)
    dif_s = np.concatenate([-s, s] * 4, 0)
    return np.stack([ret_c, ret_s, dif_c, dif_s]).astype(np.float32)


def build(debug=False):
    nc = bass.Bass("TRN2", target_bir_lowering=False)

    def din(name, shape):
        return nc.dram_tensor(name, list(shape), F32, kind="ExternalInput").ap()

    def dout(name, shape):
        return nc.dram_tensor(name, list(shape), F32, kind="ExternalOutput").ap()
    x_tm = din("x_tm", [TTOT, D])
    cvec = din("cvec", [128, 16])
    w_mod = din("w_mod", [L, D, 6 * D])
    w_in_fm = din("w_in_fm", [L, D, NFM * 128])
    w_in_tm = din("w_in_tm", [L, D, NTM])
    w_out = din("w_out", [L, D, D])
    w_fup = din("w_fup", [L, D, 2 * FFN])
    w_fdn = din("w_fdn", [L, FFN, D])
    pp_in = din("pp", [L, 128, NPP])
    cp_in = din("cp", [128, NCP])
    rope_in = din("rope", [4, 128, TS])
    st_rwkv = din("st_rwkv", [L, 2, 4, 64, 64])
    st_ret = din("st_ret", [L, 2, 4, 64, 64])
    st_ssd = din("st_ssd", [L, 2, 4, 64, 64])
    ck_in = din("ck", [L, 512, 256])
    cv_in = din("cv", [L, 512, 256])

    y_p = dout("y_p", [NPS * TP, D])
    y_s = dout("y_s", [TS, D])
    o_rwkv = dout("o_rwkv", [NPS, L, 2, 4, 64, 64])
    o_ret = dout("o_ret", [NPS, L, 2, 4, 64, 64])
    o_ssd = dout("o_ssd", [NPS, L, 2, 4, 64, 64])
    o_k = dout("o_k", [NPS, L, TP, 256])
    o_v = dout("o_v", [NPS, L, TP, 256])

    es = ExitStack()
    with es:
        P = Prog(nc, es)
        mk = (lambda n, s: dout(n, s)) if (debug and debug != 'B') else (lambda n, s: P.dram(n, s))
        XT = [mk("XT0", [D, TTOT]), mk("XT1", [D, TTOT])]
        FT = mk("FT", [NFM * 128, TTOT])
        FTM = mk("FTM", [TTOT, NTM])
        OTDT = F32 if debug == 'B' else BF16
        if debug == 'B':
            OT = nc.dram_tensor("OT_dbg", [D, TTOT], F32, kind="ExternalOutput").ap()
        elif debug and debug[0] == 'C':
            OT = nc.dram_tensor("OT_in", [D, TTOT], F32, kind="ExternalInput").ap()
        else:
            OT = nc.dram_tensor("OT", [D, TTOT], BF16, kind="Internal").ap()
        dXT = [[Dep() for _ in SEQS] for _ in range(2)]
        dFT = [Dep() for _ in SEQS]
        dFTM = [Dep() for _ in SEQS]
        dOT = [Dep() for _ in SEQS]
        dOUT = Dep('outs')
        RW = P.dram("RW", [11 * 256, TTOT])
        YS = P.dram("YS", [2 * 256, TTOT])
        dRW = [Dep() for _ in SEQS]
        dYS = [Dep() for _ in SEQS]

        P.ph = es
        cp = T(P.sb("cp", [128, NCP]))
        pp = [T(P.sb("pp%d" % l, [128, NPP])) for l in range(L)]
        modv = [T(P.sb("modv%d" % l, [128, 48, 2])) for l in range(L)]
        modA = [T(P.sb("modA%d" % l, [128, 2, 8, 2])) for l in range(L)]
        cpb = T(P.sb("cpb", [128, 640], BF16))
        P.ph = None

        def C(name, rows=128, sub=None):
            o, w = CPC[name]
            if sub is not None:
                return cp[0:rows, o + sub[0]:o + sub[1]]
            return cp[0:rows, o:o + w]

        def PPv(l, name, col=0, rows=128, ncol=1):
            o, w = PPC[name]
            return pp[l][0:rows, o + col:o + col + ncol]

        P.begin()
        P.dma(SP, cp[:, :], cp_in[:, :], W=[cp.d])
        for l in range(L):
            P.dma(SP, pp[l][:, :], pp_in[l], W=[pp[l].d])
        P.op(DVE, lambda e: e.tensor_copy(out=cpb[:, 0:384], in_=cp[:, 0:384]), R=[cp.d], W=[cpb.d])
        cv = T(P.sb("cv", [128, 16]))
        scv = T(P.sb("scv", [128, 8, 2]))
        P.dma(SP, cv[:, :], cvec[:, :], W=[cv.d])
        P.op(ACT, lambda e: e.activation(out=scv[:, :, 0], in_=cv[:, 0:8], func=AF.Silu), R=[cv.d], W=[scv.d])
        P.op(ACT, lambda e: e.activation(out=scv[:, :, 1], in_=cv[:, 8:16], func=AF.Silu), R=[cv.d], W=[scv.d])
        wm = Ring([T(P.sb("wm", [128, 8, 512])) for _ in range(2)])
        pm = TPS(P.psum("pm", [128, 512]))
        for l in range(L):
            for g in range(12):
                w = wm.next()
                P.dma(SP, w[:, :, :], w_mod[l, :, g * 512:(g + 1) * 512].rearrange("(k p) n -> p k n", p=128), W=[w.d])
                for cc in range(4):
                    ch = g * 4 + cc
                    for k in range(8):
                        P.op(PE, lambda e, w=w, cc=cc, k=k, ch=ch: e.matmul(
                            pm[:, ch * 2:ch * 2 + 2], lhsT=w[:, k, cc * 128:(cc + 1) * 128], rhs=scv[:, k, :],
                            start=(k == 0), stop=(k == 7)), R=[w.d, scv.d], W=[pm.d])
            o, _ = PPC['bmod']
            P.op(DVE, lambda e, l=l, o=o: e.tensor_tensor(
                out=modv[l][:, :, :], in0=pm[:, 0:96].rearrange("p (c t) -> p c t", t=2),
                in1=pp[l][:, o:o + 48].unsqueeze(2).to_broadcast([128, 48, 2]), op=ALU.add),
                R=[pm.d, pp[l].d], W=[modv[l].d])
            for n, (gname, which) in enumerate([('n1g', 1), ('n2g', 4)]):
                og, _ = PPC[gname]
                P.op(DVE, lambda e, l=l, n=n, og=og, which=which: e.scalar_tensor_tensor(
                    out=modA[l][:, n, :, :], in0=modv[l][:, which * 8:(which + 1) * 8, :], scalar=1.0,
                    in1=pp[l][:, og:og + 8].unsqueeze(2).to_broadcast([128, 8, 2]), op0=ALU.add, op1=ALU.mult),
                    R=[modv[l].d, pp[l].d], W=[modA[l].d])
        P.end()

        def modcol(l, which, j, cond):
            return modv[l][:, which * 8 + j, cond:cond + 1]

        def rmsnorm_tile(xt, n, l, which_norm, cond, hT, sq, ps_ring, rs, tmp_ring):
            ps = ps_ring.next()
            for k in range(8):
                sqk = sq.next()
                P.op(ACT, lambda e, k=k, sqk=sqk: e.activation(out=sqk[:, 0:n], in_=xt[:, k, 0:n], func=AF.Square), R=[xt.d], W=[sqk.d])
                P.op(PE, lambda e, k=k, sqk=sqk: e.matmul(ps[:, 0:n], lhsT=C('ones'), rhs=sqk[:, 0:n], start=(k == 0), stop=(k == 7)),
                     R=[sqk.d, cp.d], W=[ps.d])
            P.op(ACT, lambda e: e.activation(out=rs[:, 0:n], in_=ps[:, 0:n], func=AF.Sqrt, bias=C('eps'), scale=1.0 / D),
                 R=[ps.d, cp.d], W=[rs.d])
            P.op(DVE, lambda e: e.reciprocal(out=rs[:, 0:n], in_=rs[:, 0:n]), R=[rs.d], W=[rs.d])
            shift_which = 0 if which_norm == 0 else 3
            for j in range(8):
                tmp = tmp_ring.next()
                P.op(DVE, lambda e, j=j, tmp=tmp: e.scalar_tensor_tensor(
                    out=tmp[:, 0:n], in0=xt[:, j, 0:n], scalar=modA[l][:, which_norm, j, cond:cond + 1], in1=rs[:, 0:n],
                    op0=ALU.mult, op1=ALU.mult), R=[xt.d, modA[l].d, rs.d], W=[tmp.d])
                P.op(ACT, lambda e, j=j, tmp=tmp: e.activation(
                    out=hT[:, j, 0:n], in_=tmp[:, 0:n], func=AF.Identity, bias=modcol(l, shift_which, j, cond), scale=1.0),
                    R=[tmp.d, modv[l].d], W=[hT.d])

        evac_flip = [0]

        def evac(out_ap, in_ap, R, W):
            evac_flip[0] ^= 1
            if evac_flip[0]:
                P.op(ACT, lambda e: e.activation(out=out_ap, in_=in_ap, func=AF.Copy), R=R, W=W)
            else:
                P.op(DVE, lambda e: e.tensor_copy(out=out_ap, in_=in_ap), R=R, W=W)

        def phase_B(l):
            P.begin()
            zt = T(P.sb("zt", [128, 2048], OTDT))
            P.op(DVE, lambda e: e.memset(zt[:, :], 0.0), W=[zt.d])
            for si, (kind, sidx, Tn, off) in enumerate(SEQS):
                for r0 in ([] if 'rwkv' in MIXERS else [0, 128]) + ([] if 'ssd' in MIXERS else [512, 640]):
                    for t0 in range(0, Tn, 2048):
                        n = min(2048, Tn - t0)
                        P.dma(ACT, OT[r0:r0 + 128, off + t0:off + t0 + n], zt[:, 0:n], R=[zt.d], W=[dOT[si]])
            P.end()
            for name, fn in (('ret', mixer_ret), ('diff', mixer_diff), ('ssd', mixer_ssd), ('rwkv', mixer_rwkv)):
                if name in MIXERS:
                    P.begin()
                    fn(l)
                    P.end()

        def mixer_diff(l):
            lam_init = 0.8 - 0.6 * math.exp(-0.3 * l)
            AX = mybir.AxisListType.X
            ropc = T(P.sb("ropc", [64, TS]))
            rops = T(P.sb("rops", [64, TS]))
            P.dma(SP, ropc[:, :], rope_in[2, 0:64, :], W=[ropc.d])
            P.dma(SP, rops[:, :], rope_in[3, 0:64, :], W=[rops.d])
            qf32 = T(P.sb("qf32", [64, TS]))
            qs32 = T(P.sb("qs32", [64, TS]))
            qb = T(P.sb("qb", [64, TS], BF16))
            NKMAX = TS // 128 + 4
            kall = T(P.sb("kall", [64, NKMAX * 128], BF16))
            v32 = T(P.sb("v32", [128, NKMAX, 64]))
            vall = T(P.sb("vall", [128, NKMAX, 64], BF16))
            ck32 = T(P.sb("ck32", [128, 4, 64]))
            E_r = Ring([T(P.sb("E", [128, 512], BF16)) for _ in range(3)])
            rd = [T(P.sb("rd%d" % i, [64, 512])) for i in range(2)]
            o0 = T(P.sb("o0", [64, 512]))
            o1 = T(P.sb("o1", [64, 512]))
            sqy = T(P.sb("sqy", [64, 512]))
            rsy = T(P.sb("rsy", [64, 512]))
            oy_r = Ring([T(P.sb("oy", [64, 512], OTDT)) for _ in range(2)])
            lamt = T(P.sb("lamt", [128, 40]))
            ps_sc = Ring([TPS(P.psum("ps_sc", [128, 512])) for _ in range(2)])
            ps_den = [TPS(P.psum("ps_den%d" % i, [128, 512])) for i in range(2)]
            ps_num = [TPS(P.psum("ps_num%d" % i, [128, 512])) for i in range(2)]
            ps_x = Ring([TPS(P.psum("ps_x", [128, 512])) for _ in range(2)])
            olp, _ = PPC['lp']
            osg, _ = PPC['subg']
            for i in range(2):
                P.op(DVE, lambda e, i=i: e.tensor_tensor(out=lamt[:, 0:32], in0=pp[l][:, olp + 64 * i:olp + 64 * i + 32],
                                                         in1=pp[l][:, olp + 64 * i + 32:olp + 64 * i + 64], op=ALU.mult),
                     R=[pp[l].d], W=[lamt.d])
                P.op(DVE, lambda e, i=i: e.tensor_reduce(out=lamt[:, 32 + i:33 + i], in_=lamt[:, 0:32], axis=AX, op=ALU.add),
                     R=[lamt.d], W=[lamt.d])
            P.op(ACT, lambda e: e.activation(out=lamt[:, 34:36], in_=lamt[:, 32:34], func=AF.Exp), R=[lamt.d], W=[lamt.d])
            P.op(DVE, lambda e: e.scalar_tensor_tensor(out=lamt[:, 36:37], in0=lamt[:, 35:36], scalar=-lam_init, in1=lamt[:, 34:35],
                                                       op0=ALU.add, op1=ALU.subtract), R=[lamt.d], W=[lamt.d])
            scale = 32.0 ** -0.5
            for si, (kind, sidx, Tn, off) in enumerate(SEQS):
                nch = Tn // 128
                nk = nch + (4 if kind == 's' else 0)
                for h in range(4):
                    pr = (h % 2) * 64

                    def rows(cbase):
                        r0 = (cbase + h // 2) * 128 + pr
                        return FT[r0:r0 + 64, off:off + Tn]
                    for (dst, cb, cbs) in ((qb, 24, 28), (kall, 26, 30)):
                        P.dma(SP, qf32[:, 0:Tn], rows(cb), R=[dFT[si]], W=[qf32.d])
                        if kind == 's':
                            P.dma(SP, qs32[:, 0:Tn], rows(cbs), R=[dFT[si]], W=[qs32.d])
                            P.op(DVE, lambda e: e.tensor_tensor(out=qf32[:, 0:Tn], in0=qf32[:, 0:Tn], in1=ropc[:, 0:Tn], op=ALU.mult),
                                 R=[qf32.d, ropc.d], W=[qf32.d])
                            P.op(DVE, lambda e: e.tensor_tensor(out=qs32[:, 0:Tn], in0=qs32[:, 0:Tn], in1=rops[:, 0:Tn], op=ALU.mult),
                                 R=[qs32.d, rops.d], W=[qs32.d])
                            P.op(DVE, lambda e: e.tensor_tensor(out=qf32[:, 0:Tn], in0=qf32[:, 0:Tn], in1=qs32[:, 0:Tn], op=ALU.add),
                                 R=[qf32.d, qs32.d], W=[qf32.d])
                        P.op(ACT, lambda e, dst=dst: e.activation(out=dst[:, 0:Tn], in_=qf32[:, 0:Tn], func=AF.Copy),
                             R=[qf32.d], W=[dst.d])
                    P.dma(SP, v32[:, 0:nch, :], FTM[off:off + Tn, 256 + h * 64:256 + (h + 1) * 64].rearrange("(c p) e -> p c e", p=128),
                          R=[dFTM[si]], W=[v32.d])
                    if kind == 's':
                        P.dma(SP, v32[:, nch:nch + 4, :], cv_in[l, :, h * 64:(h + 1) * 64].rearrange("(c p) e -> p c e", p=128), W=[v32.d])
                        P.dma(SP, ck32[:, :, :], ck_in[l, :, h * 64:(h + 1) * 64].rearrange("(c p) e -> p c e", p=128), W=[ck32.d])
                        px = ps_x.next()
                        for c in range(4):
                            P.op(PE, lambda e, c=c: e.transpose(px[0:64, c * 128:(c + 1) * 128], ck32[:, c, :], C('ident')),
                                 R=[ck32.d, cp.d], W=[px.d])
                        P.op(ACT, lambda e: e.activation(out=kall[:, Tn:Tn + 512], in_=px[0:64, 0:512], func=AF.Copy), R=[px.d], W=[kall.d])
                    P.op(DVE, lambda e: e.tensor_copy(out=vall[:, 0:nk, :], in_=v32[:, 0:nk, :]), R=[v32.d], W=[vall.d])
                    nq = min(512, Tn)
                    for q0 in range(0, Tn, nq):
                        n = nq
                        qsl = slice(q0, q0 + n)
                        for kc in range(nk):
                            ksl = slice(kc * 128, (kc + 1) * 128)
                            for m in range(2):
                                psl = slice(m * 32, (m + 1) * 32)
                                psc = ps_sc.next()
                                P.op(PE, lambda e: e.matmul(psc[:, 0:n], lhsT=kall[psl, ksl], rhs=qb[psl, qsl], start=True, stop=True),
                                     R=[kall.d, qb.d], W=[psc.d])
                                E = E_r.next()
                                P.op(ACT, lambda e: e.activation(out=E[:, 0:n], in_=psc[:, 0:n], func=AF.Exp, scale=scale), R=[psc.d], W=[E.d])
                                P.op(PE, lambda e: e.matmul(ps_den[m][0:64, 0:n], lhsT=cpb[:, 128:192], rhs=E[:, 0:n], start=(kc == 0), stop=(kc == nk - 1)),
                                     R=[cpb.d, E.d], W=[ps_den[m].d])
                                P.op(PE, lambda e: e.matmul(ps_num[m][0:64, 0:n], lhsT=vall[:, kc, :], rhs=E[:, 0:n], start=(kc == 0), stop=(kc == nk - 1)),
                                     R=[vall.d, E.d], W=[ps_num[m].d])
                        for m in range(2):
                            P.op(DVE, lambda e, m=m: e.reciprocal(out=rd[m][:, 0:n], in_=ps_den[m][0:64, 0:n]), R=[ps_den[m].d], W=[rd[m].d])
                        P.op(DVE, lambda e: e.tensor_tensor(out=o0[:, 0:n], in0=ps_num[0][0:64, 0:n], in1=rd[0][:, 0:n], op=ALU.mult),
                             R=[ps_num[0].d, rd[0].d], W=[o0.d])
                        P.op(DVE, lambda e: e.tensor_tensor(out=o1[:, 0:n], in0=ps_num[1][0:64, 0:n], in1=rd[1][:, 0:n], op=ALU.mult),
                             R=[ps_num[1].d, rd[1].d], W=[o1.d])
                        P.op(DVE, lambda e: e.scalar_tensor_tensor(out=o0[:, 0:n], in0=o1[:, 0:n], scalar=lamt[0:64, 36:37], in1=o0[:, 0:n],
                                                                   op0=ALU.mult, op1=ALU.add), R=[o0.d, o1.d, lamt.d], W=[o0.d])
                        P.op(ACT, lambda e: e.activation(out=sqy[:, 0:n], in_=o0[:, 0:n], func=AF.Square), R=[o0.d], W=[sqy.d])
                        pss = ps_x.next()
                        P.op(PE, lambda e: e.matmul(pss[0:64, 0:n], lhsT=C('ones', rows=64, sub=(0, 64)), rhs=sqy[:, 0:n], start=True, stop=True),
                             R=[sqy.d, cp.d], W=[pss.d])
                        P.op(ACT, lambda e: e.activation(out=rsy[:, 0:n], in_=pss[0:64, 0:n], func=AF.Sqrt, bias=C('eps', rows=64), scale=1.0 / 64),
                             R=[pss.d, cp.d], W=[rsy.d])
                        P.op(DVE, lambda e: e.reciprocal(out=rsy[:, 0:n], in_=rsy[:, 0:n]), R=[rsy.d], W=[rsy.d])
                        P.op(DVE, lambda e: e.scalar_tensor_tensor(out=o1[:, 0:n], in0=o0[:, 0:n], scalar=pp[l][0:64, osg:osg + 1],
                                                                   in1=rsy[:, 0:n], op0=ALU.mult, op1=ALU.mult),
                             R=[o0.d, pp[l].d, rsy.d], W=[o1.d])
                        oy = oy_r.next()
                        P.op(ACT, lambda e: e.activation(out=oy[:, 0:n], in_=o1[:, 0:n], func=AF.Copy, scale=1.0 - lam_init), R=[o1.d], W=[oy.d])
                        P.dma(ACT, OT[768 + h * 64:768 + (h + 1) * 64, off + q0: off + q0 + n], oy[:, 0:n], R=[oy.d], W=[dOT[si]])

        def mixer_rwkv(l):
            rwkv_pre(l)
            if RWKV_STAGES >= 2:
                P.end()
                P.begin()
                rwkv_scan(l)
            if RWKV_STAGES >= 3:
                P.end()
                P.begin()
                rwkv_post(l)

        def rwkv_pre(l):
            buf_r = Ring([T(P.sb("rbuf", [64, 514])) for _ in range(3)])
            s1 = T(P.sb("rs1", [64, 512]))
            sh = [T(P.sb("rsh%d" % i, [64, 512])) for i in range(15)]
            twd = T(P.sb("twd", [64, 512]))
            sgd = T(P.sb("sgd", [64, 512]))
            a_d = [T(P.sb("a_d%d" % d, [64, 512])) for d in range(2)]
            o_r = Ring([T(P.sb("rwo", [64, 512])) for _ in range(4)])
            kkr = T(P.sb("kkr", [64, 512]))
            kk = T(P.sb("kk", [64, 512]))
            t1 = T(P.sb("rt1", [64, 512]))
            t2 = T(P.sb("rt2", [64, 512]))
            ps_r = Ring([TPS(P.psum("ps_rp", [128, 512])) for _ in range(4)])
            omu, _ = PPC['mu64']
            ow0, _ = PPC['w0_64']
            oa0, _ = PPC['a0_64']
            okk, _ = PPC['kk64']
            oka, _ = PPC['ka64']
            ork, _ = PPC['rk64']
            owu, _ = PPC['w_up']
            oau, _ = PPC['a_up']
            ogu, _ = PPC['g_up']
            for si, (kind, sidx, Tn, off) in enumerate(SEQS):
                nq = min(512, Tn)
                for q0 in range(0, Tn, nq):
                    n = nq
                    lo, hi = max(q0 - 1, 0), min(q0 + n + 1, Tn)
                    c_lo = lo - (q0 - 1)
                    a0 = 1 if q0 == 0 else 0
                    b1 = n - 1 if q0 + n == Tn else n

                    def store(arr, h, src):
                        r0 = arr * 256 + h * 64
                        P.dma(ACT, RW[r0:r0 + 64, off + q0: off + q0 + n], src[:, 0:n], R=[src.d], W=[dRW[si]])
                    for hc in range(15):
                        buf = buf_r.next()
                        P.dma(SP, buf[:, c_lo:c_lo + hi - lo], FT[hc * 64:(hc + 1) * 64, off + lo: off + hi], R=[dFT[si]], W=[buf.d])
                        P.op(DVE, lambda e: e.tensor_tensor(out=s1[:, a0:b1], in0=buf[:, a0:b1], in1=buf[:, a0 + 2:b1 + 2], op=ALU.add),
                             R=[buf.d], W=[s1.d])
                        if a0:
                            P.op(DVE, lambda e: e.tensor_copy(out=s1[:, 0:1], in_=buf[:, 2:3]), R=[buf.d], W=[s1.d])
                        if b1 < n:
                            P.op(DVE, lambda e: e.tensor_copy(out=s1[:, n - 1:n], in_=buf[:, n - 1:n]), R=[buf.d], W=[s1.d])
                        P.op(DVE, lambda e: e.scalar_tensor_tensor(out=s1[:, 0:n], in0=s1[:, 0:n], scalar=0.5, in1=buf[:, 1:n + 1],
                                                                   op0=ALU.mult, op1=ALU.subtract), R=[s1.d, buf.d], W=[s1.d])
                        P.op(DVE, lambda e: e.scalar_tensor_tensor(out=sh[hc][:, 0:n], in0=s1[:, 0:n], scalar=pp[l][0:64, omu + hc:omu + hc + 1],
                                                                   in1=buf[:, 1:n + 1], op0=ALU.mult, op1=ALU.add),
                             R=[s1.d, buf.d, pp[l].d], W=[sh[hc].d])
                    P.op(ACT, lambda e: e.activation(out=twd[:, 0:n], in_=sh[12][:, 0:n], func=AF.Tanh), R=[sh[12].d], W=[twd.d])
                    P.op(ACT, lambda e: e.activation(out=sgd[:, 0:n], in_=sh[14][:, 0:n], func=AF.Sigmoid), R=[sh[14].d], W=[sgd.d])
                    sad = sh[13]
                    for h in range(4):
                        shr, shk, shv = sh[h], sh[4 + h], sh[8 + h]
                        store(0, h, shr)
                        store(1, h, shv)
                        hs = slice(h * 64, (h + 1) * 64)
                        for d in range(2):
                            ds_ = slice(d * 32, (d + 1) * 32)
                            col = d * 4 + h
                            pw = ps_r.next()
                            P.op(PE, lambda e: e.matmul(pw[0:64, 0:n], lhsT=pp[l][ds_, owu + h * 64:owu + (h + 1) * 64], rhs=twd[ds_, 0:n], start=True, stop=True),
                                 R=[pp[l].d, twd.d], W=[pw.d])
                            lw = o_r.next()
                            P.op(ACT, lambda e: e.activation(out=lw[:, 0:n], in_=pw[0:64, 0:n], func=AF.Sigmoid, bias=pp[l][0:64, ow0 + col:ow0 + col + 1], scale=1.0),
                                 R=[pw.d, pp[l].d], W=[lw.d])
                            P.op(DVE, lambda e: e.tensor_scalar(out=lw[:, 0:n], in0=lw[:, 0:n], scalar1=-math.exp(-0.5), scalar2=None, op0=ALU.mult),
                                 R=[lw.d], W=[lw.d])
                            store(7 + d, h, lw)
                            pa = ps_r.next()
                            P.op(PE, lambda e: e.matmul(pa[0:64, 0:n], lhsT=pp[l][ds_, oau + h * 64:oau + (h + 1) * 64], rhs=sad[ds_, 0:n], start=True, stop=True),
                                 R=[pp[l].d, sad.d], W=[pa.d])
                            P.op(ACT, lambda e: e.activation(out=a_d[d][:, 0:n], in_=pa[0:64, 0:n], func=AF.Sigmoid, bias=pp[l][0:64, oa0 + col:oa0 + col + 1], scale=1.0),
                                 R=[pa.d, pp[l].d], W=[a_d[d].d])
                        pg = ps_r.next()
                        P.op(PE, lambda e: e.matmul(pg[0:64, 0:n], lhsT=pp[l][0:64, ogu + h * 64:ogu + (h + 1) * 64], rhs=sgd[:, 0:n], start=True, stop=True),
                             R=[pp[l].d, sgd.d], W=[pg.d])
                        go = o_r.next()
                        P.op(ACT, lambda e: e.activation(out=go[:, 0:n], in_=pg[0:64, 0:n], func=AF.Copy), R=[pg.d], W=[go.d])
                        store(9, h, go)
                        P.op(DVE, lambda e: e.tensor_scalar(out=kkr[:, 0:n], in0=shk[:, 0:n], scalar1=pp[l][0:64, okk + h:okk + h + 1], scalar2=None, op0=ALU.mult),
                             R=[shk.d, pp[l].d], W=[kkr.d])
                        P.op(ACT, lambda e: e.activation(out=t1[:, 0:n], in_=kkr[:, 0:n], func=AF.Square), R=[kkr.d], W=[t1.d])
                        pk = ps_r.next()
                        P.op(PE, lambda e: e.matmul(pk[0:64, 0:n], lhsT=C('ones', rows=64, sub=(0, 64)), rhs=t1[:, 0:n], start=True, stop=True),
                             R=[t1.d, cp.d], W=[pk.d])
                        P.op(DVE, lambda e: e.tensor_scalar(out=t1[:, 0:n], in0=pk[0:64, 0:n], scalar1=1e-12, scalar2=None, op0=ALU.max), R=[pk.d], W=[t1.d])
                        P.op(ACT, lambda e: e.activation(out=t1[:, 0:n], in_=t1[:, 0:n], func=AF.Sqrt), R=[t1.d], W=[t1.d])
                        P.op(DVE, lambda e: e.reciprocal(out=t1[:, 0:n], in_=t1[:, 0:n]), R=[t1.d], W=[t1.d])
                        P.op(DVE, lambda e: e.tensor_tensor(out=kk[:, 0:n], in0=kkr[:, 0:n], in1=t1[:, 0:n], op=ALU.mult), R=[kkr.d, t1.d], W=[kk.d])
                        store(2, h, kk)
                        for d in range(2):
                            P.op(DVE, lambda e: e.tensor_scalar(out=t2[:, 0:n], in0=a_d[d][:, 0:n], scalar1=-1.0, scalar2=pp[l][0:64, oka + h:oka + h + 1],
                                                                op0=ALU.add, op1=ALU.mult), R=[a_d[d].d, pp[l].d], W=[t2.d])
                            kd = o_r.next()
                            P.op(DVE, lambda e: e.scalar_tensor_tensor(out=kd[:, 0:n], in0=t2[:, 0:n], scalar=1.0, in1=shk[:, 0:n], op0=ALU.add, op1=ALU.mult),
                                 R=[t2.d, shk.d], W=[kd.d])
                            store(3 + d, h, kd)
                            bd = o_r.next()
                            P.op(DVE, lambda e: e.tensor_tensor(out=bd[:, 0:n], in0=kk[:, 0:n], in1=a_d[d][:, 0:n], op=ALU.mult), R=[kk.d, a_d[d].d], W=[bd.d])
                            store(5 + d, h, bd)
                        P.op(DVE, lambda e: e.scalar_tensor_tensor(out=t2[:, 0:n], in0=shr[:, 0:n], scalar=pp[l][0:64, ork + h:ork + h + 1], in1=shk[:, 0:n],
                                                                   op0=ALU.mult, op1=ALU.mult), R=[shr.d, shk.d, pp[l].d], W=[t2.d])
                        pb = ps_r.next()
                        P.op(PE, lambda e: e.matmul(pb[0:64, 0:n], lhsT=C('ones', rows=64, sub=(0, 64)), rhs=t2[:, 0:n], start=True, stop=True),
                             R=[t2.d, cp.d], W=[pb.d])
                        bo = o_r.next()
                        P.op(DVE, lambda e: e.tensor_tensor(out=bo[:, 0:n], in0=pb[0:64, 0:n], in1=shv[:, 0:n], op=ALU.mult), R=[pb.d, shv.d], W=[bo.d])
                        store(10, h, bo)

        def rwkv_scan(l):
            NI = 16
            ld = {}
            for nm in ('r', 'v', 'kk', 'kd', 'bd', 'lw'):
                ld[nm] = [T(P.sb("l%s%d" % (nm, d), [64, 512])) for d in range(2)]
            Lc = [T(P.sb("Lc%d" % d, [64, 512])) for d in range(2)]
            En = [T(P.sb("En%d" % d, [64, 512])) for d in range(2)]
            Ep = [T(P.sb("Ep%d" % d, [64, 512])) for d in range(2)]
            aT = [T(P.sb("aT%d" % d, [64, 512])) for d in range(2)]
            bT = [T(P.sb("bT%d" % d, [64, 512])) for d in range(2)]
            kT = [T(P.sb("kT%d" % d, [64, 512])) for d in range(2)]
            rT = [T(P.sb("rT%d" % d, [64, 512])) for d in range(2)]
            vo = [T(P.sb("vo%d" % d, [64, 512])) for d in range(2)]
            ysb = [T(P.sb("ysb%d" % d, [64, 512])) for d in range(2)]
            yso = [T(P.sb("yso%d" % d, [64, 512])) for d in range(2)]
            ELC = [T(P.sb("ELC%d" % i, [64, 64])) for i in range(NI)]
            bhk = [T(P.sb("bhk%d" % i, [64, 128])) for i in range(NI)]
            WC = [T(P.sb("WC%d" % i, [64, 8])) for i in range(NI)]
            TM = [T(P.sb("TM%d" % i, [64, 320])) for i in range(NI)]
            AM = [T(P.sb("AM%d" % i, [64, 320])) for i in range(NI)]
            Mm = [[T(P.sb("Mm%d_%d" % (i, j), [64, 128])) for j in range(2)] for i in range(NI)]
            Pm = [[T(P.sb("Pm%d_%d" % (i, j), [64, 64])) for j in range(2)] for i in range(NI)]
            AU = [T(P.sb("AU%d" % i, [64, 128])) for i in range(NI)]
            GT = [T(P.sb("GT%d" % i, [64, 64])) for i in range(NI)]
            Hh = [T(P.sb("Hh%d" % i, [64, 64])) for i in range(NI)]
            RhT = [T(P.sb("RhT%d" % i, [64, 64])) for i in range(NI)]
            Yl = [T(P.sb("Yl%d" % i, [64, 64])) for i in range(NI)]
            St = [[T(P.sb("St%d_%d" % (d, i), [64, 64])) for i in range(2)] for d in range(2)]
            s0l = T(P.sb("s0l", [64, 64]))
            pr = Ring([TPS(P.psum("pRW", [128, 512])) for _ in range(8)])
            I64 = C('ident', rows=64, sub=(0, 64))
            for si, (kind, sidx, Tn, off) in enumerate(SEQS):
                nq = min(512, Tn)
                ntile = Tn // nq
                for h in range(4):
                    sti = [0, 0]
                    for d in range(2):
                        S0 = St[d][0]
                        if kind == 's':
                            pS = pr.next()
                            P.dma(SP, s0l[:, :], st_rwkv[l, d, h], W=[s0l.d])
                            P.op(PE, lambda e: e.transpose(pS[0:64, 0:64], s0l[:, :], I64), R=[s0l.d, cp.d], W=[pS.d])
                            P.op(DVE, lambda e: e.tensor_copy(out=S0[:, :], in_=pS[0:64, 0:64]), R=[pS.d], W=[S0.d])
                        else:
                            P.op(DVE, lambda e: e.memset(S0[:, :], 0.0), W=[S0.d])
                    for ti in range(ntile):
                        n = nq
                        for d in range(2):
                            q0 = ti * nq if d == 0 else Tn - (ti + 1) * nq

                            def view(t):
                                return t[:, 0:n] if d == 0 else t[:, 0:n][:, ::-1]
                            for nm, arr in (('r', 0), ('v', 1), ('kk', 2), ('kd', 3 + d), ('bd', 5 + d), ('lw', 7 + d)):
                                r0 = arr * 256 + h * 64
                                P.dma(SP, ld[nm][d][:, 0:n], RW[r0:r0 + 64, off + q0: off + q0 + n], R=[dRW[si]], W=[ld[nm][d].d])
                            lw = ld['lw'][d]
                            P.op(DVE, lambda e: e.tensor_tensor_scan(out=Lc[d][:, 0:n], data0=C('scanmask', rows=64, sub=(0, n)), data1=view(lw),
                                                                     initial=0.0, op0=ALU.mult, op1=ALU.add), R=[lw.d, cp.d], W=[Lc[d].d])
                            P.op(ACT, lambda e: e.activation(out=En[d][:, 0:n], in_=Lc[d][:, 0:n], func=AF.Exp, scale=-1.0), R=[Lc[d].d], W=[En[d].d])
                            P.op(ACT, lambda e: e.activation(out=Ep[d][:, 0:n], in_=Lc[d][:, 0:n], func=AF.Exp), R=[Lc[d].d], W=[Ep[d].d])
                            P.op(DVE, lambda e: e.tensor_tensor(out=rT[d][:, 0:n], in0=view(ld['r'][d]), in1=Ep[d][:, 0:n], op=ALU.mult),
                                 R=[ld['r'][d].d, Ep[d].d], W=[rT[d].d])
                            P.op(DVE, lambda e: e.tensor_tensor(out=aT[d][:, 0:n], in0=Lc[d][:, 0:n], in1=view(lw), op=ALU.subtract),
                                 R=[Lc[d].d, lw.d], W=[aT[d].d])
                            P.op(ACT, lambda e: e.activation(out=Ep[d][:, 0:n], in_=aT[d][:, 0:n], func=AF.Exp), R=[aT[d].d, rT[d].d], W=[Ep[d].d])
                            P.op(DVE, lambda e: e.scalar_tensor_tensor(out=aT[d][:, 0:n], in0=view(ld['kk'][d]), scalar=-1.0, in1=Ep[d][:, 0:n],
                                                                       op0=ALU.mult, op1=ALU.mult), R=[ld['kk'][d].d, Ep[d].d], W=[aT[d].d])
                            P.op(DVE, lambda e: e.tensor_tensor(out=bT[d][:, 0:n], in0=view(ld['bd'][d]), in1=En[d][:, 0:n], op=ALU.mult),
                                 R=[ld['bd'][d].d, En[d].d], W=[bT[d].d])
                            P.op(DVE, lambda e: e.tensor_tensor(out=kT[d][:, 0:n], in0=view(ld['kd'][d]), in1=En[d][:, 0:n], op=ALU.mult),
                                 R=[ld['kd'][d].d, En[d].d], W=[kT[d].d])
                            P.op(DVE, lambda e: e.tensor_copy(out=vo[d][:, 0:n], in_=view(ld['v'][d])), R=[ld['v'][d].d], W=[vo[d].d])
                        ncc = n // 64
                        inst = [(cc, d) for cc in range(ncc) for d in range(2)]

                        def CS(cc):
                            return slice(cc * 64, (cc + 1) * 64)
                        for ii, (cc, d) in enumerate(inst):
                            cs = CS(cc)
                            lcol = Lc[d][:, cc * 64 + 63:cc * 64 + 64]

                            def vw(t):
                                if d == 0:
                                    return t[:, cs]
                                return t[:, n - (cc + 1) * 64:n - cc * 64][:, ::-1]
                            P.op(ACT, lambda e: e.activation(out=ELC[ii][:, :], in_=Lc[d][:, cs], func=AF.Exp, bias=lcol, scale=-1.0),
                                 R=[Lc[d].d], W=[ELC[ii].d])
                            P.op(ACT, lambda e: e.activation(out=WC[ii][:, 0:1], in_=lcol, func=AF.Exp), R=[Lc[d].d], W=[WC[ii].d])
                            P.op(DVE, lambda e: e.tensor_tensor(out=bhk[ii][:, 0:64], in0=vw(ld['bd'][d]), in1=ELC[ii][:, :], op=ALU.mult),
                                 R=[ld['bd'][d].d, ELC[ii].d], W=[bhk[ii].d])
                            P.op(DVE, lambda e: e.tensor_tensor(out=bhk[ii][:, 64:128], in0=vw(ld['kd'][d]), in1=ELC[ii][:, :], op=ALU.mult),
                                 R=[ld['kd'][d].d, ELC[ii].d], W=[bhk[ii].d])
                        for ii, (cc, d) in enumerate(inst):
                            cs = CS(cc)
                            ps = pr.next()
                            P.op(PE, lambda e: e.transpose(ps[0:64, 0:64], aT[d][:, cs], I64), R=[aT[d].d, cp.d], W=[ps.d])
                            P.op(PE, lambda e: e.transpose(ps[0:64, 64:128], vo[d][:, cs], I64), R=[vo[d].d, cp.d], W=[ps.d])
                            P.op(PE, lambda e: e.transpose(ps[0:64, 128:192], bhk[ii][:, 0:64], I64), R=[bhk[ii].d, cp.d], W=[ps.d])
                            P.op(PE, lambda e: e.transpose(ps[0:64, 192:256], bhk[ii][:, 64:128], I64), R=[bhk[ii].d, cp.d], W=[ps.d])
                            tm = TM[ii]
                            if ii % 2 == 0:
                                P.op(ACT, lambda e: e.activation(out=tm[:, 0:64], in_=ps[0:64, 0:64], func=AF.Copy), R=[ps.d], W=[tm.d])
                                P.op(ACT, lambda e: e.activation(out=tm[:, 128:320], in_=ps[0:64, 64:256], func=AF.Copy), R=[ps.d], W=[tm.d])
                            else:
                                P.op(DVE, lambda e: e.tensor_copy(out=tm[:, 0:64], in_=ps[0:64, 0:64]), R=[ps.d], W=[tm.d])
                                P.op(DVE, lambda e: e.tensor_copy(out=tm[:, 128:320], in_=ps[0:64, 64:256]), R=[ps.d], W=[tm.d])
                        for ii, (cc, d) in enumerate(inst):
                            cs = CS(cc)
                            a_, b_, k_, r_ = aT[d][:, cs], bT[d][:, cs], kT[d][:, cs], rT[d][:, cs]
                            ps = pr.next()
                            for i_, (lh, rh) in enumerate(((b_, a_), (b_, r_), (k_, a_), (k_, r_), (a_, b_))):
                                P.op(PE, lambda e: e.matmul(ps[0:64, i_ * 64:(i_ + 1) * 64], lhsT=lh, rhs=rh, start=True, stop=True),
                                     R=[aT[d].d, bT[d].d, kT[d].d, rT[d].d], W=[ps.d])
                            am = AM[ii]
                            P.op(DVE, lambda e: e.tensor_tensor(out=am[:, :], in0=ps[0:64, 0:320], in1=C('rw_mask5', rows=64), op=ALU.mult),
                                 R=[ps.d, cp.d], W=[am.d])
                            P.op(DVE, lambda e: e.tensor_tensor(out=Pm[ii][0][:, :], in0=am[:, 0:64], in1=I64, op=ALU.add), R=[am.d, cp.d], W=[Pm[ii][0].d])
                        cur = [(AM[ii][:, 0:64], AM[ii][:, 256:320], AM[ii]) for ii in range(len(inst))]
                        for lvl in range(5):
                            for ii, (cc, d) in enumerate(inst):
                                Mc, Mtc, Mdep = cur[ii]
                                mn = Mm[ii][lvl % 2]
                                ps = pr.next()
                                P.op(PE, lambda e: e.matmul(ps[0:64, 0:64], lhsT=Mtc, rhs=Mc, start=True, stop=True), R=[Mdep.d], W=[ps.d])
                                P.op(PE, lambda e: e.matmul(ps[0:64, 64:128], lhsT=Mc, rhs=Mtc, start=True, stop=True), R=[Mdep.d], W=[ps.d])
                                evac(mn[:, :], ps[0:64, 0:128], [ps.d], [mn.d])
                                cur[ii] = (mn[:, 0:64], mn[:, 64:128], mn)
                            for ii, (cc, d) in enumerate(inst):
                                Mc, Mtc, Mdep = cur[ii]
                                Pc = Pm[ii][lvl % 2]
                                Pn = Pm[ii][(lvl + 1) % 2]
                                ps = pr.next()
                                P.op(PE, lambda e: e.matmul(ps[0:64, 0:64], lhsT=Mtc, rhs=Pc[:, :], start=True, stop=True), R=[Mdep.d, Pc.d], W=[ps.d])
                                P.op(DVE, lambda e: e.tensor_tensor(out=Pn[:, :], in0=ps[0:64, 0:64], in1=Pc[:, :], op=ALU.add), R=[ps.d, Pc.d], W=[Pn.d])
                        for ii, (cc, d) in enumerate(inst):
                            tm, am = TM[ii], AM[ii]
                            ps = pr.next()
                            P.op(PE, lambda e: e.matmul(ps[0:64, 0:64], lhsT=am[:, 128:192], rhs=tm[:, 128:192], start=True, stop=True), R=[am.d, tm.d], W=[ps.d])
                            evac(tm[:, 64:128], ps[0:64, 0:64], [ps.d], [tm.d])
                        for ii, (cc, d) in enumerate(inst):
                            tm, Pc, au = TM[ii], Pm[ii][1], AU[ii]
                            ps = pr.next()
                            P.op(PE, lambda e: e.matmul(ps[0:64, 0:128], lhsT=Pc[:, :], rhs=tm[:, 0:128], start=True, stop=True), R=[Pc.d, tm.d], W=[ps.d])
                            evac(au[:, :], ps[0:64, 0:128], [ps.d], [au.d])
                        for ii, (cc, d) in enumerate(inst):
                            tm, au = TM[ii], AU[ii]
                            V_, Bh_, Kh_ = tm[:, 128:192], tm[:, 192:256], tm[:, 256:320]
                            Ah, Ul = au[:, 0:64], au[:, 64:128]
                            ps = pr.next()
                            P.op(PE, lambda e: e.matmul(ps[0:64, 0:64], lhsT=Ah, rhs=Bh_, start=True, stop=True), R=[au.d, tm.d], W=[ps.d])
                            P.op(PE, lambda e: e.matmul(ps[0:64, 64:128], lhsT=Bh_, rhs=Ul, start=True, stop=False), R=[au.d, tm.d], W=[ps.d])
                            P.op(PE, lambda e: e.matmul(ps[0:64, 64:128], lhsT=Kh_, rhs=V_, start=False, stop=True), R=[tm.d], W=[ps.d])
                            P.op(DVE, lambda e: e.scalar_tensor_tensor(out=GT[ii][:, :], in0=I64, scalar=WC[ii][:, 0:1], in1=ps[0:64, 0:64],
                                                                       op0=ALU.mult, op1=ALU.add), R=[cp.d, WC[ii].d, ps.d], W=[GT[ii].d])
                            P.op(DVE, lambda e: e.tensor_copy(out=Hh[ii][:, :], in_=ps[0:64, 64:128]), R=[ps.d], W=[Hh[ii].d])
                        for ii, (cc, d) in enumerate(inst):
                            cs = CS(cc)
                            tm, au, am = TM[ii], AU[ii], AM[ii]
                            V_ = tm[:, 128:192]
                            Ah, Ul = au[:, 0:64], au[:, 64:128]
                            ArbT, ArkT = am[:, 64:128], am[:, 192:256]
                            ps = pr.next()
                            P.op(PE, lambda e: e.matmul(ps[0:64, 0:64], lhsT=Ah, rhs=ArbT, start=True, stop=True), R=[au.d, am.d], W=[ps.d])
                            P.op(PE, lambda e: e.matmul(ps[0:64, 64:128], lhsT=Ul, rhs=ArbT, start=True, stop=False), R=[au.d, am.d], W=[ps.d])
                            P.op(PE, lambda e: e.matmul(ps[0:64, 64:128], lhsT=V_, rhs=ArkT, start=False, stop=True), R=[tm.d, am.d], W=[ps.d])
                            P.op(DVE, lambda e: e.tensor_tensor(out=RhT[ii][:, :], in0=ps[0:64, 0:64], in1=rT[d][:, cs], op=ALU.add), R=[ps.d, rT[d].d], W=[RhT[ii].d])
                            P.op(DVE, lambda e: e.tensor_copy(out=Yl[ii][:, :], in_=ps[0:64, 64:128]), R=[ps.d], W=[Yl[ii].d])
                        for ii, (cc, d) in enumerate(inst):
                            cs = CS(cc)
                            Sc = St[d][sti[d] % 2]
                            Sn = St[d][(sti[d] + 1) % 2]
                            sti[d] += 1
                            ps = pr.next()
                            P.op(PE, lambda e: e.matmul(ps[0:64, 0:64], lhsT=Sc[:, :], rhs=RhT[ii][:, :], start=True, stop=True), R=[Sc.d, RhT[ii].d], W=[ps.d])
                            P.op(PE, lambda e: e.matmul(ps[0:64, 64:128], lhsT=GT[ii][:, :], rhs=Sc[:, :], start=True, stop=True), R=[GT[ii].d, Sc.d], W=[ps.d])
                            P.op(DVE, lambda e: e.tensor_tensor(out=Sn[:, :], in0=ps[0:64, 64:128], in1=Hh[ii][:, :], op=ALU.add), R=[ps.d, Hh[ii].d], W=[Sn.d])
                            P.op(DVE, lambda e: e.tensor_tensor(out=ysb[d][:, cs], in0=ps[0:64, 0:64], in1=Yl[ii][:, :], op=ALU.add), R=[ps.d, Yl[ii].d], W=[ysb[d].d])
                        for d in range(2):
                            q0 = ti * nq if d == 0 else Tn - (ti + 1) * nq
                            src = ysb[d]
                            if d == 1:
                                P.op(DVE, lambda e: e.tensor_copy(out=yso[d][:, 0:n], in_=ysb[d][:, 0:n][:, ::-1]), R=[ysb[d].d], W=[yso[d].d])
                                src = yso[d]
                            r0 = d * 256 + h * 64
                            P.dma(ACT, YS[r0:r0 + 64, off + q0: off + q0 + n], src[:, 0:n], R=[src.d], W=[dYS[si]])
                    if kind == 'p':
                        for d in range(2):
                            Sc = St[d][sti[d] % 2]
                            pS = pr.next()
                            P.op(PE, lambda e: e.transpose(pS[0:64, 0:64], Sc[:, :], I64), R=[Sc.d, cp.d], W=[pS.d])
                            P.op(DVE, lambda e: e.tensor_copy(out=s0l[:, :], in_=pS[0:64, 0:64]), R=[pS.d], W=[s0l.d])
                            P.dma(SP, o_rwkv[sidx, l, d, h], s0l[:, :], R=[s0l.d], W=[dOUT])

        def rwkv_post(l):
            yf = Ring([T(P.sb("pyf", [64, 512])) for _ in range(2)])
            yb = Ring([T(P.sb("pyb", [64, 512])) for _ in range(2)])
            gt = Ring([T(P.sb("pgt", [64, 512])) for _ in range(2)])
            bt = Ring([T(P.sb("pbt", [64, 512])) for _ in range(2)])
            yc = T(P.sb("pyc", [64, 512]))
            sq = T(P.sb("psq", [64, 512]))
            rs = T(P.sb("prs", [64, 512]))
            oy_r = Ring([T(P.sb("oy", [64, 512], OTDT)) for _ in range(2)])
            ps_r = Ring([TPS(P.psum("ps_po", [128, 512])) for _ in range(4)])
            olg, _ = PPC['lng64']
            olb, _ = PPC['lnb64']
            epsg = T(P.sb("epsg", [64, 1]))
            P.op(DVE, lambda e: e.memset(epsg[:, :], 64e-5), W=[epsg.d])
            for si, (kind, sidx, Tn, off) in enumerate(SEQS):
                nq = min(512, Tn)
                for h in range(4):
                    for q0 in range(0, Tn, nq):
                        n = nq
                        a, b, g_, bo = yf.next(), yb.next(), gt.next(), bt.next()
                        cols = slice(off + q0, off + q0 + n)
                        P.dma(SP, a[:, 0:n], YS[h * 64:(h + 1) * 64, cols], R=[dYS[si]], W=[a.d])
                        P.dma(SP, b[:, 0:n], YS[256 + h * 64:256 + (h + 1) * 64, cols], R=[dYS[si]], W=[b.d])
                        P.dma(SP, g_[:, 0:n], RW[9 * 256 + h * 64:9 * 256 + (h + 1) * 64, cols], R=[dRW[si]], W=[g_.d])
                        P.dma(SP, bo[:, 0:n], RW[10 * 256 + h * 64:10 * 256 + (h + 1) * 64, cols], R=[dRW[si]], W=[bo.d])
                        P.op(DVE, lambda e: e.tensor_tensor(out=a[:, 0:n], in0=a[:, 0:n], in1=b[:, 0:n], op=ALU.add), R=[a.d, b.d], W=[a.d])
                        pm_ = ps_r.next()
                        P.op(PE, lambda e: e.matmul(pm_[0:64, 0:n], lhsT=C('ones', rows=64, sub=(0, 64)), rhs=a[:, 0:n], start=True, stop=True),
                             R=[a.d, cp.d], W=[pm_.d])
                        P.op(DVE, lambda e: e.scalar_tensor_tensor(out=yc[:, 0:n], in0=pm_[0:64, 0:n], scalar=-1.0 / 64, in1=a[:, 0:n],
                                                                   op0=ALU.mult, op1=ALU.add), R=[pm_.d, a.d], W=[yc.d])
                        P.op(ACT, lambda e: e.activation(out=sq[:, 0:n], in_=yc[:, 0:n], func=AF.Square), R=[yc.d], W=[sq.d])
                        pv = ps_r.next()
                        P.op(PE, lambda e: e.matmul(pv[0:64, 0:n], lhsT=C('ones', rows=64, sub=(0, 64)), rhs=sq[:, 0:n], start=True, stop=True),
                             R=[sq.d, cp.d], W=[pv.d])
                        P.op(ACT, lambda e: e.activation(out=rs[:, 0:n], in_=pv[0:64, 0:n], func=AF.Sqrt, bias=epsg[:, 0:1], scale=1.0 / 64),
                             R=[pv.d, epsg.d], W=[rs.d])
                        P.op(DVE, lambda e: e.reciprocal(out=rs[:, 0:n], in_=rs[:, 0:n]), R=[rs.d], W=[rs.d])
                        P.op(DVE, lambda e: e.scalar_tensor_tensor(out=yc[:, 0:n], in0=yc[:, 0:n], scalar=pp[l][0:64, olg + h:olg + h + 1], in1=rs[:, 0:n],
                                                                   op0=ALU.mult, op1=ALU.mult), R=[yc.d, rs.d, pp[l].d], W=[yc.d])
                        P.op(DVE, lambda e: e.scalar_tensor_tensor(out=yc[:, 0:n], in0=yc[:, 0:n], scalar=pp[l][0:64, olb + h:olb + h + 1], in1=bo[:, 0:n],
                                                                   op0=ALU.add, op1=ALU.add), R=[yc.d, bo.d, pp[l].d], W=[yc.d])
                        oy = oy_r.next()
                        P.op(DVE, lambda e: e.tensor_tensor(out=oy[:, 0:n], in0=yc[:, 0:n], in1=g_[:, 0:n], op=ALU.mult), R=[yc.d, g_.d], W=[oy.d])
                        P.dma(ACT, OT[h * 64:(h + 1) * 64, cols], oy[:, 0:n], R=[oy.d], W=[dOT[si]])

        def mixer_ssd(l):
            NCH = TS // 128
            raw = T(P.sb("raw", [64, TS]))
            tcv = T(P.sb("tcv", [64, TS]))
            xcf = [T(P.sb("xcf%d" % h, [64, TS], BF16)) for h in range(4)]
            Bb = [T(P.sb("Bb%d" % g, [64, TS], BF16)) for g in range(2)]
            Cb = [T(P.sb("Cb%d" % g, [64, TS], BF16)) for g in range(2)]
            xT = [T(P.sb("xT%d" % h, [128, NCH, 64], BF16)) for h in range(4)]
            yz = [T(P.sb("yz%d" % h, [64, TS], BF16)) for h in range(4)]
            dtr = T(P.sb("dtr", [128, NCH, 8]))
            dts = T(P.sb("dts", [128, NCH, 8]))
            gg = T(P.sb("gg", [128, NCH, 8]))
            nea = T(P.sb("nea", [128, 8]))
            S_ = [T(P.sb("S%d" % d, [64, 64])) for d in range(2)]
            Sfb = T(P.sb("Sfb", [64, 64], BF16))
            Sbs = T(P.sb("Sbs", [64, NCH, 64], BF16))
            cumc = Ring([T(P.sb("cumc", [128, 8])) for _ in range(2)])
            gB_r = Ring([T(P.sb("gB", [128, 128])) for _ in range(2)])
            arg_r = Ring([T(P.sb("arg", [128, 128])) for _ in range(2)])
            Dm = [T(P.sb("Dm%d" % d, [128, 128])) for d in range(2)]
            Ds = T(P.sb("Ds", [128, 128]))
            PT_r = Ring([T(P.sb("PT", [128, 128], BF16)) for _ in range(2)])
            sm_r = Ring([T(P.sb("sm", [128, 4])) for _ in range(4)])
            ecr_r = Ring([T(P.sb("ecr", [64, 128])) for _ in range(2)])
            qd_r = Ring([T(P.sb("qd", [64, 128], BF16)) for _ in range(4)])
            Bw_r = Ring([T(P.sb("Bw", [128, 64], BF16)) for _ in range(2)])
            zt_ = T(P.sb("zt_", [64, 512]))
            t5 = T(P.sb("t5", [64, 512]))
            rsy = T(P.sb("rsy", [64, 512]))
            oy_r = Ring([T(P.sb("oy", [64, 512], OTDT)) for _ in range(2)])
            ps_cc = Ring([TPS(P.psum("ps_cc", [128, 512])) for _ in range(1)])
            ps_cr = Ring([TPS(P.psum("ps_cr", [128, 512])) for _ in range(2)])
            ps_sc = Ring([TPS(P.psum("ps_sc", [128, 512])) for _ in range(1)])
            ps_tr = Ring([TPS(P.psum("ps_tr", [128, 1024], BF16)) for _ in range(1)])
            ps_up = Ring([TPS(P.psum("ps_up", [128, 512])) for _ in range(1)])
            ps_y = Ring([TPS(P.psum("ps_y", [128, 512])) for _ in range(2)])
            ocw, _ = PPC['scw64']
            ocb, _ = PPC['scb64']
            odtb, _ = PPC['dtb']
            oal, _ = PPC['alog']
            od4, _ = PPC['ssd_d4']
            og4, _ = PPC['ssd_g4']
            P.op(ACT, lambda e: e.activation(out=nea[:, :], in_=pp[l][:, oal:oal + 8], func=AF.Exp), R=[pp[l].d], W=[nea.d])
            P.op(DVE, lambda e: e.tensor_scalar(out=nea[:, :], in0=nea[:, :], scalar1=-1.0, scalar2=None, op0=ALU.mult), R=[nea.d], W=[nea.d])
            for si, (kind, sidx, Tn, off) in enumerate(SEQS):
                nch = Tn // 128

                def convsilu(hc, dst, dstb):
                    r0 = 20 * 128 + hc * 64
                    P.dma(SP, raw[:, 0:Tn], FT[r0:r0 + 64, off:off + Tn], R=[dFT[si]], W=[raw.d])
                    P.op(ACT, lambda e: e.activation(out=tcv[:, 0:Tn], in_=raw[:, 0:Tn], func=AF.Identity,
                                                     bias=pp[l][0:64, ocb + hc:ocb + hc + 1], scale=pp[l][0:64, ocw + 8 + hc:ocw + 9 + hc]),
                         R=[raw.d, pp[l].d], W=[tcv.d])
                    P.op(DVE, lambda e: e.scalar_tensor_tensor(out=tcv[:, 1:Tn], in0=raw[:, 0:Tn - 1], scalar=pp[l][0:64, ocw + hc:ocw + hc + 1],
                                                               in1=tcv[:, 1:Tn], op0=ALU.mult, op1=ALU.add), R=[raw.d, pp[l].d, tcv.d], W=[tcv.d])
                    P.op(DVE, lambda e: e.scalar_tensor_tensor(out=tcv[:, 0:Tn - 1], in0=raw[:, 1:Tn], scalar=pp[l][0:64, ocw + 16 + hc:ocw + 17 + hc],
                                                               in1=tcv[:, 0:Tn - 1], op0=ALU.mult, op1=ALU.add), R=[raw.d, pp[l].d, tcv.d], W=[tcv.d])
                    if dst is not None:
                        P.op(ACT, lambda e: e.activation(out=dst[:, 0:Tn], in_=tcv[:, 0:Tn], func=AF.Silu), R=[tcv.d], W=[dst.d])
                        P.op(DVE, lambda e: e.tensor_copy(out=dstb[:, 0:Tn], in_=dst[:, 0:Tn]), R=[dst.d], W=[dstb.d])
                    else:
                        P.op(ACT, lambda e: e.activation(out=dstb[:, 0:Tn], in_=tcv[:, 0:Tn], func=AF.Silu), R=[tcv.d], W=[dstb.d])
                for g in range(2):
                    convsilu(4 + g, None, Bb[g])
                    convsilu(6 + g, None, Cb[g])
                for h in range(4):
                    convsilu(h, None, xcf[h])
                    for c in range(nch):
                        pt = ps_tr.next()
                        P.op(PE, lambda e: e.transpose(pt[:, 0:64], xcf[h][:, c * 128:(c + 1) * 128], cpb[0:64, 0:64]), R=[xcf[h].d, cpb.d], W=[pt.d])
                        evac(xT[h][:, c, :], pt[:, 0:64], [pt.d], [xT[h].d])
                P.dma(SP, dtr[:, 0:nch, :], FTM[off:off + Tn, 768:776].rearrange("(c p) e -> p c e", p=128), R=[dFTM[si]], W=[dtr.d])
                P.op(DVE, lambda e: e.tensor_tensor(out=dtr[:, 0:nch, :], in0=dtr[:, 0:nch, :],
                                                    in1=pp[l][:, odtb:odtb + 8].unsqueeze(1).to_broadcast([128, nch, 8]), op=ALU.add),
                     R=[dtr.d, pp[l].d], W=[dtr.d])
                P.op(ACT, lambda e: e.activation(out=dtr[:, 0:nch, :], in_=dtr[:, 0:nch, :], func=AF.Exp), R=[dtr.d], W=[dtr.d])
                P.op(ACT, lambda e: e.activation(out=dts[:, 0:nch, :], in_=dtr[:, 0:nch, :], func=AF.Ln, bias=1.0, scale=1.0), R=[dtr.d], W=[dts.d])
                P.op(DVE, lambda e: e.tensor_tensor(out=gg[:, 0:nch, :], in0=dts[:, 0:nch, :],
                                                    in1=nea[:, :].unsqueeze(1).to_broadcast([128, nch, 8]), op=ALU.mult),
                     R=[dts.d, nea.d], W=[gg.d])

                def chunk_cum(c):
                    pc = ps_cc.next()
                    P.op(PE, lambda e: e.matmul(pc[:, 0:4], lhsT=C('tri_f'), rhs=gg[:, c, 0:4], start=True, stop=True), R=[cp.d, gg.d], W=[pc.d])
                    P.op(PE, lambda e: e.matmul(pc[:, 4:8], lhsT=C('tri_b'), rhs=gg[:, c, 4:8], start=True, stop=True), R=[cp.d, gg.d], W=[pc.d])
                    cc = cumc.next()
                    P.op(DVE, lambda e: e.tensor_copy(out=cc[:, :], in_=pc[:, 0:8]), R=[pc.d], W=[cc.d])
                    return cc

                def dirstuff(c, h, d, cc):
                    col = d * 4 + h
                    gB = gB_r.next()
                    P.op(DVE, lambda e: e.tensor_scalar(out=gB[:, :], in0=C('ones'), scalar1=gg[:, c, col:col + 1], scalar2=None, op0=ALU.mult),
                         R=[cp.d, gg.d], W=[gB.d])
                    pr_ = ps_cr.next()
                    P.op(PE, lambda e: e.matmul(pr_[:, 0:128], lhsT=gB[:, :], rhs=C('tri_f' if d == 0 else 'tri_b'), start=True, stop=True),
                         R=[gB.d, cp.d], W=[pr_.d])
                    sm = sm_r.next()
                    lc = 127 if d == 0 else 0
                    P.op(DVE, lambda e: e.tensor_copy(out=sm[:, 0:1], in_=pr_[:, lc:lc + 1]), R=[pr_.d], W=[sm.d])
                    P.op(ACT, lambda e: e.activation(out=sm[:, 1:2], in_=cc[:, col:col + 1], func=AF.Exp, bias=sm[:, 0:1], scale=-1.0),
                         R=[cc.d, sm.d], W=[sm.d])
                    P.op(DVE, lambda e: e.tensor_tensor(out=sm[:, 2:3], in0=sm[:, 1:2], in1=dts[:, c, col:col + 1], op=ALU.mult),
                         R=[sm.d, dts.d], W=[sm.d])
                    P.op(ACT, lambda e: e.activation(out=sm[:, 3:4], in_=sm[:, 0:1], func=AF.Exp), R=[sm.d], W=[sm.d])
                    return pr_, sm

                def state_update(S, c, h, g, sm):
                    pt = ps_tr.next()
                    P.op(PE, lambda e: e.transpose(pt[:, 0:64], Bb[g][:, c * 128:(c + 1) * 128], cpb[0:64, 0:64]), R=[Bb[g].d, cpb.d], W=[pt.d])
                    Bw = Bw_r.next()
                    P.op(DVE, lambda e: e.tensor_scalar(out=Bw[:, :], in0=pt[:, 0:64], scalar1=sm[:, 2:3], scalar2=None, op0=ALU.mult),
                         R=[pt.d, sm.d], W=[Bw.d])
                    pu = ps_up.next()
                    P.op(PE, lambda e: e.matmul(pu[0:64, 0:64], lhsT=Bw[:, :], rhs=xT[h][:, c, :], start=True, stop=True), R=[Bw.d, xT[h].d], W=[pu.d])
                    P.op(DVE, lambda e: e.scalar_tensor_tensor(out=S[:, :], in0=S[:, :], scalar=sm[0:64, 3:4], in1=pu[0:64, 0:64],
                                                               op0=ALU.mult, op1=ALU.add), R=[S.d, pu.d, sm.d], W=[S.d])

                for h in range(4):
                    g = h // 2
                    Sf, Sb = S_
                    if kind == 's':
                        P.dma(SP, Sf[:, :], st_ssd[l, 0, h], W=[Sf.d])
                        P.dma(SP, Sb[:, :], st_ssd[l, 1, h], W=[Sb.d])
                    else:
                        P.op(DVE, lambda e: e.memset(Sf[:, :], 0.0), W=[Sf.d])
                        P.op(DVE, lambda e: e.memset(Sb[:, :], 0.0), W=[Sb.d])
                    for c in range(nch - 1, -1, -1):
                        P.op(ACT, lambda e: e.activation(out=Sbs[:, c, :], in_=Sb[:, :], func=AF.Copy), R=[Sb.d], W=[Sbs.d])
                        cc = chunk_cum(c)
                        pr_, sm = dirstuff(c, h, 1, cc)
                        state_update(Sb, c, h, g, sm)
                    if kind == 'p':
                        P.dma(SP, o_ssd[sidx, l, 1, h], Sb[:, :], R=[Sb.d], W=[dOUT])
                    for c0 in range(0, nch, 4):
                        ncg = min(4, nch - c0)
                        n = ncg * 128
                        py = ps_y.next()
                        for ci in range(ncg):
                            c = c0 + ci
                            sl = slice(c * 128, (c + 1) * 128)
                            cc = chunk_cum(c)
                            psc = ps_sc.next()
                            P.op(PE, lambda e: e.matmul(psc[:, 0:128], lhsT=Bb[g][:, sl], rhs=Cb[g][:, sl], start=True, stop=True),
                                 R=[Bb[g].d, Cb[g].d], W=[psc.d])
                            qds = []
                            sms = []
                            for d in range(2):
                                col = d * 4 + h
                                pr_, sm = dirstuff(c, h, d, cc)
                                sms.append(sm)
                                arg = arg_r.next()
                                P.op(DVE, lambda e: e.scalar_tensor_tensor(out=arg[:, :], in0=pr_[:, 0:128], scalar=cc[:, col:col + 1],
                                                                           in1=C('nm_f' if d == 0 else 'nm_b'), op0=ALU.subtract, op1=ALU.add),
                                     R=[pr_.d, cc.d, cp.d], W=[arg.d])
                                P.op(ACT, lambda e: e.activation(out=Dm[d][:, :], in_=arg[:, :], func=AF.Exp), R=[arg.d], W=[Dm[d].d])
                                ecr = ecr_r.next()
                                P.op(ACT, lambda e: e.activation(out=ecr[:, :], in_=pr_[0:64, 0:128], func=AF.Exp), R=[pr_.d], W=[ecr.d])
                                qd = qd_r.next()
                                P.op(DVE, lambda e: e.tensor_tensor(out=qd[:, :], in0=Cb[g][:, sl], in1=ecr[:, :], op=ALU.mult),
                                     R=[Cb[g].d, ecr.d], W=[qd.d])
                                qds.append(qd)
                            P.op(DVE, lambda e: e.tensor_scalar(out=Ds[:, :], in0=Dm[0][:, :], scalar1=dts[:, c, h:h + 1], scalar2=None, op0=ALU.mult),
                                 R=[Dm[0].d, dts.d], W=[Ds.d])
                            P.op(DVE, lambda e: e.scalar_tensor_tensor(out=Ds[:, :], in0=Dm[1][:, :], scalar=dts[:, c, 4 + h:5 + h], in1=Ds[:, :],
                                                                       op0=ALU.mult, op1=ALU.add), R=[Dm[1].d, dts.d, Ds.d], W=[Ds.d])
                            PT = PT_r.next()
                            P.op(DVE, lambda e: e.tensor_tensor(out=PT[:, :], in0=psc[:, 0:128], in1=Ds[:, :], op=ALU.mult), R=[psc.d, Ds.d], W=[PT.d])
                            P.op(ACT, lambda e: e.activation(out=Sfb[:, :], in_=Sf[:, :], func=AF.Copy), R=[Sf.d], W=[Sfb.d])
                            yo = py[0:64, ci * 128:(ci + 1) * 128]
                            P.op(PE, lambda e: e.matmul(yo, lhsT=xT[h][:, c, :], rhs=PT[:, :], start=True, stop=False), R=[xT[h].d, PT.d], W=[py.d])
                            P.op(PE, lambda e: e.matmul(yo, lhsT=Sfb[:, :], rhs=qds[0][:, :], start=False, stop=False), R=[Sfb.d, qds[0].d], W=[py.d])
                            P.op(PE, lambda e: e.matmul(yo, lhsT=Sbs[:, c, :], rhs=qds[1][:, :], start=False, stop=True), R=[Sbs.d, qds[1].d], W=[py.d])
                            state_update(Sf, c, h, g, sms[0])
                        tsl = slice(c0 * 128, c0 * 128 + n)
                        r0 = 18 * 128 + h * 64
                        P.dma(SP, zt_[:, 0:n], FT[r0:r0 + 64, off + c0 * 128: off + c0 * 128 + n], R=[dFT[si]], W=[zt_.d])
                        P.op(ACT, lambda e: e.activation(out=zt_[:, 0:n], in_=zt_[:, 0:n], func=AF.Silu), R=[zt_.d], W=[zt_.d])
                        P.op(DVE, lambda e: e.scalar_tensor_tensor(out=t5[:, 0:n], in0=xcf[h][:, tsl], scalar=pp[l][0:64, od4 + h:od4 + h + 1],
                                                                   in1=py[0:64, 0:n], op0=ALU.mult, op1=ALU.add), R=[xcf[h].d, pp[l].d, py.d], W=[t5.d])
                        P.op(DVE, lambda e: e.tensor_tensor(out=yz[h][:, tsl], in0=t5[:, 0:n], in1=zt_[:, 0:n], op=ALU.mult),
                             R=[t5.d, zt_.d], W=[yz[h].d])
                    if kind == 'p':
                        P.dma(SP, o_ssd[sidx, l, 0, h], Sf[:, :], R=[Sf.d], W=[dOUT])
                nq = min(512, Tn)
                for q0 in range(0, Tn, nq):
                    n = nq
                    tsl = slice(q0, q0 + n)
                    pss = ps_cc.next()
                    for h in range(4):
                        P.op(ACT, lambda e: e.activation(out=t5[:, 0:n], in_=yz[h][:, tsl], func=AF.Square), R=[yz[h].d], W=[t5.d])
                        P.op(PE, lambda e: e.matmul(pss[0:64, 0:n], lhsT=C('ones', rows=64, sub=(0, 64)), rhs=t5[:, 0:n], start=(h == 0), stop=(h == 3)),
                             R=[t5.d, cp.d], W=[pss.d])
                    P.op(ACT, lambda e: e.activation(out=rsy[:, 0:n], in_=pss[0:64, 0:n], func=AF.Sqrt, bias=C('eps', rows=64), scale=1.0 / 256),
                         R=[pss.d, cp.d], W=[rsy.d])
                    P.op(DVE, lambda e: e.reciprocal(out=rsy[:, 0:n], in_=rsy[:, 0:n]), R=[rsy.d], W=[rsy.d])
                    for h in range(4):
                        oy = oy_r.next()
                        P.op(DVE, lambda e: e.scalar_tensor_tensor(out=oy[:, 0:n], in0=yz[h][:, tsl], scalar=pp[l][0:64, og4 + h:og4 + h + 1],
                                                                   in1=rsy[:, 0:n], op0=ALU.mult, op1=ALU.mult), R=[yz[h].d, pp[l].d, rsy.d], W=[oy.d])
                        P.dma(ACT, OT[512 + h * 64:512 + (h + 1) * 64, off + q0: off + q0 + n], oy[:, 0:n], R=[oy.d], W=[dOT[si]])

        def mixer_ret(l):
            ropc = T(P.sb("ropc", [64, TS]))
            rops = T(P.sb("rops", [64, TS]))
            P.dma(SP, ropc[:, :], rope_in[0, 0:64, :], W=[ropc.d])
            P.dma(SP, rops[:, :], rope_in[1, 0:64, :], W=[rops.d])
            qf32 = T(P.sb("qf32", [64, TS]))
            qs32 = T(P.sb("qs32", [64, TS]))
            qb = T(P.sb("qb", [64, TS], BF16))
            kb = T(P.sb("kb", [64, TS], BF16))
            g32 = T(P.sb("g32", [64, TS]))
            v32 = T(P.sb("v32", [128, TS // 128, 64]))
            vb = T(P.sb("vb", [128, TS // 128, 64], BF16))
            Sf = T(P.sb("Sf", [64, 64]))
            Sb = T(P.sb("Sb", [64, 64]))
            Sfb = T(P.sb("Sfb", [64, 64], BF16))
            Sbs = T(P.sb("Sbs", [64, TS // 128, 64], BF16))
            kw_r = Ring([T(P.sb("kw", [128, 64], BF16)) for _ in range(2)])
            PT_r = Ring([T(P.sb("PT", [128, 128], BF16)) for _ in range(2)])
            qd_r = Ring([T(P.sb("qd", [64, 128], BF16)) for _ in range(4)])
            sqy = T(P.sb("sqy", [64, 512]))
            rsy = T(P.sb("rsy", [64, 512]))
            sg = T(P.sb("sg", [64, 512]))
            tmy = T(P.sb("tmy", [64, 512]))
            oy_r = Ring([T(P.sb("oy", [64, 512], OTDT)) for _ in range(2)])
            ps_sc = Ring([TPS(P.psum("ps_sc", [128, 512])) for _ in range(2)])
            ps_tr = Ring([TPS(P.psum("ps_tr", [128, 1024], BF16)) for _ in range(1)])
            ps_up = Ring([TPS(P.psum("ps_up", [128, 512])) for _ in range(1)])
            ps_y = Ring([TPS(P.psum("ps_y", [128, 512])) for _ in range(2)])
            ps_ss = Ring([TPS(P.psum("ps_ss", [128, 512])) for _ in range(1)])
            og4, _ = PPC['ret_g4']
            for si, (kind, sidx, Tn, off) in enumerate(SEQS):
                nch = Tn // 128
                for h in range(4):
                    pr = (h % 2) * 64

                    def rows(cbase):
                        r0 = (cbase + h // 2) * 128 + pr
                        return FT[r0:r0 + 64, off:off + Tn]
                    for (dst, cb, cbs, scale) in ((qb, 8, 12, 1.0), (kb, 10, 14, 0.125)):
                        P.dma(SP, qf32[:, 0:Tn], rows(cb), R=[dFT[si]], W=[qf32.d])
                        if kind == 's':
                            P.dma(SP, qs32[:, 0:Tn], rows(cbs), R=[dFT[si]], W=[qs32.d])
                            P.op(DVE, lambda e: e.tensor_tensor(out=qf32[:, 0:Tn], in0=qf32[:, 0:Tn], in1=ropc[:, 0:Tn], op=ALU.mult),
                                 R=[qf32.d, ropc.d], W=[qf32.d])
                            P.op(DVE, lambda e: e.tensor_tensor(out=qs32[:, 0:Tn], in0=qs32[:, 0:Tn], in1=rops[:, 0:Tn], op=ALU.mult),
                                 R=[qs32.d, rops.d], W=[qs32.d])
                            P.op(DVE, lambda e: e.tensor_tensor(out=qf32[:, 0:Tn], in0=qf32[:, 0:Tn], in1=qs32[:, 0:Tn], op=ALU.add),
                                 R=[qf32.d, qs32.d], W=[qf32.d])
                        P.op(ACT, lambda e, dst=dst, scale=scale: e.activation(out=dst[:, 0:Tn], in_=qf32[:, 0:Tn], func=AF.Copy, scale=scale),
                             R=[qf32.d], W=[dst.d])
                    P.dma(SP, g32[:, 0:Tn], rows(16), R=[dFT[si]], W=[g32.d])
                    P.dma(SP, v32[:, 0:nch, :], FTM[off:off + Tn, h * 64:(h + 1) * 64].rearrange("(c p) e -> p c e", p=128),
                          R=[dFTM[si]], W=[v32.d])
                    P.op(DVE, lambda e: e.tensor_copy(out=vb[:, 0:nch, :], in_=v32[:, 0:nch, :]), R=[v32.d], W=[vb.d])
                    if kind == 's':
                        P.dma(SP, Sf[:, :], st_ret[l, 0, h], W=[Sf.d])
                        P.dma(SP, Sb[:, :], st_ret[l, 1, h], W=[Sb.d])
                    else:
                        P.op(DVE, lambda e: e.memset(Sf[:, :], 0.0), W=[Sf.d])
                        P.op(DVE, lambda e: e.memset(Sb[:, :], 0.0), W=[Sb.d])

                    def state_update(S, c, kwcol, gcol):
                        pt = ps_tr.next()
                        P.op(PE, lambda e: e.transpose(pt[:, 0:64], kb[:, c * 128:(c + 1) * 128], cpb[0:64, 0:64]),
                             R=[kb.d, cpb.d], W=[pt.d])
                        kw = kw_r.next()
                        P.op(DVE, lambda e: e.tensor_scalar(out=kw[:, :], in0=pt[:, 0:64], scalar1=kwcol, scalar2=None, op0=ALU.mult),
                             R=[pt.d, cp.d], W=[kw.d])
                        pu = ps_up.next()
                        P.op(PE, lambda e: e.matmul(pu[0:64, 0:64], lhsT=kw[:, :], rhs=vb[:, c, :], start=True, stop=True),
                             R=[kw.d, vb.d], W=[pu.d])
                        P.op(DVE, lambda e: e.scalar_tensor_tensor(out=S[:, :], in0=S[:, :], scalar=gcol, in1=pu[0:64, 0:64],
                                                                   op0=ALU.mult, op1=ALU.add), R=[S.d, pu.d, cp.d], W=[S.d])
                    okf, _ = CPC['ret_kwf']
                    okb, _ = CPC['ret_kwb']
                    og, _ = CPC['ret_g128']
                    for c in range(nch - 1, -1, -1):
                        P.op(ACT, lambda e, c=c: e.activation(out=Sbs[:, c, :], in_=Sb[:, :], func=AF.Copy), R=[Sb.d], W=[Sbs.d])
                        state_update(Sb, c, cp[:, okb + h:okb + h + 1], cp[0:64, og + 4 + h:og + 5 + h])
                    if kind == 'p':
                        P.dma(SP, o_ret[sidx, l, 1, h], Sb[:, :], R=[Sb.d], W=[dOUT])
                    for c0 in range(0, nch, 4):
                        ncg = min(4, nch - c0)
                        n = ncg * 128
                        py = ps_y.next()
                        for cc in range(ncg):
                            c = c0 + cc
                            sl = slice(c * 128, (c + 1) * 128)
                            psc = ps_sc.next()
                            P.op(PE, lambda e: e.matmul(psc[:, 0:128], lhsT=kb[:, sl], rhs=qb[:, sl], start=True, stop=True),
                                 R=[kb.d, qb.d], W=[psc.d])
                            PT = PT_r.next()
                            P.op(DVE, lambda e: e.tensor_tensor(out=PT[:, :], in0=psc[:, 0:128], in1=C('ret_ds', sub=(h * 128, (h + 1) * 128)), op=ALU.mult),
                                 R=[psc.d, cp.d], W=[PT.d])
                            qfw = qd_r.next()
                            qbw = qd_r.next()
                            P.op(DVE, lambda e: e.tensor_tensor(out=qfw[:, :], in0=qb[:, sl], in1=C('ret_df', rows=64, sub=(h * 128, (h + 1) * 128)), op=ALU.mult),
                                 R=[qb.d, cp.d], W=[qfw.d])
                            P.op(DVE, lambda e: e.tensor_tensor(out=qbw[:, :], in0=qb[:, sl], in1=C('ret_db', rows=64, sub=(h * 128, (h + 1) * 128)), op=ALU.mult),
                                 R=[qb.d, cp.d], W=[qbw.d])
                            P.op(ACT, lambda e: e.activation(out=Sfb[:, :], in_=Sf[:, :], func=AF.Copy), R=[Sf.d], W=[Sfb.d])
                            yo = py[0:64, cc * 128:(cc + 1) * 128]
                            P.op(PE, lambda e: e.matmul(yo, lhsT=vb[:, c, :], rhs=PT[:, :], start=True, stop=False), R=[vb.d, PT.d], W=[py.d])
                            P.op(PE, lambda e: e.matmul(yo, lhsT=Sfb[:, :], rhs=qfw[:, :], start=False, stop=False), R=[Sfb.d, qfw.d], W=[py.d])
                            P.op(PE, lambda e: e.matmul(yo, lhsT=Sbs[:, c, :], rhs=qbw[:, :], start=False, stop=True), R=[Sbs.d, qbw.d], W=[py.d])
                            state_update(Sf, c, cp[:, okf + h:okf + h + 1], cp[0:64, og + h:og + h + 1])
                        tsl = slice(c0 * 128, c0 * 128 + n)
                        P.op(ACT, lambda e: e.activation(out=sqy[:, 0:n], in_=py[0:64, 0:n], func=AF.Square), R=[py.d], W=[sqy.d])
                        pss = ps_ss.next()
                        P.op(PE, lambda e: e.matmul(pss[0:64, 0:n], lhsT=C('ones', rows=64, sub=(0, 64)), rhs=sqy[:, 0:n], start=True, stop=True),
                             R=[sqy.d, cp.d], W=[pss.d])
                        P.op(ACT, lambda e: e.activation(out=rsy[:, 0:n], in_=pss[0:64, 0:n], func=AF.Sqrt, bias=C('eps', rows=64), scale=1.0 / 64),
                             R=[pss.d, cp.d], W=[rsy.d])
                        P.op(DVE, lambda e: e.reciprocal(out=rsy[:, 0:n], in_=rsy[:, 0:n]), R=[rsy.d], W=[rsy.d])
                        P.op(ACT, lambda e: e.activation(out=sg[:, 0:n], in_=g32[:, tsl], func=AF.Silu), R=[g32.d], W=[sg.d])
                        P.op(DVE, lambda e: e.scalar_tensor_tensor(out=tmy[:, 0:n], in0=py[0:64, 0:n], scalar=pp[l][0:64, og4 + h:og4 + h + 1],
                                                                   in1=rsy[:, 0:n], op0=ALU.mult, op1=ALU.mult),
                             R=[py.d, pp[l].d, rsy.d], W=[tmy.d])
                        oy = oy_r.next()
                        P.op(DVE, lambda e: e.tensor_tensor(out=oy[:, 0:n], in0=tmy[:, 0:n], in1=sg[:, 0:n], op=ALU.mult),
                             R=[tmy.d, sg.d], W=[oy.d])
                        P.dma(ACT, OT[256 + h * 64:256 + (h + 1) * 64, off + c0 * 128: off + c0 * 128 + n], oy[:, 0:n], R=[oy.d], W=[dOT[si]])
                    if kind == 'p':
                        P.dma(SP, o_ret[sidx, l, 0, h], Sf[:, :], R=[Sf.d], W=[dOUT])

        for l in range(L):
            if debug == 'M':
                break
            xin, xout = XT[0], XT[1]
            dxin, dxout = dXT[0], dXT[1]
            P.begin()
            wfm = T(P.sb("wfm", [128, 8, NFM * 128], BF16))
            wtm = T(P.sb("wtm", [128, 8, NTM], BF16))
            for k in range(8):
                P.dma(POOL, wfm[:, k, :], w_in_fm[l, k * 128:(k + 1) * 128, :], W=[wfm.d])
            P.dma(POOL, wtm[:, :, :], w_in_tm[l].rearrange("(k p) n -> p k n", p=128), W=[wtm.d])
            xt_r = Ring([T(P.sb("xt", [128, 8, 512])) for _ in range(2)])
            xtm_r = Ring([T(P.sb("xtm", [128, 1024])) for _ in range(2)])
            sq = Ring([T(P.sb("sq", [128, 512])) for _ in range(2)])
            rs = T(P.sb("rs", [128, 512]))
            tmp_r = Ring([T(P.sb("tmp", [128, 512])) for _ in range(2)])
            hT_r = Ring([T(P.sb("hT", [128, 8, 512], BF16)) for _ in range(2)])
            stg_r = Ring([T(P.sb("stg", [128, 4, 512])) for _ in range(2)])
            stm_r = Ring([T(P.sb("stm", [128, NTM])) for _ in range(2)])
            ps_r = Ring([TPS(P.psum("psA", [128, 512])) for _ in range(7)])
            for si, (kind, sidx, Tn, off) in enumerate(SEQS):
                if debug == 'A1':
                    break
                cond = 0 if kind == 'p' else 1
                nt = min(512, Tn)
                for t0 in range(0, Tn, nt):
                    n = nt
                    xt = xt_r.next()
                    if l == 0:
                        for b in range(n // 128):
                            xtm = xtm_r.next()
                            P.dma(SP, xtm[:, :], x_tm[off + t0 + b * 128: off + t0 + (b + 1) * 128, :], W=[xtm.d])
                            for half in range(2):
                                ps = ps_r.next()
                                for jj in range(4):
                                    j = half * 4 + jj
                                    P.op(PE, lambda e, j=j, jj=jj, ps=ps, xtm=xtm: e.transpose(
                                        ps[:, jj * 128:(jj + 1) * 128], xtm[:, j * 128:(j + 1) * 128], C('ident')),
                                        R=[xtm.d, cp.d], W=[ps.d])
                                evac(xt[:, half * 4:half * 4 + 4, b * 128:(b + 1) * 128],
                                     ps[:, 0:512].rearrange("p (j t) -> p j t", t=128), [ps.d], [xt.d])
                        P.dma(POOL, xin[:, off + t0: off + t0 + n].rearrange("(j p) t -> p j t", p=128), xt[:, :, 0:n],
                              R=[xt.d], W=[dxin[si]])
                    else:
                        P.dma(SP, xt[:, :, 0:n], xin[:, off + t0: off + t0 + n].rearrange("(j p) t -> p j t", p=128),
                              R=[dxin[si]], W=[xt.d])
                    if debug == 'A2':
                        continue
                    hT = hT_r.next()
                    rmsnorm_tile(xt, n, l, 0, cond, hT, sq, ps_r, rs, tmp_r)
                    if debug == 'A3':
                        continue
                    chunks = list(range(NFM))
                    if kind == 'p':
                        chunks = [c for c in chunks if c not in (12, 13, 14, 15, 28, 29, 30, 31)]
                    groups = [chunks[i:i + 4] for i in range(0, len(chunks), 4)]
                    for grp in groups:
                        runs = []
                        for c in grp:
                            if runs and runs[-1][-1] == c - 1:
                                runs[-1].append(c)
                            else:
                                runs.append([c])
                        for run_ in runs:
                            stg = stg_r.next()
                            for ci, c in enumerate(run_):
                                ps = ps_r.next()
                                for k in range(8):
                                    P.op(PE, lambda e, k=k, c=c, ps=ps, hT=hT: e.matmul(
                                        ps[:, 0:n], lhsT=wfm[:, k, c * 128:(c + 1) * 128], rhs=hT[:, k, 0:n],
                                        start=(k == 0), stop=(k == 7)), R=[wfm.d, hT.d], W=[ps.d])
                                evac(stg[:, ci, 0:n], ps[:, 0:n], [ps.d], [stg.d])
                            c0, nc_ = run_[0], len(run_)
                            P.dma(ACT, FT[c0 * 128:(c0 + nc_) * 128, off + t0: off + t0 + n].rearrange("(c p) t -> p c t", p=128),
                                  stg[:, 0:nc_, 0:n], R=[stg.d], W=[dFT[si]])
                    if debug == 'A4':
                        continue
                    for b in range(n // 128):
                        stm = stm_r.next()
                        for (c0, c1) in ((0, 512), (512, NTM)):
                            ps = ps_r.next()
                            for k in range(8):
                                P.op(PE, lambda e, k=k, ps=ps, hT=hT, b=b, c0=c0, c1=c1: e.matmul(
                                    ps[:, 0:c1 - c0], lhsT=hT[:, k, b * 128:(b + 1) * 128], rhs=wtm[:, k, c0:c1],
                                    start=(k == 0), stop=(k == 7)), R=[wtm.d, hT.d], W=[ps.d])
                            evac(stm[:, c0:c1], ps[:, 0:c1 - c0], [ps.d], [stm.d])
                        r0 = off + t0 + b * 128
                        P.dma(ACT, FTM[r0:r0 + 128, :], stm[:, :], R=[stm.d], W=[dFTM[si]])
                        if kind == 'p' and debug != 'A5':
                            tl = t0 + b * 128
                            P.dma(ACT, o_v[sidx, l, tl:tl + 128, :], stm[:, 256:512], R=[stm.d], W=[Dep()] if debug == 'A6' else [dOUT])
                            P.dma(ACT, o_k[sidx, l, tl:tl + 128, :], stm[:, 512:768], R=[stm.d], W=[Dep()] if debug == 'A6' else [dOUT])
            P.end()
            if debug and debug[0] == 'A':
                break

            if not (debug and debug[0] == 'C'):
                phase_B(l)
            if debug == 'B':
                break

            P.begin()
            wo = T(P.sb("wo", [128, 8, D], BF16))
            P.dma(POOL, wo[:, :, :], w_out[l].rearrange("(k p) n -> p k n", p=128), W=[wo.d])
            xt_r = Ring([T(P.sb("xt", [128, 8, 512])) for _ in range(2)])
            ot_r = Ring([T(P.sb("ot", [128, 8, 512], BF16)) for _ in range(2)])
            ps_r = Ring([TPS(P.psum("psC", [128, 512])) for _ in range(6)])
            for si, (kind, sidx, Tn, off) in enumerate(SEQS):
                cond = 0 if kind == 'p' else 1
                nt = min(512, Tn)
                for t0 in range(0, Tn, nt):
                    n = nt
                    xt = xt_r.next()
                    ot = ot_r.next()
                    P.dma(SP, xt[:, :, 0:n], xin[:, off + t0: off + t0 + n].rearrange("(j p) t -> p j t", p=128),
                          R=[dxin[si]], W=[xt.d])
                    P.dma(POOL if (debug and debug[0] == 'C') else SP, ot[:, :, 0:n], OT[:, off + t0: off + t0 + n].rearrange("(j p) t -> p j t", p=128),
                          R=[dOT[si]], W=[ot.d])
                    for j in range(8):
                        ps = ps_r.next()
                        for k in range(8):
                            P.op(PE, lambda e, k=k, j=j, ps=ps, ot=ot: e.matmul(
                                ps[:, 0:n], lhsT=wo[:, k, j * 128:(j + 1) * 128], rhs=ot[:, k, 0:n],
                                start=(k == 0), stop=(k == 7)), R=[wo.d, ot.d], W=[ps.d])
                        P.op(DVE, lambda e, j=j, ps=ps, xt=xt: e.scalar_tensor_tensor(
                            out=xt[:, j, 0:n], in0=ps[:, 0:n], scalar=modcol(l, 2, j, cond), in1=xt[:, j, 0:n],
                            op0=ALU.mult, op1=ALU.add), R=[ps.d, xt.d, modv[l].d], W=[xt.d])
                    P.dma(ACT, xout[:, off + t0: off + t0 + n].rearrange("(j p) t -> p j t", p=128), xt[:, :, 0:n],
                          R=[xt.d], W=[dxout[si]])
            P.end()
            if debug == 'C1':
                break
            if debug == 'CA':
                continue
            P.begin()
            wu = T(P.sb("wu", [128, 8, 2 * FFN], BF16))
            wd = T(P.sb("wd", [128, 22, D], BF16))
            for k in range(8):
                P.dma(POOL, wu[:, k, :], w_fup[l, k * 128:(k + 1) * 128, :], W=[wu.d])
            for g in range(22):
                P.dma(POOL, wd[:, g, :], w_fdn[l, g * 128:(g + 1) * 128, :], W=[wd.d])
            xt_r = Ring([T(P.sb("xt", [128, 8, 384])) for _ in range(1)])
            sq = Ring([T(P.sb("sq", [128, 384])) for _ in range(2)])
            rs = T(P.sb("rs", [128, 384]))
            tmp_r = Ring([T(P.sb("tmp", [128, 384])) for _ in range(1)])
            tg_r = Ring([T(P.sb("tg", [128, 384])) for _ in range(1)])
            tv_r = Ring([T(P.sb("tv", [128, 384])) for _ in range(1)])
            hT_r = Ring([T(P.sb("hT", [128, 8, 384], BF16)) for _ in range(1)])
            aT = T(P.sb("aT", [128, 22, 384], BF16))
            ps_r = Ring([TPS(P.psum("psC", [128, 384])) for _ in range(7)])
            ocw, _ = PPC['fcw']
            ocb, _ = PPC['fcb']
            for si, (kind, sidx, Tn, off) in enumerate(SEQS):
                cond = 0 if kind == 'p' else 1
                nt = min(382, Tn)
                for t0 in range(0, Tn, nt):
                    n = min(nt, Tn - t0)
                    N = n + 2
                    lo, hi = max(t0 - 1, 0), min(t0 + n + 1, Tn)
                    c_lo = lo - (t0 - 1)
                    xt = xt_r.next()
                    P.dma(SP, xt[:, :, c_lo:c_lo + hi - lo], xout[:, off + lo: off + hi].rearrange("(j p) t -> p j t", p=128),
                          R=[dxout[si]], W=[xt.d])
                    hT = hT_r.next()
                    rmsnorm_tile(xt, N, l, 1, cond, hT, sq, ps_r, rs, tmp_r)
                    a0 = 1 if t0 == 0 else 0
                    b1 = n - 1 if t0 + n == Tn else n
                    for g in range(22):
                        tt = []
                        for ci, (c, ring) in enumerate(((g, tg_r), (22 + g, tv_r))):
                            ps = ps_r.next()
                            for k in range(8):
                                P.op(PE, lambda e, k=k, c=c, ps=ps, hT=hT: e.matmul(
                                    ps[:, 0:N], lhsT=wu[:, k, c * 128:(c + 1) * 128], rhs=hT[:, k, 0:N],
                                    start=(k == 0), stop=(k == 7)), R=[wu.d, hT.d], W=[ps.d])
                            t = ring.next()
                            P.op(ACT, lambda e, c=c, ps=ps, t=t: e.activation(
                                out=t[:, 0:n], in_=ps[:, 1:n + 1], func=AF.Identity,
                                bias=pp[l][:, ocb + c:ocb + c + 1], scale=pp[l][:, ocw + 44 + c:ocw + 44 + c + 1]),
                                R=[ps.d, pp[l].d], W=[t.d])
                            P.op(DVE, lambda e, c=c, ps=ps, t=t: e.scalar_tensor_tensor(
                                out=t[:, a0:n], in0=ps[:, a0:n], scalar=pp[l][:, ocw + c:ocw + c + 1], in1=t[:, a0:n],
                                op0=ALU.mult, op1=ALU.add), R=[ps.d, pp[l].d, t.d], W=[t.d])
                            P.op(DVE, lambda e, c=c, ps=ps, t=t: e.scalar_tensor_tensor(
                                out=t[:, 0:b1], in0=ps[:, 2:b1 + 2], scalar=pp[l][:, ocw + 88 + c:ocw + 88 + c + 1], in1=t[:, 0:b1],
                                op0=ALU.mult, op1=ALU.add), R=[ps.d, pp[l].d, t.d], W=[t.d])
                            tt.append(t)
                        tg, tv = tt
                        P.op(ACT, lambda e, tg=tg: e.activation(out=tg[:, 0:n], in_=tg[:, 0:n], func=AF.Silu), R=[tg.d], W=[tg.d])
                        P.op(DVE, lambda e, tg=tg, tv=tv, g=g: e.tensor_tensor(out=aT[:, g, 0:n], in0=tg[:, 0:n], in1=tv[:, 0:n], op=ALU.mult),
                             R=[tg.d, tv.d], W=[aT.d])
                    for j in range(8):
                        ps = ps_r.next()
                        for g in range(22):
                            P.op(PE, lambda e, g=g, j=j, ps=ps: e.matmul(
                                ps[:, 0:n], lhsT=wd[:, g, j * 128:(j + 1) * 128], rhs=aT[:, g, 0:n],
                                start=(g == 0), stop=(g == 21)), R=[wd.d, aT.d], W=[ps.d])
                        P.op(DVE, lambda e, j=j, ps=ps, xt=xt: e.scalar_tensor_tensor(
                            out=xt[:, j, 1:n + 1], in0=ps[:, 0:n], scalar=modcol(l, 5, j, cond), in1=xt[:, j, 1:n + 1],
                            op0=ALU.mult, op1=ALU.add), R=[ps.d, xt.d, modv[l].d], W=[xt.d])
                    P.dma(ACT, xin[:, off + t0: off + t0 + n].rearrange("(j p) t -> p j t", p=128), xt[:, :, 1:n + 1],
                          R=[xt.d], W=[dxin[si]])
            P.end()
            if debug == 'C':
                break

        if not debug:
            xin, dxin = XT[0], dXT[0]
            P.begin()
            xt_r = Ring([T(P.sb("xt", [128, 8, 512])) for _ in range(2)])
            sq = Ring([T(P.sb("sq", [128, 512])) for _ in range(2)])
            rs = T(P.sb("rs", [128, 512]))
            ytm_r = Ring([T(P.sb("ytm", [128, 1024])) for _ in range(2)])
            ps_r = Ring([TPS(P.psum("psF", [128, 512])) for _ in range(6)])
            onf, _ = PPC['nfg']
            for si, (kind, sidx, Tn, off) in enumerate(SEQS):
                nt = min(512, Tn)
                for t0 in range(0, Tn, nt):
                    n = nt
                    xt = xt_r.next()
                    P.dma(SP, xt[:, :, 0:n], xin[:, off + t0: off + t0 + n].rearrange("(j p) t -> p j t", p=128),
                          R=[dxin[si]], W=[xt.d])
                    ps = ps_r.next()
                    for k in range(8):
                        sqk = sq.next()
                        P.op(ACT, lambda e, k=k, sqk=sqk, xt=xt: e.activation(out=sqk[:, 0:n], in_=xt[:, k, 0:n], func=AF.Square), R=[xt.d], W=[sqk.d])
                        P.op(PE, lambda e, k=k, sqk=sqk, ps=ps: e.matmul(ps[:, 0:n], lhsT=C('ones'), rhs=sqk[:, 0:n], start=(k == 0), stop=(k == 7)),
                             R=[sqk.d, cp.d], W=[ps.d])
                    P.op(ACT, lambda e, ps=ps: e.activation(out=rs[:, 0:n], in_=ps[:, 0:n], func=AF.Sqrt, bias=C('eps'), scale=1.0 / D),
                         R=[ps.d, cp.d], W=[rs.d])
                    P.op(DVE, lambda e: e.reciprocal(out=rs[:, 0:n], in_=rs[:, 0:n]), R=[rs.d], W=[rs.d])
                    for j in range(8):
                        P.op(DVE, lambda e, j=j, xt=xt: e.scalar_tensor_tensor(
                            out=xt[:, j, 0:n], in0=xt[:, j, 0:n], scalar=pp[0][:, onf + j:onf + j + 1], in1=rs[:, 0:n],
                            op0=ALU.mult, op1=ALU.mult), R=[xt.d, pp[0].d, rs.d], W=[xt.d])
                    for b in range(n // 128):
                        ytm = ytm_r.next()
                        for half in range(2):
                            ps = ps_r.next()
                            for jj in range(4):
                                j = half * 4 + jj
                                P.op(PE, lambda e, j=j, jj=jj, ps=ps, xt=xt, b=b: e.transpose(
                                    ps[:, jj * 128:(jj + 1) * 128], xt[:, j, b * 128:(b + 1) * 128], C('ident')),
                                    R=[xt.d, cp.d], W=[ps.d])
                            evac(ytm[:, half * 512:(half + 1) * 512], ps[:, 0:512], [ps.d], [ytm.d])
                        tl = t0 + b * 128
                        if kind == 'p':
                            P.dma(ACT, y_p[sidx * TP + tl: sidx * TP + tl + 128, :], ytm[:, :], R=[ytm.d], W=[dOUT])
                        else:
                            P.dma(ACT, y_s[tl:tl + 128, :], ytm[:, :], R=[ytm.d], W=[dOUT])
            P.end()
        P.begin()
        for k, v in list(P.dtot.items()):
            P.q[POOL].append(([(k, v)], None, None))
        P.end()
    return nc


def make_in_maps(inp):
    fm, tm = build_perm()
    w_in = np.asarray(inp['w_in'], np.float32)
    w_aug = np.concatenate([w_in, np.zeros((L, D, 1), np.float32)], axis=2)
    w_in_fm = np.ascontiguousarray(w_aug[:, :, np.where(fm < 0, w_in.shape[2], fm)])
    w_in_tm = np.ascontiguousarray(w_in[:, :, tm])
    pp = np.stack([pack_params(inp, l) for l in range(L)])
    cp = build_consts()
    rope = rope_tables()
    maps = []
    for c in range(8):
        b = c // 4
        x_tm = np.concatenate([np.asarray(inp['x_prompt'][4 * c:4 * c + 4], np.float32).reshape(NPS * TP, D),
                               np.asarray(inp['x_sample'][b], np.float32)], axis=0)
        cvec = np.concatenate([fm_cols(inp['c_ctx']), fm_cols(inp['c'][b])], axis=1)
        maps.append(dict(
            x_tm=np.ascontiguousarray(x_tm), cvec=np.ascontiguousarray(cvec), w_mod=np.asarray(inp['w_mod'], np.float32),
            w_in_fm=w_in_fm, w_in_tm=w_in_tm, w_out=np.asarray(inp['w_out'], np.float32),
            w_fup=np.asarray(inp['ffn_w_up'], np.float32), w_fdn=np.asarray(inp['ffn_w_down'], np.float32),
            pp=pp, cp=cp, rope=rope,
            st_rwkv=np.ascontiguousarray(inp['state_rwkv'][b]), st_ret=np.ascontiguousarray(inp['state_ret'][b]),
            st_ssd=np.ascontiguousarray(inp['state_ssd'][b]),
            ck=np.ascontiguousarray(np.asarray(inp['cache_diff_k'][b]).reshape(L, 512, 256)),
            cv=np.ascontiguousarray(np.asarray(inp['cache_diff_v'][b]).reshape(L, 512, 256)),
        ))
    return maps


def kernel(**inputs):
    inp = {k: np.asarray(v) for k, v in inputs.items()}
    nc = build()
    maps = make_in_maps(inp)
    res = run_bass_kernel_spmd(nc, maps, core_ids=list(range(8)))
    r = res.results
    y_prompt = np.concatenate([r[c]['y_p'].reshape(NPS, TP, D) for c in range(8)], axis=0)
    y_sample = np.stack([np.concatenate([r[b * 4 + j]['y_s'][j * 1024:(j + 1) * 1024] for j in range(4)], axis=0) for b in range(2)])
    o_rwkv = np.concatenate([r[c]['o_rwkv'] for c in range(8)], axis=0)
    o_ret = np.concatenate([r[c]['o_ret'] for c in range(8)], axis=0)
    o_ssd = np.concatenate([r[c]['o_ssd'] for c in range(8)], axis=0)
    o_k = np.concatenate([r[c]['o_k'] for c in range(8)], axis=0).reshape(32, L, TP, 4, 2, 32)
    o_v = np.concatenate([r[c]['o_v'] for c in range(8)], axis=0).reshape(32, L, TP, 4, 64)
    return (y_prompt.astype(np.float32), y_sample.astype(np.float32), o_rwkv.astype(np.float32), o_ret.astype(np.float32),
            o_ssd.astype(np.float32), o_k.astype(np.float32), o_v.astype(np.float32))
```

```python
import math
import numpy as np
from contextlib import ExitStack
import concourse.bass as bass
import concourse.mybir as mybir
from concourse.bass_utils import run_bass_kernel_spmd

F32 = mybir.dt.float32
BF16 = mybir.dt.bfloat16
ALU = mybir.AluOpType
AF = mybir.ActivationFunctionType
PE, ACT, DVE, POOL, SP = 'pe', 'act', 'dve', 'pool', 'sp'

D = 1024
L = 2
TP = 256
TS = 4096
NPS = 4
TTOT = NPS * TP + TS
SEQS = [('p', i, TP, i * TP) for i in range(NPS)] + [('s', 0, TS, NPS * TP)]
NFM = 32
NTM = 776
FFN = 2816
EPS = 1e-6
MIXERS = ('ret', 'diff', 'ssd', 'rwkv')
RWKV_STAGES = 3
RW_DBG = {'seqs': 5, 'heads': 4, 'stop': 99, 'var': 0}


class Dep:
    __slots__ = ('name', 'lw', 'rd', 'dsem', 'sb', 'ps')

    def __init__(self, name='', sb=False):
        self.name = name
        self.lw = None
        self.rd = {}
        self.dsem = None
        self.sb = sb
        self.ps = False


class Rec:
    def __init__(self):
        self.call = None

    def __getattr__(self, name):
        def f(*a, **kw):
            self.call = (name, a, kw)
        return f


class Prog:
    def __init__(self, nc, es):
        self.nc = nc
        self.es = es
        self.ph = None
        self.q = {PE: [], ACT: [], DVE: [], POOL: [], SP: []}
        self.cnt = {PE: 0, ACT: 0, DVE: 0, POOL: 0}
        self.seen = {k: {} for k in self.q}
        self.sems = {}
        self.ndsem = 0
        self.nwsem = 0
        for e in (PE, ACT, DVE, POOL):
            self.sems[e] = es.enter_context(nc.semaphore('s_' + e))
        self.dtot = {}
        self.nuniq = 0
        self.shared_dsem = None

    def sb(self, name, shape, dt=F32):
        self.nuniq += 1
        return self.ph.enter_context(self.nc.sbuf_tensor('%s_%d' % (name, self.nuniq), list(shape), dt))

    def psum(self, name, shape, dt=F32):
        self.nuniq += 1
        return self.ph.enter_context(self.nc.psum_tensor('%s_%d' % (name, self.nuniq), list(shape), dt))

    def dram(self, name, shape, dt=F32):
        return self.nc.dram_tensor(name, list(shape), dt, kind="Internal").ap()

    def _dsem(self, d, queue=None):
        if d.dsem is None:
            if queue == POOL:
                d.dsem = 'w%d' % self.nwsem
                self.nwsem += 1
            else:
                d.dsem = 'd%d' % self.ndsem
                self.ndsem += 1
            if d.dsem not in self.sems:
                self.sems[d.dsem] = self.es.enter_context(self.nc.semaphore(d.dsem))
        return d.dsem

    def _waits(self, eng, R, W):
        waits = {}

        def add(k, v):
            if k == eng and eng == PE:
                return
            if k in self.dtot:
                v = self.dtot[k]
            if waits.get(k, 0) < v:
                waits[k] = v
        for d in R:
            if d.lw is not None:
                add(*d.lw)
            if d.ps:
                for k, v in d.rd.items():
                    if k != eng:
                        add(k, v)
        for d in W:
            if d.lw is not None and d.lw[0] != eng:
                add(*d.lw)
            for k, v in d.rd.items():
                if k != eng:
                    add(k, v)
        out = []
        seen = self.seen[eng]
        for k, v in waits.items():
            if seen.get(k, 0) < v:
                seen[k] = v
                out.append((k, v))
        return out

    def op(self, eng, fn, R=(), W=()):
        waits = self._waits(eng, R, W)
        idx = self.cnt[eng]
        self.cnt[eng] = idx + 1
        rec = Rec()
        fn(rec)
        name, a, kw = rec.call

        def fn2(e, name=name, a=a, kw=kw):
            return getattr(e, name)(*a, **kw)
        self.q[eng].append((waits, fn2, (eng, 1)))
        for d in R:
            if d.rd.get(eng, 0) < idx + 1:
                d.rd[eng] = idx + 1
        for d in W:
            d.lw = (eng, idx + 1)
            d.rd = {}

    def dma(self, queue, out, in_, R=(), W=(), **kw):
        waits = self._waits(queue, R, W)
        tgt = (list(W) + list(R))
        d0 = None
        for d in tgt:
            if d.sb:
                d0 = d
                break
        if d0 is None:
            d0 = tgt[0]
        k0 = self._dsem(d0, queue)
        self.dtot[k0] = self.dtot.get(k0, 0) + 16
        tot = self.dtot[k0]

        def fn(e, out=out, in_=in_, kw=kw):
            return e.dma_start(out=out, in_=in_, **kw)
        self.q[queue].append((waits, fn, (k0, 16)))
        for d in W:
            d.lw = (k0, tot)
            d.rd = {}
        for d in R:
            if d.rd.get(k0, 0) < tot:
                d.rd[k0] = tot

    def barrier(self):
        allk = [(k, self.cnt[k]) for k in (PE, ACT, DVE, POOL)] + list(self.dtot.items())
        for eng in self.q:
            waits = []
            seen = self.seen[eng]
            for k, v in allk:
                if k == eng:
                    continue
                if seen.get(k, 0) < v:
                    seen[k] = v
                    waits.append((k, v))
            self.q[eng].append((waits, None, None))

    def begin(self):
        self.ph = ExitStack()
        self.ndsem = 8 if self.ndsem >= 8 else self.ndsem
        self.nwsem = 0

    def end(self, final=False):
        self.barrier()
        nc = self.nc
        q = self.q
        sems = self.sems

        with nc.Block() as block:
            def run(e, name):
                for waits, fn, inc in q[name]:
                    for k, v in waits:
                        e.wait_ge(sems[k], v)
                    if fn is not None:
                        ins = fn(e)
                        ins.then_inc(sems[inc[0]], inc[1])

            @block.tensor
            def _(e):
                run(e, PE)

            @block.scalar
            def _(e):
                run(e, ACT)

            @block.vector
            def _(e):
                run(e, DVE)

            @block.gpsimd
            def _(e):
                run(e, POOL)

            @block.sync
            def _(e):
                run(e, SP)
        for k in q:
            q[k] = []
        self.ph.close()
        self.ph = None


class T:
    def __init__(self, t, name=''):
        self.t = t
        self.d = Dep(name, sb=True)

    def __getitem__(self, idx):
        return self.t[idx]


def TPS(t):
    x = T(t)
    x.d.ps = True
    x.d.sb = False
    return x


class Ring:
    def __init__(self, tiles):
        self.tiles = tiles
        self.i = 0

    def next(self):
        t = self.tiles[self.i % len(self.tiles)]
        self.i += 1
        return t


def fm_cols(v):
    v = np.asarray(v, np.float32)
    return np.ascontiguousarray(v.reshape(-1, 128).T)


def build_perm():
    RW, RET, SSD, DIF = 0, 960, 1984, 2760
    fm = []
    fm += list(range(RW, RW + 768))
    fm += list(range(RW + 768, RW + 896))
    fm += list(range(RW + 896, RW + 960)) + [-1] * 64
    q = list(range(RET, RET + 256))
    k = list(range(RET + 256, RET + 512))

    def swap(cols, blk):
        out = []
        for i in range(0, len(cols), blk):
            b = cols[i:i + blk]
            out += b[blk // 2:] + b[:blk // 2]
        return out
    fm += q + k + swap(q, 64) + swap(k, 64)
    fm += list(range(RET + 768, RET + 1024))
    fm += list(range(SSD, SSD + 256))
    fm += list(range(SSD + 256, SSD + 768))
    dq = list(range(DIF, DIF + 256))
    dk = list(range(DIF + 256, DIF + 512))
    fm += dq + dk + swap(dq, 32) + swap(dk, 32)
    assert len(fm) == NFM * 128
    tm = list(range(RET + 512, RET + 768)) + list(range(DIF + 512, DIF + 768)) + dk + list(range(SSD + 768, SSD + 776))
    assert len(tm) == NTM
    return np.array(fm), np.array(tm)


PPC = {}
_o = 0
for _n, _w in [('n1g', 8), ('n2g', 8), ('bmod', 48), ('mu', 8), ('w0', 4), ('a0', 4), ('k_k', 2), ('k_a', 2),
               ('r_k', 2), ('ln_g', 2), ('ln_b', 2), ('ret_g', 2), ('scw', 12), ('scb', 4), ('ssd_g', 2), ('ssd_d', 2),
               ('dtb', 8), ('alog', 8), ('subg', 1), ('lp', 128), ('fcw', 132), ('fcb', 44), ('w_up', 256),
               ('a_up', 256), ('g_up', 256), ('nfg', 8), ('ret_g4', 4), ('scw64', 24), ('scb64', 8), ('ssd_d4', 4), ('ssd_g4', 4), ('mu64', 16), ('w0_64', 8), ('a0_64', 8), ('kk64', 4), ('ka64', 4), ('rk64', 4), ('lng64', 4), ('lnb64', 4)]:
    PPC[_n] = (_o, _w)
    _o += _w
NPP = _o


def pack_params(inp, l):
    pp = np.zeros((128, NPP), np.float32)

    def put(name, arr):
        o, w = PPC[name]
        arr = np.asarray(arr, np.float32)
        assert arr.shape[1] == w, (name, arr.shape, w)
        pp[:arr.shape[0], o:o + w] = arr
    put('n1g', fm_cols(inp['norm1_g'][l]))
    put('n2g', fm_cols(inp['norm2_g'][l]))
    put('bmod', fm_cols(inp['b_mod'][l]))
    mu = np.zeros(1024, np.float32)
    mu[:960] = inp['rwkv_mu'][l]
    put('mu', fm_cols(mu))
    put('w0', fm_cols(inp['rwkv_w0'][l].reshape(-1)))
    put('a0', fm_cols(inp['rwkv_a0'][l].reshape(-1)))
    put('k_k', fm_cols(inp['rwkv_k_k'][l]))
    put('k_a', fm_cols(inp['rwkv_k_a'][l]))
    put('r_k', fm_cols(inp['rwkv_r_k'][l].reshape(-1)))
    put('ln_g', fm_cols(inp['rwkv_ln_g'][l]))
    put('ln_b', fm_cols(inp['rwkv_ln_b'][l]))
    put('ret_g', fm_cols(inp['ret_ln_g'][l]))
    cw = inp['ssd_conv_w'][l]
    put('scw', np.concatenate([fm_cols(cw[i]) for i in range(3)], axis=1))
    put('scb', fm_cols(inp['ssd_conv_b'][l]))
    put('ssd_g', fm_cols(inp['ssd_norm_g'][l]))
    put('ssd_d', fm_cols(np.repeat(inp['ssd_d'][l], 64)))
    put('dtb', np.broadcast_to(inp['ssd_dt_bias'][l].reshape(1, 8), (128, 8)))
    put('alog', np.broadcast_to(inp['ssd_a_log'][l].reshape(1, 8), (128, 8)))
    put('subg', np.concatenate([inp['diff_subln_g'][l], inp['diff_subln_g'][l]]).reshape(128, 1))
    put('lp', np.broadcast_to(inp['diff_lambda'][l].reshape(1, 128), (128, 128)))
    fw = inp['ffn_conv_w'][l]
    put('fcw', np.concatenate([fm_cols(fw[i]) for i in range(3)], axis=1))
    put('fcb', fm_cols(inp['ffn_conv_b'][l]))
    put('w_up', inp['rwkv_w_up'][l].reshape(64, 256))
    put('a_up', inp['rwkv_a_up'][l].reshape(64, 256))
    put('g_up', inp['rwkv_g_up'][l].reshape(64, 256))
    put('nfg', fm_cols(inp['norm_f_g']))
    put('ret_g4', inp['ret_ln_g'][l].reshape(4, 64).T)
    put('scw64', np.concatenate([cw[i].reshape(8, 64).T for i in range(3)], axis=1))
    put('scb64', inp['ssd_conv_b'][l].reshape(8, 64).T)
    put('ssd_d4', np.broadcast_to(inp['ssd_d'][l].reshape(1, 4), (64, 4)))
    put('ssd_g4', inp['ssd_norm_g'][l].reshape(4, 64).T)
    put('mu64', mu.reshape(16, 64).T)
    put('w0_64', inp['rwkv_w0'][l].reshape(8, 64).T)
    put('a0_64', inp['rwkv_a0'][l].reshape(8, 64).T)
    put('kk64', inp['rwkv_k_k'][l].reshape(4, 64).T)
    put('ka64', inp['rwkv_k_a'][l].reshape(4, 64).T)
    put('rk64', inp['rwkv_r_k'][l].reshape(4, 64).T)
    put('lng64', inp['rwkv_ln_g'][l].reshape(4, 64).T)
    put('lnb64', inp['rwkv_ln_b'][l].reshape(4, 64).T)
    return pp


CPC = {}
_o = 0
for _n, _w in [('ident', 128), ('ones', 128), ('bd64', 128), ('tri_f', 128), ('tri_b', 128), ('nm_f', 128), ('nm_b', 128),
               ('ret_ds', 512), ('ret_df', 512), ('ret_db', 512), ('ret_kwf', 4), ('ret_kwb', 4), ('ret_g128', 8),
               ('eps', 1), ('rw_ms', 64), ('rw_mi', 64), ('rw_msT', 64), ('rw_miT', 64), ('scanmask', 512), ('rw_mask5', 320)]:
    CPC[_n] = (_o, _w)
    _o += _w
NCP = _o


def build_consts():
    cp = np.zeros((128, NCP), np.float32)

    def put(name, arr):
        o, w = CPC[name]
        arr = np.asarray(arr, np.float32)
        assert arr.shape[1] == w
        cp[:arr.shape[0], o:o + w] = arr
    i = np.arange(128)
    put('ident', np.eye(128))
    put('ones', np.ones((128, 128)))
    bd = np.zeros((128, 128))
    bd[:64, :64] = 1
    bd[64:, 64:] = 1
    put('bd64', bd)
    put('tri_f', (i[:, None] <= i[None, :]).astype(np.float32))
    put('tri_b', (i[:, None] >= i[None, :]).astype(np.float32))
    put('nm_f', np.where(i[None, :] >= i[:, None], 0.0, -30000.0))
    put('nm_b', np.where(i[None, :] <= i[:, None], 0.0, -30000.0))
    ds = np.zeros((128, 512))
    df = np.zeros((128, 512))
    db = np.zeros((128, 512))
    kwf = np.zeros((128, 4))
    kwb = np.zeros((128, 4))
    g128 = np.zeros((128, 8))
    for h in range(4):
        lf = math.log(1.0 - 2.0 ** (-5.0 - h))
        lb = math.log(1.0 - 2.0 ** (-5.5 - h))
        jj, ii = i[:, None], i[None, :]
        m = np.where(ii >= jj, np.exp(lf * (ii - jj)), 0.0) + np.where(ii <= jj, np.exp(lb * (jj - ii)), 0.0)
        ds[:, h * 128:(h + 1) * 128] = m
        df[:, h * 128:(h + 1) * 128] = np.exp(lf * (i + 1))[None, :]
        db[:, h * 128:(h + 1) * 128] = np.exp(lb * (128 - i))[None, :]
        kwf[:, h] = np.exp(lf * (127 - i))
        kwb[:, h] = np.exp(lb * i)
        g128[:, h] = math.exp(lf * 128)
        g128[:, 4 + h] = math.exp(lb * 128)
    put('ret_ds', ds)
    put('ret_df', df)
    put('ret_db', db)
    put('ret_kwf', kwf)
    put('ret_kwb', kwb)
    put('ret_g128', g128)
    put('eps', np.full((128, 1), EPS))
    c = np.arange(64)
    put('rw_ms', (c[None, :] < c[:, None]).astype(np.float32))
    put('rw_mi', (c[None, :] <= c[:, None]).astype(np.float32))
    put('rw_msT', (c[:, None] < c[None, :]).astype(np.float32))
    put('rw_miT', (c[:, None] <= c[None, :]).astype(np.float32))
    sm = np.ones((128, 512))
    sm[:, ::64] = 0.0
    put('scanmask', sm)
    msT = (c[:, None] < c[None, :]).astype(np.float32)
    miT = (c[:, None] <= c[None, :]).astype(np.float32)
    ms = (c[None, :] < c[:, None]).astype(np.float32)
    put('rw_mask5', np.concatenate([msT, miT, msT, miT, ms], axis=1))
    return cp


def rope_tables():
    rows = TS // 64

    def tabs(d):
        nf = d // 4
        row = np.repeat(np.arange(rows, dtype=np.float32), 64)
        col = np.tile(np.arange(64, dtype=np.float32), rows)
        freqs = (np.float32(10000.0) ** (-np.arange(nf, dtype=np.float32) / nf)).astype(np.float32)
        ang = np.concatenate([row[:, None] * freqs, col[:, None] * freqs], axis=-1).astype(np.float32)
        return np.cos(ang).T.astype(np.float32), np.sin(ang).T.astype(np.float32)
    c, s = tabs(64)
    ret_c = np.concatenate([c, c, c, c], 0)
    ret_s = np.concatenate([-s, s, -s, s], 0)
    c, s = tabs(32)
    dif_c = np.concatenate([c, c] * 4, 0)
    dif_s = np.concatenate([-s, s] * 4, 0)
    return np.stack([ret_c, ret_s, dif_c, dif_s]).astype(np.float32)


def build(debug=False):
    nc = bass.Bass("TRN2", target_bir_lowering=False)

    def din(name, shape):
        return nc.dram_tensor(name, list(shape), F32, kind="ExternalInput").ap()

    def dout(name, shape):
        return nc.dram_tensor(name, list(shape), F32, kind="ExternalOutput").ap()
    x_tm = din("x_tm", [TTOT, D])
    cvec = din("cvec", [128, 16])
    w_mod = din("w_mod", [L, D, 6 * D])
    w_in_fm = din("w_in_fm", [L, D, NFM * 128])
    w_in_tm = din("w_in_tm", [L, D, NTM])
    w_out = din("w_out", [L, D, D])
    w_fup = din("w_fup", [L, D, 2 * FFN])
    w_fdn = din("w_fdn", [L, FFN, D])
    pp_in = din("pp", [L, 128, NPP])
    cp_in = din("cp", [128, NCP])
    rope_in = din("rope", [4, 128, TS])
    st_rwkv = din("st_rwkv", [L, 2, 4, 64, 64])
    st_ret = din("st_ret", [L, 2, 4, 64, 64])
    st_ssd = din("st_ssd", [L, 2, 4, 64, 64])
    ck_in = din("ck", [L, 512, 256])
    cv_in = din("cv", [L, 512, 256])

    y_p = dout("y_p", [NPS * TP, D])
    y_s = dout("y_s", [TS, D])
    o_rwkv = dout("o_rwkv", [NPS, L, 2, 4, 64, 64])
    o_ret = dout("o_ret", [NPS, L, 2, 4, 64, 64])
    o_ssd = dout("o_ssd", [NPS, L, 2, 4, 64, 64])
    o_k = dout("o_k", [NPS, L, TP, 256])
    o_v = dout("o_v", [NPS, L, TP, 256])

    es = ExitStack()
    with es:
        P = Prog(nc, es)
        mk = (lambda n, s: dout(n, s)) if (debug and debug != 'B') else (lambda n, s: P.dram(n, s))
        XT = [mk("XT0", [D, TTOT]), mk("XT1", [D, TTOT])]
        FT = mk("FT", [NFM * 128, TTOT])
        FTM = mk("FTM", [TTOT, NTM])
        OTDT = F32 if debug == 'B' else BF16
        if debug == 'B':
            OT = nc.dram_tensor("OT_dbg", [D, TTOT], F32, kind="ExternalOutput").ap()
        elif debug and debug[0] == 'C':
            OT = nc.dram_tensor("OT_in", [D, TTOT], F32, kind="ExternalInput").ap()
        else:
            OT = nc.dram_tensor("OT", [D, TTOT], BF16, kind="Internal").ap()
        dXT = [[Dep() for _ in SEQS] for _ in range(2)]
        dFT = [Dep() for _ in SEQS]
        dFTM = [Dep() for _ in SEQS]
        dOT = [Dep() for _ in SEQS]
        dOUT = Dep('outs')
        RW = P.dram("RW", [11 * 256, TTOT])
        YS = P.dram("YS", [2 * 256, TTOT])
        dRW = [Dep() for _ in SEQS]
        dYS = [Dep() for _ in SEQS]

        P.ph = es
        cp = T(P.sb("cp", [128, NCP]))
        pp = [T(P.sb("pp%d" % l, [128, NPP])) for l in range(L)]
        modv = [T(P.sb("modv%d" % l, [128, 48, 2])) for l in range(L)]
        modA = [T(P.sb("modA%d" % l, [128, 2, 8, 2])) for l in range(L)]
        cpb = T(P.sb("cpb", [128, 640], BF16))
        P.ph = None

        def C(name, rows=128, sub=None):
            o, w = CPC[name]
            if sub is not None:
                return cp[0:rows, o + sub[0]:o + sub[1]]
            return cp[0:rows, o:o + w]

        def PPv(l, name, col=0, rows=128, ncol=1):
            o, w = PPC[name]
            return pp[l][0:rows, o + col:o + col + ncol]

        P.begin()
        P.dma(SP, cp[:, :], cp_in[:, :], W=[cp.d])
        for l in range(L):
            P.dma(SP, pp[l][:, :], pp_in[l], W=[pp[l].d])
        P.op(DVE, lambda e: e.tensor_copy(out=cpb[:, 0:384], in_=cp[:, 0:384]), R=[cp.d], W=[cpb.d])
        cv = T(P.sb("cv", [128, 16]))
        scv = T(P.sb("scv", [128, 8, 2]))
        P.dma(SP, cv[:, :], cvec[:, :], W=[cv.d])
        P.op(ACT, lambda e: e.activation(out=scv[:, :, 0], in_=cv[:, 0:8], func=AF.Silu), R=[cv.d], W=[scv.d])
        P.op(ACT, lambda e: e.activation(out=scv[:, :, 1], in_=cv[:, 8:16], func=AF.Silu), R=[cv.d], W=[scv.d])
        wm = Ring([T(P.sb("wm", [128, 8, 512])) for _ in range(2)])
        pm = TPS(P.psum("pm", [128, 512]))
        for l in range(L):
            for g in range(12):
                w = wm.next()
                P.dma(SP, w[:, :, :], w_mod[l, :, g * 512:(g + 1) * 512].rearrange("(k p) n -> p k n", p=128), W=[w.d])
                for cc in range(4):
                    ch = g * 4 + cc
                    for k in range(8):
                        P.op(PE, lambda e, w=w, cc=cc, k=k, ch=ch: e.matmul(
                            pm[:, ch * 2:ch * 2 + 2], lhsT=w[:, k, cc * 128:(cc + 1) * 128], rhs=scv[:, k, :],
                            start=(k == 0), stop=(k == 7)), R=[w.d, scv.d], W=[pm.d])
            o, _ = PPC['bmod']
            P.op(DVE, lambda e, l=l, o=o: e.tensor_tensor(
                out=modv[l][:, :, :], in0=pm[:, 0:96].rearrange("p (c t) -> p c t", t=2),
                in1=pp[l][:, o:o + 48].unsqueeze(2).to_broadcast([128, 48, 2]), op=ALU.add),
                R=[pm.d, pp[l].d], W=[modv[l].d])
            for n, (gname, which) in enumerate([('n1g', 1), ('n2g', 4)]):
                og, _ = PPC[gname]
                P.op(DVE, lambda e, l=l, n=n, og=og, which=which: e.scalar_tensor_tensor(
                    out=modA[l][:, n, :, :], in0=modv[l][:, which * 8:(which + 1) * 8, :], scalar=1.0,
                    in1=pp[l][:, og:og + 8].unsqueeze(2).to_broadcast([128, 8, 2]), op0=ALU.add, op1=ALU.mult),
                    R=[modv[l].d, pp[l].d], W=[modA[l].d])
        P.end()

        def modcol(l, which, j, cond):
            return modv[l][:, which * 8 + j, cond:cond + 1]

        def rmsnorm_tile(xt, n, l, which_norm, cond, hT, sq, ps_ring, rs, tmp_ring):
            ps = ps_ring.next()
            for k in range(8):
                sqk = sq.next()
                P.op(ACT, lambda e, k=k, sqk=sqk: e.activation(out=sqk[:, 0:n], in_=xt[:, k, 0:n], func=AF.Square), R=[xt.d], W=[sqk.d])
                P.op(PE, lambda e, k=k, sqk=sqk: e.matmul(ps[:, 0:n], lhsT=C('ones'), rhs=sqk[:, 0:n], start=(k == 0), stop=(k == 7)),
                     R=[sqk.d, cp.d], W=[ps.d])
            P.op(ACT, lambda e: e.activation(out=rs[:, 0:n], in_=ps[:, 0:n], func=AF.Sqrt, bias=C('eps'), scale=1.0 / D),
                 R=[ps.d, cp.d], W=[rs.d])
            P.op(DVE, lambda e: e.reciprocal(out=rs[:, 0:n], in_=rs[:, 0:n]), R=[rs.d], W=[rs.d])
            shift_which = 0 if which_norm == 0 else 3
            for j in range(8):
                tmp = tmp_ring.next()
                P.op(DVE, lambda e, j=j, tmp=tmp: e.scalar_tensor_tensor(
                    out=tmp[:, 0:n], in0=xt[:, j, 0:n], scalar=modA[l][:, which_norm, j, cond:cond + 1], in1=rs[:, 0:n],
                    op0=ALU.mult, op1=ALU.mult), R=[xt.d, modA[l].d, rs.d], W=[tmp.d])
                P.op(ACT, lambda e, j=j, tmp=tmp: e.activation(
                    out=hT[:, j, 0:n], in_=tmp[:, 0:n], func=AF.Identity, bias=modcol(l, shift_which, j, cond), scale=1.0),
                    R=[tmp.d, modv[l].d], W=[hT.d])

        evac_flip = [0]

        def evac(out_ap, in_ap, R, W):
            evac_flip[0] ^= 1
            if evac_flip[0]:
                P.op(ACT, lambda e: e.activation(out=out_ap, in_=in_ap, func=AF.Copy), R=R, W=W)
            else:
                P.op(DVE, lambda e: e.tensor_copy(out=out_ap, in_=in_ap), R=R, W=W)

        def phase_B(l):
            P.begin()
            zt = T(P.sb("zt", [128, 2048], OTDT))
            P.op(DVE, lambda e: e.memset(zt[:, :], 0.0), W=[zt.d])
            for si, (kind, sidx, Tn, off) in enumerate(SEQS):
                for r0 in ([] if 'rwkv' in MIXERS else [0, 128]) + ([] if 'ssd' in MIXERS else [512, 640]):
                    for t0 in range(0, Tn, 2048):
                        n = min(2048, Tn - t0)
                        P.dma(ACT, OT[r0:r0 + 128, off + t0:off + t0 + n], zt[:, 0:n], R=[zt.d], W=[dOT[si]])
            P.end()
            for name, fn in (('ret', mixer_ret), ('diff', mixer_diff), ('ssd', mixer_ssd), ('rwkv', mixer_rwkv)):
                if name in MIXERS:
                    P.begin()
                    fn(l)
                    P.end()

        def mixer_diff(l):
            lam_init = 0.8 - 0.6 * math.exp(-0.3 * l)
            AX = mybir.AxisListType.X
            ropc = T(P.sb("ropc", [64, TS]))
            rops = T(P.sb("rops", [64, TS]))
            P.dma(SP, ropc[:, :], rope_in[2, 0:64, :], W=[ropc.d])
            P.dma(SP, rops[:, :], rope_in[3, 0:64, :], W=[rops.d])
            qf32 = T(P.sb("qf32", [64, TS]))
            qs32 = T(P.sb("qs32", [64, TS]))
            qb = T(P.sb("qb", [64, TS], BF16))
            NKMAX = TS // 128 + 4
            kall = T(P.sb("kall", [64, NKMAX * 128], BF16))
            v32 = T(P.sb("v32", [128, NKMAX, 64]))
            vall = T(P.sb("vall", [128, NKMAX, 64], BF16))
            ck32 = T(P.sb("ck32", [128, 4, 64]))
            E_r = Ring([T(P.sb("E", [128, 512], BF16)) for _ in range(3)])
            rd = [T(P.sb("rd%d" % i, [64, 512])) for i in range(2)]
            o0 = T(P.sb("o0", [64, 512]))
            o1 = T(P.sb("o1", [64, 512]))
            sqy = T(P.sb("sqy", [64, 512]))
            rsy = T(P.sb("rsy", [64, 512]))
            oy_r = Ring([T(P.sb("oy", [64, 512], OTDT)) for _ in range(2)])
            lamt = T(P.sb("lamt", [128, 40]))
            ps_sc = Ring([TPS(P.psum("ps_sc", [128, 512])) for _ in range(2)])
            ps_den = [TPS(P.psum("ps_den%d" % i, [128, 512])) for i in range(2)]
            ps_num = [TPS(P.psum("ps_num%d" % i, [128, 512])) for i in range(2)]
            ps_x = Ring([TPS(P.psum("ps_x", [128, 512])) for _ in range(2)])
            olp, _ = PPC['lp']
            osg, _ = PPC['subg']
            for i in range(2):
                P.op(DVE, lambda e, i=i: e.tensor_tensor(out=lamt[:, 0:32], in0=pp[l][:, olp + 64 * i:olp + 64 * i + 32],
                                                         in1=pp[l][:, olp + 64 * i + 32:olp + 64 * i + 64], op=ALU.mult),
                     R=[pp[l].d], W=[lamt.d])
                P.op(DVE, lambda e, i=i: e.tensor_reduce(out=lamt[:, 32 + i:33 + i], in_=lamt[:, 0:32], axis=AX, op=ALU.add),
                     R=[lamt.d], W=[lamt.d])
            P.op(ACT, lambda e: e.activation(out=lamt[:, 34:36], in_=lamt[:, 32:34], func=AF.Exp), R=[lamt.d], W=[lamt.d])
            P.op(DVE, lambda e: e.scalar_tensor_tensor(out=lamt[:, 36:37], in0=lamt[:, 35:36], scalar=-lam_init, in1=lamt[:, 34:35],
                                                       op0=ALU.add, op1=ALU.subtract), R=[lamt.d], W=[lamt.d])
            scale = 32.0 ** -0.5
            for si, (kind, sidx, Tn, off) in enumerate(SEQS):
                nch = Tn // 128
                nk = nch + (4 if kind == 's' else 0)
                for h in range(4):
                    pr = (h % 2) * 64

                    def rows(cbase):
                        r0 = (cbase + h // 2) * 128 + pr
                        return FT[r0:r0 + 64, off:off + Tn]
                    for (dst, cb, cbs) in ((qb, 24, 28), (kall, 26, 30)):
                        P.dma(SP, qf32[:, 0:Tn], rows(cb), R=[dFT[si]], W=[qf32.d])
                        if kind == 's':
                            P.dma(SP, qs32[:, 0:Tn], rows(cbs), R=[dFT[si]], W=[qs32.d])
                            P.op(DVE, lambda e: e.tensor_tensor(out=qf32[:, 0:Tn], in0=qf32[:, 0:Tn], in1=ropc[:, 0:Tn], op=ALU.mult),
                                 R=[qf32.d, ropc.d], W=[qf32.d])
                            P.op(DVE, lambda e: e.tensor_tensor(out=qs32[:, 0:Tn], in0=qs32[:, 0:Tn], in1=rops[:, 0:Tn], op=ALU.mult),
                                 R=[qs32.d, rops.d], W=[qs32.d])
                            P.op(DVE, lambda e: e.tensor_tensor(out=qf32[:, 0:Tn], in0=qf32[:, 0:Tn], in1=qs32[:, 0:Tn], op=ALU.add),
                                 R=[qf32.d, qs32.d], W=[qf32.d])
                        P.op(ACT, lambda e, dst=dst: e.activation(out=dst[:, 0:Tn], in_=qf32[:, 0:Tn], func=AF.Copy),
                             R=[qf32.d], W=[dst.d])
                    P.dma(SP, v32[:, 0:nch, :], FTM[off:off + Tn, 256 + h * 64:256 + (h + 1) * 64].rearrange("(c p) e -> p c e", p=128),
                          R=[dFTM[si]], W=[v32.d])
                    if kind == 's':
                        P.dma(SP, v32[:, nch:nch + 4, :], cv_in[l, :, h * 64:(h + 1) * 64].rearrange("(c p) e -> p c e", p=128), W=[v32.d])
                        P.dma(SP, ck32[:, :, :], ck_in[l, :, h * 64:(h + 1) * 64].rearrange("(c p) e -> p c e", p=128), W=[ck32.d])
                        px = ps_x.next()
                        for c in range(4):
                            P.op(PE, lambda e, c=c: e.transpose(px[0:64, c * 128:(c + 1) * 128], ck32[:, c, :], C('ident')),
                                 R=[ck32.d, cp.d], W=[px.d])
                        P.op(ACT, lambda e: e.activation(out=kall[:, Tn:Tn + 512], in_=px[0:64, 0:512], func=AF.Copy), R=[px.d], W=[kall.d])
                    P.op(DVE, lambda e: e.tensor_copy(out=vall[:, 0:nk, :], in_=v32[:, 0:nk, :]), R=[v32.d], W=[vall.d])
                    nq = min(512, Tn)
                    for q0 in range(0, Tn, nq):
                        n = nq
                        qsl = slice(q0, q0 + n)
                        for kc in range(nk):
                            ksl = slice(kc * 128, (kc + 1) * 128)
                            for m in range(2):
                                psl = slice(m * 32, (m + 1) * 32)
                                psc = ps_sc.next()
                                P.op(PE, lambda e: e.matmul(psc[:, 0:n], lhsT=kall[psl, ksl], rhs=qb[psl, qsl], start=True, stop=True),
                                     R=[kall.d, qb.d], W=[psc.d])
                                E = E_r.next()
                                P.op(ACT, lambda e: e.activation(out=E[:, 0:n], in_=psc[:, 0:n], func=AF.Exp, scale=scale), R=[psc.d], W=[E.d])
                                P.op(PE, lambda e: e.matmul(ps_den[m][0:64, 0:n], lhsT=cpb[:, 128:192], rhs=E[:, 0:n], start=(kc == 0), stop=(kc == nk - 1)),
                                     R=[cpb.d, E.d], W=[ps_den[m].d])
                                P.op(PE, lambda e: e.matmul(ps_num[m][0:64, 0:n], lhsT=vall[:, kc, :], rhs=E[:, 0:n], start=(kc == 0), stop=(kc == nk - 1)),
                                     R=[vall.d, E.d], W=[ps_num[m].d])
                        for m in range(2):
                            P.op(DVE, lambda e, m=m: e.reciprocal(out=rd[m][:, 0:n], in_=ps_den[m][0:64, 0:n]), R=[ps_den[m].d], W=[rd[m].d])
                        P.op(DVE, lambda e: e.tensor_tensor(out=o0[:, 0:n], in0=ps_num[0][0:64, 0:n], in1=rd[0][:, 0:n], op=ALU.mult),
                             R=[ps_num[0].d, rd[0].d], W=[o0.d])
                        P.op(DVE, lambda e: e.tensor_tensor(out=o1[:, 0:n], in0=ps_num[1][0:64, 0:n], in1=rd[1][:, 0:n], op=ALU.mult),
                             R=[ps_num[1].d, rd[1].d], W=[o1.d])
                        P.op(DVE, lambda e: e.scalar_tensor_tensor(out=o0[:, 0:n], in0=o1[:, 0:n], scalar=lamt[0:64, 36:37], in1=o0[:, 0:n],
                                                                   op0=ALU.mult, op1=ALU.add), R=[o0.d, o1.d, lamt.d], W=[o0.d])
                        P.op(ACT, lambda e: e.activation(out=sqy[:, 0:n], in_=o0[:, 0:n], func=AF.Square), R=[o0.d], W=[sqy.d])
                        pss = ps_x.next()
                        P.op(PE, lambda e: e.matmul(pss[0:64, 0:n], lhsT=C('ones', rows=64, sub=(0, 64)), rhs=sqy[:, 0:n], start=True, stop=True),
                             R=[sqy.d, cp.d], W=[pss.d])
                        P.op(ACT, lambda e: e.activation(out=rsy[:, 0:n], in_=pss[0:64, 0:n], func=AF.Sqrt, bias=C('eps', rows=64), scale=1.0 / 64),
                             R=[pss.d, cp.d], W=[rsy.d])
                        P.op(DVE, lambda e: e.reciprocal(out=rsy[:, 0:n], in_=rsy[:, 0:n]), R=[rsy.d], W=[rsy.d])
                        P.op(DVE, lambda e: e.scalar_tensor_tensor(out=o1[:, 0:n], in0=o0[:, 0:n], scalar=pp[l][0:64, osg:osg + 1],
                                                                   in1=rsy[:, 0:n], op0=ALU.mult, op1=ALU.mult),
                             R=[o0.d, pp[l].d, rsy.d], W=[o1.d])
                        oy = oy_r.next()
                        P.op(ACT, lambda e: e.activation(out=oy[:, 0:n], in_=o1[:, 0:n], func=AF.Copy, scale=1.0 - lam_init), R=[o1.d], W=[oy.d])
                        P.dma(ACT, OT[768 + h * 64:768 + (h + 1) * 64, off + q0: off + q0 + n], oy[:, 0:n], R=[oy.d], W=[dOT[si]])

        def mixer_rwkv(l):
            rwkv_pre(l)
            if RWKV_STAGES >= 2:
                P.end()
                P.begin()
                rwkv_scan(l)
            if RWKV_STAGES >= 3:
                P.end()
                P.begin()
                rwkv_post(l)

        def rwkv_pre(l):
            buf_r = Ring([T(P.sb("rbuf", [64, 514])) for _ in range(3)])
            s1 = T(P.sb("rs1", [64, 512]))
            sh = [T(P.sb("rsh%d" % i, [64, 512])) for i in range(15)]
            twd = T(P.sb("twd", [64, 512]))
            sgd = T(P.sb("sgd", [64, 512]))
            a_d = [T(P.sb("a_d%d" % d, [64, 512])) for d in range(2)]
            o_r = Ring([T(P.sb("rwo", [64, 512])) for _ in range(4)])
            kkr = T(P.sb("kkr", [64, 512]))
            kk = T(P.sb("kk", [64, 512]))
            t1 = T(P.sb("rt1", [64, 512]))
            t2 = T(P.sb("rt2", [64, 512]))
            ps_r = Ring([TPS(P.psum("ps_rp", [128, 512])) for _ in range(4)])
            omu, _ = PPC['mu64']
            ow0, _ = PPC['w0_64']
            oa0, _ = PPC['a0_64']
            okk, _ = PPC['kk64']
            oka, _ = PPC['ka64']
            ork, _ = PPC['rk64']
            owu, _ = PPC['w_up']
            oau, _ = PPC['a_up']
            ogu, _ = PPC['g_up']
            for si, (kind, sidx, Tn, off) in enumerate(SEQS):
                nq = min(512, Tn)
                for q0 in range(0, Tn, nq):
                    n = nq
                    lo, hi = max(q0 - 1, 0), min(q0 + n + 1, Tn)
                    c_lo = lo - (q0 - 1)
                    a0 = 1 if q0 == 0 else 0
                    b1 = n - 1 if q0 + n == Tn else n

                    def store(arr, h, src):
                        r0 = arr * 256 + h * 64
                        P.dma(ACT, RW[r0:r0 + 64, off + q0: off + q0 + n], src[:, 0:n], R=[src.d], W=[dRW[si]])
                    for hc in range(15):
                        buf = buf_r.next()
                        P.dma(SP, buf[:, c_lo:c_lo + hi - lo], FT[hc * 64:(hc + 1) * 64, off + lo: off + hi], R=[dFT[si]], W=[buf.d])
                        P.op(DVE, lambda e: e.tensor_tensor(out=s1[:, a0:b1], in0=buf[:, a0:b1], in1=buf[:, a0 + 2:b1 + 2], op=ALU.add),
                             R=[buf.d], W=[s1.d])
                        if a0:
                            P.op(DVE, lambda e: e.tensor_copy(out=s1[:, 0:1], in_=buf[:, 2:3]), R=[buf.d], W=[s1.d])
                        if b1 < n:
                            P.op(DVE, lambda e: e.tensor_copy(out=s1[:, n - 1:n], in_=buf[:, n - 1:n]), R=[buf.d], W=[s1.d])
                        P.op(DVE, lambda e: e.scalar_tensor_tensor(out=s1[:, 0:n], in0=s1[:, 0:n], scalar=0.5, in1=buf[:, 1:n + 1],
                                                                   op0=ALU.mult, op1=ALU.subtract), R=[s1.d, buf.d], W=[s1.d])
                        P.op(DVE, lambda e: e.scalar_tensor_tensor(out=sh[hc][:, 0:n], in0=s1[:, 0:n], scalar=pp[l][0:64, omu + hc:omu + hc + 1],
                                                                   in1=buf[:, 1:n + 1], op0=ALU.mult, op1=ALU.add),
                             R=[s1.d, buf.d, pp[l].d], W=[sh[hc].d])
                    P.op(ACT, lambda e: e.activation(out=twd[:, 0:n], in_=sh[12][:, 0:n], func=AF.Tanh), R=[sh[12].d], W=[twd.d])
                    P.op(ACT, lambda e: e.activation(out=sgd[:, 0:n], in_=sh[14][:, 0:n], func=AF.Sigmoid), R=[sh[14].d], W=[sgd.d])
                    sad = sh[13]
                    for h in range(4):
                        shr, shk, shv = sh[h], sh[4 + h], sh[8 + h]
                        store(0, h, shr)
                        store(1, h, shv)
                        hs = slice(h * 64, (h + 1) * 64)
                        for d in range(2):
                            ds_ = slice(d * 32, (d + 1) * 32)
                            col = d * 4 + h
                            pw = ps_r.next()
                            P.op(PE, lambda e: e.matmul(pw[0:64, 0:n], lhsT=pp[l][ds_, owu + h * 64:owu + (h + 1) * 64], rhs=twd[ds_, 0:n], start=True, stop=True),
                                 R=[pp[l].d, twd.d], W=[pw.d])
                            lw = o_r.next()
                            P.op(ACT, lambda e: e.activation(out=lw[:, 0:n], in_=pw[0:64, 0:n], func=AF.Sigmoid, bias=pp[l][0:64, ow0 + col:ow0 + col + 1], scale=1.0),
                                 R=[pw.d, pp[l].d], W=[lw.d])
                            P.op(DVE, lambda e: e.tensor_scalar(out=lw[:, 0:n], in0=lw[:, 0:n], scalar1=-math.exp(-0.5), scalar2=None, op0=ALU.mult),
                                 R=[lw.d], W=[lw.d])
                            store(7 + d, h, lw)
                            pa = ps_r.next()
                            P.op(PE, lambda e: e.matmul(pa[0:64, 0:n], lhsT=pp[l][ds_, oau + h * 64:oau + (h + 1) * 64], rhs=sad[ds_, 0:n], start=True, stop=True),
                                 R=[pp[l].d, sad.d], W=[pa.d])
                            P.op(ACT, lambda e: e.activation(out=a_d[d][:, 0:n], in_=pa[0:64, 0:n], func=AF.Sigmoid, bias=pp[l][0:64, oa0 + col:oa0 + col + 1], scale=1.0),
                                 R=[pa.d, pp[l].d], W=[a_d[d].d])
                        pg = ps_r.next()
                        P.op(PE, lambda e: e.matmul(pg[0:64, 0:n], lhsT=pp[l][0:64, ogu + h * 64:ogu + (h + 1) * 64], rhs=sgd[:, 0:n], start=True, stop=True),
                             R=[pp[l].d, sgd.d], W=[pg.d])
                        go = o_r.next()
                        P.op(ACT, lambda e: e.activation(out=go[:, 0:n], in_=pg[0:64, 0:n], func=AF.Copy), R=[pg.d], W=[go.d])
                        store(9, h, go)
                        P.op(DVE, lambda e: e.tensor_scalar(out=kkr[:, 0:n], in0=shk[:, 0:n], scalar1=pp[l][0:64, okk + h:okk + h + 1], scalar2=None, op0=ALU.mult),
                             R=[shk.d, pp[l].d], W=[kkr.d])
                        P.op(ACT, lambda e: e.activation(out=t1[:, 0:n], in_=kkr[:, 0:n], func=AF.Square), R=[kkr.d], W=[t1.d])
                        pk = ps_r.next()
                        P.op(PE, lambda e: e.matmul(pk[0:64, 0:n], lhsT=C('ones', rows=64, sub=(0, 64)), rhs=t1[:, 0:n], start=True, stop=True),
                             R=[t1.d, cp.d], W=[pk.d])
                        P.op(DVE, lambda e: e.tensor_scalar(out=t1[:, 0:n], in0=pk[0:64, 0:n], scalar1=1e-12, scalar2=None, op0=ALU.max), R=[pk.d], W=[t1.d])
                        P.op(ACT, lambda e: e.activation(out=t1[:, 0:n], in_=t1[:, 0:n], func=AF.Sqrt), R=[t1.d], W=[t1.d])
                        P.op(DVE, lambda e: e.reciprocal(out=t1[:, 0:n], in_=t1[:, 0:n]), R=[t1.d], W=[t1.d])
                        P.op(DVE, lambda e: e.tensor_tensor(out=kk[:, 0:n], in0=kkr[:, 0:n], in1=t1[:, 0:n], op=ALU.mult), R=[kkr.d, t1.d], W=[kk.d])
                        store(2, h, kk)
                        for d in range(2):
                            P.op(DVE, lambda e: e.tensor_scalar(out=t2[:, 0:n], in0=a_d[d][:, 0:n], scalar1=-1.0, scalar2=pp[l][0:64, oka + h:oka + h + 1],
                                                                op0=ALU.add, op1=ALU.mult), R=[a_d[d].d, pp[l].d], W=[t2.d])
                            kd = o_r.next()
                            P.op(DVE, lambda e: e.scalar_tensor_tensor(out=kd[:, 0:n], in0=t2[:, 0:n], scalar=1.0, in1=shk[:, 0:n], op0=ALU.add, op1=ALU.mult),
                                 R=[t2.d, shk.d], W=[kd.d])
                            store(3 + d, h, kd)
                            bd = o_r.next()
                            P.op(DVE, lambda e: e.tensor_tensor(out=bd[:, 0:n], in0=kk[:, 0:n], in1=a_d[d][:, 0:n], op=ALU.mult), R=[kk.d, a_d[d].d], W=[bd.d])
                            store(5 + d, h, bd)
                        P.op(DVE, lambda e: e.scalar_tensor_tensor(out=t2[:, 0:n], in0=shr[:, 0:n], scalar=pp[l][0:64, ork + h:ork + h + 1], in1=shk[:, 0:n],
                                                                   op0=ALU.mult, op1=ALU.mult), R=[shr.d, shk.d, pp[l].d], W=[t2.d])
                        pb = ps_r.next()
                        P.op(PE, lambda e: e.matmul(pb[0:64, 0:n], lhsT=C('ones', rows=64, sub=(0, 64)), rhs=t2[:, 0:n], start=True, stop=True),
                             R=[t2.d, cp.d], W=[pb.d])
                        bo = o_r.next()
                        P.op(DVE, lambda e: e.tensor_tensor(out=bo[:, 0:n], in0=pb[0:64, 0:n], in1=shv[:, 0:n], op=ALU.mult), R=[pb.d, shv.d], W=[bo.d])
                        store(10, h, bo)

        def rwkv_scan(l):
            NI = 16
            LDSETS = 1
            tcount = [0]
            ld = {}
            for nm in ('r', 'v', 'kk', 'kd', 'bd', 'lw'):
                ld[nm] = [[T(P.sb("l%s%d_%d" % (nm, d, s_), [64, 512])) for d in range(2)] for s_ in range(LDSETS)]
            Lc = [T(P.sb("Lc%d" % d, [64, 512])) for d in range(2)]
            En = [T(P.sb("En%d" % d, [64, 512])) for d in range(2)]
            Ep = [T(P.sb("Ep%d" % d, [64, 512])) for d in range(2)]
            aT = [T(P.sb("aT%d" % d, [64, 512])) for d in range(2)]
            bT = [T(P.sb("bT%d" % d, [64, 512])) for d in range(2)]
            kT = [T(P.sb("kT%d" % d, [64, 512])) for d in range(2)]
            rT = [T(P.sb("rT%d" % d, [64, 512])) for d in range(2)]
            vo = [T(P.sb("vo%d" % d, [64, 512])) for d in range(2)]
            ysb = [T(P.sb("ysb%d" % d, [64, 512])) for d in range(2)]
            yso = [T(P.sb("yso%d" % d, [64, 512])) for d in range(2)]
            ELC = [T(P.sb("ELC%d" % i, [64, 64])) for i in range(NI)]
            bhk = [T(P.sb("bhk%d" % i, [64, 128])) for i in range(NI)]
            WC = [T(P.sb("WC%d" % i, [64, 8])) for i in range(NI)]
            TM = [T(P.sb("TM%d" % i, [64, 320])) for i in range(NI)]
            AM = [T(P.sb("AM%d" % i, [64, 320])) for i in range(NI)]
            Mm = [[T(P.sb("Mm%d_%d" % (i, j), [64, 128])) for j in range(2)] for i in range(NI)]
            Pm = [[T(P.sb("Pm%d_%d" % (i, j), [64, 64])) for j in range(2)] for i in range(NI)]
            AU = [T(P.sb("AU%d" % i, [64, 128])) for i in range(NI)]
            GT = [T(P.sb("GT%d" % i, [64, 64])) for i in range(NI)]
            Hh = [T(P.sb("Hh%d" % i, [64, 64])) for i in range(NI)]
            RhT = [T(P.sb("RhT%d" % i, [64, 64])) for i in range(NI)]
            Yl = [T(P.sb("Yl%d" % i, [64, 64])) for i in range(NI)]
            St = [[T(P.sb("St%d_%d" % (d, i), [64, 64])) for i in range(2)] for d in range(2)]
            s0l = T(P.sb("s0l", [64, 64]))
            pr = Ring([TPS(P.psum("pRW", [128, 512])) for _ in range(8)])
            I64 = C('ident', rows=64, sub=(0, 64))
            for si, (kind, sidx, Tn, off) in enumerate(SEQS):
                nq = min(512, Tn)
                ntile = Tn // nq
                for h in range(4):
                    sti = [0, 0]
                    for d in range(2):
                        S0 = St[d][0]
                        if kind == 's':
                            pS = pr.next()
                            P.dma(SP, s0l[:, :], st_rwkv[l, d, h], W=[s0l.d])
                            P.op(PE, lambda e: e.transpose(pS[0:64, 0:64], s0l[:, :], I64), R=[s0l.d, cp.d], W=[pS.d])
                            P.op(DVE, lambda e: e.tensor_copy(out=S0[:, :], in_=pS[0:64, 0:64]), R=[pS.d], W=[S0.d])
                        else:
                            P.op(DVE, lambda e: e.memset(S0[:, :], 0.0), W=[S0.d])
                    for ti in range(ntile):
                        n = nq
                        tcount[0] += 1
                        ldc = {nm_: ld[nm_][tcount[0] % LDSETS] for nm_ in ld}
                        for d in range(2):
                            q0 = ti * nq if d == 0 else Tn - (ti + 1) * nq

                            def view(t):
                                return t[:, 0:n] if d == 0 else t[:, 0:n][:, ::-1]
                            for nm, arr in (('r', 0), ('v', 1), ('kk', 2), ('kd', 3 + d), ('bd', 5 + d), ('lw', 7 + d)):
                                r0 = arr * 256 + h * 64
                                P.dma(SP, ldc[nm][d][:, 0:n], RW[r0:r0 + 64, off + q0: off + q0 + n], R=[dRW[si]], W=[ldc[nm][d].d])
                            lw = ldc['lw'][d]
                            P.op(DVE, lambda e: e.tensor_tensor_scan(out=Lc[d][:, 0:n], data0=C('scanmask', rows=64, sub=(0, n)), data1=view(lw),
                                                                     initial=0.0, op0=ALU.mult, op1=ALU.add), R=[lw.d, cp.d], W=[Lc[d].d])
                            P.op(ACT, lambda e: e.activation(out=En[d][:, 0:n], in_=Lc[d][:, 0:n], func=AF.Exp, scale=-1.0), R=[Lc[d].d], W=[En[d].d])
                            P.op(ACT, lambda e: e.activation(out=Ep[d][:, 0:n], in_=Lc[d][:, 0:n], func=AF.Exp), R=[Lc[d].d], W=[Ep[d].d])
                            P.op(DVE, lambda e: e.tensor_tensor(out=rT[d][:, 0:n], in0=view(ldc['r'][d]), in1=Ep[d][:, 0:n], op=ALU.mult),
                                 R=[ldc['r'][d].d, Ep[d].d], W=[rT[d].d])
                            P.op(DVE, lambda e: e.tensor_tensor(out=aT[d][:, 0:n], in0=Lc[d][:, 0:n], in1=view(lw), op=ALU.subtract),
                                 R=[Lc[d].d, lw.d], W=[aT[d].d])
                            P.op(ACT, lambda e: e.activation(out=Ep[d][:, 0:n], in_=aT[d][:, 0:n], func=AF.Exp), R=[aT[d].d, rT[d].d], W=[Ep[d].d])
                            P.op(DVE, lambda e: e.scalar_tensor_tensor(out=aT[d][:, 0:n], in0=view(ldc['kk'][d]), scalar=-1.0, in1=Ep[d][:, 0:n],
                                                                       op0=ALU.mult, op1=ALU.mult), R=[ldc['kk'][d].d, Ep[d].d], W=[aT[d].d])
                            P.op(DVE, lambda e: e.tensor_tensor(out=bT[d][:, 0:n], in0=view(ldc['bd'][d]), in1=En[d][:, 0:n], op=ALU.mult),
                                 R=[ldc['bd'][d].d, En[d].d], W=[bT[d].d])
                            P.op(DVE, lambda e: e.tensor_tensor(out=kT[d][:, 0:n], in0=view(ldc['kd'][d]), in1=En[d][:, 0:n], op=ALU.mult),
                                 R=[ldc['kd'][d].d, En[d].d], W=[kT[d].d])
                            P.op(DVE, lambda e: e.tensor_copy(out=vo[d][:, 0:n], in_=view(ldc['v'][d])), R=[ldc['v'][d].d], W=[vo[d].d])
                        ncc = n // 64
                        inst = [(cc, d) for cc in range(ncc) for d in range(2)]

                        def CS(cc):
                            return slice(cc * 64, (cc + 1) * 64)
                        for ii, (cc, d) in enumerate(inst):
                            cs = CS(cc)
                            lcol = Lc[d][:, cc * 64 + 63:cc * 64 + 64]

                            def vw(t):
                                if d == 0:
                                    return t[:, cs]
                                return t[:, n - (cc + 1) * 64:n - cc * 64][:, ::-1]
                            P.op(ACT, lambda e: e.activation(out=ELC[ii][:, :], in_=Lc[d][:, cs], func=AF.Exp, bias=lcol, scale=-1.0),
                                 R=[Lc[d].d], W=[ELC[ii].d])
                            P.op(ACT, lambda e: e.activation(out=WC[ii][:, 0:1], in_=lcol, func=AF.Exp), R=[Lc[d].d], W=[WC[ii].d])
                            P.op(DVE, lambda e: e.tensor_tensor(out=bhk[ii][:, 0:64], in0=vw(ldc['bd'][d]), in1=ELC[ii][:, :], op=ALU.mult),
                                 R=[ldc['bd'][d].d, ELC[ii].d], W=[bhk[ii].d])
                            P.op(DVE, lambda e: e.tensor_tensor(out=bhk[ii][:, 64:128], in0=vw(ldc['kd'][d]), in1=ELC[ii][:, :], op=ALU.mult),
                                 R=[ldc['kd'][d].d, ELC[ii].d], W=[bhk[ii].d])
                        for ii, (cc, d) in enumerate(inst):
                            cs = CS(cc)
                            ps = pr.next()
                            P.op(PE, lambda e: e.transpose(ps[0:64, 0:64], aT[d][:, cs], I64), R=[aT[d].d, cp.d], W=[ps.d])
                            P.op(PE, lambda e: e.transpose(ps[0:64, 64:128], vo[d][:, cs], I64), R=[vo[d].d, cp.d], W=[ps.d])
                            P.op(PE, lambda e: e.transpose(ps[0:64, 128:192], bhk[ii][:, 0:64], I64), R=[bhk[ii].d, cp.d], W=[ps.d])
                            P.op(PE, lambda e: e.transpose(ps[0:64, 192:256], bhk[ii][:, 64:128], I64), R=[bhk[ii].d, cp.d], W=[ps.d])
                            tm = TM[ii]
                            if ii % 2 == 0:
                                P.op(ACT, lambda e: e.activation(out=tm[:, 0:64], in_=ps[0:64, 0:64], func=AF.Copy), R=[ps.d], W=[tm.d])
                                P.op(ACT, lambda e: e.activation(out=tm[:, 128:320], in_=ps[0:64, 64:256], func=AF.Copy), R=[ps.d], W=[tm.d])
                            else:
                                P.op(DVE, lambda e: e.tensor_copy(out=tm[:, 0:64], in_=ps[0:64, 0:64]), R=[ps.d], W=[tm.d])
                                P.op(DVE, lambda e: e.tensor_copy(out=tm[:, 128:320], in_=ps[0:64, 64:256]), R=[ps.d], W=[tm.d])
                        for ii, (cc, d) in enumerate(inst):
                            cs = CS(cc)
                            a_, b_, k_, r_ = aT[d][:, cs], bT[d][:, cs], kT[d][:, cs], rT[d][:, cs]
                            ps = pr.next()
                            for i_, (lh, rh) in enumerate(((b_, a_), (b_, r_), (k_, a_), (k_, r_), (a_, b_))):
                                P.op(PE, lambda e: e.matmul(ps[0:64, i_ * 64:(i_ + 1) * 64], lhsT=lh, rhs=rh, start=True, stop=True),
                                     R=[aT[d].d, bT[d].d, kT[d].d, rT[d].d], W=[ps.d])
                            am = AM[ii]
                            P.op(DVE, lambda e: e.tensor_tensor(out=am[:, :], in0=ps[0:64, 0:320], in1=C('rw_mask5', rows=64), op=ALU.mult),
                                 R=[ps.d, cp.d], W=[am.d])
                            P.op(DVE, lambda e: e.tensor_tensor(out=Pm[ii][0][:, :], in0=am[:, 0:64], in1=I64, op=ALU.add), R=[am.d, cp.d], W=[Pm[ii][0].d])
                        cur = [(AM[ii][:, 0:64], AM[ii][:, 256:320], AM[ii]) for ii in range(len(inst))]
                        for lvl in range(5):
                            for ii, (cc, d) in enumerate(inst):
                                Mc, Mtc, Mdep = cur[ii]
                                mn = Mm[ii][lvl % 2]
                                ps = pr.next()
                                if lvl < 4:
                                    P.op(PE, lambda e: e.matmul(ps[0:64, 0:64], lhsT=Mtc, rhs=Mc, start=True, stop=True), R=[Mdep.d], W=[ps.d])
                                P.op(PE, lambda e: e.matmul(ps[0:64, 64:128], lhsT=Mc, rhs=Mtc, start=True, stop=True), R=[Mdep.d], W=[ps.d])
                                if lvl < 4:
                                    evac(mn[:, :], ps[0:64, 0:128], [ps.d], [mn.d])
                                else:
                                    evac(mn[:, 64:128], ps[0:64, 64:128], [ps.d], [mn.d])
                                cur[ii] = (mn[:, 0:64], mn[:, 64:128], mn)
                            for ii, (cc, d) in enumerate(inst):
                                Mc, Mtc, Mdep = cur[ii]
                                Pc = Pm[ii][lvl % 2]
                                Pn = Pm[ii][(lvl + 1) % 2]
                                ps = pr.next()
                                P.op(PE, lambda e: e.matmul(ps[0:64, 0:64], lhsT=Mtc, rhs=Pc[:, :], start=True, stop=True), R=[Mdep.d, Pc.d], W=[ps.d])
                                P.op(DVE, lambda e: e.tensor_tensor(out=Pn[:, :], in0=ps[0:64, 0:64], in1=Pc[:, :], op=ALU.add), R=[ps.d, Pc.d], W=[Pn.d])
                        for ii, (cc, d) in enumerate(inst):
                            tm, am = TM[ii], AM[ii]
                            ps = pr.next()
                            P.op(PE, lambda e: e.matmul(ps[0:64, 0:64], lhsT=am[:, 128:192], rhs=tm[:, 128:192], start=True, stop=True), R=[am.d, tm.d], W=[ps.d])
                            evac(tm[:, 64:128], ps[0:64, 0:64], [ps.d], [tm.d])
                        for ii, (cc, d) in enumerate(inst):
                            tm, Pc, au = TM[ii], Pm[ii][1], AU[ii]
                            ps = pr.next()
                            P.op(PE, lambda e: e.matmul(ps[0:64, 0:128], lhsT=Pc[:, :], rhs=tm[:, 0:128], start=True, stop=True), R=[Pc.d, tm.d], W=[ps.d])
                            evac(au[:, :], ps[0:64, 0:128], [ps.d], [au.d])
                        for ii, (cc, d) in enumerate(inst):
                            tm, au = TM[ii], AU[ii]
                            V_, Bh_, Kh_ = tm[:, 128:192], tm[:, 192:256], tm[:, 256:320]
                            Ah, Ul = au[:, 0:64], au[:, 64:128]
                            ps = pr.next()
                            P.op(PE, lambda e: e.matmul(ps[0:64, 0:64], lhsT=Ah, rhs=Bh_, start=True, stop=True), R=[au.d, tm.d], W=[ps.d])
                            P.op(PE, lambda e: e.matmul(ps[0:64, 64:128], lhsT=Bh_, rhs=Ul, start=True, stop=False), R=[au.d, tm.d], W=[ps.d])
                            P.op(PE, lambda e: e.matmul(ps[0:64, 64:128], lhsT=Kh_, rhs=V_, start=False, stop=True), R=[tm.d], W=[ps.d])
                            P.op(DVE, lambda e: e.scalar_tensor_tensor(out=GT[ii][:, :], in0=I64, scalar=WC[ii][:, 0:1], in1=ps[0:64, 0:64],
                                                                       op0=ALU.mult, op1=ALU.add), R=[cp.d, WC[ii].d, ps.d], W=[GT[ii].d])
                            P.op(DVE, lambda e: e.tensor_copy(out=Hh[ii][:, :], in_=ps[0:64, 64:128]), R=[ps.d], W=[Hh[ii].d])
                        for ii, (cc, d) in enumerate(inst):
                            cs = CS(cc)
                            tm, au, am = TM[ii], AU[ii], AM[ii]
                            V_ = tm[:, 128:192]
                            Ah, Ul = au[:, 0:64], au[:, 64:128]
                            ArbT, ArkT = am[:, 64:128], am[:, 192:256]
                            ps = pr.next()
                            P.op(PE, lambda e: e.matmul(ps[0:64, 0:64], lhsT=Ah, rhs=ArbT, start=True, stop=True), R=[au.d, am.d], W=[ps.d])
                            P.op(PE, lambda e: e.matmul(ps[0:64, 64:128], lhsT=Ul, rhs=ArbT, start=True, stop=False), R=[au.d, am.d], W=[ps.d])
                            P.op(PE, lambda e: e.matmul(ps[0:64, 64:128], lhsT=V_, rhs=ArkT, start=False, stop=True), R=[tm.d, am.d], W=[ps.d])
                            P.op(DVE, lambda e: e.tensor_tensor(out=RhT[ii][:, :], in0=ps[0:64, 0:64], in1=rT[d][:, cs], op=ALU.add), R=[ps.d, rT[d].d], W=[RhT[ii].d])
                            P.op(DVE, lambda e: e.tensor_copy(out=Yl[ii][:, :], in_=ps[0:64, 64:128]), R=[ps.d], W=[Yl[ii].d])
                        for ii, (cc, d) in enumerate(inst):
                            cs = CS(cc)
                            Sc = St[d][sti[d] % 2]
                            Sn = St[d][(sti[d] + 1) % 2]
                            sti[d] += 1
                            ps = pr.next()
                            P.op(PE, lambda e: e.matmul(ps[0:64, 0:64], lhsT=Sc[:, :], rhs=RhT[ii][:, :], start=True, stop=True), R=[Sc.d, RhT[ii].d], W=[ps.d])
                            P.op(PE, lambda e: e.matmul(ps[0:64, 64:128], lhsT=GT[ii][:, :], rhs=Sc[:, :], start=True, stop=True), R=[GT[ii].d, Sc.d], W=[ps.d])
                            P.op(DVE, lambda e: e.tensor_tensor(out=Sn[:, :], in0=ps[0:64, 64:128], in1=Hh[ii][:, :], op=ALU.add), R=[ps.d, Hh[ii].d], W=[Sn.d])
                            P.op(DVE, lambda e: e.tensor_tensor(out=ysb[d][:, cs], in0=ps[0:64, 0:64], in1=Yl[ii][:, :], op=ALU.add), R=[ps.d, Yl[ii].d], W=[ysb[d].d])
                        for d in range(2):
                            q0 = ti * nq if d == 0 else Tn - (ti + 1) * nq
                            src = ysb[d]
                            if d == 1:
                                P.op(DVE, lambda e: e.tensor_copy(out=yso[d][:, 0:n], in_=ysb[d][:, 0:n][:, ::-1]), R=[ysb[d].d], W=[yso[d].d])
                                src = yso[d]
                            r0 = d * 256 + h * 64
                            P.dma(ACT, YS[r0:r0 + 64, off + q0: off + q0 + n], src[:, 0:n], R=[src.d], W=[dYS[si]])
                    if kind == 'p':
                        for d in range(2):
                            Sc = St[d][sti[d] % 2]
                            pS = pr.next()
                            P.op(PE, lambda e: e.transpose(pS[0:64, 0:64], Sc[:, :], I64), R=[Sc.d, cp.d], W=[pS.d])
                            P.op(DVE, lambda e: e.tensor_copy(out=s0l[:, :], in_=pS[0:64, 0:64]), R=[pS.d], W=[s0l.d])
                            P.dma(SP, o_rwkv[sidx, l, d, h], s0l[:, :], R=[s0l.d], W=[dOUT])

        def rwkv_post(l):
            yf = Ring([T(P.sb("pyf", [64, 512])) for _ in range(2)])
            yb = Ring([T(P.sb("pyb", [64, 512])) for _ in range(2)])
            gt = Ring([T(P.sb("pgt", [64, 512])) for _ in range(2)])
            bt = Ring([T(P.sb("pbt", [64, 512])) for _ in range(2)])
            yc = T(P.sb("pyc", [64, 512]))
            sq = T(P.sb("psq", [64, 512]))
            rs = T(P.sb("prs", [64, 512]))
            oy_r = Ring([T(P.sb("oy", [64, 512], OTDT)) for _ in range(2)])
            ps_r = Ring([TPS(P.psum("ps_po", [128, 512])) for _ in range(4)])
            olg, _ = PPC['lng64']
            olb, _ = PPC['lnb64']
            epsg = T(P.sb("epsg", [64, 1]))
            P.op(DVE, lambda e: e.memset(epsg[:, :], 64e-5), W=[epsg.d])
            for si, (kind, sidx, Tn, off) in enumerate(SEQS):
                nq = min(512, Tn)
                for h in range(4):
                    for q0 in range(0, Tn, nq):
                        n = nq
                        a, b, g_, bo = yf.next(), yb.next(), gt.next(), bt.next()
                        cols = slice(off + q0, off + q0 + n)
                        P.dma(SP, a[:, 0:n], YS[h * 64:(h + 1) * 64, cols], R=[dYS[si]], W=[a.d])
                        P.dma(SP, b[:, 0:n], YS[256 + h * 64:256 + (h + 1) * 64, cols], R=[dYS[si]], W=[b.d])
                        P.dma(SP, g_[:, 0:n], RW[9 * 256 + h * 64:9 * 256 + (h + 1) * 64, cols], R=[dRW[si]], W=[g_.d])
                        P.dma(SP, bo[:, 0:n], RW[10 * 256 + h * 64:10 * 256 + (h + 1) * 64, cols], R=[dRW[si]], W=[bo.d])
                        P.op(DVE, lambda e: e.tensor_tensor(out=a[:, 0:n], in0=a[:, 0:n], in1=b[:, 0:n], op=ALU.add), R=[a.d, b.d], W=[a.d])
                        pm_ = ps_r.next()
                        P.op(PE, lambda e: e.matmul(pm_[0:64, 0:n], lhsT=C('ones', rows=64, sub=(0, 64)), rhs=a[:, 0:n], start=True, stop=True),
                             R=[a.d, cp.d], W=[pm_.d])
                        P.op(DVE, lambda e: e.scalar_tensor_tensor(out=yc[:, 0:n], in0=pm_[0:64, 0:n], scalar=-1.0 / 64, in1=a[:, 0:n],
                                                                   op0=ALU.mult, op1=ALU.add), R=[pm_.d, a.d], W=[yc.d])
                        P.op(ACT, lambda e: e.activation(out=sq[:, 0:n], in_=yc[:, 0:n], func=AF.Square), R=[yc.d], W=[sq.d])
                        pv = ps_r.next()
                        P.op(PE, lambda e: e.matmul(pv[0:64, 0:n], lhsT=C('ones', rows=64, sub=(0, 64)), rhs=sq[:, 0:n], start=True, stop=True),
                             R=[sq.d, cp.d], W=[pv.d])
                        P.op(ACT, lambda e: e.activation(out=rs[:, 0:n], in_=pv[0:64, 0:n], func=AF.Sqrt, bias=epsg[:, 0:1], scale=1.0 / 64),
                             R=[pv.d, epsg.d], W=[rs.d])
                        P.op(DVE, lambda e: e.reciprocal(out=rs[:, 0:n], in_=rs[:, 0:n]), R=[rs.d], W=[rs.d])
                        P.op(DVE, lambda e: e.scalar_tensor_tensor(out=yc[:, 0:n], in0=yc[:, 0:n], scalar=pp[l][0:64, olg + h:olg + h + 1], in1=rs[:, 0:n],
                                                                   op0=ALU.mult, op1=ALU.mult), R=[yc.d, rs.d, pp[l].d], W=[yc.d])
                        P.op(DVE, lambda e: e.scalar_tensor_tensor(out=yc[:, 0:n], in0=yc[:, 0:n], scalar=pp[l][0:64, olb + h:olb + h + 1], in1=bo[:, 0:n],
                                                                   op0=ALU.add, op1=ALU.add), R=[yc.d, bo.d, pp[l].d], W=[yc.d])
                        oy = oy_r.next()
                        P.op(DVE, lambda e: e.tensor_tensor(out=oy[:, 0:n], in0=yc[:, 0:n], in1=g_[:, 0:n], op=ALU.mult), R=[yc.d, g_.d], W=[oy.d])
                        P.dma(ACT, OT[h * 64:(h + 1) * 64, cols], oy[:, 0:n], R=[oy.d], W=[dOT[si]])

        def mixer_ssd(l):
            NCH = TS // 128
            raw = T(P.sb("raw", [64, TS]))
            tcv = T(P.sb("tcv", [64, TS]))
            xcf = [T(P.sb("xcf%d" % h, [64, TS], BF16)) for h in range(4)]
            Bb = [T(P.sb("Bb%d" % g, [64, TS], BF16)) for g in range(2)]
            Cb = [T(P.sb("Cb%d" % g, [64, TS], BF16)) for g in range(2)]
            xT = [T(P.sb("xT%d" % h, [128, NCH, 64], BF16)) for h in range(4)]
            yz = [T(P.sb("yz%d" % h, [64, TS], BF16)) for h in range(4)]
            dtr = T(P.sb("dtr", [128, NCH, 8]))
            dts = T(P.sb("dts", [128, NCH, 8]))
            gg = T(P.sb("gg", [128, NCH, 8]))
            nea = T(P.sb("nea", [128, 8]))
            S_ = [T(P.sb("S%d" % d, [64, 64])) for d in range(2)]
            Sfb = T(P.sb("Sfb", [64, 64], BF16))
            Sbs = T(P.sb("Sbs", [64, NCH, 64], BF16))
            cumc = Ring([T(P.sb("cumc", [128, 8])) for _ in range(2)])
            gB_r = Ring([T(P.sb("gB", [128, 128])) for _ in range(2)])
            arg_r = Ring([T(P.sb("arg", [128, 128])) for _ in range(2)])
            Dm = [T(P.sb("Dm%d" % d, [128, 128])) for d in range(2)]
            Ds = T(P.sb("Ds", [128, 128]))
            PT_r = Ring([T(P.sb("PT", [128, 128], BF16)) for _ in range(2)])
            sm_r = Ring([T(P.sb("sm", [128, 4])) for _ in range(4)])
            ecr_r = Ring([T(P.sb("ecr", [64, 128])) for _ in range(2)])
            qd_r = Ring([T(P.sb("qd", [64, 128], BF16)) for _ in range(4)])
            Bw_r = Ring([T(P.sb("Bw", [128, 64], BF16)) for _ in range(2)])
            zt_ = T(P.sb("zt_", [64, 512]))
            t5 = T(P.sb("t5", [64, 512]))
            rsy = T(P.sb("rsy", [64, 512]))
            oy_r = Ring([T(P.sb("oy", [64, 512], OTDT)) for _ in range(2)])
            ps_cc = Ring([TPS(P.psum("ps_cc", [128, 512])) for _ in range(1)])
            ps_cr = Ring([TPS(P.psum("ps_cr", [128, 512])) for _ in range(2)])
            ps_sc = Ring([TPS(P.psum("ps_sc", [128, 512])) for _ in range(1)])
            ps_tr = Ring([TPS(P.psum("ps_tr", [128, 1024], BF16)) for _ in range(1)])
            ps_up = Ring([TPS(P.psum("ps_up", [128, 512])) for _ in range(1)])
            ps_y = Ring([TPS(P.psum("ps_y", [128, 512])) for _ in range(2)])
            ocw, _ = PPC['scw64']
            ocb, _ = PPC['scb64']
            odtb, _ = PPC['dtb']
            oal, _ = PPC['alog']
            od4, _ = PPC['ssd_d4']
            og4, _ = PPC['ssd_g4']
            P.op(ACT, lambda e: e.activation(out=nea[:, :], in_=pp[l][:, oal:oal + 8], func=AF.Exp), R=[pp[l].d], W=[nea.d])
            P.op(DVE, lambda e: e.tensor_scalar(out=nea[:, :], in0=nea[:, :], scalar1=-1.0, scalar2=None, op0=ALU.mult), R=[nea.d], W=[nea.d])
            for si, (kind, sidx, Tn, off) in enumerate(SEQS):
                nch = Tn // 128

                def convsilu(hc, dst, dstb):
                    r0 = 20 * 128 + hc * 64
                    P.dma(SP, raw[:, 0:Tn], FT[r0:r0 + 64, off:off + Tn], R=[dFT[si]], W=[raw.d])
                    P.op(ACT, lambda e: e.activation(out=tcv[:, 0:Tn], in_=raw[:, 0:Tn], func=AF.Identity,
                                                     bias=pp[l][0:64, ocb + hc:ocb + hc + 1], scale=pp[l][0:64, ocw + 8 + hc:ocw + 9 + hc]),
                         R=[raw.d, pp[l].d], W=[tcv.d])
                    P.op(DVE, lambda e: e.scalar_tensor_tensor(out=tcv[:, 1:Tn], in0=raw[:, 0:Tn - 1], scalar=pp[l][0:64, ocw + hc:ocw + hc + 1],
                                                               in1=tcv[:, 1:Tn], op0=ALU.mult, op1=ALU.add), R=[raw.d, pp[l].d, tcv.d], W=[tcv.d])
                    P.op(DVE, lambda e: e.scalar_tensor_tensor(out=tcv[:, 0:Tn - 1], in0=raw[:, 1:Tn], scalar=pp[l][0:64, ocw + 16 + hc:ocw + 17 + hc],
                                                               in1=tcv[:, 0:Tn - 1], op0=ALU.mult, op1=ALU.add), R=[raw.d, pp[l].d, tcv.d], W=[tcv.d])
                    if dst is not None:
                        P.op(ACT, lambda e: e.activation(out=dst[:, 0:Tn], in_=tcv[:, 0:Tn], func=AF.Silu), R=[tcv.d], W=[dst.d])
                        P.op(DVE, lambda e: e.tensor_copy(out=dstb[:, 0:Tn], in_=dst[:, 0:Tn]), R=[dst.d], W=[dstb.d])
                    else:
                        P.op(ACT, lambda e: e.activation(out=dstb[:, 0:Tn], in_=tcv[:, 0:Tn], func=AF.Silu), R=[tcv.d], W=[dstb.d])
                for g in range(2):
                    convsilu(4 + g, None, Bb[g])
                    convsilu(6 + g, None, Cb[g])
                for h in range(4):
                    convsilu(h, None, xcf[h])
                    for c in range(nch):
                        pt = ps_tr.next()
                        P.op(PE, lambda e: e.transpose(pt[:, 0:64], xcf[h][:, c * 128:(c + 1) * 128], cpb[0:64, 0:64]), R=[xcf[h].d, cpb.d], W=[pt.d])
                        evac(xT[h][:, c, :], pt[:, 0:64], [pt.d], [xT[h].d])
                P.dma(SP, dtr[:, 0:nch, :], FTM[off:off + Tn, 768:776].rearrange("(c p) e -> p c e", p=128), R=[dFTM[si]], W=[dtr.d])
                P.op(DVE, lambda e: e.tensor_tensor(out=dtr[:, 0:nch, :], in0=dtr[:, 0:nch, :],
                                                    in1=pp[l][:, odtb:odtb + 8].unsqueeze(1).to_broadcast([128, nch, 8]), op=ALU.add),
                     R=[dtr.d, pp[l].d], W=[dtr.d])
                P.op(ACT, lambda e: e.activation(out=dtr[:, 0:nch, :], in_=dtr[:, 0:nch, :], func=AF.Exp), R=[dtr.d], W=[dtr.d])
                P.op(ACT, lambda e: e.activation(out=dts[:, 0:nch, :], in_=dtr[:, 0:nch, :], func=AF.Ln, bias=1.0, scale=1.0), R=[dtr.d], W=[dts.d])
                P.op(DVE, lambda e: e.tensor_tensor(out=gg[:, 0:nch, :], in0=dts[:, 0:nch, :],
                                                    in1=nea[:, :].unsqueeze(1).to_broadcast([128, nch, 8]), op=ALU.mult),
                     R=[dts.d, nea.d], W=[gg.d])

                def chunk_cum(c):
                    pc = ps_cc.next()
                    P.op(PE, lambda e: e.matmul(pc[:, 0:4], lhsT=C('tri_f'), rhs=gg[:, c, 0:4], start=True, stop=True), R=[cp.d, gg.d], W=[pc.d])
                    P.op(PE, lambda e: e.matmul(pc[:, 4:8], lhsT=C('tri_b'), rhs=gg[:, c, 4:8], start=True, stop=True), R=[cp.d, gg.d], W=[pc.d])
                    cc = cumc.next()
                    P.op(DVE, lambda e: e.tensor_copy(out=cc[:, :], in_=pc[:, 0:8]), R=[pc.d], W=[cc.d])
                    return cc

                def dirstuff(c, h, d, cc):
                    col = d * 4 + h
                    gB = gB_r.next()
                    P.op(DVE, lambda e: e.tensor_scalar(out=gB[:, :], in0=C('ones'), scalar1=gg[:, c, col:col + 1], scalar2=None, op0=ALU.mult),
                         R=[cp.d, gg.d], W=[gB.d])
                    pr_ = ps_cr.next()
                    P.op(PE, lambda e: e.matmul(pr_[:, 0:128], lhsT=gB[:, :], rhs=C('tri_f' if d == 0 else 'tri_b'), start=True, stop=True),
                         R=[gB.d, cp.d], W=[pr_.d])
                    sm = sm_r.next()
                    lc = 127 if d == 0 else 0
                    P.op(DVE, lambda e: e.tensor_copy(out=sm[:, 0:1], in_=pr_[:, lc:lc + 1]), R=[pr_.d], W=[sm.d])
                    P.op(ACT, lambda e: e.activation(out=sm[:, 1:2], in_=cc[:, col:col + 1], func=AF.Exp, bias=sm[:, 0:1], scale=-1.0),
                         R=[cc.d, sm.d], W=[sm.d])
                    P.op(DVE, lambda e: e.tensor_tensor(out=sm[:, 2:3], in0=sm[:, 1:2], in1=dts[:, c, col:col + 1], op=ALU.mult),
                         R=[sm.d, dts.d], W=[sm.d])
                    P.op(ACT, lambda e: e.activation(out=sm[:, 3:4], in_=sm[:, 0:1], func=AF.Exp), R=[sm.d], W=[sm.d])
                    return pr_, sm

                def state_update(S, c, h, g, sm):
                    pt = ps_tr.next()
                    P.op(PE, lambda e: e.transpose(pt[:, 0:64], Bb[g][:, c * 128:(c + 1) * 128], cpb[0:64, 0:64]), R=[Bb[g].d, cpb.d], W=[pt.d])
                    Bw = Bw_r.next()
                    P.op(DVE, lambda e: e.tensor_scalar(out=Bw[:, :], in0=pt[:, 0:64], scalar1=sm[:, 2:3], scalar2=None, op0=ALU.mult),
                         R=[pt.d, sm.d], W=[Bw.d])
                    pu = ps_up.next()
                    P.op(PE, lambda e: e.matmul(pu[0:64, 0:64], lhsT=Bw[:, :], rhs=xT[h][:, c, :], start=True, stop=True), R=[Bw.d, xT[h].d], W=[pu.d])
                    P.op(DVE, lambda e: e.scalar_tensor_tensor(out=S[:, :], in0=S[:, :], scalar=sm[0:64, 3:4], in1=pu[0:64, 0:64],
                                                               op0=ALU.mult, op1=ALU.add), R=[S.d, pu.d, sm.d], W=[S.d])

                for h in range(4):
                    g = h // 2
                    Sf, Sb = S_
                    if kind == 's':
                        P.dma(SP, Sf[:, :], st_ssd[l, 0, h], W=[Sf.d])
                        P.dma(SP, Sb[:, :], st_ssd[l, 1, h], W=[Sb.d])
                    else:
                        P.op(DVE, lambda e: e.memset(Sf[:, :], 0.0), W=[Sf.d])
                        P.op(DVE, lambda e: e.memset(Sb[:, :], 0.0), W=[Sb.d])
                    for c in range(nch - 1, -1, -1):
                        P.op(ACT, lambda e: e.activation(out=Sbs[:, c, :], in_=Sb[:, :], func=AF.Copy), R=[Sb.d], W=[Sbs.d])
                        cc = chunk_cum(c)
                        pr_, sm = dirstuff(c, h, 1, cc)
                        state_update(Sb, c, h, g, sm)
                    if kind == 'p':
                        P.dma(SP, o_ssd[sidx, l, 1, h], Sb[:, :], R=[Sb.d], W=[dOUT])
                    for c0 in range(0, nch, 4):
                        ncg = min(4, nch - c0)
                        n = ncg * 128
                        py = ps_y.next()
                        for ci in range(ncg):
                            c = c0 + ci
                            sl = slice(c * 128, (c + 1) * 128)
                            cc = chunk_cum(c)
                            psc = ps_sc.next()
                            P.op(PE, lambda e: e.matmul(psc[:, 0:128], lhsT=Bb[g][:, sl], rhs=Cb[g][:, sl], start=True, stop=True),
                                 R=[Bb[g].d, Cb[g].d], W=[psc.d])
                            qds = []
                            sms = []
                            for d in range(2):
                                col = d * 4 + h
                                pr_, sm = dirstuff(c, h, d, cc)
                                sms.append(sm)
                                arg = arg_r.next()
                                P.op(DVE, lambda e: e.scalar_tensor_tensor(out=arg[:, :], in0=pr_[:, 0:128], scalar=cc[:, col:col + 1],
                                                                           in1=C('nm_f' if d == 0 else 'nm_b'), op0=ALU.subtract, op1=ALU.add),
                                     R=[pr_.d, cc.d, cp.d], W=[arg.d])
                                P.op(ACT, lambda e: e.activation(out=Dm[d][:, :], in_=arg[:, :], func=AF.Exp), R=[arg.d], W=[Dm[d].d])
                                ecr = ecr_r.next()
                                P.op(ACT, lambda e: e.activation(out=ecr[:, :], in_=pr_[0:64, 0:128], func=AF.Exp), R=[pr_.d], W=[ecr.d])
                                qd = qd_r.next()
                                P.op(DVE, lambda e: e.tensor_tensor(out=qd[:, :], in0=Cb[g][:, sl], in1=ecr[:, :], op=ALU.mult),
                                     R=[Cb[g].d, ecr.d], W=[qd.d])
                                qds.append(qd)
                            P.op(DVE, lambda e: e.tensor_scalar(out=Ds[:, :], in0=Dm[0][:, :], scalar1=dts[:, c, h:h + 1], scalar2=None, op0=ALU.mult),
                                 R=[Dm[0].d, dts.d], W=[Ds.d])
                            P.op(DVE, lambda e: e.scalar_tensor_tensor(out=Ds[:, :], in0=Dm[1][:, :], scalar=dts[:, c, 4 + h:5 + h], in1=Ds[:, :],
                                                                       op0=ALU.mult, op1=ALU.add), R=[Dm[1].d, dts.d, Ds.d], W=[Ds.d])
                            PT = PT_r.next()
                            P.op(DVE, lambda e: e.tensor_tensor(out=PT[:, :], in0=psc[:, 0:128], in1=Ds[:, :], op=ALU.mult), R=[psc.d, Ds.d], W=[PT.d])
                            P.op(ACT, lambda e: e.activation(out=Sfb[:, :], in_=Sf[:, :], func=AF.Copy), R=[Sf.d], W=[Sfb.d])
                            yo = py[0:64, ci * 128:(ci + 1) * 128]
                            P.op(PE, lambda e: e.matmul(yo, lhsT=xT[h][:, c, :], rhs=PT[:, :], start=True, stop=False), R=[xT[h].d, PT.d], W=[py.d])
                            P.op(PE, lambda e: e.matmul(yo, lhsT=Sfb[:, :], rhs=qds[0][:, :], start=False, stop=False), R=[Sfb.d, qds[0].d], W=[py.d])
                            P.op(PE, lambda e: e.matmul(yo, lhsT=Sbs[:, c, :], rhs=qds[1][:, :], start=False, stop=True), R=[Sbs.d, qds[1].d], W=[py.d])
                            state_update(Sf, c, h, g, sms[0])
                        tsl = slice(c0 * 128, c0 * 128 + n)
                        r0 = 18 * 128 + h * 64
                        P.dma(SP, zt_[:, 0:n], FT[r0:r0 + 64, off + c0 * 128: off + c0 * 128 + n], R=[dFT[si]], W=[zt_.d])
                        P.op(ACT, lambda e: e.activation(out=zt_[:, 0:n], in_=zt_[:, 0:n], func=AF.Silu), R=[zt_.d], W=[zt_.d])
                        P.op(DVE, lambda e: e.scalar_tensor_tensor(out=t5[:, 0:n], in0=xcf[h][:, tsl], scalar=pp[l][0:64, od4 + h:od4 + h + 1],
                                                                   in1=py[0:64, 0:n], op0=ALU.mult, op1=ALU.add), R=[xcf[h].d, pp[l].d, py.d], W=[t5.d])
                        P.op(DVE, lambda e: e.tensor_tensor(out=yz[h][:, tsl], in0=t5[:, 0:n], in1=zt_[:, 0:n], op=ALU.mult),
                             R=[t5.d, zt_.d], W=[yz[h].d])
                    if kind == 'p':
                        P.dma(SP, o_ssd[sidx, l, 0, h], Sf[:, :], R=[Sf.d], W=[dOUT])
                nq = min(512, Tn)
                for q0 in range(0, Tn, nq):
                    n = nq
                    tsl = slice(q0, q0 + n)
                    pss = ps_cc.next()
                    for h in range(4):
                        P.op(ACT, lambda e: e.activation(out=t5[:, 0:n], in_=yz[h][:, tsl], func=AF.Square), R=[yz[h].d], W=[t5.d])
                        P.op(PE, lambda e: e.matmul(pss[0:64, 0:n], lhsT=C('ones', rows=64, sub=(0, 64)), rhs=t5[:, 0:n], start=(h == 0), stop=(h == 3)),
                             R=[t5.d, cp.d], W=[pss.d])
                    P.op(ACT, lambda e: e.activation(out=rsy[:, 0:n], in_=pss[0:64, 0:n], func=AF.Sqrt, bias=C('eps', rows=64), scale=1.0 / 256),
                         R=[pss.d, cp.d], W=[rsy.d])
                    P.op(DVE, lambda e: e.reciprocal(out=rsy[:, 0:n], in_=rsy[:, 0:n]), R=[rsy.d], W=[rsy.d])
                    for h in range(4):
                        oy = oy_r.next()
                        P.op(DVE, lambda e: e.scalar_tensor_tensor(out=oy[:, 0:n], in0=yz[h][:, tsl], scalar=pp[l][0:64, og4 + h:og4 + h + 1],
                                                                   in1=rsy[:, 0:n], op0=ALU.mult, op1=ALU.mult), R=[yz[h].d, pp[l].d, rsy.d], W=[oy.d])
                        P.dma(ACT, OT[512 + h * 64:512 + (h + 1) * 64, off + q0: off + q0 + n], oy[:, 0:n], R=[oy.d], W=[dOT[si]])

        def mixer_ret(l):
            ropc = T(P.sb("ropc", [64, TS]))
            rops = T(P.sb("rops", [64, TS]))
            P.dma(SP, ropc[:, :], rope_in[0, 0:64, :], W=[ropc.d])
            P.dma(SP, rops[:, :], rope_in[1, 0:64, :], W=[rops.d])
            qf32 = T(P.sb("qf32", [64, TS]))
            qs32 = T(P.sb("qs32", [64, TS]))
            qb = T(P.sb("qb", [64, TS], BF16))
            kb = T(P.sb("kb", [64, TS], BF16))
            g32 = T(P.sb("g32", [64, TS]))
            v32 = T(P.sb("v32", [128, TS // 128, 64]))
            vb = T(P.sb("vb", [128, TS // 128, 64], BF16))
            Sf = T(P.sb("Sf", [64, 64]))
            Sb = T(P.sb("Sb", [64, 64]))
            Sfb = T(P.sb("Sfb", [64, 64], BF16))
            Sbs = T(P.sb("Sbs", [64, TS // 128, 64], BF16))
            kw_r = Ring([T(P.sb("kw", [128, 64], BF16)) for _ in range(2)])
            PT_r = Ring([T(P.sb("PT", [128, 128], BF16)) for _ in range(2)])
            qd_r = Ring([T(P.sb("qd", [64, 128], BF16)) for _ in range(4)])
            sqy = T(P.sb("sqy", [64, 512]))
            rsy = T(P.sb("rsy", [64, 512]))
            sg = T(P.sb("sg", [64, 512]))
            tmy = T(P.sb("tmy", [64, 512]))
            oy_r = Ring([T(P.sb("oy", [64, 512], OTDT)) for _ in range(2)])
            ps_sc = Ring([TPS(P.psum("ps_sc", [128, 512])) for _ in range(2)])
            ps_tr = Ring([TPS(P.psum("ps_tr", [128, 1024], BF16)) for _ in range(1)])
            ps_up = Ring([TPS(P.psum("ps_up", [128, 512])) for _ in range(1)])
            ps_y = Ring([TPS(P.psum("ps_y", [128, 512])) for _ in range(2)])
            ps_ss = Ring([TPS(P.psum("ps_ss", [128, 512])) for _ in range(1)])
            og4, _ = PPC['ret_g4']
            for si, (kind, sidx, Tn, off) in enumerate(SEQS):
                nch = Tn // 128
                for h in range(4):
                    pr = (h % 2) * 64

                    def rows(cbase):
                        r0 = (cbase + h // 2) * 128 + pr
                        return FT[r0:r0 + 64, off:off + Tn]
                    for (dst, cb, cbs, scale) in ((qb, 8, 12, 1.0), (kb, 10, 14, 0.125)):
                        P.dma(SP, qf32[:, 0:Tn], rows(cb), R=[dFT[si]], W=[qf32.d])
                        if kind == 's':
                            P.dma(SP, qs32[:, 0:Tn], rows(cbs), R=[dFT[si]], W=[qs32.d])
                            P.op(DVE, lambda e: e.tensor_tensor(out=qf32[:, 0:Tn], in0=qf32[:, 0:Tn], in1=ropc[:, 0:Tn], op=ALU.mult),
                                 R=[qf32.d, ropc.d], W=[qf32.d])
                            P.op(DVE, lambda e: e.tensor_tensor(out=qs32[:, 0:Tn], in0=qs32[:, 0:Tn], in1=rops[:, 0:Tn], op=ALU.mult),
                                 R=[qs32.d, rops.d], W=[qs32.d])
                            P.op(DVE, lambda e: e.tensor_tensor(out=qf32[:, 0:Tn], in0=qf32[:, 0:Tn], in1=qs32[:, 0:Tn], op=ALU.add),
                                 R=[qf32.d, qs32.d], W=[qf32.d])
                        P.op(ACT, lambda e, dst=dst, scale=scale: e.activation(out=dst[:, 0:Tn], in_=qf32[:, 0:Tn], func=AF.Copy, scale=scale),
                             R=[qf32.d], W=[dst.d])
                    P.dma(SP, g32[:, 0:Tn], rows(16), R=[dFT[si]], W=[g32.d])
                    P.dma(SP, v32[:, 0:nch, :], FTM[off:off + Tn, h * 64:(h + 1) * 64].rearrange("(c p) e -> p c e", p=128),
                          R=[dFTM[si]], W=[v32.d])
                    P.op(DVE, lambda e: e.tensor_copy(out=vb[:, 0:nch, :], in_=v32[:, 0:nch, :]), R=[v32.d], W=[vb.d])
                    if kind == 's':
                        P.dma(SP, Sf[:, :], st_ret[l, 0, h], W=[Sf.d])
                        P.dma(SP, Sb[:, :], st_ret[l, 1, h], W=[Sb.d])
                    else:
                        P.op(DVE, lambda e: e.memset(Sf[:, :], 0.0), W=[Sf.d])
                        P.op(DVE, lambda e: e.memset(Sb[:, :], 0.0), W=[Sb.d])

                    def state_update(S, c, kwcol, gcol):
                        pt = ps_tr.next()
                        P.op(PE, lambda e: e.transpose(pt[:, 0:64], kb[:, c * 128:(c + 1) * 128], cpb[0:64, 0:64]),
                             R=[kb.d, cpb.d], W=[pt.d])
                        kw = kw_r.next()
                        P.op(DVE, lambda e: e.tensor_scalar(out=kw[:, :], in0=pt[:, 0:64], scalar1=kwcol, scalar2=None, op0=ALU.mult),
                             R=[pt.d, cp.d], W=[kw.d])
                        pu = ps_up.next()
                        P.op(PE, lambda e: e.matmul(pu[0:64, 0:64], lhsT=kw[:, :], rhs=vb[:, c, :], start=True, stop=True),
                             R=[kw.d, vb.d], W=[pu.d])
                        P.op(DVE, lambda e: e.scalar_tensor_tensor(out=S[:, :], in0=S[:, :], scalar=gcol, in1=pu[0:64, 0:64],
                                                                   op0=ALU.mult, op1=ALU.add), R=[S.d, pu.d, cp.d], W=[S.d])
                    okf, _ = CPC['ret_kwf']
                    okb, _ = CPC['ret_kwb']
                    og, _ = CPC['ret_g128']
                    for c in range(nch - 1, -1, -1):
                        P.op(ACT, lambda e, c=c: e.activation(out=Sbs[:, c, :], in_=Sb[:, :], func=AF.Copy), R=[Sb.d], W=[Sbs.d])
                        state_update(Sb, c, cp[:, okb + h:okb + h + 1], cp[0:64, og + 4 + h:og + 5 + h])
                    if kind == 'p':
                        P.dma(SP, o_ret[sidx, l, 1, h], Sb[:, :], R=[Sb.d], W=[dOUT])
                    for c0 in range(0, nch, 4):
                        ncg = min(4, nch - c0)
                        n = ncg * 128
                        py = ps_y.next()
                        for cc in range(ncg):
                            c = c0 + cc
                            sl = slice(c * 128, (c + 1) * 128)
                            psc = ps_sc.next()
                            P.op(PE, lambda e: e.matmul(psc[:, 0:128], lhsT=kb[:, sl], rhs=qb[:, sl], start=True, stop=True),
                                 R=[kb.d, qb.d], W=[psc.d])
                            PT = PT_r.next()
                            P.op(DVE, lambda e: e.tensor_tensor(out=PT[:, :], in0=psc[:, 0:128], in1=C('ret_ds', sub=(h * 128, (h + 1) * 128)), op=ALU.mult),
                                 R=[psc.d, cp.d], W=[PT.d])
                            qfw = qd_r.next()
                            qbw = qd_r.next()
                            P.op(DVE, lambda e: e.tensor_tensor(out=qfw[:, :], in0=qb[:, sl], in1=C('ret_df', rows=64, sub=(h * 128, (h + 1) * 128)), op=ALU.mult),
                                 R=[qb.d, cp.d], W=[qfw.d])
                            P.op(DVE, lambda e: e.tensor_tensor(out=qbw[:, :], in0=qb[:, sl], in1=C('ret_db', rows=64, sub=(h * 128, (h + 1) * 128)), op=ALU.mult),
                                 R=[qb.d, cp.d], W=[qbw.d])
                            P.op(ACT, lambda e: e.activation(out=Sfb[:, :], in_=Sf[:, :], func=AF.Copy), R=[Sf.d], W=[Sfb.d])
                            yo = py[0:64, cc * 128:(cc + 1) * 128]
                            P.op(PE, lambda e: e.matmul(yo, lhsT=vb[:, c, :], rhs=PT[:, :], start=True, stop=False), R=[vb.d, PT.d], W=[py.d])
                            P.op(PE, lambda e: e.matmul(yo, lhsT=Sfb[:, :], rhs=qfw[:, :], start=False, stop=False), R=[Sfb.d, qfw.d], W=[py.d])
                            P.op(PE, lambda e: e.matmul(yo, lhsT=Sbs[:, c, :], rhs=qbw[:, :], start=False, stop=True), R=[Sbs.d, qbw.d], W=[py.d])
                            state_update(Sf, c, cp[:, okf + h:okf + h + 1], cp[0:64, og + h:og + h + 1])
                        tsl = slice(c0 * 128, c0 * 128 + n)
                        P.op(ACT, lambda e: e.activation(out=sqy[:, 0:n], in_=py[0:64, 0:n], func=AF.Square), R=[py.d], W=[sqy.d])
                        pss = ps_ss.next()
                        P.op(PE, lambda e: e.matmul(pss[0:64, 0:n], lhsT=C('ones', rows=64, sub=(0, 64)), rhs=sqy[:, 0:n], start=True, stop=True),
                             R=[sqy.d, cp.d], W=[pss.d])
                        P.op(ACT, lambda e: e.activation(out=rsy[:, 0:n], in_=pss[0:64, 0:n], func=AF.Sqrt, bias=C('eps', rows=64), scale=1.0 / 64),
                             R=[pss.d, cp.d], W=[rsy.d])
                        P.op(DVE, lambda e: e.reciprocal(out=rsy[:, 0:n], in_=rsy[:, 0:n]), R=[rsy.d], W=[rsy.d])
                        P.op(ACT, lambda e: e.activation(out=sg[:, 0:n], in_=g32[:, tsl], func=AF.Silu), R=[g32.d], W=[sg.d])
                        P.op(DVE, lambda e: e.scalar_tensor_tensor(out=tmy[:, 0:n], in0=py[0:64, 0:n], scalar=pp[l][0:64, og4 + h:og4 + h + 1],
                                                                   in1=rsy[:, 0:n], op0=ALU.mult, op1=ALU.mult),
                             R=[py.d, pp[l].d, rsy.d], W=[tmy.d])
                        oy = oy_r.next()
                        P.op(DVE, lambda e: e.tensor_tensor(out=oy[:, 0:n], in0=tmy[:, 0:n], in1=sg[:, 0:n], op=ALU.mult),
                             R=[tmy.d, sg.d], W=[oy.d])
                        P.dma(ACT, OT[256 + h * 64:256 + (h + 1) * 64, off + c0 * 128: off + c0 * 128 + n], oy[:, 0:n], R=[oy.d], W=[dOT[si]])
                    if kind == 'p':
                        P.dma(SP, o_ret[sidx, l, 0, h], Sf[:, :], R=[Sf.d], W=[dOUT])

        for l in range(L):
            if debug == 'M':
                break
            xin, xout = XT[0], XT[1]
            dxin, dxout = dXT[0], dXT[1]
            P.begin()
            wfm = T(P.sb("wfm", [128, 8, NFM * 128], BF16))
            wtm = T(P.sb("wtm", [128, 8, NTM], BF16))
            for k in range(8):
                P.dma(POOL, wfm[:, k, :], w_in_fm[l, k * 128:(k + 1) * 128, :], W=[wfm.d])
            P.dma(POOL, wtm[:, :, :], w_in_tm[l].rearrange("(k p) n -> p k n", p=128), W=[wtm.d])
            xt_r = Ring([T(P.sb("xt", [128, 8, 512])) for _ in range(2)])
            xtm_r = Ring([T(P.sb("xtm", [128, 1024])) for _ in range(2)])
            sq = Ring([T(P.sb("sq", [128, 512])) for _ in range(2)])
            rs = T(P.sb("rs", [128, 512]))
            tmp_r = Ring([T(P.sb("tmp", [128, 512])) for _ in range(2)])
            hT_r = Ring([T(P.sb("hT", [128, 8, 512], BF16)) for _ in range(2)])
            stg_r = Ring([T(P.sb("stg", [128, 4, 512])) for _ in range(2)])
            stm_r = Ring([T(P.sb("stm", [128, NTM])) for _ in range(2)])
            ps_r = Ring([TPS(P.psum("psA", [128, 512])) for _ in range(7)])
            for si, (kind, sidx, Tn, off) in enumerate(SEQS):
                if debug == 'A1':
                    break
                cond = 0 if kind == 'p' else 1
                nt = min(512, Tn)
                for t0 in range(0, Tn, nt):
                    n = nt
                    xt = xt_r.next()
                    if l == 0:
                        for b in range(n // 128):
                            xtm = xtm_r.next()
                            P.dma(SP, xtm[:, :], x_tm[off + t0 + b * 128: off + t0 + (b + 1) * 128, :], W=[xtm.d])
                            for half in range(2):
                                ps = ps_r.next()
                                for jj in range(4):
                                    j = half * 4 + jj
                                    P.op(PE, lambda e, j=j, jj=jj, ps=ps, xtm=xtm: e.transpose(
                                        ps[:, jj * 128:(jj + 1) * 128], xtm[:, j * 128:(j + 1) * 128], C('ident')),
                                        R=[xtm.d, cp.d], W=[ps.d])
                                evac(xt[:, half * 4:half * 4 + 4, b * 128:(b + 1) * 128],
                                     ps[:, 0:512].rearrange("p (j t) -> p j t", t=128), [ps.d], [xt.d])
                        P.dma(POOL, xin[:, off + t0: off + t0 + n].rearrange("(j p) t -> p j t", p=128), xt[:, :, 0:n],
                              R=[xt.d], W=[dxin[si]])
                    else:
                        P.dma(SP, xt[:, :, 0:n], xin[:, off + t0: off + t0 + n].rearrange("(j p) t -> p j t", p=128),
                              R=[dxin[si]], W=[xt.d])
                    if debug == 'A2':
                        continue
                    hT = hT_r.next()
                    rmsnorm_tile(xt, n, l, 0, cond, hT, sq, ps_r, rs, tmp_r)
                    if debug == 'A3':
                        continue
                    chunks = list(range(NFM))
                    if kind == 'p':
                        chunks = [c for c in chunks if c not in (12, 13, 14, 15, 28, 29, 30, 31)]
                    groups = [chunks[i:i + 4] for i in range(0, len(chunks), 4)]
                    for grp in groups:
                        runs = []
                        for c in grp:
                            if runs and runs[-1][-1] == c - 1:
                                runs[-1].append(c)
                            else:
                                runs.append([c])
                        for run_ in runs:
                            stg = stg_r.next()
                            for ci, c in enumerate(run_):
                                ps = ps_r.next()
                                for k in range(8):
                                    P.op(PE, lambda e, k=k, c=c, ps=ps, hT=hT: e.matmul(
                                        ps[:, 0:n], lhsT=wfm[:, k, c * 128:(c + 1) * 128], rhs=hT[:, k, 0:n],
                                        start=(k == 0), stop=(k == 7)), R=[wfm.d, hT.d], W=[ps.d])
                                evac(stg[:, ci, 0:n], ps[:, 0:n], [ps.d], [stg.d])
                            c0, nc_ = run_[0], len(run_)
                            P.dma(ACT, FT[c0 * 128:(c0 + nc_) * 128, off + t0: off + t0 + n].rearrange("(c p) t -> p c t", p=128),
                                  stg[:, 0:nc_, 0:n], R=[stg.d], W=[dFT[si]])
                    if debug == 'A4':
                        continue
                    for b in range(n // 128):
                        stm = stm_r.next()
                        for (c0, c1) in ((0, 512), (512, NTM)):
                            ps = ps_r.next()
                            for k in range(8):
                                P.op(PE, lambda e, k=k, ps=ps, hT=hT, b=b, c0=c0, c1=c1: e.matmul(
                                    ps[:, 0:c1 - c0], lhsT=hT[:, k, b * 128:(b + 1) * 128], rhs=wtm[:, k, c0:c1],
                                    start=(k == 0), stop=(k == 7)), R=[wtm.d, hT.d], W=[ps.d])
                            evac(stm[:, c0:c1], ps[:, 0:c1 - c0], [ps.d], [stm.d])
                        r0 = off + t0 + b * 128
                        P.dma(ACT, FTM[r0:r0 + 128, :], stm[:, :], R=[stm.d], W=[dFTM[si]])
                        if kind == 'p' and debug != 'A5':
                            tl = t0 + b * 128
                            P.dma(ACT, o_v[sidx, l, tl:tl + 128, :], stm[:, 256:512], R=[stm.d], W=[Dep()] if debug == 'A6' else [dOUT])
                            P.dma(ACT, o_k[sidx, l, tl:tl + 128, :], stm[:, 512:768], R=[stm.d], W=[Dep()] if debug == 'A6' else [dOUT])
            P.end()
            if debug and debug[0] == 'A':
                break

            if not (debug and debug[0] == 'C'):
                phase_B(l)
            if debug == 'B':
                break

            P.begin()
            wo = T(P.sb("wo", [128, 8, D], BF16))
            P.dma(POOL, wo[:, :, :], w_out[l].rearrange("(k p) n -> p k n", p=128), W=[wo.d])
            xt_r = Ring([T(P.sb("xt", [128, 8, 512])) for _ in range(2)])
            ot_r = Ring([T(P.sb("ot", [128, 8, 512], BF16)) for _ in range(2)])
            ps_r = Ring([TPS(P.psum("psC", [128, 512])) for _ in range(6)])
            for si, (kind, sidx, Tn, off) in enumerate(SEQS):
                cond = 0 if kind == 'p' else 1
                nt = min(512, Tn)
                for t0 in range(0, Tn, nt):
                    n = nt
                    xt = xt_r.next()
                    ot = ot_r.next()
                    P.dma(SP, xt[:, :, 0:n], xin[:, off + t0: off + t0 + n].rearrange("(j p) t -> p j t", p=128),
                          R=[dxin[si]], W=[xt.d])
                    P.dma(POOL if (debug and debug[0] == 'C') else SP, ot[:, :, 0:n], OT[:, off + t0: off + t0 + n].rearrange("(j p) t -> p j t", p=128),
                          R=[dOT[si]], W=[ot.d])
                    for j in range(8):
                        ps = ps_r.next()
                        for k in range(8):
                            P.op(PE, lambda e, k=k, j=j, ps=ps, ot=ot: e.matmul(
                                ps[:, 0:n], lhsT=wo[:, k, j * 128:(j + 1) * 128], rhs=ot[:, k, 0:n],
                                start=(k == 0), stop=(k == 7)), R=[wo.d, ot.d], W=[ps.d])
                        P.op(DVE, lambda e, j=j, ps=ps, xt=xt: e.scalar_tensor_tensor(
                            out=xt[:, j, 0:n], in0=ps[:, 0:n], scalar=modcol(l, 2, j, cond), in1=xt[:, j, 0:n],
                            op0=ALU.mult, op1=ALU.add), R=[ps.d, xt.d, modv[l].d], W=[xt.d])
                    P.dma(ACT, xout[:, off + t0: off + t0 + n].rearrange("(j p) t -> p j t", p=128), xt[:, :, 0:n],
                          R=[xt.d], W=[dxout[si]])
            P.end()
            if debug == 'C1':
                break
            if debug == 'CA':
                continue
            P.begin()
            wu = T(P.sb("wu", [128, 8, 2 * FFN], BF16))
            wd = T(P.sb("wd", [128, 22, D], BF16))
            for k in range(8):
                P.dma(POOL, wu[:, k, :], w_fup[l, k * 128:(k + 1) * 128, :], W=[wu.d])
            for g in range(22):
                P.dma(POOL, wd[:, g, :], w_fdn[l, g * 128:(g + 1) * 128, :], W=[wd.d])
            xt_r = Ring([T(P.sb("xt", [128, 8, 384])) for _ in range(1)])
            sq = Ring([T(P.sb("sq", [128, 384])) for _ in range(2)])
            rs = T(P.sb("rs", [128, 384]))
            tmp_r = Ring([T(P.sb("tmp", [128, 384])) for _ in range(1)])
            tg_r = Ring([T(P.sb("tg", [128, 384])) for _ in range(1)])
            tv_r = Ring([T(P.sb("tv", [128, 384])) for _ in range(1)])
            hT_r = Ring([T(P.sb("hT", [128, 8, 384], BF16)) for _ in range(1)])
            aT = T(P.sb("aT", [128, 22, 384], BF16))
            ps_r = Ring([TPS(P.psum("psC", [128, 384])) for _ in range(7)])
            ocw, _ = PPC['fcw']
            ocb, _ = PPC['fcb']
            for si, (kind, sidx, Tn, off) in enumerate(SEQS):
                cond = 0 if kind == 'p' else 1
                nt = min(382, Tn)
                for t0 in range(0, Tn, nt):
                    n = min(nt, Tn - t0)
                    N = n + 2
                    lo, hi = max(t0 - 1, 0), min(t0 + n + 1, Tn)
                    c_lo = lo - (t0 - 1)
                    xt = xt_r.next()
                    P.dma(SP, xt[:, :, c_lo:c_lo + hi - lo], xout[:, off + lo: off + hi].rearrange("(j p) t -> p j t", p=128),
                          R=[dxout[si]], W=[xt.d])
                    hT = hT_r.next()
                    rmsnorm_tile(xt, N, l, 1, cond, hT, sq, ps_r, rs, tmp_r)
                    a0 = 1 if t0 == 0 else 0
                    b1 = n - 1 if t0 + n == Tn else n
                    for g in range(22):
                        tt = []
                        for ci, (c, ring) in enumerate(((g, tg_r), (22 + g, tv_r))):
                            ps = ps_r.next()
                            for k in range(8):
                                P.op(PE, lambda e, k=k, c=c, ps=ps, hT=hT: e.matmul(
                                    ps[:, 0:N], lhsT=wu[:, k, c * 128:(c + 1) * 128], rhs=hT[:, k, 0:N],
                                    start=(k == 0), stop=(k == 7)), R=[wu.d, hT.d], W=[ps.d])
                            t = ring.next()
                            P.op(ACT, lambda e, c=c, ps=ps, t=t: e.activation(
                                out=t[:, 0:n], in_=ps[:, 1:n + 1], func=AF.Identity,
                                bias=pp[l][:, ocb + c:ocb + c + 1], scale=pp[l][:, ocw + 44 + c:ocw + 44 + c + 1]),
                                R=[ps.d, pp[l].d], W=[t.d])
                            P.op(DVE, lambda e, c=c, ps=ps, t=t: e.scalar_tensor_tensor(
                                out=t[:, a0:n], in0=ps[:, a0:n], scalar=pp[l][:, ocw + c:ocw + c + 1], in1=t[:, a0:n],
                                op0=ALU.mult, op1=ALU.add), R=[ps.d, pp[l].d, t.d], W=[t.d])
                            P.op(DVE, lambda e, c=c, ps=ps, t=t: e.scalar_tensor_tensor(
                                out=t[:, 0:b1], in0=ps[:, 2:b1 + 2], scalar=pp[l][:, ocw + 88 + c:ocw + 88 + c + 1], in1=t[:, 0:b1],
                                op0=ALU.mult, op1=ALU.add), R=[ps.d, pp[l].d, t.d], W=[t.d])
                            tt.append(t)
                        tg, tv = tt
                        P.op(ACT, lambda e, tg=tg: e.activation(out=tg[:, 0:n], in_=tg[:, 0:n], func=AF.Silu), R=[tg.d], W=[tg.d])
                        P.op(DVE, lambda e, tg=tg, tv=tv, g=g: e.tensor_tensor(out=aT[:, g, 0:n], in0=tg[:, 0:n], in1=tv[:, 0:n], op=ALU.mult),
                             R=[tg.d, tv.d], W=[aT.d])
                    for j in range(8):
                        ps = ps_r.next()
                        for g in range(22):
                            P.op(PE, lambda e, g=g, j=j, ps=ps: e.matmul(
                                ps[:, 0:n], lhsT=wd[:, g, j * 128:(j + 1) * 128], rhs=aT[:, g, 0:n],
                                start=(g == 0), stop=(g == 21)), R=[wd.d, aT.d], W=[ps.d])
                        P.op(DVE, lambda e, j=j, ps=ps, xt=xt: e.scalar_tensor_tensor(
                            out=xt[:, j, 1:n + 1], in0=ps[:, 0:n], scalar=modcol(l, 5, j, cond), in1=xt[:, j, 1:n + 1],
                            op0=ALU.mult, op1=ALU.add), R=[ps.d, xt.d, modv[l].d], W=[xt.d])
                    P.dma(ACT, xin[:, off + t0: off + t0 + n].rearrange("(j p) t -> p j t", p=128), xt[:, :, 1:n + 1],
                          R=[xt.d], W=[dxin[si]])
            P.end()
            if debug == 'C':
                break

        if not debug:
            xin, dxin = XT[0], dXT[0]
            P.begin()
            xt_r = Ring([T(P.sb("xt", [128, 8, 512])) for _ in range(2)])
            sq = Ring([T(P.sb("sq", [128, 512])) for _ in range(2)])
            rs = T(P.sb("rs", [128, 512]))
            ytm_r = Ring([T(P.sb("ytm", [128, 1024])) for _ in range(2)])
            ps_r = Ring([TPS(P.psum("psF", [128, 512])) for _ in range(6)])
            onf, _ = PPC['nfg']
            for si, (kind, sidx, Tn, off) in enumerate(SEQS):
                nt = min(512, Tn)
                for t0 in range(0, Tn, nt):
                    n = nt
                    xt = xt_r.next()
                    P.dma(SP, xt[:, :, 0:n], xin[:, off + t0: off + t0 + n].rearrange("(j p) t -> p j t", p=128),
                          R=[dxin[si]], W=[xt.d])
                    ps = ps_r.next()
                    for k in range(8):
                        sqk = sq.next()
                        P.op(ACT, lambda e, k=k, sqk=sqk, xt=xt: e.activation(out=sqk[:, 0:n], in_=xt[:, k, 0:n], func=AF.Square), R=[xt.d], W=[sqk.d])
                        P.op(PE, lambda e, k=k, sqk=sqk, ps=ps: e.matmul(ps[:, 0:n], lhsT=C('ones'), rhs=sqk[:, 0:n], start=(k == 0), stop=(k == 7)),
                             R=[sqk.d, cp.d], W=[ps.d])
                    P.op(ACT, lambda e, ps=ps: e.activation(out=rs[:, 0:n], in_=ps[:, 0:n], func=AF.Sqrt, bias=C('eps'), scale=1.0 / D),
                         R=[ps.d, cp.d], W=[rs.d])
                    P.op(DVE, lambda e: e.reciprocal(out=rs[:, 0:n], in_=rs[:, 0:n]), R=[rs.d], W=[rs.d])
                    for j in range(8):
                        P.op(DVE, lambda e, j=j, xt=xt: e.scalar_tensor_tensor(
                            out=xt[:, j, 0:n], in0=xt[:, j, 0:n], scalar=pp[0][:, onf + j:onf + j + 1], in1=rs[:, 0:n],
                            op0=ALU.mult, op1=ALU.mult), R=[xt.d, pp[0].d, rs.d], W=[xt.d])
                    for b in range(n // 128):
                        ytm = ytm_r.next()
                        for half in range(2):
                            ps = ps_r.next()
                            for jj in range(4):
                                j = half * 4 + jj
                                P.op(PE, lambda e, j=j, jj=jj, ps=ps, xt=xt, b=b: e.transpose(
                                    ps[:, jj * 128:(jj + 1) * 128], xt[:, j, b * 128:(b + 1) * 128], C('ident')),
                                    R=[xt.d, cp.d], W=[ps.d])
                            evac(ytm[:, half * 512:(half + 1) * 512], ps[:, 0:512], [ps.d], [ytm.d])
                        tl = t0 + b * 128
                        if kind == 'p':
                            P.dma(ACT, y_p[sidx * TP + tl: sidx * TP + tl + 128, :], ytm[:, :], R=[ytm.d], W=[dOUT])
                        else:
                            P.dma(ACT, y_s[tl:tl + 128, :], ytm[:, :], R=[ytm.d], W=[dOUT])
            P.end()
        P.begin()
        for k, v in list(P.dtot.items()):
            P.q[POOL].append(([(k, v)], None, None))
        P.end()
    return nc


def make_in_maps(inp):
    fm, tm = build_perm()
    w_in = np.asarray(inp['w_in'], np.float32)
    w_aug = np.concatenate([w_in, np.zeros((L, D, 1), np.float32)], axis=2)
    w_in_fm = np.ascontiguousarray(w_aug[:, :, np.where(fm < 0, w_in.shape[2], fm)])
    w_in_tm = np.ascontiguousarray(w_in[:, :, tm])
    pp = np.stack([pack_params(inp, l) for l in range(L)])
    cp = build_consts()
    rope = rope_tables()
    maps = []
    for c in range(8):
        b = c // 4
        x_tm = np.concatenate([np.asarray(inp['x_prompt'][4 * c:4 * c + 4], np.float32).reshape(NPS * TP, D),
                               np.asarray(inp['x_sample'][b], np.float32)], axis=0)
        cvec = np.concatenate([fm_cols(inp['c_ctx']), fm_cols(inp['c'][b])], axis=1)
        maps.append(dict(
            x_tm=np.ascontiguousarray(x_tm), cvec=np.ascontiguousarray(cvec), w_mod=np.asarray(inp['w_mod'], np.float32),
            w_in_fm=w_in_fm, w_in_tm=w_in_tm, w_out=np.asarray(inp['w_out'], np.float32),
            w_fup=np.asarray(inp['ffn_w_up'], np.float32), w_fdn=np.asarray(inp['ffn_w_down'], np.float32),
            pp=pp, cp=cp, rope=rope,
            st_rwkv=np.ascontiguousarray(inp['state_rwkv'][b]), st_ret=np.ascontiguousarray(inp['state_ret'][b]),
            st_ssd=np.ascontiguousarray(inp['state_ssd'][b]),
            ck=np.ascontiguousarray(np.asarray(inp['cache_diff_k'][b]).reshape(L, 512, 256)),
            cv=np.ascontiguousarray(np.asarray(inp['cache_diff_v'][b]).reshape(L, 512, 256)),
        ))
    return maps


def kernel(**inputs):
    inp = {k: np.asarray(v) for k, v in inputs.items()}
    nc = build()
    maps = make_in_maps(inp)
    res = run_bass_kernel_spmd(nc, maps, core_ids=list(range(8)))
    r = res.results
    y_prompt = np.concatenate([r[c]['y_p'].reshape(NPS, TP, D) for c in range(8)], axis=0)
    y_sample = np.stack([np.concatenate([r[b * 4 + j]['y_s'][j * 1024:(j + 1) * 1024] for j in range(4)], axis=0) for b in range(2)])
    o_rwkv = np.concatenate([r[c]['o_rwkv'] for c in range(8)], axis=0)
    o_ret = np.concatenate([r[c]['o_ret'] for c in range(8)], axis=0)
    o_ssd = np.concatenate([r[c]['o_ssd'] for c in range(8)], axis=0)
    o_k = np.concatenate([r[c]['o_k'] for c in range(8)], axis=0).reshape(32, L, TP, 4, 2, 32)
    o_v = np.concatenate([r[c]['o_v'] for c in range(8)], axis=0).reshape(32, L, TP, 4, 64)
    return (y_prompt.astype(np.float32), y_sample.astype(np.float32), o_rwkv.astype(np.float32), o_ret.astype(np.float32),
            o_ssd.astype(np.float32), o_k.astype(np.float32), o_v.astype(np.float32))
```

```python
import math
import numpy as np
from contextlib import ExitStack
import concourse.bass as bass
import concourse.mybir as mybir
from concourse.bass_utils import run_bass_kernel_spmd

F32 = mybir.dt.float32
BF16 = mybir.dt.bfloat16
ALU = mybir.AluOpType
AF = mybir.ActivationFunctionType
PE, ACT, DVE, POOL, SP = 'pe', 'act', 'dve', 'pool', 'sp'

D = 1024
L = 2
TP = 256
TS = 4096
NPS = 4
TTOT = NPS * TP + TS
SEQS = [('p', i, TP, i * TP) for i in range(NPS)] + [('s', 0, TS, NPS * TP)]
NFM = 32
NTM = 776
FFN = 2816
EPS = 1e-6
MIXERS = ('ret', 'diff', 'ssd', 'rwkv')
RWKV_STAGES = 3
RW_DBG = {'seqs': 5, 'heads': 4, 'stop': 99, 'var': 0}


class Dep:
    __slots__ = ('name', 'lw', 'rd', 'dsem', 'sb', 'ps')

    def __init__(self, name='', sb=False):
        self.name = name
        self.lw = None
        self.rd = {}
        self.dsem = None
        self.sb = sb
        self.ps = False


class Rec:
    def __init__(self):
        self.call = None

    def __getattr__(self, name):
        def f(*a, **kw):
            self.call = (name, a, kw)
        return f


class Prog:
    def __init__(self, nc, es):
        self.nc = nc
        self.es = es
        self.ph = None
        self.q = {PE: [], ACT: [], DVE: [], POOL: [], SP: []}
        self.cnt = {PE: 0, ACT: 0, DVE: 0, POOL: 0}
        self.seen = {k: {} for k in self.q}
        self.sems = {}
        self.ndsem = 0
        self.nwsem = 0
        for e in (PE, ACT, DVE, POOL):
            self.sems[e] = es.enter_context(nc.semaphore('s_' + e))
        self.dtot = {}
        self.nuniq = 0
        self.shared_dsem = None

    def sb(self, name, shape, dt=F32):
        self.nuniq += 1
        return self.ph.enter_context(self.nc.sbuf_tensor('%s_%d' % (name, self.nuniq), list(shape), dt))

    def psum(self, name, shape, dt=F32):
        self.nuniq += 1
        return self.ph.enter_context(self.nc.psum_tensor('%s_%d' % (name, self.nuniq), list(shape), dt))

    def dram(self, name, shape, dt=F32):
        return self.nc.dram_tensor(name, list(shape), dt, kind="Internal").ap()

    def _dsem(self, d, queue=None):
        if d.dsem is None:
            if queue == POOL:
                d.dsem = 'w%d' % self.nwsem
                self.nwsem += 1
            else:
                d.dsem = 'd%d' % self.ndsem
                self.ndsem += 1
            if d.dsem not in self.sems:
                self.sems[d.dsem] = self.es.enter_context(self.nc.semaphore(d.dsem))
        return d.dsem

    def _waits(self, eng, R, W):
        waits = {}

        def add(k, v):
            if k == eng and eng == PE:
                return
            if k in self.dtot:
                v = self.dtot[k]
            if waits.get(k, 0) < v:
                waits[k] = v
        for d in R:
            if d.lw is not None:
                add(*d.lw)
            if d.ps:
                for k, v in d.rd.items():
                    if k != eng:
                        add(k, v)
        for d in W:
            if d.lw is not None and d.lw[0] != eng:
                add(*d.lw)
            for k, v in d.rd.items():
                if k != eng:
                    add(k, v)
        out = []
        seen = self.seen[eng]
        for k, v in waits.items():
            if seen.get(k, 0) < v:
                seen[k] = v
                out.append((k, v))
        return out

    def op(self, eng, fn, R=(), W=()):
        waits = self._waits(eng, R, W)
        idx = self.cnt[eng]
        self.cnt[eng] = idx + 1
        rec = Rec()
        fn(rec)
        name, a, kw = rec.call

        def fn2(e, name=name, a=a, kw=kw):
            return getattr(e, name)(*a, **kw)
        self.q[eng].append((waits, fn2, (eng, 1)))
        for d in R:
            if d.rd.get(eng, 0) < idx + 1:
                d.rd[eng] = idx + 1
        for d in W:
            d.lw = (eng, idx + 1)
            d.rd = {}

    def dma(self, queue, out, in_, R=(), W=(), **kw):
        waits = self._waits(queue, R, W)
        tgt = (list(W) + list(R))
        d0 = None
        for d in tgt:
            if d.sb:
                d0 = d
                break
        if d0 is None:
            d0 = tgt[0]
        k0 = self._dsem(d0, queue)
        self.dtot[k0] = self.dtot.get(k0, 0) + 16
        tot = self.dtot[k0]

        def fn(e, out=out, in_=in_, kw=kw):
            return e.dma_start(out=out, in_=in_, **kw)
        self.q[queue].append((waits, fn, (k0, 16)))
        for d in W:
            d.lw = (k0, tot)
            d.rd = {}
        for d in R:
            if d.rd.get(k0, 0) < tot:
                d.rd[k0] = tot

    def barrier(self):
        allk = [(k, self.cnt[k]) for k in (PE, ACT, DVE, POOL)] + list(self.dtot.items())
        for eng in self.q:
            waits = []
            seen = self.seen[eng]
            for k, v in allk:
                if k == eng:
                    continue
                if seen.get(k, 0) < v:
                    seen[k] = v
                    waits.append((k, v))
            self.q[eng].append((waits, None, None))

    def begin(self):
        self.ph = ExitStack()
        self.ndsem = 8 if self.ndsem >= 8 else self.ndsem
        self.nwsem = 0

    def end(self, final=False):
        self.barrier()
        nc = self.nc
        q = self.q
        sems = self.sems

        with nc.Block() as block:
            def run(e, name):
                for waits, fn, inc in q[name]:
                    for k, v in waits:
                        e.wait_ge(sems[k], v)
                    if fn is not None:
                        ins = fn(e)
                        ins.then_inc(sems[inc[0]], inc[1])

            @block.tensor
            def _(e):
                run(e, PE)

            @block.scalar
            def _(e):
                run(e, ACT)

            @block.vector
            def _(e):
                run(e, DVE)

            @block.gpsimd
            def _(e):
                run(e, POOL)

            @block.sync
            def _(e):
                run(e, SP)
        for k in q:
            q[k] = []
        self.ph.close()
        self.ph = None


class T:
    def __init__(self, t, name=''):
        self.t = t
        self.d = Dep(name, sb=True)

    def __getitem__(self, idx):
        return self.t[idx]


def TPS(t):
    x = T(t)
    x.d.ps = True
    x.d.sb = False
    return x


class Ring:
    def __init__(self, tiles):
        self.tiles = tiles
        self.i = 0

    def next(self):
        t = self.tiles[self.i % len(self.tiles)]
        self.i += 1
        return t


def fm_cols(v):
    v = np.asarray(v, np.float32)
    return np.ascontiguousarray(v.reshape(-1, 128).T)


def build_perm():
    RW, RET, SSD, DIF = 0, 960, 1984, 2760
    fm = []
    fm += list(range(RW, RW + 768))
    fm += list(range(RW + 768, RW + 896))
    fm += list(range(RW + 896, RW + 960)) + [-1] * 64
    q = list(range(RET, RET + 256))
    k = list(range(RET + 256, RET + 512))

    def swap(cols, blk):
        out = []
        for i in range(0, len(cols), blk):
            b = cols[i:i + blk]
            out += b[blk // 2:] + b[:blk // 2]
        return out
    fm += q + k + swap(q, 64) + swap(k, 64)
    fm += list(range(RET + 768, RET + 1024))
    fm += list(range(SSD, SSD + 256))
    fm += list(range(SSD + 256, SSD + 768))
    dq = list(range(DIF, DIF + 256))
    dk = list(range(DIF + 256, DIF + 512))
    fm += dq + dk + swap(dq, 32) + swap(dk, 32)
    assert len(fm) == NFM * 128
    tm = list(range(RET + 512, RET + 768)) + list(range(DIF + 512, DIF + 768)) + dk + list(range(SSD + 768, SSD + 776))
    assert len(tm) == NTM
    return np.array(fm), np.array(tm)


PPC = {}
_o = 0
for _n, _w in [('n1g', 8), ('n2g', 8), ('bmod', 48), ('mu', 8), ('w0', 4), ('a0', 4), ('k_k', 2), ('k_a', 2),
               ('r_k', 2), ('ln_g', 2), ('ln_b', 2), ('ret_g', 2), ('scw', 12), ('scb', 4), ('ssd_g', 2), ('ssd_d', 2),
               ('dtb', 8), ('alog', 8), ('subg', 1), ('lp', 128), ('fcw', 132), ('fcb', 44), ('w_up', 256),
               ('a_up', 256), ('g_up', 256), ('nfg', 8), ('ret_g4', 4), ('scw64', 24), ('scb64', 8), ('ssd_d4', 4), ('ssd_g4', 4), ('mu64', 16), ('w0_64', 8), ('a0_64', 8), ('kk64', 4), ('ka64', 4), ('rk64', 4), ('lng64', 4), ('lnb64', 4)]:
    PPC[_n] = (_o, _w)
    _o += _w
NPP = _o


def pack_params(inp, l):
    pp = np.zeros((128, NPP), np.float32)

    def put(name, arr):
        o, w = PPC[name]
        arr = np.asarray(arr, np.float32)
        assert arr.shape[1] == w, (name, arr.shape, w)
        pp[:arr.shape[0], o:o + w] = arr
    put('n1g', fm_cols(inp['norm1_g'][l]))
    put('n2g', fm_cols(inp['norm2_g'][l]))
    put('bmod', fm_cols(inp['b_mod'][l]))
    mu = np.zeros(1024, np.float32)
    mu[:960] = inp['rwkv_mu'][l]
    put('mu', fm_cols(mu))
    put('w0', fm_cols(inp['rwkv_w0'][l].reshape(-1)))
    put('a0', fm_cols(inp['rwkv_a0'][l].reshape(-1)))
    put('k_k', fm_cols(inp['rwkv_k_k'][l]))
    put('k_a', fm_cols(inp['rwkv_k_a'][l]))
    put('r_k', fm_cols(inp['rwkv_r_k'][l].reshape(-1)))
    put('ln_g', fm_cols(inp['rwkv_ln_g'][l]))
    put('ln_b', fm_cols(inp['rwkv_ln_b'][l]))
    put('ret_g', fm_cols(inp['ret_ln_g'][l]))
    cw = inp['ssd_conv_w'][l]
    put('scw', np.concatenate([fm_cols(cw[i]) for i in range(3)], axis=1))
    put('scb', fm_cols(inp['ssd_conv_b'][l]))
    put('ssd_g', fm_cols(inp['ssd_norm_g'][l]))
    put('ssd_d', fm_cols(np.repeat(inp['ssd_d'][l], 64)))
    put('dtb', np.broadcast_to(inp['ssd_dt_bias'][l].reshape(1, 8), (128, 8)))
    put('alog', np.broadcast_to(inp['ssd_a_log'][l].reshape(1, 8), (128, 8)))
    put('subg', np.concatenate([inp['diff_subln_g'][l], inp['diff_subln_g'][l]]).reshape(128, 1))
    put('lp', np.broadcast_to(inp['diff_lambda'][l].reshape(1, 128), (128, 128)))
    fw = inp['ffn_conv_w'][l]
    put('fcw', np.concatenate([fm_cols(fw[i]) for i in range(3)], axis=1))
    put('fcb', fm_cols(inp['ffn_conv_b'][l]))
    put('w_up', inp['rwkv_w_up'][l].reshape(64, 256))
    put('a_up', inp['rwkv_a_up'][l].reshape(64, 256))
    put('g_up', inp['rwkv_g_up'][l].reshape(64, 256))
    put('nfg', fm_cols(inp['norm_f_g']))
    put('ret_g4', inp['ret_ln_g'][l].reshape(4, 64).T)
    put('scw64', np.concatenate([cw[i].reshape(8, 64).T for i in range(3)], axis=1))
    put('scb64', inp['ssd_conv_b'][l].reshape(8, 64).T)
    put('ssd_d4', np.broadcast_to(inp['ssd_d'][l].reshape(1, 4), (64, 4)))
    put('ssd_g4', inp['ssd_norm_g'][l].reshape(4, 64).T)
    put('mu64', mu.reshape(16, 64).T)
    put('w0_64', inp['rwkv_w0'][l].reshape(8, 64).T)
    put('a0_64', inp['rwkv_a0'][l].reshape(8, 64).T)
    put('kk64', inp['rwkv_k_k'][l].reshape(4, 64).T)
    put('ka64', inp['rwkv_k_a'][l].reshape(4, 64).T)
    put('rk64', inp['rwkv_r_k'][l].reshape(4, 64).T)
    put('lng64', inp['rwkv_ln_g'][l].reshape(4, 64).T)
    put('lnb64', inp['rwkv_ln_b'][l].reshape(4, 64).T)
    return pp


CPC = {}
_o = 0
for _n, _w in [('ident', 128), ('ones', 128), ('bd64', 128), ('tri_f', 128), ('tri_b', 128), ('nm_f', 128), ('nm_b', 128),
               ('ret_ds', 512), ('ret_df', 512), ('ret_db', 512), ('ret_kwf', 4), ('ret_kwb', 4), ('ret_g128', 8),
               ('eps', 1), ('rw_ms', 64), ('rw_mi', 64), ('rw_msT', 64), ('rw_miT', 64), ('scanmask', 512), ('rw_mask5', 320)]:
    CPC[_n] = (_o, _w)
    _o += _w
NCP = _o


def build_consts():
    cp = np.zeros((128, NCP), np.float32)

    def put(name, arr):
        o, w = CPC[name]
        arr = np.asarray(arr, np.float32)
        assert arr.shape[1] == w
        cp[:arr.shape[0], o:o + w] = arr
    i = np.arange(128)
    put('ident', np.eye(128))
    put('ones', np.ones((128, 128)))
    bd = np.zeros((128, 128))
    bd[:64, :64] = 1
    bd[64:, 64:] = 1
    put('bd64', bd)
    put('tri_f', (i[:, None] <= i[None, :]).astype(np.float32))
    put('tri_b', (i[:, None] >= i[None, :]).astype(np.float32))
    put('nm_f', np.where(i[None, :] >= i[:, None], 0.0, -30000.0))
    put('nm_b', np.where(i[None, :] <= i[:, None], 0.0, -30000.0))
    ds = np.zeros((128, 512))
    df = np.zeros((128, 512))
    db = np.zeros((128, 512))
    kwf = np.zeros((128, 4))
    kwb = np.zeros((128, 4))
    g128 = np.zeros((128, 8))
    for h in range(4):
        lf = math.log(1.0 - 2.0 ** (-5.0 - h))
        lb = math.log(1.0 - 2.0 ** (-5.5 - h))
        jj, ii = i[:, None], i[None, :]
        m = np.where(ii >= jj, np.exp(lf * (ii - jj)), 0.0) + np.where(ii <= jj, np.exp(lb * (jj - ii)), 0.0)
        ds[:, h * 128:(h + 1) * 128] = m
        df[:, h * 128:(h + 1) * 128] = np.exp(lf * (i + 1))[None, :]
        db[:, h * 128:(h + 1) * 128] = np.exp(lb * (128 - i))[None, :]
        kwf[:, h] = np.exp(lf * (127 - i))
        kwb[:, h] = np.exp(lb * i)
        g128[:, h] = math.exp(lf * 128)
        g128[:, 4 + h] = math.exp(lb * 128)
    put('ret_ds', ds)
    put('ret_df', df)
    put('ret_db', db)
    put('ret_kwf', kwf)
    put('ret_kwb', kwb)
    put('ret_g128', g128)
    put('eps', np.full((128, 1), EPS))
    c = np.arange(64)
    put('rw_ms', (c[None, :] < c[:, None]).astype(np.float32))
    put('rw_mi', (c[None, :] <= c[:, None]).astype(np.float32))
    put('rw_msT', (c[:, None] < c[None, :]).astype(np.float32))
    put('rw_miT', (c[:, None] <= c[None, :]).astype(np.float32))
    sm = np.ones((128, 512))
    sm[:, ::64] = 0.0
    put('scanmask', sm)
    msT = (c[:, None] < c[None, :]).astype(np.float32)
    miT = (c[:, None] <= c[None, :]).astype(np.float32)
    ms = (c[None, :] < c[:, None]).astype(np.float32)
    put('rw_mask5', np.concatenate([msT, miT, msT, miT, ms], axis=1))
    return cp


def rope_tables():
    rows = TS // 64

    def tabs(d):
        nf = d // 4
        row = np.repeat(np.arange(rows, dtype=np.float32), 64)
        col = np.tile(np.arange(64, dtype=np.float32), rows)
        freqs = (np.float32(10000.0) ** (-np.arange(nf, dtype=np.float32) / nf)).astype(np.float32)
        ang = np.concatenate([row[:, None] * freqs, col[:, None] * freqs], axis=-1).astype(np.float32)
        return np.cos(ang).T.astype(np.float32), np.sin(ang).T.astype(np.float32)
    c, s = tabs(64)
    ret_c = np.concatenate([c, c, c, c], 0)
    ret_s = np.concatenate([-s, s, -s, s], 0)
    c, s = tabs(32)
    dif_c = np.concatenate([c, c] * 4, 0)
    dif_s = np.concatenate([-s, s] * 4, 0)
    return np.stack([ret_c, ret_s, dif_c, dif_s]).astype(np.float32)


def build(debug=False):
    nc = bass.Bass("TRN2", target_bir_lowering=False)

    def din(name, shape):
        return nc.dram_tensor(name, list(shape), F32, kind="ExternalInput").ap()

    def dout(name, shape):
        return nc.dram_tensor(name, list(shape), F32, kind="ExternalOutput").ap()
    x_tm = din("x_tm", [TTOT, D])
    cvec = din("cvec", [128, 16])
    w_mod = din("w_mod", [L, D, 6 * D])
    w_in_fm = din("w_in_fm", [L, D, NFM * 128])
    w_in_tm = din("w_in_tm", [L, D, NTM])
    w_out = din("w_out", [L, D, D])
    w_fup = din("w_fup", [L, D, 2 * FFN])
    w_fdn = din("w_fdn", [L, FFN, D])
    pp_in = din("pp", [L, 128, NPP])
    cp_in = din("cp", [128, NCP])
    rope_in = din("rope", [4, 128, TS])
    st_rwkv = din("st_rwkv", [L, 2, 4, 64, 64])
    st_ret = din("st_ret", [L, 2, 4, 64, 64])
    st_ssd = din("st_ssd", [L, 2, 4, 64, 64])
    ck_in = din("ck", [L, 512, 256])
    cv_in = din("cv", [L, 512, 256])

    y_p = dout("y_p", [NPS * TP, D])
    y_s = dout("y_s", [TS, D])
    o_rwkv = dout("o_rwkv", [NPS, L, 2, 4, 64, 64])
    o_ret = dout("o_ret", [NPS, L, 2, 4, 64, 64])
    o_ssd = dout("o_ssd", [NPS, L, 2, 4, 64, 64])
    o_k = dout("o_k", [NPS, L, TP, 256])
    o_v = dout("o_v", [NPS, L, TP, 256])

    es = ExitStack()
    with es:
        P = Prog(nc, es)
        mk = (lambda n, s: dout(n, s)) if (debug and debug != 'B') else (lambda n, s: P.dram(n, s))
        XT = [mk("XT0", [D, TTOT]), mk("XT1", [D, TTOT])]
        FT = mk("FT", [NFM * 128, TTOT])
        FTM = mk("FTM", [TTOT, NTM])
        OTDT = F32 if debug == 'B' else BF16
        if debug == 'B':
            OT = nc.dram_tensor("OT_dbg", [D, TTOT], F32, kind="ExternalOutput").ap()
        elif debug and debug[0] == 'C':
            OT = nc.dram_tensor("OT_in", [D, TTOT], F32, kind="ExternalInput").ap()
        else:
            OT = nc.dram_tensor("OT", [D, TTOT], BF16, kind="Internal").ap()
        dXT = [[Dep() for _ in SEQS] for _ in range(2)]
        dFT = [Dep() for _ in SEQS]
        dFTM = [Dep() for _ in SEQS]
        dOT = [Dep() for _ in SEQS]
        dOUT = Dep('outs')
        RW = P.dram("RW", [11 * 256, TTOT])
        YS = P.dram("YS", [2 * 256, TTOT])
        dRW = [Dep() for _ in SEQS]
        dYS = [Dep() for _ in SEQS]

        P.ph = es
        cp = T(P.sb("cp", [128, NCP]))
        pp = [T(P.sb("pp%d" % l, [128, NPP])) for l in range(L)]
        modv = [T(P.sb("modv%d" % l, [128, 48, 2])) for l in range(L)]
        modA = [T(P.sb("modA%d" % l, [128, 2, 8, 2])) for l in range(L)]
        cpb = T(P.sb("cpb", [128, 640], BF16))
        P.ph = None

        def C(name, rows=128, sub=None):
            o, w = CPC[name]
            if sub is not None:
                return cp[0:rows, o + sub[0]:o + sub[1]]
            return cp[0:rows, o:o + w]

        def PPv(l, name, col=0, rows=128, ncol=1):
            o, w = PPC[name]
            return pp[l][0:rows, o + col:o + col + ncol]

        P.begin()
        P.dma(SP, cp[:, :], cp_in[:, :], W=[cp.d])
        for l in range(L):
            P.dma(SP, pp[l][:, :], pp_in[l], W=[pp[l].d])
        P.op(DVE, lambda e: e.tensor_copy(out=cpb[:, 0:384], in_=cp[:, 0:384]), R=[cp.d], W=[cpb.d])
        cv = T(P.sb("cv", [128, 16]))
        scv = T(P.sb("scv", [128, 8, 2]))
        P.dma(SP, cv[:, :], cvec[:, :], W=[cv.d])
        P.op(ACT, lambda e: e.activation(out=scv[:, :, 0], in_=cv[:, 0:8], func=AF.Silu), R=[cv.d], W=[scv.d])
        P.op(ACT, lambda e: e.activation(out=scv[:, :, 1], in_=cv[:, 8:16], func=AF.Silu), R=[cv.d], W=[scv.d])
        wm = Ring([T(P.sb("wm", [128, 8, 512])) for _ in range(2)])
        pm = TPS(P.psum("pm", [128, 512]))
        for l in range(L):
            for g in range(12):
                w = wm.next()
                P.dma(SP, w[:, :, :], w_mod[l, :, g * 512:(g + 1) * 512].rearrange("(k p) n -> p k n", p=128), W=[w.d])
                for cc in range(4):
                    ch = g * 4 + cc
                    for k in range(8):
                        P.op(PE, lambda e, w=w, cc=cc, k=k, ch=ch: e.matmul(
                            pm[:, ch * 2:ch * 2 + 2], lhsT=w[:, k, cc * 128:(cc + 1) * 128], rhs=scv[:, k, :],
                            start=(k == 0), stop=(k == 7)), R=[w.d, scv.d], W=[pm.d])
            o, _ = PPC['bmod']
            P.op(DVE, lambda e, l=l, o=o: e.tensor_tensor(
                out=modv[l][:, :, :], in0=pm[:, 0:96].rearrange("p (c t) -> p c t", t=2),
                in1=pp[l][:, o:o + 48].unsqueeze(2).to_broadcast([128, 48, 2]), op=ALU.add),
                R=[pm.d, pp[l].d], W=[modv[l].d])
            for n, (gname, which) in enumerate([('n1g', 1), ('n2g', 4)]):
                og, _ = PPC[gname]
                P.op(DVE, lambda e, l=l, n=n, og=og, which=which: e.scalar_tensor_tensor(
                    out=modA[l][:, n, :, :], in0=modv[l][:, which * 8:(which + 1) * 8, :], scalar=1.0,
                    in1=pp[l][:, og:og + 8].unsqueeze(2).to_broadcast([128, 8, 2]), op0=ALU.add, op1=ALU.mult),
                    R=[modv[l].d, pp[l].d], W=[modA[l].d])
        P.end()

        def modcol(l, which, j, cond):
            return modv[l][:, which * 8 + j, cond:cond + 1]

        def rmsnorm_tile(xt, n, l, which_norm, cond, hT, sq, ps_ring, rs, tmp_ring):
            ps = ps_ring.next()
            for k in range(8):
                sqk = sq.next()
                P.op(ACT, lambda e, k=k, sqk=sqk: e.activation(out=sqk[:, 0:n], in_=xt[:, k, 0:n], func=AF.Square), R=[xt.d], W=[sqk.d])
                P.op(PE, lambda e, k=k, sqk=sqk: e.matmul(ps[:, 0:n], lhsT=C('ones'), rhs=sqk[:, 0:n], start=(k == 0), stop=(k == 7)),
                     R=[sqk.d, cp.d], W=[ps.d])
            P.op(ACT, lambda e: e.activation(out=rs[:, 0:n], in_=ps[:, 0:n], func=AF.Sqrt, bias=C('eps'), scale=1.0 / D),
                 R=[ps.d, cp.d], W=[rs.d])
            P.op(DVE, lambda e: e.reciprocal(out=rs[:, 0:n], in_=rs[:, 0:n]), R=[rs.d], W=[rs.d])
            shift_which = 0 if which_norm == 0 else 3
            for j in range(8):
                tmp = tmp_ring.next()
                P.op(DVE, lambda e, j=j, tmp=tmp: e.scalar_tensor_tensor(
                    out=tmp[:, 0:n], in0=xt[:, j, 0:n], scalar=modA[l][:, which_norm, j, cond:cond + 1], in1=rs[:, 0:n],
                    op0=ALU.mult, op1=ALU.mult), R=[xt.d, modA[l].d, rs.d], W=[tmp.d])
                P.op(ACT, lambda e, j=j, tmp=tmp: e.activation(
                    out=hT[:, j, 0:n], in_=tmp[:, 0:n], func=AF.Identity, bias=modcol(l, shift_which, j, cond), scale=1.0),
                    R=[tmp.d, modv[l].d], W=[hT.d])

        evac_flip = [0]

        def evac(out_ap, in_ap, R, W):
            evac_flip[0] ^= 1
            if evac_flip[0]:
                P.op(ACT, lambda e: e.activation(out=out_ap, in_=in_ap, func=AF.Copy), R=R, W=W)
            else:
                P.op(DVE, lambda e: e.tensor_copy(out=out_ap, in_=in_ap), R=R, W=W)

        def phase_B(l):
            P.begin()
            zt = T(P.sb("zt", [128, 2048], OTDT))
            P.op(DVE, lambda e: e.memset(zt[:, :], 0.0), W=[zt.d])
            for si, (kind, sidx, Tn, off) in enumerate(SEQS):
                for r0 in ([] if 'rwkv' in MIXERS else [0, 128]) + ([] if 'ssd' in MIXERS else [512, 640]):
                    for t0 in range(0, Tn, 2048):
                        n = min(2048, Tn - t0)
                        P.dma(ACT, OT[r0:r0 + 128, off + t0:off + t0 + n], zt[:, 0:n], R=[zt.d], W=[dOT[si]])
            P.end()
            for name, fn in (('ret', mixer_ret), ('diff', mixer_diff), ('ssd', mixer_ssd), ('rwkv', mixer_rwkv)):
                if name in MIXERS:
                    P.begin()
                    fn(l)
                    P.end()

        def mixer_diff(l):
            lam_init = 0.8 - 0.6 * math.exp(-0.3 * l)
            AX = mybir.AxisListType.X
            ropc = T(P.sb("ropc", [64, TS]))
            rops = T(P.sb("rops", [64, TS]))
            P.dma(SP, ropc[:, :], rope_in[2, 0:64, :], W=[ropc.d])
            P.dma(SP, rops[:, :], rope_in[3, 0:64, :], W=[rops.d])
            qf32 = T(P.sb("qf32", [64, TS]))
            qs32 = T(P.sb("qs32", [64, TS]))
            qb = T(P.sb("qb", [64, TS], BF16))
            NKMAX = TS // 128 + 4
            kall = T(P.sb("kall", [64, NKMAX * 128], BF16))
            v32 = T(P.sb("v32", [128, NKMAX, 64]))
            vall = T(P.sb("vall", [128, NKMAX, 64], BF16))
            ck32 = T(P.sb("ck32", [128, 4, 64]))
            E_r = Ring([T(P.sb("E", [128, 512], BF16)) for _ in range(3)])
            rd = [T(P.sb("rd%d" % i, [64, 512])) for i in range(2)]
            o0 = T(P.sb("o0", [64, 512]))
            o1 = T(P.sb("o1", [64, 512]))
            sqy = T(P.sb("sqy", [64, 512]))
            rsy = T(P.sb("rsy", [64, 512]))
            oy_r = Ring([T(P.sb("oy", [64, 512], OTDT)) for _ in range(2)])
            lamt = T(P.sb("lamt", [128, 40]))
            ps_sc = Ring([TPS(P.psum("ps_sc", [128, 512])) for _ in range(2)])
            ps_den = [TPS(P.psum("ps_den%d" % i, [128, 512])) for i in range(2)]
            ps_num = [TPS(P.psum("ps_num%d" % i, [128, 512])) for i in range(2)]
            ps_x = Ring([TPS(P.psum("ps_x", [128, 512])) for _ in range(2)])
            olp, _ = PPC['lp']
            osg, _ = PPC['subg']
            for i in range(2):
                P.op(DVE, lambda e, i=i: e.tensor_tensor(out=lamt[:, 0:32], in0=pp[l][:, olp + 64 * i:olp + 64 * i + 32],
                                                         in1=pp[l][:, olp + 64 * i + 32:olp + 64 * i + 64], op=ALU.mult),
                     R=[pp[l].d], W=[lamt.d])
                P.op(DVE, lambda e, i=i: e.tensor_reduce(out=lamt[:, 32 + i:33 + i], in_=lamt[:, 0:32], axis=AX, op=ALU.add),
                     R=[lamt.d], W=[lamt.d])
            P.op(ACT, lambda e: e.activation(out=lamt[:, 34:36], in_=lamt[:, 32:34], func=AF.Exp), R=[lamt.d], W=[lamt.d])
            P.op(DVE, lambda e: e.scalar_tensor_tensor(out=lamt[:, 36:37], in0=lamt[:, 35:36], scalar=-lam_init, in1=lamt[:, 34:35],
                                                       op0=ALU.add, op1=ALU.subtract), R=[lamt.d], W=[lamt.d])
            scale = 32.0 ** -0.5
            for si, (kind, sidx, Tn, off) in enumerate(SEQS):
                nch = Tn // 128
                nk = nch + (4 if kind == 's' else 0)
                for h in range(4):
                    pr = (h % 2) * 64

                    def rows(cbase):
                        r0 = (cbase + h // 2) * 128 + pr
                        return FT[r0:r0 + 64, off:off + Tn]
                    for (dst, cb, cbs) in ((qb, 24, 28), (kall, 26, 30)):
                        P.dma(SP, qf32[:, 0:Tn], rows(cb), R=[dFT[si]], W=[qf32.d])
                        if kind == 's':
                            P.dma(SP, qs32[:, 0:Tn], rows(cbs), R=[dFT[si]], W=[qs32.d])
                            P.op(DVE, lambda e: e.tensor_tensor(out=qf32[:, 0:Tn], in0=qf32[:, 0:Tn], in1=ropc[:, 0:Tn], op=ALU.mult),
                                 R=[qf32.d, ropc.d], W=[qf32.d])
                            P.op(DVE, lambda e: e.tensor_tensor(out=qs32[:, 0:Tn], in0=qs32[:, 0:Tn], in1=rops[:, 0:Tn], op=ALU.mult),
                                 R=[qs32.d, rops.d], W=[qs32.d])
                            P.op(DVE, lambda e: e.tensor_tensor(out=qf32[:, 0:Tn], in0=qf32[:, 0:Tn], in1=qs32[:, 0:Tn], op=ALU.add),
                                 R=[qf32.d, qs32.d], W=[qf32.d])
                        P.op(ACT, lambda e, dst=dst: e.activation(out=dst[:, 0:Tn], in_=qf32[:, 0:Tn], func=AF.Copy),
                             R=[qf32.d], W=[dst.d])
                    P.dma(SP, v32[:, 0:nch, :], FTM[off:off + Tn, 256 + h * 64:256 + (h + 1) * 64].rearrange("(c p) e -> p c e", p=128),
                          R=[dFTM[si]], W=[v32.d])
                    if kind == 's':
                        P.dma(SP, v32[:, nch:nch + 4, :], cv_in[l, :, h * 64:(h + 1) * 64].rearrange("(c p) e -> p c e", p=128), W=[v32.d])
                        P.dma(SP, ck32[:, :, :], ck_in[l, :, h * 64:(h + 1) * 64].rearrange("(c p) e -> p c e", p=128), W=[ck32.d])
                        px = ps_x.next()
                        for c in range(4):
                            P.op(PE, lambda e, c=c: e.transpose(px[0:64, c * 128:(c + 1) * 128], ck32[:, c, :], C('ident')),
                                 R=[ck32.d, cp.d], W=[px.d])
                        P.op(ACT, lambda e: e.activation(out=kall[:, Tn:Tn + 512], in_=px[0:64, 0:512], func=AF.Copy), R=[px.d], W=[kall.d])
                    P.op(DVE, lambda e: e.tensor_copy(out=vall[:, 0:nk, :], in_=v32[:, 0:nk, :]), R=[v32.d], W=[vall.d])
                    nq = min(512, Tn)
                    for q0 in range(0, Tn, nq):
                        n = nq
                        qsl = slice(q0, q0 + n)
                        def qk_exp(kc, m):
                            ksl = slice(kc * 128, (kc + 1) * 128)
                            psl = slice(m * 32, (m + 1) * 32)
                            psc = ps_sc.next()
                            P.op(PE, lambda e: e.matmul(psc[:, 0:n], lhsT=kall[psl, ksl], rhs=qb[psl, qsl], start=True, stop=True),
                                 R=[kall.d, qb.d], W=[psc.d])
                            E = E_r.next()
                            P.op(ACT, lambda e: e.activation(out=E[:, 0:n], in_=psc[:, 0:n], func=AF.Exp, scale=scale), R=[psc.d], W=[E.d])
                            return E

                        def den_num(kc, m, E):
                            P.op(PE, lambda e: e.matmul(ps_den[m][0:64, 0:n], lhsT=cpb[:, 128:192], rhs=E[:, 0:n], start=(kc == 0), stop=(kc == nk - 1)),
                                 R=[cpb.d, E.d], W=[ps_den[m].d])
                            P.op(PE, lambda e: e.matmul(ps_num[m][0:64, 0:n], lhsT=vall[:, kc, :], rhs=E[:, 0:n], start=(kc == 0), stop=(kc == nk - 1)),
                                 R=[vall.d, E.d], W=[ps_num[m].d])
                        pend = None
                        for kc in range(nk):
                            for m in range(2):
                                E = qk_exp(kc, m)
                                if pend is not None:
                                    den_num(*pend)
                                pend = (kc, m, E)
                        den_num(*pend)
                        for m in range(2):
                            P.op(DVE, lambda e, m=m: e.reciprocal(out=rd[m][:, 0:n], in_=ps_den[m][0:64, 0:n]), R=[ps_den[m].d], W=[rd[m].d])
                        P.op(DVE, lambda e: e.tensor_tensor(out=o0[:, 0:n], in0=ps_num[0][0:64, 0:n], in1=rd[0][:, 0:n], op=ALU.mult),
                             R=[ps_num[0].d, rd[0].d], W=[o0.d])
                        P.op(DVE, lambda e: e.tensor_tensor(out=o1[:, 0:n], in0=ps_num[1][0:64, 0:n], in1=rd[1][:, 0:n], op=ALU.mult),
                             R=[ps_num[1].d, rd[1].d], W=[o1.d])
                        P.op(DVE, lambda e: e.scalar_tensor_tensor(out=o0[:, 0:n], in0=o1[:, 0:n], scalar=lamt[0:64, 36:37], in1=o0[:, 0:n],
                                                                   op0=ALU.mult, op1=ALU.add), R=[o0.d, o1.d, lamt.d], W=[o0.d])
                        P.op(ACT, lambda e: e.activation(out=sqy[:, 0:n], in_=o0[:, 0:n], func=AF.Square), R=[o0.d], W=[sqy.d])
                        pss = ps_x.next()
                        P.op(PE, lambda e: e.matmul(pss[0:64, 0:n], lhsT=C('ones', rows=64, sub=(0, 64)), rhs=sqy[:, 0:n], start=True, stop=True),
                             R=[sqy.d, cp.d], W=[pss.d])
                        P.op(ACT, lambda e: e.activation(out=rsy[:, 0:n], in_=pss[0:64, 0:n], func=AF.Sqrt, bias=C('eps', rows=64), scale=1.0 / 64),
                             R=[pss.d, cp.d], W=[rsy.d])
                        P.op(DVE, lambda e: e.reciprocal(out=rsy[:, 0:n], in_=rsy[:, 0:n]), R=[rsy.d], W=[rsy.d])
                        P.op(DVE, lambda e: e.scalar_tensor_tensor(out=o1[:, 0:n], in0=o0[:, 0:n], scalar=pp[l][0:64, osg:osg + 1],
                                                                   in1=rsy[:, 0:n], op0=ALU.mult, op1=ALU.mult),
                             R=[o0.d, pp[l].d, rsy.d], W=[o1.d])
                        oy = oy_r.next()
                        P.op(ACT, lambda e: e.activation(out=oy[:, 0:n], in_=o1[:, 0:n], func=AF.Copy, scale=1.0 - lam_init), R=[o1.d], W=[oy.d])
                        P.dma(ACT, OT[768 + h * 64:768 + (h + 1) * 64, off + q0: off + q0 + n], oy[:, 0:n], R=[oy.d], W=[dOT[si]])

        def mixer_rwkv(l):
            rwkv_pre(l)
            if RWKV_STAGES >= 2:
                P.end()
                P.begin()
                rwkv_scan(l)
            if RWKV_STAGES >= 3:
                P.end()
                P.begin()
                rwkv_post(l)

        def rwkv_pre(l):
            buf_r = Ring([T(P.sb("rbuf", [64, 514])) for _ in range(3)])
            s1 = T(P.sb("rs1", [64, 512]))
            sh = [T(P.sb("rsh%d" % i, [64, 512])) for i in range(15)]
            twd = T(P.sb("twd", [64, 512]))
            sgd = T(P.sb("sgd", [64, 512]))
            a_d = [T(P.sb("a_d%d" % d, [64, 512])) for d in range(2)]
            o_r = Ring([T(P.sb("rwo", [64, 512])) for _ in range(4)])
            kkr = T(P.sb("kkr", [64, 512]))
            kk = T(P.sb("kk", [64, 512]))
            t1 = T(P.sb("rt1", [64, 512]))
            t2 = T(P.sb("rt2", [64, 512]))
            ps_r = Ring([TPS(P.psum("ps_rp", [128, 512])) for _ in range(4)])
            omu, _ = PPC['mu64']
            ow0, _ = PPC['w0_64']
            oa0, _ = PPC['a0_64']
            okk, _ = PPC['kk64']
            oka, _ = PPC['ka64']
            ork, _ = PPC['rk64']
            owu, _ = PPC['w_up']
            oau, _ = PPC['a_up']
            ogu, _ = PPC['g_up']
            for si, (kind, sidx, Tn, off) in enumerate(SEQS):
                nq = min(512, Tn)
                for q0 in range(0, Tn, nq):
                    n = nq
                    lo, hi = max(q0 - 1, 0), min(q0 + n + 1, Tn)
                    c_lo = lo - (q0 - 1)
                    a0 = 1 if q0 == 0 else 0
                    b1 = n - 1 if q0 + n == Tn else n

                    def store(arr, h, src):
                        r0 = arr * 256 + h * 64
                        P.dma(ACT, RW[r0:r0 + 64, off + q0: off + q0 + n], src[:, 0:n], R=[src.d], W=[dRW[si]])
                    for hc in range(15):
                        buf = buf_r.next()
                        P.dma(SP, buf[:, c_lo:c_lo + hi - lo], FT[hc * 64:(hc + 1) * 64, off + lo: off + hi], R=[dFT[si]], W=[buf.d])
                        P.op(DVE, lambda e: e.tensor_tensor(out=s1[:, a0:b1], in0=buf[:, a0:b1], in1=buf[:, a0 + 2:b1 + 2], op=ALU.add),
                             R=[buf.d], W=[s1.d])
                        if a0:
                            P.op(DVE, lambda e: e.tensor_copy(out=s1[:, 0:1], in_=buf[:, 2:3]), R=[buf.d], W=[s1.d])
                        if b1 < n:
                            P.op(DVE, lambda e: e.tensor_copy(out=s1[:, n - 1:n], in_=buf[:, n - 1:n]), R=[buf.d], W=[s1.d])
                        P.op(DVE, lambda e: e.scalar_tensor_tensor(out=s1[:, 0:n], in0=s1[:, 0:n], scalar=0.5, in1=buf[:, 1:n + 1],
                                                                   op0=ALU.mult, op1=ALU.subtract), R=[s1.d, buf.d], W=[s1.d])
                        P.op(DVE, lambda e: e.scalar_tensor_tensor(out=sh[hc][:, 0:n], in0=s1[:, 0:n], scalar=pp[l][0:64, omu + hc:omu + hc + 1],
                                                                   in1=buf[:, 1:n + 1], op0=ALU.mult, op1=ALU.add),
                             R=[s1.d, buf.d, pp[l].d], W=[sh[hc].d])
                    P.op(ACT, lambda e: e.activation(out=twd[:, 0:n], in_=sh[12][:, 0:n], func=AF.Tanh), R=[sh[12].d], W=[twd.d])
                    P.op(ACT, lambda e: e.activation(out=sgd[:, 0:n], in_=sh[14][:, 0:n], func=AF.Sigmoid), R=[sh[14].d], W=[sgd.d])
                    sad = sh[13]
                    for h in range(4):
                        shr, shk, shv = sh[h], sh[4 + h], sh[8 + h]
                        store(0, h, shr)
                        store(1, h, shv)
                        hs = slice(h * 64, (h + 1) * 64)
                        for d in range(2):
                            ds_ = slice(d * 32, (d + 1) * 32)
                            col = d * 4 + h
                            pw = ps_r.next()
                            P.op(PE, lambda e: e.matmul(pw[0:64, 0:n], lhsT=pp[l][ds_, owu + h * 64:owu + (h + 1) * 64], rhs=twd[ds_, 0:n], start=True, stop=True),
                                 R=[pp[l].d, twd.d], W=[pw.d])
                            lw = o_r.next()
                            P.op(ACT, lambda e: e.activation(out=lw[:, 0:n], in_=pw[0:64, 0:n], func=AF.Sigmoid, bias=pp[l][0:64, ow0 + col:ow0 + col + 1], scale=1.0),
                                 R=[pw.d, pp[l].d], W=[lw.d])
                            P.op(DVE, lambda e: e.tensor_scalar(out=lw[:, 0:n], in0=lw[:, 0:n], scalar1=-math.exp(-0.5), scalar2=None, op0=ALU.mult),
                                 R=[lw.d], W=[lw.d])
                            store(7 + d, h, lw)
                            pa = ps_r.next()
                            P.op(PE, lambda e: e.matmul(pa[0:64, 0:n], lhsT=pp[l][ds_, oau + h * 64:oau + (h + 1) * 64], rhs=sad[ds_, 0:n], start=True, stop=True),
                                 R=[pp[l].d, sad.d], W=[pa.d])
                            P.op(ACT, lambda e: e.activation(out=a_d[d][:, 0:n], in_=pa[0:64, 0:n], func=AF.Sigmoid, bias=pp[l][0:64, oa0 + col:oa0 + col + 1], scale=1.0),
                                 R=[pa.d, pp[l].d], W=[a_d[d].d])
                        pg = ps_r.next()
                        P.op(PE, lambda e: e.matmul(pg[0:64, 0:n], lhsT=pp[l][0:64, ogu + h * 64:ogu + (h + 1) * 64], rhs=sgd[:, 0:n], start=True, stop=True),
                             R=[pp[l].d, sgd.d], W=[pg.d])
                        go = o_r.next()
                        P.op(ACT, lambda e: e.activation(out=go[:, 0:n], in_=pg[0:64, 0:n], func=AF.Copy), R=[pg.d], W=[go.d])
                        store(9, h, go)
                        P.op(DVE, lambda e: e.tensor_scalar(out=kkr[:, 0:n], in0=shk[:, 0:n], scalar1=pp[l][0:64, okk + h:okk + h + 1], scalar2=None, op0=ALU.mult),
                             R=[shk.d, pp[l].d], W=[kkr.d])
                        P.op(ACT, lambda e: e.activation(out=t1[:, 0:n], in_=kkr[:, 0:n], func=AF.Square), R=[kkr.d], W=[t1.d])
                        pk = ps_r.next()
                        P.op(PE, lambda e: e.matmul(pk[0:64, 0:n], lhsT=C('ones', rows=64, sub=(0, 64)), rhs=t1[:, 0:n], start=True, stop=True),
                             R=[t1.d, cp.d], W=[pk.d])
                        P.op(DVE, lambda e: e.tensor_scalar(out=t1[:, 0:n], in0=pk[0:64, 0:n], scalar1=1e-12, scalar2=None, op0=ALU.max), R=[pk.d], W=[t1.d])
                        P.op(ACT, lambda e: e.activation(out=t1[:, 0:n], in_=t1[:, 0:n], func=AF.Sqrt), R=[t1.d], W=[t1.d])
                        P.op(DVE, lambda e: e.reciprocal(out=t1[:, 0:n], in_=t1[:, 0:n]), R=[t1.d], W=[t1.d])
                        P.op(DVE, lambda e: e.tensor_tensor(out=kk[:, 0:n], in0=kkr[:, 0:n], in1=t1[:, 0:n], op=ALU.mult), R=[kkr.d, t1.d], W=[kk.d])
                        store(2, h, kk)
                        for d in range(2):
                            P.op(DVE, lambda e: e.tensor_scalar(out=t2[:, 0:n], in0=a_d[d][:, 0:n], scalar1=-1.0, scalar2=pp[l][0:64, oka + h:oka + h + 1],
                                                                op0=ALU.add, op1=ALU.mult), R=[a_d[d].d, pp[l].d], W=[t2.d])
                            kd = o_r.next()
                            P.op(DVE, lambda e: e.scalar_tensor_tensor(out=kd[:, 0:n], in0=t2[:, 0:n], scalar=1.0, in1=shk[:, 0:n], op0=ALU.add, op1=ALU.mult),
                                 R=[t2.d, shk.d], W=[kd.d])
                            store(3 + d, h, kd)
                            bd = o_r.next()
                            P.op(DVE, lambda e: e.tensor_tensor(out=bd[:, 0:n], in0=kk[:, 0:n], in1=a_d[d][:, 0:n], op=ALU.mult), R=[kk.d, a_d[d].d], W=[bd.d])
                            store(5 + d, h, bd)
                        P.op(DVE, lambda e: e.scalar_tensor_tensor(out=t2[:, 0:n], in0=shr[:, 0:n], scalar=pp[l][0:64, ork + h:ork + h + 1], in1=shk[:, 0:n],
                                                                   op0=ALU.mult, op1=ALU.mult), R=[shr.d, shk.d, pp[l].d], W=[t2.d])
                        pb = ps_r.next()
                        P.op(PE, lambda e: e.matmul(pb[0:64, 0:n], lhsT=C('ones', rows=64, sub=(0, 64)), rhs=t2[:, 0:n], start=True, stop=True),
                             R=[t2.d, cp.d], W=[pb.d])
                        bo = o_r.next()
                        P.op(DVE, lambda e: e.tensor_tensor(out=bo[:, 0:n], in0=pb[0:64, 0:n], in1=shv[:, 0:n], op=ALU.mult), R=[pb.d, shv.d], W=[bo.d])
                        store(10, h, bo)

        def rwkv_scan(l):
            NI = 16
            LDSETS = 1
            tcount = [0]
            ld = {}
            for nm in ('r', 'v', 'kk', 'kd', 'bd', 'lw'):
                ld[nm] = [[T(P.sb("l%s%d_%d" % (nm, d, s_), [64, 512])) for d in range(2)] for s_ in range(LDSETS)]
            Lc = [T(P.sb("Lc%d" % d, [64, 512])) for d in range(2)]
            En = [T(P.sb("En%d" % d, [64, 512])) for d in range(2)]
            Ep = [T(P.sb("Ep%d" % d, [64, 512])) for d in range(2)]
            aT = [T(P.sb("aT%d" % d, [64, 512])) for d in range(2)]
            bT = [T(P.sb("bT%d" % d, [64, 512])) for d in range(2)]
            kT = [T(P.sb("kT%d" % d, [64, 512])) for d in range(2)]
            rT = [T(P.sb("rT%d" % d, [64, 512])) for d in range(2)]
            vo = [T(P.sb("vo%d" % d, [64, 512])) for d in range(2)]
            ysb = [T(P.sb("ysb%d" % d, [64, 512])) for d in range(2)]
            yso = [T(P.sb("yso%d" % d, [64, 512])) for d in range(2)]
            ELC = [T(P.sb("ELC%d" % i, [64, 64])) for i in range(NI)]
            bhk = [T(P.sb("bhk%d" % i, [64, 128])) for i in range(NI)]
            WC = [T(P.sb("WC%d" % i, [64, 8])) for i in range(NI)]
            TM = [T(P.sb("TM%d" % i, [64, 320])) for i in range(NI)]
            AM = [T(P.sb("AM%d" % i, [64, 320])) for i in range(NI)]
            Mm = [[T(P.sb("Mm%d_%d" % (i, j), [64, 128])) for j in range(2)] for i in range(NI)]
            Pm = [[T(P.sb("Pm%d_%d" % (i, j), [64, 64])) for j in range(2)] for i in range(NI)]
            AU = [T(P.sb("AU%d" % i, [64, 128])) for i in range(NI)]
            GT = [T(P.sb("GT%d" % i, [64, 64])) for i in range(NI)]
            Hh = [T(P.sb("Hh%d" % i, [64, 64])) for i in range(NI)]
            RhT = [T(P.sb("RhT%d" % i, [64, 64])) for i in range(NI)]
            Yl = [T(P.sb("Yl%d" % i, [64, 64])) for i in range(NI)]
            St = [[T(P.sb("St%d_%d" % (d, i), [64, 64])) for i in range(2)] for d in range(2)]
            s0l = T(P.sb("s0l", [64, 64]))
            pr = Ring([TPS(P.psum("pRW", [128, 512])) for _ in range(8)])
            I64 = C('ident', rows=64, sub=(0, 64))
            for si, (kind, sidx, Tn, off) in enumerate(SEQS):
                nq = min(512, Tn)
                ntile = Tn // nq
                for h in range(4):
                    sti = [0, 0]
                    for d in range(2):
                        S0 = St[d][0]
                        if kind == 's':
                            pS = pr.next()
                            P.dma(SP, s0l[:, :], st_rwkv[l, d, h], W=[s0l.d])
                            P.op(PE, lambda e: e.transpose(pS[0:64, 0:64], s0l[:, :], I64), R=[s0l.d, cp.d], W=[pS.d])
                            P.op(DVE, lambda e: e.tensor_copy(out=S0[:, :], in_=pS[0:64, 0:64]), R=[pS.d], W=[S0.d])
                        else:
                            P.op(DVE, lambda e: e.memset(S0[:, :], 0.0), W=[S0.d])
                    for ti in range(ntile):
                        n = nq
                        tcount[0] += 1
                        ldc = {nm_: ld[nm_][tcount[0] % LDSETS] for nm_ in ld}
                        for d in range(2):
                            q0 = ti * nq if d == 0 else Tn - (ti + 1) * nq

                            def view(t):
                                return t[:, 0:n] if d == 0 else t[:, 0:n][:, ::-1]
                            for nm, arr in (('r', 0), ('v', 1), ('kk', 2), ('kd', 3 + d), ('bd', 5 + d), ('lw', 7 + d)):
                                r0 = arr * 256 + h * 64
                                P.dma(SP, ldc[nm][d][:, 0:n], RW[r0:r0 + 64, off + q0: off + q0 + n], R=[dRW[si]], W=[ldc[nm][d].d])
                            lw = ldc['lw'][d]
                            P.op(DVE, lambda e: e.tensor_tensor_scan(out=Lc[d][:, 0:n], data0=C('scanmask', rows=64, sub=(0, n)), data1=view(lw),
                                                                     initial=0.0, op0=ALU.mult, op1=ALU.add), R=[lw.d, cp.d], W=[Lc[d].d])
                            P.op(ACT, lambda e: e.activation(out=En[d][:, 0:n], in_=Lc[d][:, 0:n], func=AF.Exp, scale=-1.0), R=[Lc[d].d], W=[En[d].d])
                            P.op(ACT, lambda e: e.activation(out=Ep[d][:, 0:n], in_=Lc[d][:, 0:n], func=AF.Exp), R=[Lc[d].d], W=[Ep[d].d])
                            P.op(DVE, lambda e: e.tensor_tensor(out=rT[d][:, 0:n], in0=view(ldc['r'][d]), in1=Ep[d][:, 0:n], op=ALU.mult),
                                 R=[ldc['r'][d].d, Ep[d].d], W=[rT[d].d])
                            P.op(DVE, lambda e: e.tensor_tensor(out=aT[d][:, 0:n], in0=Lc[d][:, 0:n], in1=view(lw), op=ALU.subtract),
                                 R=[Lc[d].d, lw.d], W=[aT[d].d])
                            P.op(ACT, lambda e: e.activation(out=Ep[d][:, 0:n], in_=aT[d][:, 0:n], func=AF.Exp), R=[aT[d].d, rT[d].d], W=[Ep[d].d])
                            P.op(DVE, lambda e: e.scalar_tensor_tensor(out=aT[d][:, 0:n], in0=view(ldc['kk'][d]), scalar=-1.0, in1=Ep[d][:, 0:n],
                                                                       op0=ALU.mult, op1=ALU.mult), R=[ldc['kk'][d].d, Ep[d].d], W=[aT[d].d])
                            P.op(DVE, lambda e: e.tensor_tensor(out=bT[d][:, 0:n], in0=view(ldc['bd'][d]), in1=En[d][:, 0:n], op=ALU.mult),
                                 R=[ldc['bd'][d].d, En[d].d], W=[bT[d].d])
                            P.op(DVE, lambda e: e.tensor_tensor(out=kT[d][:, 0:n], in0=view(ldc['kd'][d]), in1=En[d][:, 0:n], op=ALU.mult),
                                 R=[ldc['kd'][d].d, En[d].d], W=[kT[d].d])
                            P.op(DVE, lambda e: e.tensor_copy(out=vo[d][:, 0:n], in_=view(ldc['v'][d])), R=[ldc['v'][d].d], W=[vo[d].d])
                        ncc = n // 64
                        inst = [(cc, d) for cc in range(ncc) for d in range(2)]

                        def CS(cc):
                            return slice(cc * 64, (cc + 1) * 64)
                        for ii, (cc, d) in enumerate(inst):
                            cs = CS(cc)
                            lcol = Lc[d][:, cc * 64 + 63:cc * 64 + 64]

                            def vw(t):
                                if d == 0:
                                    return t[:, cs]
                                return t[:, n - (cc + 1) * 64:n - cc * 64][:, ::-1]
                            P.op(ACT, lambda e: e.activation(out=ELC[ii][:, :], in_=Lc[d][:, cs], func=AF.Exp, bias=lcol, scale=-1.0),
                                 R=[Lc[d].d], W=[ELC[ii].d])
                            P.op(ACT, lambda e: e.activation(out=WC[ii][:, 0:1], in_=lcol, func=AF.Exp), R=[Lc[d].d], W=[WC[ii].d])
                            P.op(DVE, lambda e: e.tensor_tensor(out=bhk[ii][:, 0:64], in0=vw(ldc['bd'][d]), in1=ELC[ii][:, :], op=ALU.mult),
                                 R=[ldc['bd'][d].d, ELC[ii].d], W=[bhk[ii].d])
                            P.op(DVE, lambda e: e.tensor_tensor(out=bhk[ii][:, 64:128], in0=vw(ldc['kd'][d]), in1=ELC[ii][:, :], op=ALU.mult),
                                 R=[ldc['kd'][d].d, ELC[ii].d], W=[bhk[ii].d])
                        for ii, (cc, d) in enumerate(inst):
                            cs = CS(cc)
                            ps = pr.next()
                            P.op(PE, lambda e: e.transpose(ps[0:64, 0:64], aT[d][:, cs], I64), R=[aT[d].d, cp.d], W=[ps.d])
                            P.op(PE, lambda e: e.transpose(ps[0:64, 64:128], vo[d][:, cs], I64), R=[vo[d].d, cp.d], W=[ps.d])
                            P.op(PE, lambda e: e.transpose(ps[0:64, 128:192], bhk[ii][:, 0:64], I64), R=[bhk[ii].d, cp.d], W=[ps.d])
                            P.op(PE, lambda e: e.transpose(ps[0:64, 192:256], bhk[ii][:, 64:128], I64), R=[bhk[ii].d, cp.d], W=[ps.d])
                            tm = TM[ii]
                            if ii % 2 == 0:
                                P.op(ACT, lambda e: e.activation(out=tm[:, 0:64], in_=ps[0:64, 0:64], func=AF.Copy), R=[ps.d], W=[tm.d])
                                P.op(ACT, lambda e: e.activation(out=tm[:, 128:320], in_=ps[0:64, 64:256], func=AF.Copy), R=[ps.d], W=[tm.d])
                            else:
                                P.op(DVE, lambda e: e.tensor_copy(out=tm[:, 0:64], in_=ps[0:64, 0:64]), R=[ps.d], W=[tm.d])
                                P.op(DVE, lambda e: e.tensor_copy(out=tm[:, 128:320], in_=ps[0:64, 64:256]), R=[ps.d], W=[tm.d])
                        for ii, (cc, d) in enumerate(inst):
                            cs = CS(cc)
                            a_, b_, k_, r_ = aT[d][:, cs], bT[d][:, cs], kT[d][:, cs], rT[d][:, cs]
                            ps = pr.next()
                            for i_, (lh, rh) in enumerate(((b_, a_), (b_, r_), (k_, a_), (k_, r_), (a_, b_))):
                                P.op(PE, lambda e: e.matmul(ps[0:64, i_ * 64:(i_ + 1) * 64], lhsT=lh, rhs=rh, start=True, stop=True),
                                     R=[aT[d].d, bT[d].d, kT[d].d, rT[d].d], W=[ps.d])
                            am = AM[ii]
                            P.op(DVE, lambda e: e.tensor_tensor(out=am[:, :], in0=ps[0:64, 0:320], in1=C('rw_mask5', rows=64), op=ALU.mult),
                                 R=[ps.d, cp.d], W=[am.d])
                            P.op(DVE, lambda e: e.tensor_tensor(out=Pm[ii][0][:, :], in0=am[:, 0:64], in1=I64, op=ALU.add), R=[am.d, cp.d], W=[Pm[ii][0].d])
                        cur = [(AM[ii][:, 0:64], AM[ii][:, 256:320], AM[ii]) for ii in range(len(inst))]
                        for lvl in range(5):
                            for ii, (cc, d) in enumerate(inst):
                                Mc, Mtc, Mdep = cur[ii]
                                mn = Mm[ii][lvl % 2]
                                ps = pr.next()
                                if lvl < 4:
                                    P.op(PE, lambda e: e.matmul(ps[0:64, 0:64], lhsT=Mtc, rhs=Mc, start=True, stop=True), R=[Mdep.d], W=[ps.d])
                                P.op(PE, lambda e: e.matmul(ps[0:64, 64:128], lhsT=Mc, rhs=Mtc, start=True, stop=True), R=[Mdep.d], W=[ps.d])
                                if lvl < 4:
                                    evac(mn[:, :], ps[0:64, 0:128], [ps.d], [mn.d])
                                else:
                                    evac(mn[:, 64:128], ps[0:64, 64:128], [ps.d], [mn.d])
                                cur[ii] = (mn[:, 0:64], mn[:, 64:128], mn)
                            for ii, (cc, d) in enumerate(inst):
                                Mc, Mtc, Mdep = cur[ii]
                                Pc = Pm[ii][lvl % 2]
                                Pn = Pm[ii][(lvl + 1) % 2]
                                ps = pr.next()
                                P.op(PE, lambda e: e.matmul(ps[0:64, 0:64], lhsT=Mtc, rhs=Pc[:, :], start=True, stop=True), R=[Mdep.d, Pc.d], W=[ps.d])
                                P.op(DVE, lambda e: e.tensor_tensor(out=Pn[:, :], in0=ps[0:64, 0:64], in1=Pc[:, :], op=ALU.add), R=[ps.d, Pc.d], W=[Pn.d])
                        for ii, (cc, d) in enumerate(inst):
                            tm, am = TM[ii], AM[ii]
                            ps = pr.next()
                            P.op(PE, lambda e: e.matmul(ps[0:64, 0:64], lhsT=am[:, 128:192], rhs=tm[:, 128:192], start=True, stop=True), R=[am.d, tm.d], W=[ps.d])
                            evac(tm[:, 64:128], ps[0:64, 0:64], [ps.d], [tm.d])
                        for ii, (cc, d) in enumerate(inst):
                            tm, Pc, au = TM[ii], Pm[ii][1], AU[ii]
                            ps = pr.next()
                            P.op(PE, lambda e: e.matmul(ps[0:64, 0:128], lhsT=Pc[:, :], rhs=tm[:, 0:128], start=True, stop=True), R=[Pc.d, tm.d], W=[ps.d])
                            evac(au[:, :], ps[0:64, 0:128], [ps.d], [au.d])
                        for ii, (cc, d) in enumerate(inst):
                            tm, au = TM[ii], AU[ii]
                            V_, Bh_, Kh_ = tm[:, 128:192], tm[:, 192:256], tm[:, 256:320]
                            Ah, Ul = au[:, 0:64], au[:, 64:128]
                            ps = pr.next()
                            P.op(PE, lambda e: e.matmul(ps[0:64, 0:64], lhsT=Ah, rhs=Bh_, start=True, stop=True), R=[au.d, tm.d], W=[ps.d])
                            P.op(PE, lambda e: e.matmul(ps[0:64, 64:128], lhsT=Bh_, rhs=Ul, start=True, stop=False), R=[au.d, tm.d], W=[ps.d])
                            P.op(PE, lambda e: e.matmul(ps[0:64, 64:128], lhsT=Kh_, rhs=V_, start=False, stop=True), R=[tm.d], W=[ps.d])
                            P.op(DVE, lambda e: e.scalar_tensor_tensor(out=GT[ii][:, :], in0=I64, scalar=WC[ii][:, 0:1], in1=ps[0:64, 0:64],
                                                                       op0=ALU.mult, op1=ALU.add), R=[cp.d, WC[ii].d, ps.d], W=[GT[ii].d])
                            P.op(DVE, lambda e: e.tensor_copy(out=Hh[ii][:, :], in_=ps[0:64, 64:128]), R=[ps.d], W=[Hh[ii].d])
                        for ii, (cc, d) in enumerate(inst):
                            cs = CS(cc)
                            tm, au, am = TM[ii], AU[ii], AM[ii]
                            V_ = tm[:, 128:192]
                            Ah, Ul = au[:, 0:64], au[:, 64:128]
                            ArbT, ArkT = am[:, 64:128], am[:, 192:256]
                            ps = pr.next()
                            P.op(PE, lambda e: e.matmul(ps[0:64, 0:64], lhsT=Ah, rhs=ArbT, start=True, stop=True), R=[au.d, am.d], W=[ps.d])
                            P.op(PE, lambda e: e.matmul(ps[0:64, 64:128], lhsT=Ul, rhs=ArbT, start=True, stop=False), R=[au.d, am.d], W=[ps.d])
                            P.op(PE, lambda e: e.matmul(ps[0:64, 64:128], lhsT=V_, rhs=ArkT, start=False, stop=True), R=[tm.d, am.d], W=[ps.d])
                            P.op(DVE, lambda e: e.tensor_tensor(out=RhT[ii][:, :], in0=ps[0:64, 0:64], in1=rT[d][:, cs], op=ALU.add), R=[ps.d, rT[d].d], W=[RhT[ii].d])
                            P.op(DVE, lambda e: e.tensor_copy(out=Yl[ii][:, :], in_=ps[0:64, 64:128]), R=[ps.d], W=[Yl[ii].d])
                        for ii, (cc, d) in enumerate(inst):
                            cs = CS(cc)
                            Sc = St[d][sti[d] % 2]
                            Sn = St[d][(sti[d] + 1) % 2]
                            sti[d] += 1
                            ps = pr.next()
                            P.op(PE, lambda e: e.matmul(ps[0:64, 0:64], lhsT=Sc[:, :], rhs=RhT[ii][:, :], start=True, stop=True), R=[Sc.d, RhT[ii].d], W=[ps.d])
                            P.op(PE, lambda e: e.matmul(ps[0:64, 64:128], lhsT=GT[ii][:, :], rhs=Sc[:, :], start=True, stop=True), R=[GT[ii].d, Sc.d], W=[ps.d])
                            P.op(DVE, lambda e: e.tensor_tensor(out=Sn[:, :], in0=ps[0:64, 64:128], in1=Hh[ii][:, :], op=ALU.add), R=[ps.d, Hh[ii].d], W=[Sn.d])
                            P.op(DVE, lambda e: e.tensor_tensor(out=ysb[d][:, cs], in0=ps[0:64, 0:64], in1=Yl[ii][:, :], op=ALU.add), R=[ps.d, Yl[ii].d], W=[ysb[d].d])
                        for d in range(2):
                            q0 = ti * nq if d == 0 else Tn - (ti + 1) * nq
                            src = ysb[d]
                            if d == 1:
                                P.op(DVE, lambda e: e.tensor_copy(out=yso[d][:, 0:n], in_=ysb[d][:, 0:n][:, ::-1]), R=[ysb[d].d], W=[yso[d].d])
                                src = yso[d]
                            r0 = d * 256 + h * 64
                            P.dma(ACT, YS[r0:r0 + 64, off + q0: off + q0 + n], src[:, 0:n], R=[src.d], W=[dYS[si]])
                    if kind == 'p':
                        for d in range(2):
                            Sc = St[d][sti[d] % 2]
                            pS = pr.next()
                            P.op(PE, lambda e: e.transpose(pS[0:64, 0:64], Sc[:, :], I64), R=[Sc.d, cp.d], W=[pS.d])
                            P.op(DVE, lambda e: e.tensor_copy(out=s0l[:, :], in_=pS[0:64, 0:64]), R=[pS.d], W=[s0l.d])
                            P.dma(SP, o_rwkv[sidx, l, d, h], s0l[:, :], R=[s0l.d], W=[dOUT])

        def rwkv_post(l):
            yf = Ring([T(P.sb("pyf", [64, 512])) for _ in range(2)])
            yb = Ring([T(P.sb("pyb", [64, 512])) for _ in range(2)])
            gt = Ring([T(P.sb("pgt", [64, 512])) for _ in range(2)])
            bt = Ring([T(P.sb("pbt", [64, 512])) for _ in range(2)])
            yc = T(P.sb("pyc", [64, 512]))
            sq = T(P.sb("psq", [64, 512]))
            rs = T(P.sb("prs", [64, 512]))
            oy_r = Ring([T(P.sb("oy", [64, 512], OTDT)) for _ in range(2)])
            ps_r = Ring([TPS(P.psum("ps_po", [128, 512])) for _ in range(4)])
            olg, _ = PPC['lng64']
            olb, _ = PPC['lnb64']
            epsg = T(P.sb("epsg", [64, 1]))
            P.op(DVE, lambda e: e.memset(epsg[:, :], 64e-5), W=[epsg.d])
            for si, (kind, sidx, Tn, off) in enumerate(SEQS):
                nq = min(512, Tn)
                for h in range(4):
                    for q0 in range(0, Tn, nq):
                        n = nq
                        a, b, g_, bo = yf.next(), yb.next(), gt.next(), bt.next()
                        cols = slice(off + q0, off + q0 + n)
                        P.dma(SP, a[:, 0:n], YS[h * 64:(h + 1) * 64, cols], R=[dYS[si]], W=[a.d])
                        P.dma(SP, b[:, 0:n], YS[256 + h * 64:256 + (h + 1) * 64, cols], R=[dYS[si]], W=[b.d])
                        P.dma(SP, g_[:, 0:n], RW[9 * 256 + h * 64:9 * 256 + (h + 1) * 64, cols], R=[dRW[si]], W=[g_.d])
                        P.dma(SP, bo[:, 0:n], RW[10 * 256 + h * 64:10 * 256 + (h + 1) * 64, cols], R=[dRW[si]], W=[bo.d])
                        P.op(DVE, lambda e: e.tensor_tensor(out=a[:, 0:n], in0=a[:, 0:n], in1=b[:, 0:n], op=ALU.add), R=[a.d, b.d], W=[a.d])
                        pm_ = ps_r.next()
                        P.op(PE, lambda e: e.matmul(pm_[0:64, 0:n], lhsT=C('ones', rows=64, sub=(0, 64)), rhs=a[:, 0:n], start=True, stop=True),
                             R=[a.d, cp.d], W=[pm_.d])
                        P.op(DVE, lambda e: e.scalar_tensor_tensor(out=yc[:, 0:n], in0=pm_[0:64, 0:n], scalar=-1.0 / 64, in1=a[:, 0:n],
                                                                   op0=ALU.mult, op1=ALU.add), R=[pm_.d, a.d], W=[yc.d])
                        P.op(ACT, lambda e: e.activation(out=sq[:, 0:n], in_=yc[:, 0:n], func=AF.Square), R=[yc.d], W=[sq.d])
                        pv = ps_r.next()
                        P.op(PE, lambda e: e.matmul(pv[0:64, 0:n], lhsT=C('ones', rows=64, sub=(0, 64)), rhs=sq[:, 0:n], start=True, stop=True),
                             R=[sq.d, cp.d], W=[pv.d])
                        P.op(ACT, lambda e: e.activation(out=rs[:, 0:n], in_=pv[0:64, 0:n], func=AF.Sqrt, bias=epsg[:, 0:1], scale=1.0 / 64),
                             R=[pv.d, epsg.d], W=[rs.d])
                        P.op(DVE, lambda e: e.reciprocal(out=rs[:, 0:n], in_=rs[:, 0:n]), R=[rs.d], W=[rs.d])
                        P.op(DVE, lambda e: e.scalar_tensor_tensor(out=yc[:, 0:n], in0=yc[:, 0:n], scalar=pp[l][0:64, olg + h:olg + h + 1], in1=rs[:, 0:n],
                                                                   op0=ALU.mult, op1=ALU.mult), R=[yc.d, rs.d, pp[l].d], W=[yc.d])
                        P.op(DVE, lambda e: e.scalar_tensor_tensor(out=yc[:, 0:n], in0=yc[:, 0:n], scalar=pp[l][0:64, olb + h:olb + h + 1], in1=bo[:, 0:n],
                                                                   op0=ALU.add, op1=ALU.add), R=[yc.d, bo.d, pp[l].d], W=[yc.d])
                        oy = oy_r.next()
                        P.op(DVE, lambda e: e.tensor_tensor(out=oy[:, 0:n], in0=yc[:, 0:n], in1=g_[:, 0:n], op=ALU.mult), R=[yc.d, g_.d], W=[oy.d])
                        P.dma(ACT, OT[h * 64:(h + 1) * 64, cols], oy[:, 0:n], R=[oy.d], W=[dOT[si]])

        def mixer_ssd(l):
            NCH = TS // 128
            raw = T(P.sb("raw", [64, TS]))
            tcv = T(P.sb("tcv", [64, TS]))
            xcf = [T(P.sb("xcf%d" % h, [64, TS], BF16)) for h in range(4)]
            Bb = [T(P.sb("Bb%d" % g, [64, TS], BF16)) for g in range(2)]
            Cb = [T(P.sb("Cb%d" % g, [64, TS], BF16)) for g in range(2)]
            xT = [T(P.sb("xT%d" % h, [128, NCH, 64], BF16)) for h in range(4)]
            yz = [T(P.sb("yz%d" % h, [64, TS], BF16)) for h in range(4)]
            dtr = T(P.sb("dtr", [128, NCH, 8]))
            dts = T(P.sb("dts", [128, NCH, 8]))
            gg = T(P.sb("gg", [128, NCH, 8]))
            nea = T(P.sb("nea", [128, 8]))
            S_ = [T(P.sb("S%d" % d, [64, 64])) for d in range(2)]
            Sfb = T(P.sb("Sfb", [64, 64], BF16))
            Sbs = T(P.sb("Sbs", [64, NCH, 64], BF16))
            cumc = Ring([T(P.sb("cumc", [128, 8])) for _ in range(2)])
            gB_r = Ring([T(P.sb("gB", [128, 128])) for _ in range(2)])
            arg_r = Ring([T(P.sb("arg", [128, 128])) for _ in range(2)])
            Dm = [T(P.sb("Dm%d" % d, [128, 128])) for d in range(2)]
            Ds = T(P.sb("Ds", [128, 128]))
            PT_r = Ring([T(P.sb("PT", [128, 128], BF16)) for _ in range(2)])
            sm_r = Ring([T(P.sb("sm", [128, 4])) for _ in range(4)])
            ecr_r = Ring([T(P.sb("ecr", [64, 128])) for _ in range(2)])
            qd_r = Ring([T(P.sb("qd", [64, 128], BF16)) for _ in range(4)])
            Bw_r = Ring([T(P.sb("Bw", [128, 64], BF16)) for _ in range(2)])
            zt_ = T(P.sb("zt_", [64, 512]))
            t5 = T(P.sb("t5", [64, 512]))
            rsy = T(P.sb("rsy", [64, 512]))
            oy_r = Ring([T(P.sb("oy", [64, 512], OTDT)) for _ in range(2)])
            ps_cc = Ring([TPS(P.psum("ps_cc", [128, 512])) for _ in range(1)])
            ps_cr = Ring([TPS(P.psum("ps_cr", [128, 512])) for _ in range(2)])
            ps_sc = Ring([TPS(P.psum("ps_sc", [128, 512])) for _ in range(1)])
            ps_tr = Ring([TPS(P.psum("ps_tr", [128, 1024], BF16)) for _ in range(1)])
            ps_up = Ring([TPS(P.psum("ps_up", [128, 512])) for _ in range(1)])
            ps_y = Ring([TPS(P.psum("ps_y", [128, 512])) for _ in range(2)])
            ocw, _ = PPC['scw64']
            ocb, _ = PPC['scb64']
            odtb, _ = PPC['dtb']
            oal, _ = PPC['alog']
            od4, _ = PPC['ssd_d4']
            og4, _ = PPC['ssd_g4']
            P.op(ACT, lambda e: e.activation(out=nea[:, :], in_=pp[l][:, oal:oal + 8], func=AF.Exp), R=[pp[l].d], W=[nea.d])
            P.op(DVE, lambda e: e.tensor_scalar(out=nea[:, :], in0=nea[:, :], scalar1=-1.0, scalar2=None, op0=ALU.mult), R=[nea.d], W=[nea.d])
            for si, (kind, sidx, Tn, off) in enumerate(SEQS):
                nch = Tn // 128

                def convsilu(hc, dst, dstb):
                    r0 = 20 * 128 + hc * 64
                    P.dma(SP, raw[:, 0:Tn], FT[r0:r0 + 64, off:off + Tn], R=[dFT[si]], W=[raw.d])
                    P.op(ACT, lambda e: e.activation(out=tcv[:, 0:Tn], in_=raw[:, 0:Tn], func=AF.Identity,
                                                     bias=pp[l][0:64, ocb + hc:ocb + hc + 1], scale=pp[l][0:64, ocw + 8 + hc:ocw + 9 + hc]),
                         R=[raw.d, pp[l].d], W=[tcv.d])
                    P.op(DVE, lambda e: e.scalar_tensor_tensor(out=tcv[:, 1:Tn], in0=raw[:, 0:Tn - 1], scalar=pp[l][0:64, ocw + hc:ocw + hc + 1],
                                                               in1=tcv[:, 1:Tn], op0=ALU.mult, op1=ALU.add), R=[raw.d, pp[l].d, tcv.d], W=[tcv.d])
                    P.op(DVE, lambda e: e.scalar_tensor_tensor(out=tcv[:, 0:Tn - 1], in0=raw[:, 1:Tn], scalar=pp[l][0:64, ocw + 16 + hc:ocw + 17 + hc],
                                                               in1=tcv[:, 0:Tn - 1], op0=ALU.mult, op1=ALU.add), R=[raw.d, pp[l].d, tcv.d], W=[tcv.d])
                    if dst is not None:
                        P.op(ACT, lambda e: e.activation(out=dst[:, 0:Tn], in_=tcv[:, 0:Tn], func=AF.Silu), R=[tcv.d], W=[dst.d])
                        P.op(DVE, lambda e: e.tensor_copy(out=dstb[:, 0:Tn], in_=dst[:, 0:Tn]), R=[dst.d], W=[dstb.d])
                    else:
                        P.op(ACT, lambda e: e.activation(out=dstb[:, 0:Tn], in_=tcv[:, 0:Tn], func=AF.Silu), R=[tcv.d], W=[dstb.d])
                for g in range(2):
                    convsilu(4 + g, None, Bb[g])
                    convsilu(6 + g, None, Cb[g])
                for h in range(4):
                    convsilu(h, None, xcf[h])
                    for c in range(nch):
                        pt = ps_tr.next()
                        P.op(PE, lambda e: e.transpose(pt[:, 0:64], xcf[h][:, c * 128:(c + 1) * 128], cpb[0:64, 0:64]), R=[xcf[h].d, cpb.d], W=[pt.d])
                        evac(xT[h][:, c, :], pt[:, 0:64], [pt.d], [xT[h].d])
                P.dma(SP, dtr[:, 0:nch, :], FTM[off:off + Tn, 768:776].rearrange("(c p) e -> p c e", p=128), R=[dFTM[si]], W=[dtr.d])
                P.op(DVE, lambda e: e.tensor_tensor(out=dtr[:, 0:nch, :], in0=dtr[:, 0:nch, :],
                                                    in1=pp[l][:, odtb:odtb + 8].unsqueeze(1).to_broadcast([128, nch, 8]), op=ALU.add),
                     R=[dtr.d, pp[l].d], W=[dtr.d])
                P.op(ACT, lambda e: e.activation(out=dtr[:, 0:nch, :], in_=dtr[:, 0:nch, :], func=AF.Exp), R=[dtr.d], W=[dtr.d])
                P.op(ACT, lambda e: e.activation(out=dts[:, 0:nch, :], in_=dtr[:, 0:nch, :], func=AF.Ln, bias=1.0, scale=1.0), R=[dtr.d], W=[dts.d])
                P.op(DVE, lambda e: e.tensor_tensor(out=gg[:, 0:nch, :], in0=dts[:, 0:nch, :],
                                                    in1=nea[:, :].unsqueeze(1).to_broadcast([128, nch, 8]), op=ALU.mult),
                     R=[dts.d, nea.d], W=[gg.d])

                def chunk_cum(c):
                    pc = ps_cc.next()
                    P.op(PE, lambda e: e.matmul(pc[:, 0:4], lhsT=C('tri_f'), rhs=gg[:, c, 0:4], start=True, stop=True), R=[cp.d, gg.d], W=[pc.d])
                    P.op(PE, lambda e: e.matmul(pc[:, 4:8], lhsT=C('tri_b'), rhs=gg[:, c, 4:8], start=True, stop=True), R=[cp.d, gg.d], W=[pc.d])
                    cc = cumc.next()
                    P.op(DVE, lambda e: e.tensor_copy(out=cc[:, :], in_=pc[:, 0:8]), R=[pc.d], W=[cc.d])
                    return cc

                def dirstuff(c, h, d, cc):
                    col = d * 4 + h
                    gB = gB_r.next()
                    P.op(DVE, lambda e: e.tensor_scalar(out=gB[:, :], in0=C('ones'), scalar1=gg[:, c, col:col + 1], scalar2=None, op0=ALU.mult),
                         R=[cp.d, gg.d], W=[gB.d])
                    pr_ = ps_cr.next()
                    P.op(PE, lambda e: e.matmul(pr_[:, 0:128], lhsT=gB[:, :], rhs=C('tri_f' if d == 0 else 'tri_b'), start=True, stop=True),
                         R=[gB.d, cp.d], W=[pr_.d])
                    sm = sm_r.next()
                    lc = 127 if d == 0 else 0
                    P.op(DVE, lambda e: e.tensor_copy(out=sm[:, 0:1], in_=pr_[:, lc:lc + 1]), R=[pr_.d], W=[sm.d])
                    P.op(ACT, lambda e: e.activation(out=sm[:, 1:2], in_=cc[:, col:col + 1], func=AF.Exp, bias=sm[:, 0:1], scale=-1.0),
                         R=[cc.d, sm.d], W=[sm.d])
                    P.op(DVE, lambda e: e.tensor_tensor(out=sm[:, 2:3], in0=sm[:, 1:2], in1=dts[:, c, col:col + 1], op=ALU.mult),
                         R=[sm.d, dts.d], W=[sm.d])
                    P.op(ACT, lambda e: e.activation(out=sm[:, 3:4], in_=sm[:, 0:1], func=AF.Exp), R=[sm.d], W=[sm.d])
                    return pr_, sm

                def state_update(S, c, h, g, sm):
                    pt = ps_tr.next()
                    P.op(PE, lambda e: e.transpose(pt[:, 0:64], Bb[g][:, c * 128:(c + 1) * 128], cpb[0:64, 0:64]), R=[Bb[g].d, cpb.d], W=[pt.d])
                    Bw = Bw_r.next()
                    P.op(DVE, lambda e: e.tensor_scalar(out=Bw[:, :], in0=pt[:, 0:64], scalar1=sm[:, 2:3], scalar2=None, op0=ALU.mult),
                         R=[pt.d, sm.d], W=[Bw.d])
                    pu = ps_up.next()
                    P.op(PE, lambda e: e.matmul(pu[0:64, 0:64], lhsT=Bw[:, :], rhs=xT[h][:, c, :], start=True, stop=True), R=[Bw.d, xT[h].d], W=[pu.d])
                    P.op(DVE, lambda e: e.scalar_tensor_tensor(out=S[:, :], in0=S[:, :], scalar=sm[0:64, 3:4], in1=pu[0:64, 0:64],
                                                               op0=ALU.mult, op1=ALU.add), R=[S.d, pu.d, sm.d], W=[S.d])

                for h in range(4):
                    g = h // 2
                    Sf, Sb = S_
                    if kind == 's':
                        P.dma(SP, Sf[:, :], st_ssd[l, 0, h], W=[Sf.d])
                        P.dma(SP, Sb[:, :], st_ssd[l, 1, h], W=[Sb.d])
                    else:
                        P.op(DVE, lambda e: e.memset(Sf[:, :], 0.0), W=[Sf.d])
                        P.op(DVE, lambda e: e.memset(Sb[:, :], 0.0), W=[Sb.d])
                    for c in range(nch - 1, -1, -1):
                        P.op(ACT, lambda e: e.activation(out=Sbs[:, c, :], in_=Sb[:, :], func=AF.Copy), R=[Sb.d], W=[Sbs.d])
                        cc = chunk_cum(c)
                        pr_, sm = dirstuff(c, h, 1, cc)
                        state_update(Sb, c, h, g, sm)
                    if kind == 'p':
                        P.dma(SP, o_ssd[sidx, l, 1, h], Sb[:, :], R=[Sb.d], W=[dOUT])
                    for c0 in range(0, nch, 4):
                        ncg = min(4, nch - c0)
                        n = ncg * 128
                        py = ps_y.next()
                        for ci in range(ncg):
                            c = c0 + ci
                            sl = slice(c * 128, (c + 1) * 128)
                            cc = chunk_cum(c)
                            psc = ps_sc.next()
                            P.op(PE, lambda e: e.matmul(psc[:, 0:128], lhsT=Bb[g][:, sl], rhs=Cb[g][:, sl], start=True, stop=True),
                                 R=[Bb[g].d, Cb[g].d], W=[psc.d])
                            qds = []
                            sms = []
                            for d in range(2):
                                col = d * 4 + h
                                pr_, sm = dirstuff(c, h, d, cc)
                                sms.append(sm)
                                arg = arg_r.next()
                                P.op(DVE, lambda e: e.scalar_tensor_tensor(out=arg[:, :], in0=pr_[:, 0:128], scalar=cc[:, col:col + 1],
                                                                           in1=C('nm_f' if d == 0 else 'nm_b'), op0=ALU.subtract, op1=ALU.add),
                                     R=[pr_.d, cc.d, cp.d], W=[arg.d])
                                P.op(ACT, lambda e: e.activation(out=Dm[d][:, :], in_=arg[:, :], func=AF.Exp), R=[arg.d], W=[Dm[d].d])
                                ecr = ecr_r.next()
                                P.op(ACT, lambda e: e.activation(out=ecr[:, :], in_=pr_[0:64, 0:128], func=AF.Exp), R=[pr_.d], W=[ecr.d])
                                qd = qd_r.next()
                                P.op(DVE, lambda e: e.tensor_tensor(out=qd[:, :], in0=Cb[g][:, sl], in1=ecr[:, :], op=ALU.mult),
                                     R=[Cb[g].d, ecr.d], W=[qd.d])
                                qds.append(qd)
                            P.op(DVE, lambda e: e.tensor_scalar(out=Ds[:, :], in0=Dm[0][:, :], scalar1=dts[:, c, h:h + 1], scalar2=None, op0=ALU.mult),
                                 R=[Dm[0].d, dts.d], W=[Ds.d])
                            P.op(DVE, lambda e: e.scalar_tensor_tensor(out=Ds[:, :], in0=Dm[1][:, :], scalar=dts[:, c, 4 + h:5 + h], in1=Ds[:, :],
                                                                       op0=ALU.mult, op1=ALU.add), R=[Dm[1].d, dts.d, Ds.d], W=[Ds.d])
                            PT = PT_r.next()
                            P.op(DVE, lambda e: e.tensor_tensor(out=PT[:, :], in0=psc[:, 0:128], in1=Ds[:, :], op=ALU.mult), R=[psc.d, Ds.d], W=[PT.d])
                            P.op(ACT, lambda e: e.activation(out=Sfb[:, :], in_=Sf[:, :], func=AF.Copy), R=[Sf.d], W=[Sfb.d])
                            yo = py[0:64, ci * 128:(ci + 1) * 128]
                            P.op(PE, lambda e: e.matmul(yo, lhsT=xT[h][:, c, :], rhs=PT[:, :], start=True, stop=False), R=[xT[h].d, PT.d], W=[py.d])
                            P.op(PE, lambda e: e.matmul(yo, lhsT=Sfb[:, :], rhs=qds[0][:, :], start=False, stop=False), R=[Sfb.d, qds[0].d], W=[py.d])
                            P.op(PE, lambda e: e.matmul(yo, lhsT=Sbs[:, c, :], rhs=qds[1][:, :], start=False, stop=True), R=[Sbs.d, qds[1].d], W=[py.d])
                            state_update(Sf, c, h, g, sms[0])
                        tsl = slice(c0 * 128, c0 * 128 + n)
                        r0 = 18 * 128 + h * 64
                        P.dma(SP, zt_[:, 0:n], FT[r0:r0 + 64, off + c0 * 128: off + c0 * 128 + n], R=[dFT[si]], W=[zt_.d])
                        P.op(ACT, lambda e: e.activation(out=zt_[:, 0:n], in_=zt_[:, 0:n], func=AF.Silu), R=[zt_.d], W=[zt_.d])
                        P.op(DVE, lambda e: e.scalar_tensor_tensor(out=t5[:, 0:n], in0=xcf[h][:, tsl], scalar=pp[l][0:64, od4 + h:od4 + h + 1],
                                                                   in1=py[0:64, 0:n], op0=ALU.mult, op1=ALU.add), R=[xcf[h].d, pp[l].d, py.d], W=[t5.d])
                        P.op(DVE, lambda e: e.tensor_tensor(out=yz[h][:, tsl], in0=t5[:, 0:n], in1=zt_[:, 0:n], op=ALU.mult),
                             R=[t5.d, zt_.d], W=[yz[h].d])
                    if kind == 'p':
                        P.dma(SP, o_ssd[sidx, l, 0, h], Sf[:, :], R=[Sf.d], W=[dOUT])
                nq = min(512, Tn)
                for q0 in range(0, Tn, nq):
                    n = nq
                    tsl = slice(q0, q0 + n)
                    pss = ps_cc.next()
                    for h in range(4):
                        P.op(ACT, lambda e: e.activation(out=t5[:, 0:n], in_=yz[h][:, tsl], func=AF.Square), R=[yz[h].d], W=[t5.d])
                        P.op(PE, lambda e: e.matmul(pss[0:64, 0:n], lhsT=C('ones', rows=64, sub=(0, 64)), rhs=t5[:, 0:n], start=(h == 0), stop=(h == 3)),
                             R=[t5.d, cp.d], W=[pss.d])
                    P.op(ACT, lambda e: e.activation(out=rsy[:, 0:n], in_=pss[0:64, 0:n], func=AF.Sqrt, bias=C('eps', rows=64), scale=1.0 / 256),
                         R=[pss.d, cp.d], W=[rsy.d])
                    P.op(DVE, lambda e: e.reciprocal(out=rsy[:, 0:n], in_=rsy[:, 0:n]), R=[rsy.d], W=[rsy.d])
                    for h in range(4):
                        oy = oy_r.next()
                        P.op(DVE, lambda e: e.scalar_tensor_tensor(out=oy[:, 0:n], in0=yz[h][:, tsl], scalar=pp[l][0:64, og4 + h:og4 + h + 1],
                                                                   in1=rsy[:, 0:n], op0=ALU.mult, op1=ALU.mult), R=[yz[h].d, pp[l].d, rsy.d], W=[oy.d])
                        P.dma(ACT, OT[512 + h * 64:512 + (h + 1) * 64, off + q0: off + q0 + n], oy[:, 0:n], R=[oy.d], W=[dOT[si]])

        def mixer_ret(l):
            ropc = T(P.sb("ropc", [64, TS]))
            rops = T(P.sb("rops", [64, TS]))
            P.dma(SP, ropc[:, :], rope_in[0, 0:64, :], W=[ropc.d])
            P.dma(SP, rops[:, :], rope_in[1, 0:64, :], W=[rops.d])
            qf32 = T(P.sb("qf32", [64, TS]))
            qs32 = T(P.sb("qs32", [64, TS]))
            qb = T(P.sb("qb", [64, TS], BF16))
            kb = T(P.sb("kb", [64, TS], BF16))
            g32 = T(P.sb("g32", [64, TS]))
            v32 = T(P.sb("v32", [128, TS // 128, 64]))
            vb = T(P.sb("vb", [128, TS // 128, 64], BF16))
            Sf = T(P.sb("Sf", [64, 64]))
            Sb = T(P.sb("Sb", [64, 64]))
            Sfb = T(P.sb("Sfb", [64, 64], BF16))
            Sbs = T(P.sb("Sbs", [64, TS // 128, 64], BF16))
            kw_r = Ring([T(P.sb("kw", [128, 64], BF16)) for _ in range(2)])
            PT_r = Ring([T(P.sb("PT", [128, 128], BF16)) for _ in range(2)])
            qd_r = Ring([T(P.sb("qd", [64, 128], BF16)) for _ in range(4)])
            sqy = T(P.sb("sqy", [64, 512]))
            rsy = T(P.sb("rsy", [64, 512]))
            sg = T(P.sb("sg", [64, 512]))
            tmy = T(P.sb("tmy", [64, 512]))
            oy_r = Ring([T(P.sb("oy", [64, 512], OTDT)) for _ in range(2)])
            ps_sc = Ring([TPS(P.psum("ps_sc", [128, 512])) for _ in range(2)])
            ps_tr = Ring([TPS(P.psum("ps_tr", [128, 1024], BF16)) for _ in range(1)])
            ps_up = Ring([TPS(P.psum("ps_up", [128, 512])) for _ in range(1)])
            ps_y = Ring([TPS(P.psum("ps_y", [128, 512])) for _ in range(2)])
            ps_ss = Ring([TPS(P.psum("ps_ss", [128, 512])) for _ in range(1)])
            og4, _ = PPC['ret_g4']
            for si, (kind, sidx, Tn, off) in enumerate(SEQS):
                nch = Tn // 128
                for h in range(4):
                    pr = (h % 2) * 64

                    def rows(cbase):
                        r0 = (cbase + h // 2) * 128 + pr
                        return FT[r0:r0 + 64, off:off + Tn]
                    for (dst, cb, cbs, scale) in ((qb, 8, 12, 1.0), (kb, 10, 14, 0.125)):
                        P.dma(SP, qf32[:, 0:Tn], rows(cb), R=[dFT[si]], W=[qf32.d])
                        if kind == 's':
                            P.dma(SP, qs32[:, 0:Tn], rows(cbs), R=[dFT[si]], W=[qs32.d])
                            P.op(DVE, lambda e: e.tensor_tensor(out=qf32[:, 0:Tn], in0=qf32[:, 0:Tn], in1=ropc[:, 0:Tn], op=ALU.mult),
                                 R=[qf32.d, ropc.d], W=[qf32.d])
                            P.op(DVE, lambda e: e.tensor_tensor(out=qs32[:, 0:Tn], in0=qs32[:, 0:Tn], in1=rops[:, 0:Tn], op=ALU.mult),
                                 R=[qs32.d, rops.d], W=[qs32.d])
                            P.op(DVE, lambda e: e.tensor_tensor(out=qf32[:, 0:Tn], in0=qf32[:, 0:Tn], in1=qs32[:, 0:Tn], op=ALU.add),
                                 R=[qf32.d, qs32.d], W=[qf32.d])
                        P.op(ACT, lambda e, dst=dst, scale=scale: e.activation(out=dst[:, 0:Tn], in_=qf32[:, 0:Tn], func=AF.Copy, scale=scale),
                             R=[qf32.d], W=[dst.d])
                    P.dma(SP, g32[:, 0:Tn], rows(16), R=[dFT[si]], W=[g32.d])
                    P.dma(SP, v32[:, 0:nch, :], FTM[off:off + Tn, h * 64:(h + 1) * 64].rearrange("(c p) e -> p c e", p=128),
                          R=[dFTM[si]], W=[v32.d])
                    P.op(DVE, lambda e: e.tensor_copy(out=vb[:, 0:nch, :], in_=v32[:, 0:nch, :]), R=[v32.d], W=[vb.d])
                    if kind == 's':
                        P.dma(SP, Sf[:, :], st_ret[l, 0, h], W=[Sf.d])
                        P.dma(SP, Sb[:, :], st_ret[l, 1, h], W=[Sb.d])
                    else:
                        P.op(DVE, lambda e: e.memset(Sf[:, :], 0.0), W=[Sf.d])
                        P.op(DVE, lambda e: e.memset(Sb[:, :], 0.0), W=[Sb.d])

                    def state_update(S, c, kwcol, gcol):
                        pt = ps_tr.next()
                        P.op(PE, lambda e: e.transpose(pt[:, 0:64], kb[:, c * 128:(c + 1) * 128], cpb[0:64, 0:64]),
                             R=[kb.d, cpb.d], W=[pt.d])
                        kw = kw_r.next()
                        P.op(DVE, lambda e: e.tensor_scalar(out=kw[:, :], in0=pt[:, 0:64], scalar1=kwcol, scalar2=None, op0=ALU.mult),
                             R=[pt.d, cp.d], W=[kw.d])
                        pu = ps_up.next()
                        P.op(PE, lambda e: e.matmul(pu[0:64, 0:64], lhsT=kw[:, :], rhs=vb[:, c, :], start=True, stop=True),
                             R=[kw.d, vb.d], W=[pu.d])
                        P.op(DVE, lambda e: e.scalar_tensor_tensor(out=S[:, :], in0=S[:, :], scalar=gcol, in1=pu[0:64, 0:64],
                                                                   op0=ALU.mult, op1=ALU.add), R=[S.d, pu.d, cp.d], W=[S.d])
                    okf, _ = CPC['ret_kwf']
                    okb, _ = CPC['ret_kwb']
                    og, _ = CPC['ret_g128']
                    for c in range(nch - 1, -1, -1):
                        P.op(ACT, lambda e, c=c: e.activation(out=Sbs[:, c, :], in_=Sb[:, :], func=AF.Copy), R=[Sb.d], W=[Sbs.d])
                        state_update(Sb, c, cp[:, okb + h:okb + h + 1], cp[0:64, og + 4 + h:og + 5 + h])
                    if kind == 'p':
                        P.dma(SP, o_ret[sidx, l, 1, h], Sb[:, :], R=[Sb.d], W=[dOUT])
                    for c0 in range(0, nch, 4):
                        ncg = min(4, nch - c0)
                        n = ncg * 128
                        py = ps_y.next()
                        for cc in range(ncg):
                            c = c0 + cc
                            sl = slice(c * 128, (c + 1) * 128)
                            psc = ps_sc.next()
                            P.op(PE, lambda e: e.matmul(psc[:, 0:128], lhsT=kb[:, sl], rhs=qb[:, sl], start=True, stop=True),
                                 R=[kb.d, qb.d], W=[psc.d])
                            PT = PT_r.next()
                            P.op(DVE, lambda e: e.tensor_tensor(out=PT[:, :], in0=psc[:, 0:128], in1=C('ret_ds', sub=(h * 128, (h + 1) * 128)), op=ALU.mult),
                                 R=[psc.d, cp.d], W=[PT.d])
                            qfw = qd_r.next()
                            qbw = qd_r.next()
                            P.op(DVE, lambda e: e.tensor_tensor(out=qfw[:, :], in0=qb[:, sl], in1=C('ret_df', rows=64, sub=(h * 128, (h + 1) * 128)), op=ALU.mult),
                                 R=[qb.d, cp.d], W=[qfw.d])
                            P.op(DVE, lambda e: e.tensor_tensor(out=qbw[:, :], in0=qb[:, sl], in1=C('ret_db', rows=64, sub=(h * 128, (h + 1) * 128)), op=ALU.mult),
                                 R=[qb.d, cp.d], W=[qbw.d])
                            P.op(ACT, lambda e: e.activation(out=Sfb[:, :], in_=Sf[:, :], func=AF.Copy), R=[Sf.d], W=[Sfb.d])
                            yo = py[0:64, cc * 128:(cc + 1) * 128]
                            P.op(PE, lambda e: e.matmul(yo, lhsT=vb[:, c, :], rhs=PT[:, :], start=True, stop=False), R=[vb.d, PT.d], W=[py.d])
                            P.op(PE, lambda e: e.matmul(yo, lhsT=Sfb[:, :], rhs=qfw[:, :], start=False, stop=False), R=[Sfb.d, qfw.d], W=[py.d])
                            P.op(PE, lambda e: e.matmul(yo, lhsT=Sbs[:, c, :], rhs=qbw[:, :], start=False, stop=True), R=[Sbs.d, qbw.d], W=[py.d])
                            state_update(Sf, c, cp[:, okf + h:okf + h + 1], cp[0:64, og + h:og + h + 1])
                        tsl = slice(c0 * 128, c0 * 128 + n)
                        P.op(ACT, lambda e: e.activation(out=sqy[:, 0:n], in_=py[0:64, 0:n], func=AF.Square), R=[py.d], W=[sqy.d])
                        pss = ps_ss.next()
                        P.op(PE, lambda e: e.matmul(pss[0:64, 0:n], lhsT=C('ones', rows=64, sub=(0, 64)), rhs=sqy[:, 0:n], start=True, stop=True),
                             R=[sqy.d, cp.d], W=[pss.d])
                        P.op(ACT, lambda e: e.activation(out=rsy[:, 0:n], in_=pss[0:64, 0:n], func=AF.Sqrt, bias=C('eps', rows=64), scale=1.0 / 64),
                             R=[pss.d, cp.d], W=[rsy.d])
                        P.op(DVE, lambda e: e.reciprocal(out=rsy[:, 0:n], in_=rsy[:, 0:n]), R=[rsy.d], W=[rsy.d])
                        P.op(ACT, lambda e: e.activation(out=sg[:, 0:n], in_=g32[:, tsl], func=AF.Silu), R=[g32.d], W=[sg.d])
                        P.op(DVE, lambda e: e.scalar_tensor_tensor(out=tmy[:, 0:n], in0=py[0:64, 0:n], scalar=pp[l][0:64, og4 + h:og4 + h + 1],
                                                                   in1=rsy[:, 0:n], op0=ALU.mult, op1=ALU.mult),
                             R=[py.d, pp[l].d, rsy.d], W=[tmy.d])
                        oy = oy_r.next()
                        P.op(DVE, lambda e: e.tensor_tensor(out=oy[:, 0:n], in0=tmy[:, 0:n], in1=sg[:, 0:n], op=ALU.mult),
                             R=[tmy.d, sg.d], W=[oy.d])
                        P.dma(ACT, OT[256 + h * 64:256 + (h + 1) * 64, off + c0 * 128: off + c0 * 128 + n], oy[:, 0:n], R=[oy.d], W=[dOT[si]])
                    if kind == 'p':
                        P.dma(SP, o_ret[sidx, l, 0, h], Sf[:, :], R=[Sf.d], W=[dOUT])

        for l in range(L):
            if debug == 'M':
                break
            xin, xout = XT[0], XT[1]
            dxin, dxout = dXT[0], dXT[1]
            P.begin()
            wfm = T(P.sb("wfm", [128, 8, NFM * 128], BF16))
            wtm = T(P.sb("wtm", [128, 8, NTM], BF16))
            for k in range(8):
                P.dma(POOL, wfm[:, k, :], w_in_fm[l, k * 128:(k + 1) * 128, :], W=[wfm.d])
            P.dma(POOL, wtm[:, :, :], w_in_tm[l].rearrange("(k p) n -> p k n", p=128), W=[wtm.d])
            xt_r = Ring([T(P.sb("xt", [128, 8, 512])) for _ in range(2)])
            xtm_r = Ring([T(P.sb("xtm", [128, 1024])) for _ in range(2)])
            sq = Ring([T(P.sb("sq", [128, 512])) for _ in range(2)])
            rs = T(P.sb("rs", [128, 512]))
            tmp_r = Ring([T(P.sb("tmp", [128, 512])) for _ in range(2)])
            hT_r = Ring([T(P.sb("hT", [128, 8, 512], BF16)) for _ in range(2)])
            stg_r = Ring([T(P.sb("stg", [128, 4, 512])) for _ in range(2)])
            stm_r = Ring([T(P.sb("stm", [128, NTM])) for _ in range(2)])
            ps_r = Ring([TPS(P.psum("psA", [128, 512])) for _ in range(7)])
            for si, (kind, sidx, Tn, off) in enumerate(SEQS):
                if debug == 'A1':
                    break
                cond = 0 if kind == 'p' else 1
                nt = min(512, Tn)
                for t0 in range(0, Tn, nt):
                    n = nt
                    xt = xt_r.next()
                    if l == 0:
                        for b in range(n // 128):
                            xtm = xtm_r.next()
                            P.dma(SP, xtm[:, :], x_tm[off + t0 + b * 128: off + t0 + (b + 1) * 128, :], W=[xtm.d])
                            for half in range(2):
                                ps = ps_r.next()
                                for jj in range(4):
                                    j = half * 4 + jj
                                    P.op(PE, lambda e, j=j, jj=jj, ps=ps, xtm=xtm: e.transpose(
                                        ps[:, jj * 128:(jj + 1) * 128], xtm[:, j * 128:(j + 1) * 128], C('ident')),
                                        R=[xtm.d, cp.d], W=[ps.d])
                                evac(xt[:, half * 4:half * 4 + 4, b * 128:(b + 1) * 128],
                                     ps[:, 0:512].rearrange("p (j t) -> p j t", t=128), [ps.d], [xt.d])
                        P.dma(POOL, xin[:, off + t0: off + t0 + n].rearrange("(j p) t -> p j t", p=128), xt[:, :, 0:n],
                              R=[xt.d], W=[dxin[si]])
                    else:
                        P.dma(SP, xt[:, :, 0:n], xin[:, off + t0: off + t0 + n].rearrange("(j p) t -> p j t", p=128),
                              R=[dxin[si]], W=[xt.d])
                    if debug == 'A2':
                        continue
                    hT = hT_r.next()
                    rmsnorm_tile(xt, n, l, 0, cond, hT, sq, ps_r, rs, tmp_r)
                    if debug == 'A3':
                        continue
                    chunks = list(range(NFM))
                    if kind == 'p':
                        chunks = [c for c in chunks if c not in (12, 13, 14, 15, 28, 29, 30, 31)]
                    groups = [chunks[i:i + 4] for i in range(0, len(chunks), 4)]
                    for grp in groups:
                        runs = []
                        for c in grp:
                            if runs and runs[-1][-1] == c - 1:
                                runs[-1].append(c)
                            else:
                                runs.append([c])
                        for run_ in runs:
                            stg = stg_r.next()
                            for ci, c in enumerate(run_):
                                ps = ps_r.next()
                                for k in range(8):
                                    P.op(PE, lambda e, k=k, c=c, ps=ps, hT=hT: e.matmul(
                                        ps[:, 0:n], lhsT=wfm[:, k, c * 128:(c + 1) * 128], rhs=hT[:, k, 0:n],
                                        start=(k == 0), stop=(k == 7)), R=[wfm.d, hT.d], W=[ps.d])
                                evac(stg[:, ci, 0:n], ps[:, 0:n], [ps.d], [stg.d])
                            c0, nc_ = run_[0], len(run_)
                            P.dma(ACT, FT[c0 * 128:(c0 + nc_) * 128, off + t0: off + t0 + n].rearrange("(c p) t -> p c t", p=128),
                                  stg[:, 0:nc_, 0:n], R=[stg.d], W=[dFT[si]])
                    if debug == 'A4':
                        continue
                    for b in range(n // 128):
                        stm = stm_r.next()
                        for (c0, c1) in ((0, 512), (512, NTM)):
                            ps = ps_r.next()
                            for k in range(8):
                                P.op(PE, lambda e, k=k, ps=ps, hT=hT, b=b, c0=c0, c1=c1: e.matmul(
                                    ps[:, 0:c1 - c0], lhsT=hT[:, k, b * 128:(b + 1) * 128], rhs=wtm[:, k, c0:c1],
                                    start=(k == 0), stop=(k == 7)), R=[wtm.d, hT.d], W=[ps.d])
                            evac(stm[:, c0:c1], ps[:, 0:c1 - c0], [ps.d], [stm.d])
                        r0 = off + t0 + b * 128
                        P.dma(ACT, FTM[r0:r0 + 128, :], stm[:, :], R=[stm.d], W=[dFTM[si]])
                        if kind == 'p' and debug != 'A5':
                            tl = t0 + b * 128
                            P.dma(ACT, o_v[sidx, l, tl:tl + 128, :], stm[:, 256:512], R=[stm.d], W=[Dep()] if debug == 'A6' else [dOUT])
                            P.dma(ACT, o_k[sidx, l, tl:tl + 128, :], stm[:, 512:768], R=[stm.d], W=[Dep()] if debug == 'A6' else [dOUT])
            P.end()
            if debug and debug[0] == 'A':
                break

            if not (debug and debug[0] == 'C'):
                phase_B(l)
            if debug == 'B':
                break

            P.begin()
            wo = T(P.sb("wo", [128, 8, D], BF16))
            P.dma(POOL, wo[:, :, :], w_out[l].rearrange("(k p) n -> p k n", p=128), W=[wo.d])
            xt_r = Ring([T(P.sb("xt", [128, 8, 512])) for _ in range(2)])
            ot_r = Ring([T(P.sb("ot", [128, 8, 512], BF16)) for _ in range(2)])
            ps_r = Ring([TPS(P.psum("psC", [128, 512])) for _ in range(6)])
            for si, (kind, sidx, Tn, off) in enumerate(SEQS):
                cond = 0 if kind == 'p' else 1
                nt = min(512, Tn)
                for t0 in range(0, Tn, nt):
                    n = nt
                    xt = xt_r.next()
                    ot = ot_r.next()
                    P.dma(SP, xt[:, :, 0:n], xin[:, off + t0: off + t0 + n].rearrange("(j p) t -> p j t", p=128),
                          R=[dxin[si]], W=[xt.d])
                    P.dma(POOL if (debug and debug[0] == 'C') else SP, ot[:, :, 0:n], OT[:, off + t0: off + t0 + n].rearrange("(j p) t -> p j t", p=128),
                          R=[dOT[si]], W=[ot.d])
                    for j in range(8):
                        ps = ps_r.next()
                        for k in range(8):
                            P.op(PE, lambda e, k=k, j=j, ps=ps, ot=ot: e.matmul(
                                ps[:, 0:n], lhsT=wo[:, k, j * 128:(j + 1) * 128], rhs=ot[:, k, 0:n],
                                start=(k == 0), stop=(k == 7)), R=[wo.d, ot.d], W=[ps.d])
                        P.op(DVE, lambda e, j=j, ps=ps, xt=xt: e.scalar_tensor_tensor(
                            out=xt[:, j, 0:n], in0=ps[:, 0:n], scalar=modcol(l, 2, j, cond), in1=xt[:, j, 0:n],
                            op0=ALU.mult, op1=ALU.add), R=[ps.d, xt.d, modv[l].d], W=[xt.d])
                    P.dma(ACT, xout[:, off + t0: off + t0 + n].rearrange("(j p) t -> p j t", p=128), xt[:, :, 0:n],
                          R=[xt.d], W=[dxout[si]])
            P.end()
            if debug == 'C1':
                break
            if debug == 'CA':
                continue
            P.begin()
            wu = T(P.sb("wu", [128, 8, 2 * FFN], BF16))
            wd = T(P.sb("wd", [128, 22, D], BF16))
            for k in range(8):
                P.dma(POOL, wu[:, k, :], w_fup[l, k * 128:(k + 1) * 128, :], W=[wu.d])
            for g in range(22):
                P.dma(POOL, wd[:, g, :], w_fdn[l, g * 128:(g + 1) * 128, :], W=[wd.d])
            xt_r = Ring([T(P.sb("xt", [128, 8, 384])) for _ in range(1)])
            sq = Ring([T(P.sb("sq", [128, 384])) for _ in range(2)])
            rs = T(P.sb("rs", [128, 384]))
            tmp_r = Ring([T(P.sb("tmp", [128, 384])) for _ in range(1)])
            tg_r = Ring([T(P.sb("tg", [128, 384])) for _ in range(1)])
            tv_r = Ring([T(P.sb("tv", [128, 384])) for _ in range(1)])
            hT_r = Ring([T(P.sb("hT", [128, 8, 384], BF16)) for _ in range(1)])
            aT = T(P.sb("aT", [128, 22, 384], BF16))
            ps_r = Ring([TPS(P.psum("psC", [128, 384])) for _ in range(7)])
            ocw, _ = PPC['fcw']
            ocb, _ = PPC['fcb']
            for si, (kind, sidx, Tn, off) in enumerate(SEQS):
                cond = 0 if kind == 'p' else 1
                nt = min(382, Tn)
                for t0 in range(0, Tn, nt):
                    n = min(nt, Tn - t0)
                    N = n + 2
                    lo, hi = max(t0 - 1, 0), min(t0 + n + 1, Tn)
                    c_lo = lo - (t0 - 1)
                    xt = xt_r.next()
                    P.dma(SP, xt[:, :, c_lo:c_lo + hi - lo], xout[:, off + lo: off + hi].rearrange("(j p) t -> p j t", p=128),
                          R=[dxout[si]], W=[xt.d])
                    hT = hT_r.next()
                    rmsnorm_tile(xt, N, l, 1, cond, hT, sq, ps_r, rs, tmp_r)
                    a0 = 1 if t0 == 0 else 0
                    b1 = n - 1 if t0 + n == Tn else n
                    for g in range(22):
                        tt = []
                        for ci, (c, ring) in enumerate(((g, tg_r), (22 + g, tv_r))):
                            ps = ps_r.next()
                            for k in range(8):
                                P.op(PE, lambda e, k=k, c=c, ps=ps, hT=hT: e.matmul(
                                    ps[:, 0:N], lhsT=wu[:, k, c * 128:(c + 1) * 128], rhs=hT[:, k, 0:N],
                                    start=(k == 0), stop=(k == 7)), R=[wu.d, hT.d], W=[ps.d])
                            t = ring.next()
                            P.op(ACT, lambda e, c=c, ps=ps, t=t: e.activation(
                                out=t[:, 0:n], in_=ps[:, 1:n + 1], func=AF.Identity,
                                bias=pp[l][:, ocb + c:ocb + c + 1], scale=pp[l][:, ocw + 44 + c:ocw + 44 + c + 1]),
                                R=[ps.d, pp[l].d], W=[t.d])
                            P.op(DVE, lambda e, c=c, ps=ps, t=t: e.scalar_tensor_tensor(
                                out=t[:, a0:n], in0=ps[:, a0:n], scalar=pp[l][:, ocw + c:ocw + c + 1], in1=t[:, a0:n],
                                op0=ALU.mult, op1=ALU.add), R=[ps.d, pp[l].d, t.d], W=[t.d])
                            P.op(DVE, lambda e, c=c, ps=ps, t=t: e.scalar_tensor_tensor(
                                out=t[:, 0:b1], in0=ps[:, 2:b1 + 2], scalar=pp[l][:, ocw + 88 + c:ocw + 88 + c + 1], in1=t[:, 0:b1],
                                op0=ALU.mult, op1=ALU.add), R=[ps.d, pp[l].d, t.d], W=[t.d])
                            tt.append(t)
                        tg, tv = tt
                        P.op(ACT, lambda e, tg=tg: e.activation(out=tg[:, 0:n], in_=tg[:, 0:n], func=AF.Silu), R=[tg.d], W=[tg.d])
                        P.op(DVE, lambda e, tg=tg, tv=tv, g=g: e.tensor_tensor(out=aT[:, g, 0:n], in0=tg[:, 0:n], in1=tv[:, 0:n], op=ALU.mult),
                             R=[tg.d, tv.d], W=[aT.d])
                    for j in range(8):
                        ps = ps_r.next()
                        for g in range(22):
                            P.op(PE, lambda e, g=g, j=j, ps=ps: e.matmul(
                                ps[:, 0:n], lhsT=wd[:, g, j * 128:(j + 1) * 128], rhs=aT[:, g, 0:n],
                                start=(g == 0), stop=(g == 21)), R=[wd.d, aT.d], W=[ps.d])
                        P.op(DVE, lambda e, j=j, ps=ps, xt=xt: e.scalar_tensor_tensor(
                            out=xt[:, j, 1:n + 1], in0=ps[:, 0:n], scalar=modcol(l, 5, j, cond), in1=xt[:, j, 1:n + 1],
                            op0=ALU.mult, op1=ALU.add), R=[ps.d, xt.d, modv[l].d], W=[xt.d])
                    P.dma(ACT, xin[:, off + t0: off + t0 + n].rearrange("(j p) t -> p j t", p=128), xt[:, :, 1:n + 1],
                          R=[xt.d], W=[dxin[si]])
            P.end()
            if debug == 'C':
                break

        if not debug:
            xin, dxin = XT[0], dXT[0]
            P.begin()
            xt_r = Ring([T(P.sb("xt", [128, 8, 512])) for _ in range(2)])
            sq = Ring([T(P.sb("sq", [128, 512])) for _ in range(2)])
            rs = T(P.sb("rs", [128, 512]))
            ytm_r = Ring([T(P.sb("ytm", [128, 1024])) for _ in range(2)])
            ps_r = Ring([TPS(P.psum("psF", [128, 512])) for _ in range(6)])
            onf, _ = PPC['nfg']
            for si, (kind, sidx, Tn, off) in enumerate(SEQS):
                nt = min(512, Tn)
                for t0 in range(0, Tn, nt):
                    n = nt
                    xt = xt_r.next()
                    P.dma(SP, xt[:, :, 0:n], xin[:, off + t0: off + t0 + n].rearrange("(j p) t -> p j t", p=128),
                          R=[dxin[si]], W=[xt.d])
                    ps = ps_r.next()
                    for k in range(8):
                        sqk = sq.next()
                        P.op(ACT, lambda e, k=k, sqk=sqk, xt=xt: e.activation(out=sqk[:, 0:n], in_=xt[:, k, 0:n], func=AF.Square), R=[xt.d], W=[sqk.d])
                        P.op(PE, lambda e, k=k, sqk=sqk, ps=ps: e.matmul(ps[:, 0:n], lhsT=C('ones'), rhs=sqk[:, 0:n], start=(k == 0), stop=(k == 7)),
                             R=[sqk.d, cp.d], W=[ps.d])
                    P.op(ACT, lambda e, ps=ps: e.activation(out=rs[:, 0:n], in_=ps[:, 0:n], func=AF.Sqrt, bias=C('eps'), scale=1.0 / D),
                         R=[ps.d, cp.d], W=[rs.d])
                    P.op(DVE, lambda e: e.reciprocal(out=rs[:, 0:n], in_=rs[:, 0:n]), R=[rs.d], W=[rs.d])
                    for j in range(8):
                        P.op(DVE, lambda e, j=j, xt=xt: e.scalar_tensor_tensor(
                            out=xt[:, j, 0:n], in0=xt[:, j, 0:n], scalar=pp[0][:, onf + j:onf + j + 1], in1=rs[:, 0:n],
                            op0=ALU.mult, op1=ALU.mult), R=[xt.d, pp[0].d, rs.d], W=[xt.d])
                    for b in range(n // 128):
                        ytm = ytm_r.next()
                        for half in range(2):
                            ps = ps_r.next()
                            for jj in range(4):
                                j = half * 4 + jj
                                P.op(PE, lambda e, j=j, jj=jj, ps=ps, xt=xt, b=b: e.transpose(
                                    ps[:, jj * 128:(jj + 1) * 128], xt[:, j, b * 128:(b + 1) * 128], C('ident')),
                                    R=[xt.d, cp.d], W=[ps.d])
                            evac(ytm[:, half * 512:(half + 1) * 512], ps[:, 0:512], [ps.d], [ytm.d])
                        tl = t0 + b * 128
                        if kind == 'p':
                            P.dma(ACT, y_p[sidx * TP + tl: sidx * TP + tl + 128, :], ytm[:, :], R=[ytm.d], W=[dOUT])
                        else:
                            P.dma(ACT, y_s[tl:tl + 128, :], ytm[:, :], R=[ytm.d], W=[dOUT])
            P.end()
        P.begin()
        for k, v in list(P.dtot.items()):
            P.q[POOL].append(([(k, v)], None, None))
        P.end()
    return nc


def make_in_maps(inp):
    fm, tm = build_perm()
    w_in = np.asarray(inp['w_in'], np.float32)
    w_aug = np.concatenate([w_in, np.zeros((L, D, 1), np.float32)], axis=2)
    w_in_fm = np.ascontiguousarray(w_aug[:, :, np.where(fm < 0, w_in.shape[2], fm)])
    w_in_tm = np.ascontiguousarray(w_in[:, :, tm])
    pp = np.stack([pack_params(inp, l) for l in range(L)])
    cp = build_consts()
    rope = rope_tables()
    maps = []
    for c in range(8):
        b = c // 4
        x_tm = np.concatenate([np.asarray(inp['x_prompt'][4 * c:4 * c + 4], np.float32).reshape(NPS * TP, D),
                               np.asarray(inp['x_sample'][b], np.float32)], axis=0)
        cvec = np.concatenate([fm_cols(inp['c_ctx']), fm_cols(inp['c'][b])], axis=1)
        maps.append(dict(
            x_tm=np.ascontiguousarray(x_tm), cvec=np.ascontiguousarray(cvec), w_mod=np.asarray(inp['w_mod'], np.float32),
            w_in_fm=w_in_fm, w_in_tm=w_in_tm, w_out=np.asarray(inp['w_out'], np.float32),
            w_fup=np.asarray(inp['ffn_w_up'], np.float32), w_fdn=np.asarray(inp['ffn_w_down'], np.float32),
            pp=pp, cp=cp, rope=rope,
            st_rwkv=np.ascontiguousarray(inp['state_rwkv'][b]), st_ret=np.ascontiguousarray(inp['state_ret'][b]),
            st_ssd=np.ascontiguousarray(inp['state_ssd'][b]),
            ck=np.ascontiguousarray(np.asarray(inp['cache_diff_k'][b]).reshape(L, 512, 256)),
            cv=np.ascontiguousarray(np.asarray(inp['cache_diff_v'][b]).reshape(L, 512, 256)),
        ))
    return maps


def kernel(**inputs):
    inp = {k: np.asarray(v) for k, v in inputs.items()}
    nc = build()
    maps = make_in_maps(inp)
    res = run_bass_kernel_spmd(nc, maps, core_ids=list(range(8)))
    r = res.results
    y_prompt = np.concatenate([r[c]['y_p'].reshape(NPS, TP, D) for c in range(8)], axis=0)
    y_sample = np.stack([np.concatenate([r[b * 4 + j]['y_s'][j * 1024:(j + 1) * 1024] for j in range(4)], axis=0) for b in range(2)])
    o_rwkv = np.concatenate([r[c]['o_rwkv'] for c in range(8)], axis=0)
    o_ret = np.concatenate([r[c]['o_ret'] for c in range(8)], axis=0)
    o_ssd = np.concatenate([r[c]['o_ssd'] for c in range(8)], axis=0)
    o_k = np.concatenate([r[c]['o_k'] for c in range(8)], axis=0).reshape(32, L, TP, 4, 2, 32)
    o_v = np.concatenate([r[c]['o_v'] for c in range(8)], axis=0).reshape(32, L, TP, 4, 64)
    return (y_prompt.astype(np.float32), y_sample.astype(np.float32), o_rwkv.astype(np.float32), o_ret.astype(np.float32),
            o_ssd.astype(np.float32), o_k.astype(np.float32), o_v.astype(np.float32))
```
